# Optimizing a Trainium2 kernel written in Bass

```python
import jax, jax.numpy as jnp
from jax import lax
import numpy as np

D_MODEL = 1024
BATCH = 2
SEQ = 8192
DEPTH = 1
DEC_BATCH = 32
DEC_SEQ = 8
PAST_LEN = 16384
PAGE_SIZE = 128

ATT_GROUPS = ((128, 1), (512, 4), (2048, 16))
N_GROUPS = 3
ATT_HEADS = 8
ATT_HEAD_DIM = 64
ATT_WIDTH = ATT_HEADS * ATT_HEAD_DIM
WIN_STEPS = 128
ATT_SCALE = ATT_HEAD_DIM ** -0.5
N_BUCKETS = 32
MAX_DISTANCE = 2048
N_BIAS_HEADS = N_GROUPS * ATT_HEADS
M_HEADS = 4
M_WIDTH = D_MODEL
M_V_DIM = M_WIDTH // M_HEADS
M_QK_DIM = M_V_DIM // 2
CONV_WIDTH = 4
M_CHUNK = 64
EPS = 1e-6
PROJ_SIZES = (N_GROUPS * ATT_WIDTH, N_GROUPS * ATT_WIDTH, N_GROUPS * ATT_WIDTH, ATT_WIDTH,
              M_WIDTH, M_WIDTH, M_WIDTH, M_HEADS, M_HEADS, D_MODEL, D_MODEL)
PROJ_WIDTH = sum(PROJ_SIZES)

kernel_name = "dilated_attn_mlstm_gated_hybrid_step"


def _t5_bucket(dist):
    n = np.asarray(dist).astype(np.int64)
    max_exact = N_BUCKETS // 2
    nf = np.maximum(n, 1).astype(np.float32)
    large = max_exact + (np.log(nf / max_exact) / np.log(np.float32(MAX_DISTANCE / max_exact))
                         * (N_BUCKETS - max_exact)).astype(np.int64)
    large = np.minimum(large, N_BUCKETS - 1)
    return np.where(n < max_exact, n, large).astype(np.int32)


def _rmsnorm(x, gain):
    x32 = x.astype(jnp.float32)
    return x32 * lax.rsqrt(jnp.mean(x32 * x32, -1, keepdims=True) + EPS) * gain


def _head_norm(h, weight):
    mu = jnp.mean(h, -1, keepdims=True)
    hc = h - mu
    hn = hc * lax.rsqrt(jnp.mean(hc * hc, -1, keepdims=True) + EPS)
    B, S = h.shape[:2]
    return hn.reshape(B, S, -1) * weight


def _dilated_prompt(q, k, v, tbl, d):
    f32 = jnp.float32
    B, S, H, E = q.shape
    U = S // d
    nb = -(-U // WIN_STEPS)
    Up = nb * WIN_STEPS

    def res(t):
        t = t.astype(f32).reshape(B, U, d, H, E)
        return jnp.pad(t, ((0, 0), (0, Up - U), (0, 0), (0, 0), (0, 0)))

    def band(t):
        tp = jnp.pad(res(t), ((0, 0), (WIN_STEPS, 0), (0, 0), (0, 0), (0, 0)))
        prev = tp[:, :Up].reshape(B, nb, WIN_STEPS, d, H, E)
        cur = tp[:, WIN_STEPS:].reshape(B, nb, WIN_STEPS, d, H, E)
        return jnp.concatenate([prev, cur], axis=2)

    qr = res(q).reshape(B, nb, WIN_STEPS, d, H, E)
    kb, vb = band(k), band(v)
    qi = np.arange(WIN_STEPS)[:, None]
    kj = np.arange(2 * WIN_STEPS)[None, :]
    dist = qi - kj + WIN_STEPS
    in_band = (dist >= 0) & (dist <= WIN_STEPS)
    bucket = _t5_bucket(np.clip(dist, 0, WIN_STEPS) * d)
    bias = jnp.transpose(tbl[bucket], (2, 0, 1)).astype(f32)
    key_u = np.arange(nb)[:, None] * WIN_STEPS - WIN_STEPS + kj
    mask = in_band[None] & (key_u >= 0)[:, None, :]
    logits = jnp.einsum('bnqrhe,bnkrhe->bnrhqk', qr, kb) * ATT_SCALE + bias
    logits = jnp.where(mask[None, :, None, None], logits, -jnp.inf)
    mx = jnp.max(logits, -1, keepdims=True)
    p = jnp.exp(logits - mx)
    s = jnp.sum(p, -1)
    o = jnp.einsum('bnrhqk,bnkrhe->bnqrhe', p, vb) / jnp.transpose(s, (0, 1, 4, 2, 3))[..., None]
    lse = jnp.transpose(mx[..., 0] + jnp.log(s), (0, 1, 4, 2, 3))
    o = o.reshape(B, Up, d, H, E)[:, :U].reshape(B, S, H, E)
    lse = lse.reshape(B, Up, d, H)[:, :U].reshape(B, S, H)
    return o, lse


def _dilated_sample(q, k, v, buf, tbl, d):
    f32 = jnp.float32
    DB, T, H, E = q.shape
    Lb = buf.shape[1]
    kc = jnp.concatenate([buf[:, :, 0].astype(f32), k.astype(f32)], axis=1)
    vc = jnp.concatenate([buf[:, :, 1].astype(f32), v.astype(f32)], axis=1)
    J = WIN_STEPS + 1
    idx = Lb + np.arange(T)[:, None] - d * np.arange(J)[None, :]
    valid = idx >= 0
    idx_c = np.maximum(idx, 0)
    kg = kc[:, idx_c]
    vg = vc[:, idx_c]
    bias = jnp.transpose(tbl[_t5_bucket(d * np.arange(J))], (1, 0)).astype(f32)
    logits = jnp.einsum('bthe,btjhe->bhtj', q.astype(f32), kg) * ATT_SCALE + bias[None, :, None, :]
    logits = jnp.where(valid[None, None], logits, -jnp.inf)
    mx = jnp.max(logits, -1, keepdims=True)
    p = jnp.exp(logits - mx)
    s = jnp.sum(p, -1)
    o = jnp.einsum('bhtj,btjhe->bthe', p, vg) / jnp.transpose(s, (0, 2, 1))[..., None]
    lse = jnp.transpose(mx[..., 0] + jnp.log(s), (0, 2, 1))
    new_buf = jnp.concatenate([buf.astype(f32), jnp.stack([k.astype(f32), v.astype(f32)], axis=2)], axis=1)[:, T:]
    return o, lse, new_buf


def _mlstm(q, k, v, i_pre, f_pre, C0, n0, m0, chunk):
    f32 = jnp.float32
    B, S, NH, DK = q.shape
    DV = v.shape[-1]
    nc = S // chunk
    q = q.astype(f32) * (DK ** -0.5)
    logf = jax.nn.log_sigmoid(f_pre.astype(f32))

    def blocks(t):
        return jnp.swapaxes(t.astype(f32).reshape((B, nc, chunk) + t.shape[2:]), 0, 1).swapaxes(2, 3)

    xs = (blocks(q), blocks(k), blocks(v), blocks(i_pre), blocks(logf))
    causal = np.tril(np.ones((chunk, chunk), dtype=bool))

    def step(carry, inp):
        C, n, m = carry
        qc, kc, vc, ic, lfc = inp
        b = jnp.cumsum(lfc, axis=-1)
        Dm = jnp.where(causal, b[..., :, None] - b[..., None, :] + ic[..., None, :], -jnp.inf)
        inter = b + m[..., None]
        m_t = jnp.maximum(jnp.max(Dm, -1), inter)
        w_intra = jnp.exp(Dm - m_t[..., None])
        w_inter = jnp.exp(inter - m_t)
        sc = jnp.einsum('bhtk,bhsk->bhts', qc, kc) * w_intra
        num = jnp.einsum('bhts,bhsv->bhtv', sc, vc) + w_inter[..., None] * jnp.einsum('bhvk,bhtk->bhtv', C, qc)
        den = jnp.sum(sc, -1) + w_inter * jnp.einsum('bhk,bhtk->bht', n, qc)
        h = num / jnp.maximum(jnp.abs(den), jnp.exp(-m_t))[..., None]
        bL = b[..., -1]
        g = bL[..., None] - b + ic
        m_new = jnp.maximum(bL + m, jnp.max(g, -1))
        wk = jnp.exp(g - m_new[..., None])
        wC = jnp.exp(bL + m - m_new)
        C = wC[..., None, None] * C + jnp.einsum('bhs,bhsv,bhsk->bhvk', wk, vc, kc)
        n = wC[..., None] * n + jnp.einsum('bhs,bhsk->bhk', wk, kc)
        return (C, n, m_new), h

    (C, n, m), hs = lax.scan(step, (C0.astype(f32), n0.astype(f32), m0.astype(f32)), xs)
    h = jnp.transpose(hs, (1, 0, 3, 2, 4)).reshape(B, S, NH, DV)
    return h, C, n, m


def _layer(x, c, lp, rel_table, conv_prev, C0, n0, m0, kv_bufs, chunk):
    (norm_gain, w_ada, b_ada, w_in, b_if, conv_w, conv_b, w_mq, w_mk,
     m_norm, m_skip, w_pa, w_pm, w_out) = lp
    f32 = jnp.float32
    B, S, _ = x.shape
    ada = jax.nn.silu(c.astype(f32)) @ w_ada + b_ada
    shift, scale, gate = jnp.split(ada, 3, axis=-1)
    h = _rmsnorm(x, norm_gain) * (1.0 + scale[:, None]) + shift[:, None]
    proj = h @ w_in
    offs = [int(o) for o in np.cumsum(PROJ_SIZES)[:-1]]
    q_a, k_a, v_a, z_a, x_m, z_m, o_m, i_pre, f_pre, g_a, g_m = jnp.split(proj, offs, axis=-1)
    grp = (B, S, N_GROUPS, ATT_HEADS, ATT_HEAD_DIM)
    q_a, k_a, v_a = q_a.reshape(grp), k_a.reshape(grp), v_a.reshape(grp)

    outs, lses, new_bufs = [], [], []
    for g, (win, dil) in enumerate(ATT_GROUPS):
        tbl = rel_table[:, g * ATT_HEADS:(g + 1) * ATT_HEADS]
        if kv_bufs is None:
            o, lse = _dilated_prompt(q_a[:, :, g], k_a[:, :, g], v_a[:, :, g], tbl, dil)
            keep = min(win, S)
            buf = jnp.stack([k_a[:, S - keep:, g], v_a[:, S - keep:, g]], axis=2)
        else:
            o, lse, buf = _dilated_sample(q_a[:, :, g], k_a[:, :, g], v_a[:, :, g], kv_bufs[g], tbl, dil)
        outs.append(o)
        lses.append(lse)
        new_bufs.append(buf)
    alpha = jax.nn.softmax(jnp.stack(lses, 0), axis=0)
    o_att = jnp.sum(alpha[..., None] * jnp.stack(outs, 0), axis=0).reshape(B, S, ATT_WIDTH)
    a_branch = (o_att * jax.nn.silu(z_a)) @ w_pa

    conv_in = jnp.concatenate([conv_prev.astype(f32), x_m], axis=1)
    conv = conv_b + sum(conv_w[j] * conv_in[:, j:j + S] for j in range(CONV_WIDTH))
    c_act = jax.nn.silu(conv)
    new_conv = conv_in[:, S:]
    ch = c_act.reshape(B, S, M_HEADS, M_V_DIM)
    mq = jnp.einsum('bshe,hed->bshd', ch, w_mq)
    mk = jnp.einsum('bshe,hed->bshd', ch, w_mk)
    mv = x_m.reshape(B, S, M_HEADS, M_V_DIM)
    hcell, C, n, m = _mlstm(mq, mk, mv, i_pre + b_if[:M_HEADS], f_pre + b_if[M_HEADS:], C0, n0, m0, chunk)
    m_out = (jax.nn.sigmoid(o_m) * _head_norm(hcell, m_norm) + m_skip * c_act) * jax.nn.silu(z_m)
    m_branch = m_out @ w_pm

    merged = jax.nn.sigmoid(g_a) * a_branch + jax.nn.sigmoid(g_m) * m_branch
    y = x + gate[:, None] * (merged @ w_out)
    return y, new_bufs, new_conv, C, n, m


def setup_inputs(seed: int = 0) -> dict:
    key = jax.random.key(seed)
    ks = jax.random.split(key, 32)
    f32 = jnp.float32

    def nrm(k, shape, s):
        return jax.random.normal(k, shape, f32) * s

    lb = [min(w, PAST_LEN) for w, _ in ATT_GROUPS]
    kv_shape = lambda L: (DEPTH, DEC_BATCH, L, 2, ATT_HEADS, ATT_HEAD_DIM)
    b_i = nrm(ks[26], (DEPTH, M_HEADS), 0.1)
    b_f = 3.0 + jnp.linspace(0.0, 3.0, M_HEADS, dtype=f32)[None] + nrm(ks[27], (DEPTH, M_HEADS), 0.1)
    return {
        "x_prompt": nrm(ks[0], (BATCH, SEQ, D_MODEL), 1.0),
        "x_sample": nrm(ks[1], (DEC_BATCH, DEC_SEQ, D_MODEL), 1.0),
        "cache_kv_w128": nrm(ks[2], kv_shape(lb[0]), 1.0),
        "cache_kv_w512": nrm(ks[3], kv_shape(lb[1]), 1.0),
        "cache_kv_w2048": nrm(ks[4], kv_shape(lb[2]), 1.0),
        "state_conv": nrm(ks[5], (DEPTH, DEC_BATCH, CONV_WIDTH - 1, M_WIDTH), 1.0),
        "state_C": nrm(ks[6], (DEPTH, DEC_BATCH, M_HEADS, M_V_DIM, M_QK_DIM), 1.0),
        "state_n": nrm(ks[7], (DEPTH, DEC_BATCH, M_HEADS, M_QK_DIM), 1.0),
        "state_m": nrm(ks[8], (DEPTH, DEC_BATCH, M_HEADS), 0.5),
        "c_prompt": nrm(ks[9], (BATCH, D_MODEL), 1.0),
        "c_sample": nrm(ks[10], (DEC_BATCH, D_MODEL), 1.0),
        "rel_table": nrm(ks[11], (N_BUCKETS, N_BIAS_HEADS), 0.5),
        "norm_gain": 1.0 + nrm(ks[12], (DEPTH, D_MODEL), 0.1),
        "w_ada": nrm(ks[13], (DEPTH, D_MODEL, 3 * D_MODEL), 0.3 * D_MODEL ** -0.5),
        "b_ada": nrm(ks[14], (DEPTH, 3 * D_MODEL), 0.02),
        "w_in": nrm(ks[15], (DEPTH, D_MODEL, PROJ_WIDTH), D_MODEL ** -0.5),
        "b_if": jnp.concatenate([b_i, b_f], axis=-1),
        "conv_w": nrm(ks[16], (DEPTH, CONV_WIDTH, M_WIDTH), CONV_WIDTH ** -0.5),
        "conv_b": nrm(ks[17], (DEPTH, M_WIDTH), 0.02),
        "w_mq": nrm(ks[18], (DEPTH, M_HEADS, M_V_DIM, M_QK_DIM), M_V_DIM ** -0.5),
        "w_mk": nrm(ks[19], (DEPTH, M_HEADS, M_V_DIM, M_QK_DIM), M_V_DIM ** -0.5),
        "m_norm": 1.0 + nrm(ks[20], (DEPTH, M_WIDTH), 0.1),
        "m_skip": 1.0 + nrm(ks[21], (DEPTH, M_WIDTH), 0.1),
        "w_pa": nrm(ks[22], (DEPTH, ATT_WIDTH, D_MODEL), ATT_WIDTH ** -0.5),
        "w_pm": nrm(ks[23], (DEPTH, M_WIDTH, D_MODEL), M_WIDTH ** -0.5),
        "w_out": nrm(ks[24], (DEPTH, D_MODEL, D_MODEL), D_MODEL ** -0.5),
        "final_gain": 1.0 + nrm(ks[25], (D_MODEL,), 0.1),
    }


def reference(x_prompt, x_sample, cache_kv_w128, cache_kv_w512, cache_kv_w2048, state_conv, state_C,
              state_n, state_m, c_prompt, c_sample, rel_table, norm_gain, w_ada, b_ada, w_in, b_if,
              conv_w, conv_b, w_mq, w_mk, m_norm, m_skip, w_pa, w_pm, w_out, final_gain):
    f32 = jnp.float32
    B, S = x_prompt.shape[:2]
    T = x_sample.shape[1]
    names = ('kv128', 'kv512', 'kv2048', 'conv', 'C', 'n', 'm')
    new_p = {nm: [] for nm in names}
    new_s = {nm: [] for nm in names}
    xp, xs = x_prompt, x_sample
    for l in range(DEPTH):
        lp = (norm_gain[l], w_ada[l], b_ada[l], w_in[l], b_if[l], conv_w[l], conv_b[l], w_mq[l], w_mk[l],
              m_norm[l], m_skip[l], w_pa[l], w_pm[l], w_out[l])
        xp, bufs, cv, C, n, m = _layer(
            xp, c_prompt, lp, rel_table,
            jnp.zeros((B, CONV_WIDTH - 1, M_WIDTH), f32),
            jnp.zeros((B, M_HEADS, M_V_DIM, M_QK_DIM), f32),
            jnp.zeros((B, M_HEADS, M_QK_DIM), f32),
            jnp.zeros((B, M_HEADS), f32),
            None, min(M_CHUNK, S))
        for nm, val in zip(names, (bufs[0], bufs[1], bufs[2], cv, C, n, m)):
            new_p[nm].append(val)
        xs, bufs, cv, C, n, m = _layer(
            xs, c_sample, lp, rel_table, state_conv[l], state_C[l], state_n[l], state_m[l],
            (cache_kv_w128[l], cache_kv_w512[l], cache_kv_w2048[l]), T)
        for nm, val in zip(names, (bufs[0], bufs[1], bufs[2], cv, C, n, m)):
            new_s[nm].append(val)
    sp = {nm: jnp.stack(v, 0) for nm, v in new_p.items()}
    ss = {nm: jnp.stack(v, 0) for nm, v in new_s.items()}
    y_prompt = _rmsnorm(xp, final_gain).astype(x_prompt.dtype)
    y_sample = _rmsnorm(xs, final_gain).astype(x_sample.dtype)
    return (y_prompt, y_sample, sp['kv128'], ss['kv128'], sp['kv512'], ss['kv512'], sp['kv2048'], ss['kv2048'],
            sp['conv'], ss['conv'], sp['C'], ss['C'], sp['n'], ss['n'], sp['m'], ss['m'])
```

```python
import numpy as np
import concourse.bass as bass
import concourse.mybir as mybir
from concourse.bass_utils import run_bass_kernel_spmd

F32 = mybir.dt.float32
BF16 = mybir.dt.bfloat16
AF = mybir.ActivationFunctionType
ALU = mybir.AluOpType
AX = mybir.AxisListType

NCORES = 8
D = 1024
SEQ = 8192
SEG = 2048
NLT = SEG // 128
HALO = 2048
PREFIX = 6144
TS = 32
PW = 10248
OFF_Q, OFF_K, OFF_V, OFF_ZA, OFF_XM, OFF_ZM, OFF_OM, OFF_I, OFF_F, OFF_GA, OFF_GM = (
    0, 1536, 3072, 4608, 5120, 6144, 7168, 8192, 8196, 8200, 9224)
GROUPS = ((128, 1), (512, 4), (2048, 16))
EPS = 1e-6
NEG = -30000.0
RAW_GAP = 1

DEBUG = {}


class _Op:
    __slots__ = ("eng", "fn", "deps", "stream", "signal", "val", "idx", "pos", "slot")


class Prog:
    ENGS = ("pe", "act", "dve", "pool", "sp")

    def __init__(self, nc):
        self.nc = nc
        self.ops = {e: [] for e in self.ENGS}
        self.lastw = {}
        self.readers = {}
        self.streams = {}
        self.barrier_deps = []
        self.nops = 0

    def add(self, eng, fn, r=(), w=(), dma=None):
        op = _Op()
        op.eng = eng
        op.fn = fn
        op.stream = dma
        op.signal = False
        op.val = None
        op.idx = self.nops
        self.nops += 1
        deps = {}
        for k in r:
            d = self.lastw.get(k)
            if d is not None:
                deps[d.idx] = (d, True)
            if isinstance(k, tuple) and k[0] == "P":
                for d in self.readers.get(k, ()):
                    if d.eng != eng and d.idx not in deps:
                        deps[d.idx] = (d, False)
        for k in w:
            d = self.lastw.get(k)
            if d is not None and d.idx not in deps:
                deps[d.idx] = (d, False)
            for d in self.readers.get(k, ()):
                if d.idx not in deps:
                    deps[d.idx] = (d, False)
        for d in self.barrier_deps:
            if d.idx not in deps:
                deps[d.idx] = (d, True)
        op.deps = list(deps.values())
        for k in w:
            self.lastw[k] = op
            self.readers[k] = []
        for k in r:
            self.readers.setdefault(k, []).append(op)
        op.pos = len(self.ops[eng])
        self.ops[eng].append(op)
        if dma is not None:
            self.streams.setdefault(dma, []).append(op)
        return op

    KSEM = 8
    KSEM_STREAM = {"cpy": 32}
    PERSIST = ("cpy",)

    def kof(self, s):
        return self.KSEM_STREAM.get(s, self.KSEM)

    def barrier(self):
        deps = []
        for e in self.ENGS:
            if self.ops[e]:
                for op in reversed(self.ops[e]):
                    if op.stream is None:
                        deps.append(op)
                        break
        for s, lst in self.streams.items():
            if s in self.PERSIST:
                continue
            deps.extend(lst[-self.kof(s):])
        self.barrier_deps = deps
        self.lastw = {k: v for k, v in self.lastw.items() if v.stream in self.PERSIST}
        self.readers = {}

    def emit(self, final_streams):
        nc = self.nc
        K = self.KSEM
        need = {}
        for e in self.ENGS:
            for op in self.ops[e]:
                lst = []
                for d, raw in op.deps:
                    if d.stream is None and d.eng == e:
                        if e in ("pe", "sp"):
                            continue
                        if not raw:
                            continue
                        if op.pos - d.pos > RAW_GAP:
                            continue
                    d.signal = True
                    lst.append(d)
                need[op.idx] = lst
        for e in self.ENGS:
            cnt = 0
            for op in self.ops[e]:
                if op.stream is None and op.signal:
                    cnt += 1
                    op.val = cnt
        for s, lst in self.streams.items():
            Ks = self.kof(s)
            for i, op in enumerate(lst):
                op.slot = i % Ks
                op.val = 16 * (i // Ks + 1)
        import contextlib
        with contextlib.ExitStack() as st:
            sems = {e: st.enter_context(nc.semaphore("s_" + e)) for e in self.ENGS}
            ssems = {s: [st.enter_context(nc.semaphore("d_%s_%d" % (s, k))) for k in range(min(self.kof(s), len(lst)))]
                     for s, lst in self.streams.items()}
            block = st.enter_context(nc.Block())

            def run(e, eng):
                waited = {}

                def wait(key, v):
                    if v > waited.get(key, 0):
                        waited[key] = v
                        sem = ssems[key[1]][key[2]] if key[0] == "s" else sems[key[1]]
                        eng.wait_ge(sem, v)

                for op in self.ops[e]:
                    w = {}
                    for d in need[op.idx]:
                        key = ("s", d.stream, d.slot) if d.stream is not None else ("e", d.eng)
                        if d.val > w.get(key, 0):
                            w[key] = d.val
                    for key, v in w.items():
                        wait(key, v)
                    if op.stream is not None and op.val > 16:
                        wait(("s", op.stream, op.slot), op.val - 16)
                    ins = op.fn(eng)
                    if op.stream is not None:
                        ins.then_inc(ssems[op.stream][op.slot], 16)
                    elif op.signal:
                        ins.then_inc(sems[e], 1)
                if e == "sp":
                    for s in final_streams:
                        if s in self.streams:
                            for op in self.streams[s][-self.kof(s):]:
                                wait(("s", s, op.slot), op.val)

            block.tensor(lambda t: run("pe", t))
            block.scalar(lambda t: run("act", t))
            block.vector(lambda t: run("dve", t))
            block.gpsimd(lambda t: run("pool", t))
            block.sync(lambda t: run("sp", t))


class Arena:
    def __init__(self, nc, words):
        self.t = nc.alloc_sbuf_tensor("arena", [128, words], F32)
        self.words = words
        self.top = 0
        self.marks = []

    def push(self):
        self.marks.append(self.top)

    def pop(self):
        self.top = self.marks.pop()

    def alloc(self, shape, dt=F32):
        n = int(np.prod(shape))
        words = (n + 1) // 2 if dt == BF16 else n
        words = (words + 7) // 8 * 8
        assert self.top + words <= self.words, ("SBUF arena overflow", self.top, words, self.words)
        ap = self.t[:, self.top:self.top + words]
        self.top += words
        if dt == BF16:
            ap = ap.bitcast(BF16)
        ap = ap[:, 0:n]
        if len(shape) == 2:
            ap = ap.rearrange("p (a b) -> p a b", a=shape[0])
        elif len(shape) == 3:
            ap = ap.rearrange("p (a b c) -> p a b c", a=shape[0], b=shape[1])
        elif len(shape) == 4:
            ap = ap.rearrange("p (a b c d) -> p a b c d", a=shape[0], b=shape[1], c=shape[2])
        return ap


def _t5_bucket(dist):
    n = np.asarray(dist).astype(np.int64)
    max_exact = 16
    nf = np.maximum(n, 1).astype(np.float32)
    large = max_exact + (np.log(nf / max_exact) / np.log(np.float32(2048 / max_exact))
                         * (32 - max_exact)).astype(np.int64)
    large = np.minimum(large, 31)
    return np.where(n < max_exact, n, large).astype(np.int32)


def _consts():
    c = {}
    c["ident"] = np.eye(128, dtype=np.float32)
    c["antiid"] = np.eye(128, dtype=np.float32)[::-1].copy()
    s = np.arange(128)
    c["tri"] = (s[:, None] <= s[None, :]).astype(np.float32)
    c["cmaskT"] = np.where(s[:, None] <= s[None, :], 0.0, NEG).astype(np.float32)
    oh = np.zeros((32, 3, 129), np.float32)
    for g, (_, d) in enumerate(GROUPS):
        b = _t5_bucket(np.arange(129) * d)
        oh[b, g, np.arange(129)] = 1.0
    c["ohd"] = oh
    selp = np.zeros((5, 128), np.float32); selp[0] = 1.0
    sels = np.zeros((5, 128), np.float32)
    for i in range(TS):
        sels[1 + i // 8, i] = 1.0
    c["selp"] = selp
    c["sels"] = sels
    return c


def _fm(v):
    return np.ascontiguousarray(v.reshape(-1, 128).T)


class Builder:
    def __init__(self):
        nc = bass.Bass("TRN2", target_bir_lowering=False)
        self.nc = nc
        self.P = Prog(nc)
        self.A = Arena(nc, 53184)
        self.ps = nc.alloc_psum_tensor("psum", [128, 8, 512], F32)
        self.din = {}
        self.dout = {}
        self.uid = 0

    def inp(self, name, shape, dt=F32):
        t = self.nc.dram_tensor(name, list(shape), dt, kind="ExternalInput").ap()
        self.din[name] = t
        return t

    def outp(self, name, shape, dt=F32):
        t = self.nc.dram_tensor(name, list(shape), dt, kind="ExternalOutput").ap()
        self.dout[name] = t
        return t

    def scratch(self, name, shape, dt=F32):
        return self.nc.dram_tensor(name, list(shape), dt).ap()

    def mm(self, out, lhsT, rhs, start=True, stop=True, r=(), w=()):
        return self.P.add("pe", lambda e: e.matmul(out, lhsT, rhs, start=start, stop=stop), r, w)

    def tr(self, out, in_, ident, r=(), w=()):
        return self.P.add("pe", lambda e: e.transpose(out, in_, ident), r, w)

    def act(self, out, in_, func, r=(), w=(), **kw):
        return self.P.add("act", lambda e: e.activation(out=out, in_=in_, func=func, **kw), r, w)

    def v(self, eng, name, *args, r=(), w=(), **kw):
        return self.P.add(eng, lambda e: getattr(e, name)(*args, **kw), r, w)

    def dma(self, eng, out, in_, stream, r=(), w=(), **kw):
        return self.P.add(eng, lambda e: e.dma_start(out=out, in_=in_, **kw), r, w, dma=stream)

    def dbg(self, name, ap, shape, dt=F32, r=()):
        if not DEBUG.get(name):
            return
        o = self.outp("dbg_" + name, shape, dt)
        self.P.barrier()
        self.dma("sp", o, ap, "dbg", r=r)

    def key(self, base):
        self.uid += 1
        return (base, self.uid)

    def phase0(self):
        A, P, ps = self.A, self.P, self.ps
        C = self.C = {}

        def load(name, shape, parts=128, dt=F32, eng="sp", src=None):
            t = A.alloc(list(shape[1:]), dt)
            src = self.inp(name, shape) if src is None else src
            self.dma(eng, t[0:parts], src, "ld0", w=[name])
            C[name] = t
            return t

        load("ident", [128, 128])
        C["ident_bf"] = A.alloc([128], BF16)
        self.dma("pool", C["ident_bf"], self.din["ident"], "ldc", w=["ident_bf"])
        C["antiid_bf"] = A.alloc([128], BF16)
        self.dma("pool", C["antiid_bf"], self.inp("antiid", [128, 128]), "ldc", w=["antiid_bf"])
        load("tri", [128, 128])
        load("antiid", [128, 128], src=self.din["antiid"])
        C["cmaskT_bf"] = A.alloc([128], BF16)
        self.dma("pool", C["cmaskT_bf"], self.inp("cmaskT", [128, 128]), "ldc", w=["cmaskT_bf"])
        load("ohd", [32, 3, 129], parts=32)
        load("selp", [5, 128], parts=5)
        load("sels", [5, 128], parts=5)
        load("rel_table", [32, 24], parts=32)
        for nm in ("gainT", "conv_bT", "m_normT", "m_skipT"):
            load(nm, [128, 8])
        load("b_adaT", [128, 24])
        load("conv_wT", [128, 8, 4])
        load("b_if_bc", [128, 8])
        load("tvalid", [128, 64])
        load("cT", [128, 8, 5])

        siluT = A.alloc([8, 5], BF16)
        self.act(siluT, C["cT"], AF.Silu, r=["cT"], w=["siluT"])

        ada = A.alloc([24, 5])
        mult = A.alloc([8, 5])
        gate_p = A.alloc([1024])
        gate_s = A.alloc([1024])
        A.push()
        load("b_gate_rows", [5, 1024], parts=5)
        gate_rows = A.alloc([1024])
        w_ada = self.inp("w_ada", [1024, 3072])
        wada = A.alloc([8, 3072], BF16)
        wv = w_ada.rearrange("(c p) n -> p c n", p=128)
        for c in range(8):
            self.dma("pool", wada[:, c, :], wv[:, c, :], "ldw", w=[("wada", c)])
        adaps = ps[:, 0, 0:120].rearrange("p (a b) -> p a b", a=24)
        for cb in range(24):
            for c in range(8):
                self.mm(adaps[:, cb, :], wada[:, c, cb * 128:(cb + 1) * 128], siluT[:, c, :],
                        start=(c == 0), stop=(c == 7), r=[("wada", c), "siluT"], w=[("P", 0)])
        for half in range(2):
            for c in range(8):
                self.mm(ps[0:5, 1 + half, :], siluT[:, c, :], wada[:, c, 2048 + half * 512:2048 + (half + 1) * 512],
                        start=(c == 0), stop=(c == 7), r=[("wada", c), "siluT"], w=[("P", 1 + half)])
        self.v("dve", "tensor_tensor", ada, adaps, C["b_adaT"].unsqueeze(2).to_broadcast([128, 24, 5]), ALU.add,
               r=[("P", 0), "b_adaT"], w=["ada"])
        self.v("dve", "tensor_scalar", mult, ada[:, 8:16, :], 1.0, None, op0=ALU.add, r=["ada"], w=["mult"])
        self.v("dve", "tensor_tensor", mult, mult, C["gainT"].unsqueeze(2).to_broadcast([128, 8, 5]), ALU.mult,
               r=["mult", "gainT"], w=["mult"])
        C["mult"] = mult
        C["shift"] = ada[:, 0:8, :]
        C["ada"] = ada
        for half in range(2):
            self.v("dve", "tensor_tensor", gate_rows[0:5, half * 512:(half + 1) * 512], ps[0:5, 1 + half, :],
                   C["b_gate_rows"][0:5, half * 512:(half + 1) * 512], ALU.add,
                   r=[("P", 1 + half), "b_gate_rows"], w=[("gate_rows", half)])
        for half in range(2):
            sl = slice(half * 512, (half + 1) * 512)
            self.mm(ps[:, 3, :], C["selp"][0:5, :], gate_rows[0:5, sl], r=[("gate_rows", half), "selp"], w=[("P", 3)])
            self.act(gate_p[:, sl], ps[:, 3, :], AF.Copy, r=[("P", 3)], w=[("gate_p", half)])
            self.mm(ps[:, 4, :], C["sels"][0:5, :], gate_rows[0:5, sl], r=[("gate_rows", half), "sels"], w=[("P", 4)])
            self.act(gate_s[:, sl], ps[:, 4, :], AF.Copy, r=[("P", 4)], w=[("gate_s", half)])
        A.pop()
        P.barrier()
        C["gate_p"] = gate_p
        C["gate_s"] = gate_s
        self.dbg("ada", ada, [128, 24, 5], r=["ada"])
        self.dbg("gate_s", gate_s, [128, 1024], r=[("gate_s", 0), ("gate_s", 1)])

    def norm_tile(self, xsrc, ntok, hT_dst, kind, slot, hkey, bank0=6):
        A, P, ps, C = self.A, self.P, self.ps, self.C
        W = self.W1
        i3, i2 = slot % len(W["xt"]), slot % 2
        xt = W["xt"][i3]
        self.dma("sp", xt[0:ntok], xsrc, "ldx", w=[("xt", i3)])
        ss = W["ss"][:, slot % 4:slot % 4 + 1]
        self.act(W["junk"][0:ntok], xt[0:ntok], AF.Square, r=[("xt", i3)], w=["junk", ("ss", slot % 4)],
                 accum_out=ss[0:ntok])
        self.v("dve", "tensor_scalar", ss[0:ntok], ss[0:ntok], 1.0 / D, EPS, op0=ALU.mult, op1=ALU.add,
               r=[("ss", slot % 4)], w=[("ss", slot % 4)])
        self.act(ss[0:ntok], ss[0:ntok], AF.Ln, r=[("ss", slot % 4)], w=[("ss", slot % 4)])
        self.act(ss[0:ntok], ss[0:ntok], AF.Exp, r=[("ss", slot % 4)], w=[("ss", slot % 4)], scale=-0.5)
        xn = W["xn"][i2]
        self.act(xn[0:ntok], xt[0:ntok], AF.Copy, r=[("xt", i3), ("ss", slot % 4)], w=[("xn", i2)], scale=ss[0:ntok])
        if DEBUG.get("stage", 9) < 1:
            return
        bank = bank0 + i2
        pt = ps[:, bank, :].bitcast(BF16).rearrange("p (c t) -> p c t", c=8)
        for c in range(8):
            self.tr(pt[:, c, 0:ntok], xn[0:ntok, c * 128:(c + 1) * 128], C["ident_bf"][0:ntok, 0:ntok],
                    r=[("xn", i2), "ident_bf"], w=[("P", bank0 + i2)])
        if DEBUG.get("stage", 9) < 2:
            return
        for c in range(8):
            if kind == "p":
                if True:
                    self.act(hT_dst[:, c, :], pt[:, c, 0:ntok], AF.Identity, r=[("P", bank0 + i2), "mult", "ada"], w=[(hkey, c)],
                             scale=C["mult"][:, c, 0:1], bias=C["shift"][:, c, 0:1])
                else:
                    self.v("dve", "tensor_scalar", hT_dst[:, c, :], pt[:, c, 0:ntok], C["mult"][:, c, 0:1], C["shift"][:, c, 0:1],
                           op0=ALU.mult, op1=ALU.add, r=[("P", bank0 + i2), "mult", "ada"], w=[(hkey, c)])
            else:
                tmp = W["stmp"]
                tmp0 = W["stmp0"]
                self.act(tmp0, pt[:, c, 0:ntok], AF.Copy, r=[("P", bank0 + i2)], w=["stmp0"])
                self.v("dve", "tensor_tensor", tmp.rearrange("p (s t) -> p s t", s=4),
                       tmp0.rearrange("p (s t) -> p s t", s=4),
                       C["mult"][:, c, 1:5].unsqueeze(2).to_broadcast([128, 4, 8]), ALU.mult,
                       r=["stmp0", "mult"], w=["stmp"])
                self.v("dve", "tensor_tensor", hT_dst[:, c, :].rearrange("p (s t) -> p s t", s=4),
                       tmp.rearrange("p (s t) -> p s t", s=4),
                       C["shift"][:, c, 1:5].unsqueeze(2).to_broadcast([128, 4, 8]), ALU.add,
                       r=["stmp", "ada"], w=[(hkey, c)])

    def phase1(self):
        A = self.A
        self.hT = A.alloc([8, HALO + SEG + TS], BF16)
        A.push()
        self.W1 = {
            "xt": [A.alloc([1024]) for _ in range(3)],
            "ss": A.alloc([4]),
            "junk": A.alloc([1024], BF16),
            "xn": [A.alloc([1024], BF16) for _ in range(2)],
            "stmp": A.alloc([32]),
            "stmp0": A.alloc([32]),
        }
        xh = self.inp("xh", [HALO + SEG, 1024])
        xs = self.inp("xs", [TS, 1024])
        slot = 0
        if not DEBUG.get("skip_s"):
            self.norm_tile(xs, TS, self.hT[:, :, HALO + SEG:HALO + SEG + TS], "s", slot, ("hT", 32))
        slot += 1
        for ti in range(DEBUG.get("ntiles", (HALO + SEG) // 128)):
            self.norm_tile(xh[ti * 128:(ti + 1) * 128, :], 128, self.hT[:, :, ti * 128:(ti + 1) * 128], "p", slot, ("hT", ti))
            slot += 1
        self.dbg("hT", self.hT, [128, 8, HALO + SEG + TS], BF16, r=[])
        self.dbg("xn", self.W1["xn"][1], [128, 1024], BF16, r=[])
        A.pop()


def build_program(upto=99):
    b = Builder()
    b.phase0()
    b.sample_copies()
    b.w_in = b.inp("w_in", [1024, PW])
    if upto >= 1:
        b.P.barrier()
        b.phase1()
    if upto >= 2:
        b.P.barrier()
        b.attT = b.A.alloc([4, SEG + TS], BF16)
        b.C["F"] = b.A.alloc([3, 129])
        b.A.push()
        b.phase_bias()
        b.P.barrier()
        if DEBUG.get("att_stage", 9) >= 1:
            b.phase_attention()
        b.A.pop()
        b.P.barrier()
        if not DEBUG.get("no_sattn"):
            b.phase_sample_attn()
        b.dbg("attT2", b.attT, [128, 4, SEG + TS], BF16)
    if upto >= 3:
        b.P.barrier()
        b.phase_mlstm()
    if upto >= 4:
        b.P.barrier()
        b.phase_out()
    if DEBUG.get("dmult"):
        DEBUG["mult_end"] = True
        b.dbg("mult_end", b.C["mult"], [128, 8, 5])
    b.P.emit(final_streams=list(b.P.streams.keys()))
    return b


def make_in_maps(inp, cores):
    consts = _consts()
    f32 = np.float32
    maps = []
    xp = inp["x_prompt"]
    for c in cores:
        b, p = c // 4, c % 4
        s0 = p * SEG
        m = dict(consts)
        ext = np.zeros((PREFIX + SEG, D), f32)
        lo = s0 - PREFIX
        src_lo = max(lo, 0)
        ext[src_lo - lo:] = xp[b, src_lo:s0 + SEG]
        m["xf"] = np.ascontiguousarray(ext[:PREFIX - HALO])
        m["xh"] = np.ascontiguousarray(ext[PREFIX - HALO:])
        m["xs"] = np.ascontiguousarray(inp["x_sample"][4 * c:4 * c + 4].reshape(TS, D))
        tv = np.zeros(64, f32)
        tv[(src_lo - lo) // 128:] = 1.0
        m["tvalid"] = np.ascontiguousarray(np.broadcast_to(tv, (128, 64)))
        call = np.concatenate([inp["c_prompt"][b:b + 1], inp["c_sample"][4 * c:4 * c + 4]], 0)
        m["cT"] = np.ascontiguousarray(call.T.reshape(8, 128, 5).transpose(1, 0, 2))
        m["w_ada"] = inp["w_ada"][0]
        m["rel_table"] = inp["rel_table"]
        m["gainT"] = _fm(inp["norm_gain"][0])
        m["conv_bT"] = _fm(inp["conv_b"][0])
        m["m_normT"] = _fm(inp["m_norm"][0])
        m["m_skipT"] = _fm(inp["m_skip"][0])
        m["b_adaT"] = _fm(inp["b_ada"][0])
        m["conv_wT"] = np.ascontiguousarray(inp["conv_w"][0].reshape(4, 8, 128).transpose(2, 1, 0))
        m["b_gate_rows"] = np.ascontiguousarray(np.broadcast_to(inp["b_ada"][0][2048:], (5, 1024)))
        m["fgain_bc"] = np.ascontiguousarray(np.broadcast_to(inp["final_gain"], (128, 1024)))
        m["b_if_bc"] = np.ascontiguousarray(np.broadcast_to(inp["b_if"][0], (128, 8)))
        m["w_in"] = inp["w_in"][0]
        st_ = np.zeros((8, 8, 128), f32)
        for t in range(8):
            st_[t, t, :] = 1.0
        m["selt"] = st_
        osl = np.zeros((128, 8, 8), f32)
        for t in range(8):
            osl[:, t, t] = 1.0
        m["onesel"] = osl
        for g, nm in enumerate(("cache_kv_w128", "cache_kv_w512", "cache_kv_w2048")):
            m["cache%d" % g] = np.ascontiguousarray(inp[nm][0, 4 * c:4 * c + 4].reshape(4, -1, 2, 512))
        m["w_pa"] = inp["w_pa"][0]
        m["w_pm"] = inp["w_pm"][0]
        m["w_out"] = inp["w_out"][0]
        m["w_mq"] = inp["w_mq"][0]
        m["w_mk"] = inp["w_mk"][0]
        eh = np.zeros((4, 4, 128), f32)
        for h in range(4):
            eh[h, h, :] = 1.0
        m["ehsel"] = eh
        sq = slice(4 * c, 4 * c + 4)
        Cst = inp["state_C"][0, sq]
        nst = inp["state_n"][0, sq]
        c0 = np.concatenate([Cst.transpose(0, 3, 1, 2), nst.transpose(0, 2, 1)[..., None]], axis=-1)
        m["C0T"] = np.ascontiguousarray(c0)
        mst = inp["state_m"][0, sq]
        m["m0row"] = np.ascontiguousarray(mst[:, :, None])
        m["m0bc"] = np.ascontiguousarray(np.broadcast_to(mst[:, None, :], (4, 128, 4)))
        cvs = inp["state_conv"][0, sq]
        m["conv0"] = np.ascontiguousarray(cvs.reshape(4, 3, 8, 128).transpose(0, 3, 2, 1))
        m["coremask"] = np.full((128, 128), NEG if p == 0 else 0.0, f32)
        sm = np.zeros((128, 4, 128), f32)
        for e in range(64):
            sm[e, 0, e] = 1.0
            sm[e, 1, 64 + e] = 1.0
            sm[64 + e, 2, e] = 1.0
            sm[64 + e, 3, 64 + e] = 1.0
        m["selmats"] = sm
        maps.append(m)
    return maps


def run_cores(inp, cores, upto=99):
    b = build_program(upto)
    maps = make_in_maps(inp, cores)
    maps = [{k: np.ascontiguousarray(v, dtype=np.float32) for k, v in m.items() if k in b.din} for m in maps]
    res = run_bass_kernel_spmd(b.nc, maps, core_ids=list(range(len(cores))))
    return res.results


def _phase_bias(self):
    A, P, ps, C = self.A, self.P, self.ps, self.C
    F_sb = C["F"]
    gv = A.alloc([3, 2, 256])
    self.v("pool", "memset", gv[0:8], NEG, w=["gv"])
    for g in range(3):
        self.mm(ps[0:8, 5, 0:129], C["rel_table"][0:32, g * 8:(g + 1) * 8], C["ohd"][0:32, g, :],
                r=["rel_table", "ohd"], w=[("P", 5)])
        self.act(F_sb[0:8, g, :], ps[0:8, 5, 0:129], AF.Copy, r=[("P", 5)], w=[("F", g)])
        self.v("pool", "tensor_copy", gv[0:8, g, 1, 127:255], F_sb[0:8, g, 0:128], r=[("F", g), "gv"], w=[("gv", g)])
        self.v("pool", "tensor_copy", gv[0:8, g, 0, 0:128], F_sb[0:8, g, 1:129], r=[("F", g), "gv"], w=[("gv", g)])
    gvd = self.scratch("gvd", [8, 3, 2, 256])
    self.dma("sp", gvd, gv[0:8], "gvw", r=[("gv", 0), ("gv", 1), ("gv", 2)], w=["gvd"])
    biasH = A.alloc([24, 256], BF16)
    for g in range(3):
        for h in range(8):
            for kb in range(2):
                off = ((h * 3 + g) * 2 + kb) * 256
                src = bass.AP(tensor=gvd.tensor, offset=off, ap=[[1, 128], [1, 128]])
                self.dma("pool", biasH[:, g * 8 + h, kb * 128:(kb + 1) * 128], src, "ldc", r=["gvd"], w=[("biasH", g)])
    C["biasH"] = biasH
    cm = A.alloc([128], BF16)
    self.dma("pool", cm, self.inp("coremask", [128, 128]), "ldc", w=["coremask"])
    C["coremask"] = cm
    C["selmats"] = A.alloc([4, 128])
    self.dma("sp", C["selmats"], self.inp("selmats", [128, 4, 128]), "ld0", w=["selmats"])


def _phase_attention(self):
    A, P, ps, C, hT = self.A, self.P, self.ps, self.C, self.hT
    w_in = self.w_in
    wv_in = w_in.rearrange("(c p) n -> p c n", p=128)
    kvp = [self.outp("kvp%d" % g, [GROUPS[g][0], 2, 512]) for g in range(3)]
    A.push()
    acc = A.alloc([2, SEG])
    wq = A.alloc([8, 128], BF16)
    wkv = A.alloc([8, 256], BF16)
    wz = A.alloc([8, 128], BF16)
    qT = A.alloc([SEG], BF16)
    kT = A.alloc([4096], BF16)
    vaug = A.alloc([32, 2, 128], BF16)
    pT = [A.alloc([256], BF16) for _ in range(4)]
    stage = [A.alloc([256]) for _ in range(2)]
    rbuf = A.alloc([512])
    att = A.alloc([512])
    sz = A.alloc([512])
    if not DEBUG.get("no_vones"):
        self.v("pool", "memset", vaug[:, :, :, 64:128], 1.0, w=["vones"])
    cnt = 0
    STG = DEBUG.get("att_stage", 9)
    for hp in range(DEBUG.get("att_hp", 4)):
        for g, (win, d) in enumerate(GROUPS):
            if g not in DEBUG.get("att_groups", (0, 1, 2)):
                continue
            U = SEG // d
            U2 = U + 128
            col = g * 512 + hp * 128
            allc = lambda nm: [(nm, c) for c in range(8)]
            self.dma("pool", wq, wv_in[:, :, OFF_Q + col:OFF_Q + col + 128], "ldw", w=allc("wq"))
            self.dma("pool", wkv[:, :, 0:128], wv_in[:, :, OFF_K + col:OFF_K + col + 128], "ldw", w=allc("wkv"))
            self.dma("pool", wkv[:, :, 128:256], wv_in[:, :, OFF_V + col:OFF_V + col + 128], "ldw", w=allc("wkv"))
            hq = [hT[:, c, HALO:HALO + SEG].rearrange("p (u r) -> p r u", r=d) for c in range(8)]
            hk = [hT[:, c, HALO - 128 * d:HALO + SEG].rearrange("p (u r) -> p r u", r=d) for c in range(8)]

            def chunks(Ux, total):
                res = []
                if Ux >= 512:
                    for r in range(d):
                        u0 = 0
                        while u0 < Ux:
                            n = min(512, Ux - u0)
                            res.append((r, 1, u0, n))
                            u0 += n
                else:
                    nr = 512 // Ux
                    for r0 in range(0, d, nr):
                        res.append((r0, nr, 0, Ux))
                return res

            def tile_keys(view_lo, r0, nr, u0, n, c):
                lo = view_lo + r0 + d * u0
                hi = view_lo + r0 + nr - 1 + d * (u0 + n - 1)
                return [(("hT", t), c) for t in range(lo // 128, hi // 128 + 1)]

            for (r0, nr, u0, n) in chunks(U, SEG):
                bank = cnt % 2
                cnt += 1
                pso = ps[:, bank, 0:nr * n]
                pso3 = pso if nr == 1 else pso.rearrange("p (a b) -> p a b", a=nr)
                for c in range(8):
                    rhs = hq[c][:, r0, u0:u0 + n] if nr == 1 else hq[c][:, r0:r0 + nr, :]
                    self.mm(pso3, wq[:, c, :], rhs, start=(c == 0), stop=(c == 7),
                            r=[("wq", c)] + tile_keys(HALO, r0, nr, u0, n, c), w=[("P", bank)])
                f0 = r0 * U + u0
                self.act(qT[:, f0:f0 + nr * n], pso, AF.Copy, r=[("P", bank)], w=["qT"], scale=0.125)
            if STG < 2:
                continue
            for (r0, nr, u0, n) in chunks(U2, U2 * d):
                bank = cnt % 2
                cnt += 1
                pso = ps[:, bank, 0:nr * n]
                pso3 = pso if nr == 1 else pso.rearrange("p (a b) -> p a b", a=nr)
                for c in range(8):
                    rhs = hk[c][:, r0, u0:u0 + n] if nr == 1 else hk[c][:, r0:r0 + nr, :]
                    self.mm(pso3, wkv[:, c, 0:128], rhs, start=(c == 0), stop=(c == 7),
                            r=[("wkv", c)] + tile_keys(HALO - 128 * d, r0, nr, u0, n, c), w=[("P", bank)])
                f0 = r0 * U2 + u0
                self.act(kT[:, f0:f0 + nr * n], pso, AF.Copy, r=[("P", bank)], w=["kT"])
            nm = U2 // 128
            if STG < 3:
                continue
            for r in range(d):
                for m in range(nm):
                    blk = r * nm + m
                    bank = cnt % 2
                    cnt += 1
                    pso = ps[:, bank, 0:256]
                    lastb = (m == nm - 1)
                    for c in range(8):
                        self.mm(pso if lastb else pso[:, 128:256], hk[c][:, r, 128 * m:128 * m + 128],
                                wkv[:, c, :] if lastb else wkv[:, c, 128:256], start=(c == 0), stop=(c == 7),
                                r=[("wkv", c)] + tile_keys(HALO - 128 * d, r, 1, 128 * m, 128, c), w=[("P", bank)])
                    self.act(vaug[:, blk, :, 0:64], pso[:, 128:256].rearrange("p (h e) -> p h e", h=2), AF.Copy,
                             r=[("P", bank), "vones"], w=[("vaug", blk)])
                    if m == nm - 1 and not DEBUG.get("no_kvout"):
                        st = stage[blk % 2]
                        if DEBUG.get("kv_act"):
                            self.act(st, pso, AF.Copy, r=[("P", bank)], w=[("stage", blk % 2)])
                        else:
                            self.v("dve", "tensor_copy", st, pso, r=[("P", bank)], w=[("stage", blk % 2)])
                        dst = kvp[g].rearrange("(i r) k c -> r i k c", r=d)[r, :, :, hp * 128:(hp + 1) * 128]
                        if DEBUG.get("kv_plain"):
                            dst = kvp[g][0:128, :, hp * 128:(hp + 1) * 128]
                        if not DEBUG.get("kv_nodma"):
                            self.dma("sp", dst, st.rearrange("p (k c) -> p k c", k=2), "out", r=[("stage", blk % 2)], w=[])
            if STG < 4:
                continue
            for r in range(d):
                for n in range(U // 128):
                    for h in range(2):
                        gh = g * 8 + hp * 2 + h
                        hs = slice(h * 64, h * 64 + 64)
                        si = cnt % 3
                        cnt += 1
                        sbk, obk = (2, 3, 6)[si], (4, 5, 7)[si]
                        S = ps[:, sbk, 0:256]
                        O = ps[:, obk, 0:128]
                        self.mm(S, C["antiid_bf"], C["biasH"][:, gh, :], start=True, stop=False,
                                r=["antiid_bf", ("biasH", g)], w=[("P", sbk)])
                        if n == 0:
                            self.mm(S[:, 0:128], C["ident_bf"], C["coremask"], start=False, stop=False,
                                    r=["ident_bf", "coremask"], w=[("P", sbk)])
                        q_ap = qT[hs, r * U + 128 * n:r * U + 128 * n + 128]
                        self.mm(S[:, 0:128], kT[hs, r * U2 + 128 * n:r * U2 + 128 * n + 128], q_ap, start=False, stop=False,
                                r=["qT", "kT"], w=[("P", sbk)])
                        self.mm(S[:, 128:256], kT[hs, r * U2 + 128 * (n + 1):r * U2 + 128 * (n + 2)], q_ap, start=False, stop=True,
                                r=["qT", "kT"], w=[("P", sbk)])
                        self.act(pT[si], S, AF.Exp, r=[("P", sbk)], w=[("pT", si)])
                        b0 = r * nm + n
                        self.mm(O, vaug[:, b0, h, :], pT[si][:, 0:128], start=True, stop=False,
                                r=[("vaug", b0), ("pT", si)], w=[("P", obk)])
                        self.mm(O, vaug[:, b0 + 1, h, :], pT[si][:, 128:256], start=False, stop=True,
                                r=[("vaug", b0 + 1), ("pT", si)], w=[("P", obk)])
                        av = acc[:, h, :].rearrange("p (u r) -> p r u", r=d)[:, r, 128 * n:128 * n + 128]
                        if d == 1:
                            ak = [("acc", h, n // 4, rr) for rr in range(16)]
                        elif d == 4:
                            ak = [("acc", h, n, r + 4 * j) for j in range(4)]
                        else:
                            ak = [("acc", h, qq, r) for qq in range(4)]
                        if g == 0:
                            self.v("dve", "tensor_copy", av, O, r=[("P", obk)], w=ak)
                        else:
                            self.v("dve", "tensor_tensor", av, av, O, ALU.add, r=[("P", obk)] + ak, w=ak)
        if STG < 5:
            continue
        P.barrier()
        zc = OFF_ZA + hp * 128
        self.dma("pool", wz, wv_in[:, :, zc:zc + 128], "ldw", w=[("wz", c) for c in range(8)])
        for k in range(4):
            tk = slice(512 * k, 512 * k + 512)
            for j, (bank, sm) in enumerate(((6, (0, 1)), (7, (2, 3)))):
                for h in range(2):
                    self.mm(ps[:, bank, :], C["selmats"][:, sm[h], :], acc[:, h, tk], start=(h == 0), stop=(h == 1),
                            r=["selmats"], w=[("P", 6 + j)])
            self.v("dve", "reciprocal", rbuf, ps[:, 7, :], r=[("P", 7)], w=["rbuf"])
            self.v("dve", "tensor_tensor", att, ps[:, 6, :], rbuf, ALU.mult, r=[("P", 6), "rbuf"], w=["att"])
            for c in range(8):
                self.mm(ps[:, 0, :], wz[:, c, :], hT[:, c, HALO + 512 * k:HALO + 512 * k + 512], start=(c == 0), stop=(c == 7),
                        r=[("wz", c)] + [(("hT", t), c) for t in range(16 + 4 * k, 16 + 4 * k + 4)], w=[("P", 0)])
            self.act(sz, ps[:, 0, :], AF.Silu, r=[("P", 0)], w=["sz"])
            self.v("dve", "tensor_tensor", self.attT[:, hp, tk], att, sz, ALU.mult, r=["att", "sz"], w=[("attT", hp)])
        P.barrier()
    A.pop()
    self.dbg("attT", self.attT, [128, 4, SEG + TS], BF16)


Builder.phase_bias = _phase_bias
Builder.phase_attention = _phase_attention


def _mlstm_setup(self):
    A, P, C = self.A, self.P, self.C
    wv_in = self.w_in.rearrange("(c p) n -> p c n", p=128)
    M = self.M = {}
    M["wxm"] = A.alloc([8, 1024], BF16)
    M["wg"] = A.alloc([8, 8], BF16)
    M["wmq"] = A.alloc([2, 4, 128], BF16)
    M["wmk"] = A.alloc([2, 4, 128], BF16)
    for c in range(8):
        self.dma("pool", M["wxm"][:, c, :], wv_in[:, c, OFF_XM:OFF_XM + 1024], "ldw", w=["wxm"])
        self.dma("pool", M["wg"][:, c, :], wv_in[:, c, OFF_I:OFF_I + 8], "ldw", w=["wg"])
    wq_d = self.inp("w_mq", [4, 256, 128]).rearrange("h (c p) k -> p c h k", p=128)
    wk_d = self.inp("w_mk", [4, 256, 128]).rearrange("h (c p) k -> p c h k", p=128)
    for ec in range(2):
        for h in range(4):
            self.dma("pool", M["wmq"][:, ec, h, :], wq_d[:, ec, h, :], "ldw", w=["wmq"])
            self.dma("pool", M["wmk"][:, ec, h, :], wk_d[:, ec, h, :], "ldw", w=["wmk"])
    M["ones"] = A.alloc([128])
    self.v("pool", "memset", M["ones"], 1.0, w=["ones"])
    M["ehsel"] = A.alloc([4, 128])
    self.dma("sp", M["ehsel"][0:4], self.inp("ehsel", [4, 4, 128]), "ld0", w=["ehsel"])
    M["negbig"] = A.alloc([64])
    self.v("dve", "tensor_scalar", M["negbig"], C["tvalid"], 1.0e4, -1.0e4, op0=ALU.mult, op1=ALU.add,
           r=["tvalid"], w=["negbig"])
    M["one1"] = A.alloc([1])
    M["zero1"] = A.alloc([1])
    self.v("pool", "memset", M["one1"], 1.0, w=["one1"])
    self.v("pool", "memset", M["zero1"], 0.0, w=["zero1"])
    M["neg1"] = A.alloc([1])
    self.v("pool", "memset", M["neg1"], -1.0, w=["neg1"])
    M["CT"] = A.alloc([4, 257], F32)
    M["m_row"] = A.alloc([1], F32)
    M["m_bc"] = A.alloc([4], F32)
    M["convbuf"] = A.alloc([8, 131], F32)


def _mlstm_local_setup(self):
    A, P, C, M = self.A, self.P, self.C, self.M
    wv_in = self.w_in.rearrange("(c p) n -> p c n", p=128)
    M["wzm"] = A.alloc([8, 1024], BF16)
    M["wom"] = A.alloc([8, 1024], BF16)
    for c in range(8):
        self.dma("pool", M["wzm"][:, c, :], wv_in[:, c, OFF_ZM:OFF_ZM + 1024], "ldw", w=["wzm"])
        self.dma("pool", M["wom"][:, c, :], wv_in[:, c, OFF_OM:OFF_OM + 1024], "ldw", w=["wom"])
    for nm, shp, dt in (("cacc", [8, 128], F32), ("cexp", [8, 128], F32), ("c_act", [8, 128], BF16),
                        ("vaug", [4, 257], BF16), ("kmw", [4, 128], BF16), ("qmT", [4, 128], BF16), ("kmT", [4, 128], BF16),
                        ("gt", [8], F32), ("lf", [4], F32), ("ie", [4], F32), ("b_tok", [4], F32), ("a_tok", [4], F32),
                        ("a_row", [128], F32), ("cm_row", [128], F32), ("M_row", [128], F32), ("negM_row", [128], F32),
                        ("AT", [1], F32), ("Mend_row", [1], F32), ("dg", [4], F32), ("Mend_bc", [4], F32),
                        ("tmp4", [4], F32), ("wk", [4], F32), ("wC", [4], F32), ("M_tok", [4], F32), ("emt", [4], F32),
                        ("DT", [128], F32), ("Wbc", [128], F32), ("scT", [128], BF16), ("qtil", [128], BF16),
                        ("CTb", [4, 257], BF16), ("hh", [256], F32), ("hn", [256], BF16), ("hnm", [8, 128], F32),
                        ("st", [8], F32), ("so", [128], F32), ("szm", [128], F32), ("t1", [128], F32), ("t2", [128], F32),
                        ("sq", [256], BF16)):
        M[nm] = A.alloc(shp, dt)
    self.v("pool", "memset", M["vaug"][:, :, 256:257], 1.0, w=["vaug1"])
    M["so_all"] = A.alloc([8, 512], BF16)
    M["sz_all"] = A.alloc([8, 512], BF16)


def _mlstm_gates4(self, tok0, tiles):
    ps, M, hT = self.ps, self.M, self.hT
    for fb in range(8):
        for (wname, bank, func, dst, dk) in (("wom", 5, AF.Sigmoid, "so_all", "so_all"), ("wzm", 6, AF.Silu, "sz_all", "sz_all")):
            for c in range(8):
                self.mm(ps[:, bank, :], M[wname][:, c, fb * 128:(fb + 1) * 128], hT[:, c, tok0:tok0 + 512], start=(c == 0), stop=(c == 7),
                        r=[wname] + [(("hT", t), c) for t in tiles], w=[("P", bank)])
            self.act(M[dst][:, fb, :], ps[:, bank, :], func, r=[("P", bank)], w=[(dk, fb)])


def _mlstm_tile(self, hTt, hkeys, ntok, tcol, with_out, mout_dst, moutkey, gate_off=None):
    A, P, ps, C, M = self.A, self.P, self.ps, self.C, self.M
    N = ntok
    B = lambda i: ("P", i)
    tv = C["tvalid"][:, tcol:tcol + 1] if tcol is not None else M["one1"]
    nb = M["negbig"][:, tcol:tcol + 1] if tcol is not None else M["zero1"]
    tvk = ["tvalid", "negbig", "one1", "zero1"]
    cb = M["convbuf"]
    xmps = ps[:, 0:2, :].rearrange("p a (b t) -> p (a b) t", t=128)
    for fb in range(8):
        for c in range(8):
            self.mm(xmps[:, fb, 0:N], M["wxm"][:, c, fb * 128:(fb + 1) * 128], hTt[:, c, :], start=(c == 0), stop=(c == 7),
                    r=["wxm"] + [(k, c) for k in hkeys], w=[B(fb // 4)])
    for half in range(2):
        self.act(cb[:, 4 * half:4 * half + 4, 3:3 + N], xmps[:, 4 * half:4 * half + 4, 0:N], AF.Copy,
                 r=[B(half)] + tvk, w=[("cb", half)], scale=tv)
    for half in range(2):
        for c in range(8):
            self.mm(ps[0:N, 2 + half, :], hTt[:, c, :], M["wxm"][:, c, half * 512:(half + 1) * 512], start=(c == 0), stop=(c == 7),
                    r=["wxm"] + [(k, c) for k in hkeys], w=[B(2 + half)])
        self.act(M["vaug"][0:N, 2 * half:2 * half + 2, 0:256], ps[0:N, 2 + half, :].rearrange("p (h v) -> p h v", h=2), AF.Copy,
                 r=[B(2 + half), "vaug1"], w=[("vaug", half)])
    gps = ps[0:N, 7, 0:8]
    for c in range(8):
        self.mm(gps, hTt[:, c, :], M["wg"][:, c, :], start=(c == 0), stop=(c == 7),
                r=["wg"] + [(k, c) for k in hkeys], w=[B(7)])
    gt = M["gt"]
    self.v("dve", "tensor_tensor", gt[0:N], gps, C["b_if_bc"][0:N], ALU.add, r=[B(7), "b_if_bc"], w=["gt"])
    lf = M["lf"]
    self.act(lf[0:N], gt[0:N, 4:8], AF.Exp, r=["gt"], w=["lf"], scale=-1.0)
    self.act(lf[0:N], lf[0:N], AF.Ln, r=["lf"], w=["lf"], bias=M["one1"][0:N])
    self.v("dve", "tensor_scalar", lf[0:N], lf[0:N], tv[0:N], M["neg1"][0:N], op0=ALU.mult, op1=ALU.mult, r=["lf", "neg1"] + tvk, w=["lf"])
    ie = M["ie"]
    self.v("dve", "tensor_scalar", ie[0:N], gt[0:N, 0:4], tv[0:N], nb[0:N], op0=ALU.mult, op1=ALU.add, r=["gt"] + tvk, w=["ie"])
    tri, ident = C["tri"], C["ident"]
    self.mm(ps[0:N, 7, 8:12], tri[0:N, 0:N], lf[0:N], r=["lf", "tri"], w=[B(7)])
    self.mm(ps[:, 7, 12:16], M["ones"][0:N, :], lf[0:N], r=["lf", "ones"], w=[B(7)])
    self.mm(ps[0:4, 7, 144:144 + N], lf[0:N], tri[0:N, 0:N], r=["lf", "tri"], w=[B(7)])
    b_tok, a_tok = M["b_tok"], M["a_tok"]
    self.v("dve", "tensor_copy", b_tok[0:N], ps[0:N, 7, 8:12], r=[B(7)], w=["b_tok"])
    self.v("dve", "tensor_tensor", a_tok[0:N], ie[0:N], b_tok[0:N], ALU.subtract, r=["ie", "b_tok"], w=["a_tok"])
    self.mm(ps[0:4, 7, 16:16 + N], a_tok[0:N], ident[0:N, 0:N], r=["a_tok", "ident"], w=[B(7)])
    a_row = M["a_row"]
    self.v("dve", "tensor_copy", a_row[0:4, 0:N], ps[0:4, 7, 16:16 + N], r=[B(7)], w=["a_row"])
    cw, cbias = C["conv_wT"], C["conv_bT"]
    for fb in range(8):
        self.act(M["cacc"][:, fb, 0:N], cb[:, fb, 3:3 + N], AF.Identity, r=[("cb", fb // 4), "conv_wT", "conv_bT"], w=[("cacc", fb)],
                 scale=cw[:, fb, 3:4], bias=cbias[:, fb:fb + 1])
    for j in range(3):
        for fb in range(8):
            ca = M["cacc"][:, fb, 0:N]
            self.v("dve", "scalar_tensor_tensor", ca, cb[:, fb, j:j + N], cw[:, fb, j:j + 1], ca, op0=ALU.mult, op1=ALU.add,
                   r=[("cb", fb // 4), ("cacc", fb)], w=[("cacc", fb)])
    for fb in range(8):
        self.act(M["c_act"][:, fb, 0:N], M["cacc"][:, fb, 0:N], AF.Silu, r=[("cacc", fb)], w=[("c_act", fb)])
    for half in range(2):
        self.v("pool", "tensor_copy", cb[:, 4 * half:4 * half + 4, 0:3], cb[:, 4 * half:4 * half + 4, N:N + 3],
               r=[("cb", half)], w=[("cb", half)])
    kmps = ps[0:N, 4, :].rearrange("p (h k) -> p h k", h=4)
    for h in range(4):
        for ec in range(2):
            self.mm(kmps[:, h, :], M["c_act"][:, 2 * h + ec, 0:N], M["wmk"][:, ec, h, :], start=(ec == 0), stop=(ec == 1),
                    r=[("c_act", 2 * h + ec), "wmk"], w=[B(4)])
    if with_out:
        qps = ps[:, 5, :].rearrange("p (h t) -> p h t", h=4)
        kps = ps[:, 6, :].rearrange("p (h t) -> p h t", h=4)
        for h in range(4):
            for ec in range(2):
                self.mm(qps[:, h, 0:N], M["wmq"][:, ec, h, :], M["c_act"][:, 2 * h + ec, 0:N], start=(ec == 0), stop=(ec == 1),
                        r=[("c_act", 2 * h + ec), "wmq"], w=[B(5)])
            for ec in range(2):
                self.mm(kps[:, h, 0:N], M["wmk"][:, ec, h, :], M["c_act"][:, 2 * h + ec, 0:N], start=(ec == 0), stop=(ec == 1),
                        r=[("c_act", 2 * h + ec), "wmk"], w=[B(6)])
        self.act(M["qmT"][:, :, 0:N], qps[:, :, 0:N], AF.Copy, r=[B(5)], w=["qmT"], scale=float(128 ** -0.5))
        self.act(M["kmT"][:, :, 0:N], kps[:, :, 0:N], AF.Copy, r=[B(6)], w=["kmT"])
        self.v("pool", "tensor_copy", M["CTb"], M["CT"], r=["CT"], w=["CTb"])
        self.v("dve", "tensor_tensor_scan", M["cm_row"][0:4, 0:N], M["ones"][0:4, 0:N], a_row[0:4, 0:N], -1.0e30,
               op0=ALU.mult, op1=ALU.max, r=["a_row", "ones"], w=["cm_row"])
        self.v("dve", "tensor_tensor", M["M_row"][0:4, 0:N], M["cm_row"][0:4, 0:N], M["m_row"][0:4, 0:1].to_broadcast([4, N]), ALU.max,
               r=["cm_row", "m_row"], w=["M_row"])
        self.v("dve", "tensor_scalar", M["negM_row"][0:4, 0:N], M["M_row"][0:4, 0:N], -1.0, None, op0=ALU.mult,
               r=["M_row"], w=["negM_row"])
        self.mm(ps[0:N, 7, 276:280], M["M_row"][0:4, 0:N], ident[0:4, 0:4], r=["M_row", "ident"], w=[B(7)])
        self.v("dve", "tensor_tensor", M["emt"][0:N], b_tok[0:N], ps[0:N, 7, 276:280], ALU.add, r=["b_tok", B(7)], w=["emt"])
        self.act(M["emt"][0:N], M["emt"][0:N], AF.Exp, r=["emt"], w=["emt"], scale=-1.0)
    self.v("dve", "tensor_reduce", M["AT"][0:4], a_row[0:4, 0:N], AX.X, ALU.max, r=["a_row"], w=["AT"])
    self.v("dve", "tensor_tensor", M["Mend_row"][0:4], M["AT"][0:4], M["m_row"][0:4], ALU.max, r=["AT", "m_row"], w=["Mend_row"])
    self.v("dve", "tensor_tensor", M["dg"][0:4], ident[0:4, 0:4], M["Mend_row"][0:4, 0:1].to_broadcast([4, 4]), ALU.mult,
           r=["Mend_row", "ident"], w=["dg"])
    self.mm(ps[:, 7, 272:276], M["ones"][0:4, :], M["dg"][0:4], r=["dg", "ones"], w=[B(7)])
    self.v("dve", "tensor_copy", M["Mend_bc"], ps[:, 7, 272:276], r=[B(7)], w=["Mend_bc"])
    self.v("dve", "tensor_tensor", M["wk"][0:N], a_tok[0:N], M["Mend_bc"][0:N], ALU.subtract, r=["a_tok", "Mend_bc"], w=["wk"])
    self.act(M["wk"][0:N], M["wk"][0:N], AF.Exp, r=["wk"], w=["wk"])
    self.v("dve", "tensor_tensor", M["wC"], M["m_bc"], M["Mend_bc"], ALU.subtract, r=["m_bc", "Mend_bc"], w=["wC"])
    self.act(M["wC"], M["wC"], AF.Exp, r=["wC"], w=["wC"])
    self.v("dve", "tensor_tensor", M["kmw"][0:N], kmps, M["wk"][0:N].unsqueeze(2).to_broadcast([N, 4, 128]), ALU.mult,
           r=[B(4), "wk"], w=["kmw"])
    if with_out:
        for h in range(4):
            hsl = slice(h, h + 1)
            par = h % 2
            sfx = ""
            hb0, hb1 = (0, 1)
            pl, pm, pst = ps[:, hb0, 0:N], ps[:, hb0, 128:128 + N], ps[:, hb0, 256:256 + N]
            self.mm(pl, M["ehsel"][0:4, h, :], M["negM_row"][0:4, 0:N], r=["ehsel", "negM_row"], w=[B(hb0)])
            self.mm(pm, M["ehsel"][0:4, h, :], M["negM_row"][0:4, 0:N], start=True, stop=False, r=["ehsel", "negM_row"], w=[B(hb0)])
            self.mm(pm[0:N], C["ident_bf"][0:N, 0:N], C["cmaskT_bf"][0:N, 0:N], start=False, stop=True,
                    r=["ident_bf", "cmaskT_bf"], w=[B(hb0)])
            self.mm(pst[0:N], M["kmT"][:, h, 0:N], M["qmT"][:, h, 0:N], r=["kmT", "qmT"], w=[B(hb0)])
            self.act(M["DT" + sfx][0:N, 0:N], pm[0:N], AF.Exp, r=[B(hb0), "a_tok"], w=["DT" + sfx], bias=a_tok[0:N, hsl])
            self.act(M["Wbc" + sfx][:, 0:N], pl, AF.Exp, r=[B(hb0), "m_bc"], w=["Wbc" + sfx], bias=M["m_bc"][:, hsl])
            self.v("dve", "tensor_tensor", M["scT" + sfx][0:N, 0:N], pst[0:N], M["DT" + sfx][0:N, 0:N], ALU.mult, r=[B(hb0), "DT" + sfx], w=["scT" + sfx])
            self.v("pool", "tensor_tensor", M["qtil" + sfx][:, 0:N], M["qmT"][:, h, 0:N], M["Wbc" + sfx][:, 0:N], ALU.mult,
                   r=["qmT", "Wbc" + sfx], w=["qtil" + sfx])
            nd = ps[0:N, hb1, 0:257]
            self.mm(nd, M["scT" + sfx][0:N, 0:N], M["vaug"][0:N, h, :], start=True, stop=False, r=["scT" + sfx, ("vaug", h // 2), "vaug1"], w=[B(hb1)])
            self.mm(nd, M["qtil" + sfx][:, 0:N], M["CTb"][:, h, :], start=False, stop=True, r=["qtil" + sfx, "CTb"], w=[B(hb1)])
            st = M["st" + sfx]
            self.act(st[0:N, 0:1], nd[:, 256:257], AF.Abs, r=[B(hb1)], w=["st" + sfx])
            self.v("dve", "tensor_tensor", st[0:N, 0:1], st[0:N, 0:1], M["emt"][0:N, hsl], ALU.max, r=["st" + sfx, "emt"], w=["st" + sfx])
            self.v("dve", "reciprocal", st[0:N, 0:1], st[0:N, 0:1], r=["st" + sfx], w=["st" + sfx])
            self.act(M["hh" + sfx][0:N], nd[:, 0:256], AF.Copy, r=[B(hb1), "st" + sfx], w=["hh" + sfx, "st1" + sfx], scale=st[0:N, 0:1], accum_out=st[0:N, 1:2])
            self.act(M["sq" + sfx][0:N], M["hh" + sfx][0:N], AF.Square, r=["hh" + sfx], w=["sq" + sfx, "st2" + sfx], accum_out=st[0:N, 2:3])
            self.v("dve", "tensor_scalar", st[0:N, 3:4], st[0:N, 1:2], 1.0 / 256, None, op0=ALU.mult, r=["st1" + sfx], w=["st3" + sfx])
            self.v("dve", "tensor_tensor", st[0:N, 4:5], st[0:N, 3:4], st[0:N, 3:4], ALU.mult, r=["st3" + sfx], w=["st4" + sfx])
            self.v("dve", "scalar_tensor_tensor", st[0:N, 5:6], st[0:N, 2:3], 1.0 / 256, st[0:N, 4:5], op0=ALU.mult, op1=ALU.subtract,
                   r=["st2" + sfx, "st4" + sfx], w=["st5" + sfx])
            self.v("dve", "tensor_scalar", st[0:N, 5:6], st[0:N, 5:6], EPS, None, op0=ALU.add, r=["st5" + sfx], w=["st5" + sfx])
            self.act(st[0:N, 5:6], st[0:N, 5:6], AF.Ln, r=["st5" + sfx], w=["st5" + sfx])
            self.act(st[0:N, 5:6], st[0:N, 5:6], AF.Exp, r=["st5" + sfx], w=["st5" + sfx], scale=-0.5)
            self.v("dve", "scalar_tensor_tensor", st[0:N, 6:7], st[0:N, 3:4], -1.0, st[0:N, 5:6], op0=ALU.mult, op1=ALU.mult,
                   r=["st3" + sfx, "st5" + sfx], w=["st6" + sfx])
            self.act(M["hn" + sfx][0:N], M["hh" + sfx][0:N], AF.Identity, r=["hh" + sfx, "st5" + sfx, "st6" + sfx], w=["hn" + sfx], scale=st[0:N, 5:6], bias=st[0:N, 6:7])
            pt = ps[:, 4, 0:128].bitcast(BF16).rearrange("p (b t) -> p b t", b=2)
            for vb in range(2):
                self.tr(pt[:, vb, 0:N], M["hn" + sfx][0:N, vb * 128:(vb + 1) * 128], C["ident_bf"][0:N, 0:N], r=["hn" + sfx, "ident_bf"], w=[B(4)])
            for vb in range(2):
                fb = 2 * h + vb
                self.act(M["hnm"][:, fb, 0:N], pt[:, vb, 0:N], AF.Copy, r=[B(4), "m_normT"], w=[("hnm", fb)], scale=C["m_normT"][:, fb:fb + 1])
    for h in range(4):
        bk = 2 + (h % 2)
        dps = ps[:, bk, 0:257]
        self.mm(dps, M["kmw"][0:N, h, :], M["vaug"][0:N, h, :], r=["kmw", ("vaug", h // 2), "vaug1"], w=[B(bk)])
        self.v("dve", "scalar_tensor_tensor", M["CT"][:, h, :], M["CT"][:, h, :], M["wC"][:, h:h + 1], dps, op0=ALU.mult, op1=ALU.add,
               r=["wC", B(bk), "CT", "CTb"], w=["CT"])
    self.v("dve", "tensor_tensor", M["m_row"][0:4], M["Mend_row"][0:4], ps[0:4, 7, 144 + N - 1:144 + N], ALU.add,
           r=["Mend_row", B(7)], w=["m_row"])
    self.v("dve", "tensor_tensor", M["m_bc"], M["Mend_bc"], ps[:, 7, 12:16], ALU.add, r=["Mend_bc", B(7)], w=["m_bc"])
    if with_out:
        for fb in range(8):
            if gate_off is None:
                for (wname, bank) in (("wom", 5), ("wzm", 6)):
                    for c in range(8):
                        self.mm(ps[:, bank, 0:N], M[wname][:, c, fb * 128:(fb + 1) * 128], hTt[:, c, :], start=(c == 0), stop=(c == 7),
                                r=[wname] + [(k, c) for k in hkeys], w=[B(bank)])
                self.act(M["so"][:, 0:N], ps[:, 5, 0:N], AF.Sigmoid, r=[B(5)], w=["so"])
                self.act(M["szm"][:, 0:N], ps[:, 6, 0:N], AF.Silu, r=[B(6)], w=["szm"])
                so_ap, sz_ap, sok, szk = M["so"][:, 0:N], M["szm"][:, 0:N], "so", "szm"
            else:
                so_ap, sz_ap = M["so_all"][:, fb, gate_off:gate_off + N], M["sz_all"][:, fb, gate_off:gate_off + N]
                sok, szk = ("so_all", fb), ("sz_all", fb)
            self.v("dve", "tensor_tensor", M["t1"][:, 0:N], so_ap, M["hnm"][:, fb, 0:N], ALU.mult, r=[sok, ("hnm", fb)], w=["t1"])
            self.v("dve", "scalar_tensor_tensor", M["t2"][:, 0:N], M["c_act"][:, fb, 0:N], C["m_skipT"][:, fb:fb + 1], M["t1"][:, 0:N],
                   op0=ALU.mult, op1=ALU.add, r=[("c_act", fb), "t1", "m_skipT"], w=["t2"])
            self.v("dve", "tensor_tensor", mout_dst[:, fb, :], M["t2"][:, 0:N], sz_ap, ALU.mult, r=["t2", szk], w=[(moutkey, fb)])


Builder.mlstm_setup = _mlstm_setup
Builder.mlstm_local_setup = _mlstm_local_setup
Builder.mlstm_tile = _mlstm_tile
Builder.mlstm_gates4 = _mlstm_gates4


def _phase_mlstm(self):
    A, P, ps, C, hT = self.A, self.P, self.ps, self.C, self.hT
    self.moutS = A.alloc([8, TS], BF16)
    A.push()
    self.mlstm_setup()
    M = self.M
    self.v("pool", "memset", M["CT"], 0.0, w=["CT"])
    A.push()
    self.W1 = {
        "xt": [A.alloc([1024]) for _ in range(2)],
        "ss": A.alloc([4]),
        "junk": A.alloc([1024], BF16),
        "xn": [A.alloc([1024], BF16) for _ in range(2)],
    }
    self.mlstm_prefix()
    if DEBUG.get("sbuf"):
        print("mlstm prefix sbuf top", A.top)
    A.pop()
    P.barrier()
    self.mlstm_local_setup()
    if DEBUG.get("sbuf"):
        print("mlstm local sbuf top", A.top)
    for j in range(DEBUG.get("nloc", 16)):
        if j % 4 == 0:
            self.mlstm_gates4(HALO + j * 128, [16 + j + q for q in range(4)])
        self.mlstm_tile(hT[:, :, HALO + j * 128:HALO + (j + 1) * 128], [("hT", 16 + j)], 128, 48 + j, True,
                        hT[:, :, j * 128:(j + 1) * 128], ("hT", j), gate_off=(j % 4) * 128)
    o_conv = self.outp("convp", [128, 8, 3])
    o_C = self.outp("Cp", [128, 4, 257])
    o_m = self.outp("mp", [4, 1])
    self.dma("sp", o_conv, M["convbuf"][:, :, 0:3], "out", r=[("cb", 0), ("cb", 1)])
    self.dma("sp", o_C, M["CT"], "out", r=["CT"])
    self.dma("sp", o_m, M["m_row"][0:4], "out", r=["m_row"])
    C0 = self.inp("C0T", [4, 128, 4, 257])
    m0r = self.inp("m0row", [4, 4, 1])
    m0b = self.inp("m0bc", [4, 128, 4])
    cv0 = self.inp("conv0", [4, 128, 8, 3])
    o_convs = self.outp("convs", [4, 128, 8, 3])
    o_Cs = self.outp("Cs", [4, 128, 4, 257])
    o_ms = self.outp("ms", [4, 4, 1])
    for j in range(4 if not DEBUG.get("no_smp") else 0):
        self.dma("sp", M["CT"], C0[j], "ldx", w=["CT"])
        self.dma("sp", M["m_row"][0:4], m0r[j], "ldx", w=["m_row"])
        self.dma("sp", M["m_bc"], m0b[j], "ldx", w=["m_bc"])
        self.dma("sp", M["convbuf"][:, :, 0:3], cv0[j], "ldx", w=[("cb", 0), ("cb", 1)])
        self.mlstm_tile(hT[:, :, HALO + SEG + 8 * j:HALO + SEG + 8 * j + 8], [("hT", 32)], 8, None, True,
                        self.moutS[:, :, 8 * j:8 * j + 8], ("moutS", j))
        self.dma("sp", o_convs[j], M["convbuf"][:, :, 0:3], "out", r=[("cb", 0), ("cb", 1)])
        self.dma("sp", o_Cs[j], M["CT"], "out", r=["CT"])
        self.dma("sp", o_ms[j], M["m_row"][0:4], "out", r=["m_row"])
    if DEBUG.get("mdump"):
        for nm, shp, dt in (("convbuf", [128, 8, 131], F32), ("c_act", [128, 8, 128], BF16), ("vaug", [128, 4, 257], BF16),
                            ("kmw", [128, 4, 128], BF16), ("gt", [128, 8], F32), ("lf", [128, 4], F32), ("ie", [128, 4], F32),
                            ("b_tok", [128, 4], F32), ("a_tok", [128, 4], F32), ("a_row", [128, 128], F32),
                            ("Mend_bc", [128, 4], F32), ("wk", [128, 4], F32), ("wC", [128, 4], F32), ("CT", [128, 4, 257], F32),
                            ("m_bc", [128, 4], F32), ("hh", [128, 256], F32), ("hnm", [128, 8, 128], F32), ("st", [128, 8], F32),
                            ("emt", [128, 4], F32), ("qmT", [128, 4, 128], BF16), ("kmT", [128, 4, 128], BF16), ("DT", [128, 128], F32),
                            ("Wbc", [128, 128], F32), ("M_row", [128, 128], F32)):
            DEBUG["md_" + nm] = True
            self.dbg("md_" + nm, M[nm], shp, dt)
        for nm, ap, shp, dt in (("hTp1", M["hTp"][1], [128, 8, 128], BF16), ("xn1", self.W1["xn"][1], [128, 1024], BF16),
                                ("xt1", self.W1["xt"][1], [128, 1024], F32), ("ss", self.W1["ss"], [128, 4], F32),
                                ("wg", M["wg"], [128, 8, 8], BF16), ("mult", C["mult"], [128, 8, 5], F32)):
            DEBUG["md_" + nm] = True
            self.dbg("md_" + nm, ap, shp, dt)
    A.pop()
    self.P.barrier()
    self.dbg("mout", self.hT[:, :, 0:SEG], [128, 8, SEG], BF16)
    self.dbg("moutS", self.moutS, [128, 8, TS], BF16)


Builder.phase_mlstm = _phase_mlstm


def _phase_out(self):
    A, P, ps, C, hT = self.A, self.P, self.ps, self.C, self.hT
    wv_in = self.w_in.rearrange("(c p) n -> p c n", p=128)
    A.push()
    wpa = A.alloc([4, 1024], BF16)
    wpm = A.alloc([8, 1024], BF16)
    wout = A.alloc([8, 1024], BF16)
    wga = A.alloc([8, 128], BF16)
    wgm = A.alloc([8, 128], BF16)
    merged = A.alloc([8, SEG + TS], BF16)
    fgain = A.alloc([1024])
    sga, sgm, t1, t2 = (A.alloc([512]) for _ in range(4))
    xt = [A.alloc([1024])] * 2
    yt = [A.alloc([1024]) for _ in range(2)]
    ss = A.alloc([4])
    self.dma("sp", fgain, self.inp("fgain_bc", [128, 1024]), "ld0", w=["fgain"])
    wpa_d = self.inp("w_pa", [512, 1024]).rearrange("(c p) n -> p c n", p=128)
    wpm_d = self.inp("w_pm", [1024, 1024]).rearrange("(c p) n -> p c n", p=128)
    wout_d = self.inp("w_out", [1024, 1024]).rearrange("(c p) n -> p c n", p=128)
    for c in range(4):
        self.dma("pool", wpa[:, c, :], wpa_d[:, c, :], "ldw", w=["wpa"])
    for c in range(8):
        self.dma("pool", wpm[:, c, :], wpm_d[:, c, :], "ldw", w=["wpm"])
        self.dma("pool", wout[:, c, :], wout_d[:, c, :], "ldw", w=["wout"])
    chunks = []
    for k in range(4):
        chunks.append(dict(n=512, m0=512 * k,
                           att=lambda hp, k=k: self.attT[:, hp, 512 * k:512 * k + 512],
                           mout=lambda fb, k=k: hT[:, fb, 512 * k:512 * k + 512],
                           hs=lambda c, k=k: hT[:, c, HALO + 512 * k:HALO + 512 * k + 512]))
    chunks.append(dict(n=TS, m0=SEG,
                       att=lambda hp: self.attT[:, hp, SEG:SEG + TS],
                       mout=lambda fb: self.moutS[:, fb, :],
                       hs=lambda c: hT[:, c, HALO + SEG:HALO + SEG + TS]))
    for cb in range(8):
        cs = slice(cb * 128, (cb + 1) * 128)
        self.dma("pool", wga, wv_in[:, :, OFF_GA + cb * 128:OFF_GA + (cb + 1) * 128], "ldw", w=["wga"])
        self.dma("pool", wgm, wv_in[:, :, OFF_GM + cb * 128:OFF_GM + (cb + 1) * 128], "ldw", w=["wgm"])
        for ch in chunks:
            n = ch["n"]
            for hp in range(4):
                self.mm(ps[:, 0, 0:n], wpa[:, hp, cs], ch["att"](hp), start=(hp == 0), stop=(hp == 3), r=["wpa"], w=[("P", 0)])
            for fb in range(8):
                self.mm(ps[:, 1, 0:n], wpm[:, fb, cs], ch["mout"](fb), start=(fb == 0), stop=(fb == 7), r=["wpm"], w=[("P", 1)])
            for c in range(8):
                self.mm(ps[:, 2, 0:n], wga[:, c, :], ch["hs"](c), start=(c == 0), stop=(c == 7), r=["wga"], w=[("P", 2)])
            for c in range(8):
                self.mm(ps[:, 3, 0:n], wgm[:, c, :], ch["hs"](c), start=(c == 0), stop=(c == 7), r=["wgm"], w=[("P", 3)])
            self.act(sga[:, 0:n], ps[:, 2, 0:n], AF.Sigmoid, r=[("P", 2)], w=["sga"])
            self.act(sgm[:, 0:n], ps[:, 3, 0:n], AF.Sigmoid, r=[("P", 3)], w=["sgm"])
            self.v("dve", "tensor_tensor", t1[:, 0:n], ps[:, 0, 0:n], sga[:, 0:n], ALU.mult, r=[("P", 0), "sga"], w=["t1"])
            self.v("dve", "tensor_tensor", t2[:, 0:n], ps[:, 1, 0:n], sgm[:, 0:n], ALU.mult, r=[("P", 1), "sgm"], w=["t2"])
            self.v("pool", "tensor_tensor", merged[:, cb, ch["m0"]:ch["m0"] + n], t1[:, 0:n], t2[:, 0:n], ALU.add,
                   r=["t1", "t2"], w=[("merged", cb)])
    self.dbg("merged", merged, [128, 8, SEG + TS], BF16)
    xh = self.din["xh"]
    xs = self.din["xs"]
    y_p = self.outp("y_p", [SEG, 1024])
    y_s = self.outp("y_s", [TS, 1024])
    tiles = [(xh[HALO + j * 128:HALO + (j + 1) * 128, :], y_p[j * 128:(j + 1) * 128, :], 128, j * 128, C["gate_p"]) for j in range(NLT)]
    tiles.append((xs, y_s, TS, SEG, C["gate_s"]))
    for i, (xsrc, ydst, n, m0, gate) in enumerate(tiles):
        i2 = i % 2
        self.dma("sp", xt[i2][0:n], xsrc, "ldx", w=[("xt", 0)])
        for half in range(2):
            hs_ = slice(half * 512, (half + 1) * 512)
            for cb in range(8):
                self.mm(ps[0:n, 4 + half, :], merged[:, cb, m0:m0 + n], wout[:, cb, hs_], start=(cb == 0), stop=(cb == 7),
                        r=["wout", ("merged", cb)], w=[("P", 4 + half)])
            self.v("dve", "tensor_tensor", yt[i2][0:n, hs_], ps[0:n, 4 + half, :], gate[0:n, hs_], ALU.mult,
                   r=[("P", 4 + half), ("gate_p", half), ("gate_s", half)], w=[("yt", i2, half)])
            self.v("pool", "tensor_tensor", yt[i2][0:n, hs_], yt[i2][0:n, hs_], xt[i2][0:n, hs_], ALU.add,
                   r=[("yt", i2, half), ("xt", 0)], w=[("yt", i2, half)])
        sl = ss[:, i % 4:i % 4 + 1]
        sk = ("ss", i % 4)
        self.act(xt[0][0:n], yt[i2][0:n], AF.Square, r=[("yt", i2, 0), ("yt", i2, 1)], w=[("xt", 0), sk], accum_out=sl[0:n])
        self.v("dve", "tensor_scalar", sl[0:n], sl[0:n], 1.0 / D, EPS, op0=ALU.mult, op1=ALU.add, r=[sk], w=[sk])
        self.act(sl[0:n], sl[0:n], AF.Ln, r=[sk], w=[sk])
        self.act(sl[0:n], sl[0:n], AF.Exp, r=[sk], w=[sk], scale=-0.5)
        self.act(yt[i2][0:n], yt[i2][0:n], AF.Copy, r=[("yt", i2, 0), ("yt", i2, 1), sk], w=[("yt", i2, 0), ("yt", i2, 1)], scale=sl[0:n])
        self.v("pool", "tensor_tensor", yt[i2][0:n], yt[i2][0:n], fgain[0:n], ALU.mult,
               r=[("yt", i2, 0), ("yt", i2, 1), "fgain"], w=[("yt", i2, 0), ("yt", i2, 1)])
        self.dma("sp", ydst, yt[i2][0:n], "out", r=[("yt", i2, 0), ("yt", i2, 1)])
    A.pop()


Builder.phase_out = _phase_out


def _sample_copies(self):
    LB = [w for (w, d) in GROUPS]
    self.s_cache = cache = [self.inp("cache%d" % g, [4, LB[g], 2, 512]) for g in range(3)]
    self.s_kvs = kvs = [self.outp("kvs%d" % g, [4, LB[g], 2, 512]) for g in range(3)]
    self.s_cat = cat = [self.scratch("cat%d" % g, [4, LB[g] + 8, 2, 512]) for g in range(2)]
    for g in range(3):
        for j in range(4):
            nsp = 4 if g == 2 else 1
            rows = LB[g] - 8
            step = (rows + nsp - 1) // nsp
            for a in range(0, rows, step):
                b_ = min(rows, a + step)
                self.dma("sp", kvs[g][j, a:b_], cache[g][j, 8 + a:8 + b_], "cpy", w=[("kvs", g, j, a)])
            if g < 2:
                self.dma("sp", cat[g][j, 0:LB[g]], cache[g][j], "cpy", w=[("catb", g, j)])


def _phase_sample_attn(self):
    A, P, ps, C, hT = self.A, self.P, self.ps, self.C, self.hT
    wv_in = self.w_in.rearrange("(c p) n -> p c n", p=128)
    LB = [w for (w, d) in GROUPS]
    cache, kvs, cat = self.s_cache, self.s_kvs, self.s_cat
    A.push()
    ident, F_sb = C["ident"], C["F"]
    ones = A.alloc([128])
    self.v("pool", "memset", ones, 1.0, w=["s_ones"])
    z8 = A.alloc([8])
    self.v("pool", "memset", z8, 0.0, w=["z8"])
    onesel = A.alloc([8, 8])
    self.dma("sp", onesel, self.inp("onesel", [128, 8, 8]), "ld0", w=["onesel"])
    selt = A.alloc([8, 128])
    self.dma("sp", selt[0:8], self.inp("selt", [8, 8, 128]), "ld0", w=["selt"])
    biasS = A.alloc([3, 8])
    bold = A.alloc([3, 8])
    tmpF = A.alloc([8])
    dgF = A.alloc([8])
    for g in range(3):
        self.mm(ps[:, 0, 0:8], F_sb[0:8, g, 0:128], ident[0:8, 0:8], r=[("F", g), "ident"], w=[("P", 0)])
        self.act(tmpF, ps[:, 0, 0:8], AF.Copy, r=[("P", 0)], w=["tmpF"])
        self.mm(ps[:, 0, 8:16], C["antiid"], tmpF, r=["tmpF", "antiid"], w=[("P", 0)])
        self.act(biasS[:, g, :], ps[:, 0, 8:16], AF.Copy, r=[("P", 0)], w=[("biasS", g)])
        self.v("dve", "tensor_tensor", dgF[0:8], ident[0:8, 0:8], F_sb[0:8, g, 128:129].to_broadcast([8, 8]), ALU.mult,
               r=[("F", g), "ident"], w=["dgF"])
        self.mm(ps[0:8, 0, 16:24], ones[0:8, 0:8], dgF[0:8], r=["dgF", "s_ones"], w=[("P", 0)])
        self.act(bold[0:8, g, :], ps[0:8, 0, 16:24], AF.Copy, r=[("P", 0)], w=[("bold", g)])
    wq = [A.alloc([8, 512], BF16) for _ in range(2)]
    qj = [A.alloc([1536]) for _ in range(4)]
    kj = A.alloc([1536])
    vj = A.alloc([1536])
    oldkv = [A.alloc([3, 2, 512]) for _ in range(1)][0]
    wcnt = 0
    for kind, off in (("q", OFF_Q), ("k", OFF_K), ("v", OFF_V)):
        for g in range(3):
            wb = wq[wcnt % 2]
            wk_ = ("swq", wcnt % 2)
            wcnt += 1
            self.dma("pool", wb, wv_in[:, :, off + g * 512:off + (g + 1) * 512], "ldw", w=[wk_])
            for j in range(4):
                bank = 1 + (j % 2)
                for c in range(8):
                    self.mm(ps[0:8, bank, :], hT[:, c, HALO + SEG + 8 * j:HALO + SEG + 8 * j + 8], wb[:, c, :],
                            start=(c == 0), stop=(c == 7), r=[wk_, (("hT", 32), c)], w=[("P", bank)])
                if kind == "q":
                    self.act(qj[j][0:8, g * 512:(g + 1) * 512], ps[0:8, bank, :], AF.Copy, r=[("P", bank)], w=[("qj", j, g)], scale=0.125)
                else:
                    st = kj if kind == "k" else vj
                    sk = ("kvst", kind, j % 3)
                    sl_ = st[0:8, (j % 3) * 512:(j % 3 + 1) * 512]
                    self.act(sl_, ps[0:8, bank, :], AF.Copy, r=[("P", bank)], w=[sk])
                    kvi = 0 if kind == "k" else 1
                    self.dma("sp", kvs[g][j, LB[g] - 8:LB[g], kvi, :], sl_, "out", r=[sk], w=[("kvsn", g, j, kvi)])
                    if g < 2:
                        self.dma("sp", cat[g][j, LB[g]:LB[g] + 8, kvi, :], sl_, "out", r=[sk], w=[("catn", g, j, kvi)])
    szs = A.alloc([4, TS])
    wz = wq[0]
    self.dma("pool", wz, wv_in[:, :, OFF_ZA:OFF_ZA + 512], "ldw", w=[("swq", 0)])
    for hp in range(4):
        for c in range(8):
            self.mm(ps[:, 3, 0:TS], wz[:, c, hp * 128:(hp + 1) * 128], hT[:, c, HALO + SEG:HALO + SEG + TS], start=(c == 0), stop=(c == 7),
                    r=[("swq", 0), (("hT", 32), c)], w=[("P", 3)])
        self.act(szs[:, hp, :], ps[:, 3, 0:TS], AF.Silu, r=[("P", 3)], w=[("szs", hp)])
    NKG = 6
    Kg = [A.alloc([2, 512]) for _ in range(NKG)]
    prod = A.alloc([512])
    lg = [A.alloc([8]) for _ in range(4)]
    Pz = [A.alloc([8, 8]) for _ in range(NKG)]
    numS = A.alloc([512])
    denS = A.alloc([8])
    pold = A.alloc([8])
    tmpo = A.alloc([512])
    oj = A.alloc([512])
    u = 0
    for j in range(4):
        for g in range(3):
            self.dma("sp", oldkv[0:8, g], cache[g][j, 0:8], "gat", w=[("old", g)])
        self.mm(ps[0:8, 5, :], z8[0:8, 0:8], qj[j][0:8, 0:512], start=True, stop=False, r=["z8", ("qj", j, 0)], w=[("P", 5)])
        self.mm(ps[0:8, 6, 0:8], z8[0:8, 0:8], qj[j][0:8, 0:8], start=True, stop=False, r=["z8", ("qj", j, 0)], w=[("P", 6)])
        first = False
        for g, (win, d) in enumerate(GROUPS):
            for t in range(8):
                kb = Kg[u % NKG]
                kk = ("Kg", u % NKG)
                pz = Pz[u % NKG]
                pk = ("Pz", u % NKG)
                lgt = lg[u % 4]
                lk = ("lg", u % 4)
                u += 1
                a0 = d + t
                if g < 2:
                    src = cat[g][j, a0:a0 + 127 * d + 1:d]
                    deps = [("catb", g, j), ("catn", g, j, 0), ("catn", g, j, 1)]
                else:
                    src = kvs[2][j, a0 - 8:a0 - 8 + 127 * d + 1:d]
                    deps = [("kvs", 2, j, a) for a in range(0, LB[2] - 8, (LB[2] - 8 + 3) // 4)] + [("kvsn", 2, j, 0), ("kvsn", 2, j, 1)]
                self.dma("sp", kb, src, "gat", r=deps, w=[kk])
                self.mm(ps[:, 4, :], selt[0:8, t, :], qj[j][0:8, g * 512:(g + 1) * 512], r=["selt", ("qj", j, g)], w=[("P", 4)])
                self.v("dve", "tensor_tensor", prod, kb[:, 0, :], ps[:, 4, :], ALU.mult, r=[kk, ("P", 4)], w=["prod"])
                self.v("dve", "tensor_reduce", lgt, prod.rearrange("p (h e) -> p h e", h=8), AX.X, ALU.add, r=["prod"], w=[lk])
                self.v("dve", "tensor_tensor", lgt, lgt, biasS[:, g, :], ALU.add, r=[lk, ("biasS", g)], w=[lk])
                pe_ = pz[:, 0, :]
                self.act(pe_, lgt, AF.Exp, r=[lk], w=[pk])
                last = (g == 2 and t == 7)
                kv3 = kb[:, 1, :].rearrange("p (h e) -> p h e", h=8)
                self.v("dve", "tensor_tensor", kv3, kv3, pe_.unsqueeze(2).to_broadcast([128, 8, 64]), ALU.mult, r=[pk, kk], w=[kk])
                self.mm(ps[0:8, 5, :], onesel[:, t, :], kb[:, 1, :], start=False, stop=last, r=["onesel", kk], w=[("P", 5)])
                self.mm(ps[0:8, 6, 0:8], onesel[:, t, :], pe_, start=False, stop=last, r=["onesel", pk], w=[("P", 6)])
        self.v("dve", "tensor_copy", numS[0:8], ps[0:8, 5, :], r=[("P", 5)], w=["numS"])
        self.v("dve", "tensor_copy", denS[0:8], ps[0:8, 6, 0:8], r=[("P", 6)], w=["denS"])
        for g in range(3):
            self.v("dve", "tensor_tensor", tmpo[0:8], qj[j][0:8, g * 512:(g + 1) * 512], oldkv[0:8, g, 0, :], ALU.mult,
                   r=[("qj", j, g), ("old", g)], w=["tmpo"])
            self.v("dve", "tensor_reduce", pold[0:8], tmpo[0:8].rearrange("p (h e) -> p h e", h=8), AX.X, ALU.add, r=["tmpo"], w=["pold"])
            self.v("dve", "tensor_tensor", pold[0:8], pold[0:8], bold[0:8, g, :], ALU.add, r=["pold", ("bold", g)], w=["pold"])
            self.act(pold[0:8], pold[0:8], AF.Exp, r=["pold"], w=["pold"])
            self.v("dve", "tensor_tensor", denS[0:8], denS[0:8], pold[0:8], ALU.add, r=["denS", "pold"], w=["denS"])
            self.v("dve", "tensor_tensor", tmpo[0:8].rearrange("p (h e) -> p h e", h=8),
                   oldkv[0:8, g, 1, :].rearrange("p (h e) -> p h e", h=8), pold[0:8].unsqueeze(2).to_broadcast([8, 8, 64]), ALU.mult,
                   r=[("old", g), "pold"], w=["tmpo"])
            self.v("dve", "tensor_tensor", numS[0:8], numS[0:8], tmpo[0:8], ALU.add, r=["numS", "tmpo"], w=["numS"])
        self.v("dve", "reciprocal", denS[0:8], denS[0:8], r=["denS"], w=["denS"])
        self.v("dve", "tensor_tensor", oj[0:8].rearrange("p (h e) -> p h e", h=8), numS[0:8].rearrange("p (h e) -> p h e", h=8),
               denS[0:8].unsqueeze(2).to_broadcast([8, 8, 64]), ALU.mult, r=["numS", "denS"], w=["oj"])
        for hp in range(4):
            self.tr(ps[:, 7, 0:8], oj[0:8, hp * 128:(hp + 1) * 128], ident[0:8, 0:8], r=["oj", "ident"], w=[("P", 7)])
            self.v("dve", "tensor_tensor", self.attT[:, hp, SEG + 8 * j:SEG + 8 * j + 8], ps[:, 7, 0:8], szs[:, hp, 8 * j:8 * j + 8], ALU.mult,
                   r=[("P", 7), ("szs", hp)], w=[("attTs", hp, j)])
    A.pop()


Builder.phase_sample_attn = _phase_sample_attn
Builder.sample_copies = _sample_copies


_PROG_CACHE = {}


def kernel(**inputs):
    inp = {k: np.asarray(v) for k, v in inputs.items()}
    if "prog" not in _PROG_CACHE:
        _PROG_CACHE["prog"] = build_program(99)
    b = _PROG_CACHE["prog"]
    cores = list(range(NCORES))
    maps = make_in_maps(inp, cores)
    maps = [{k: np.ascontiguousarray(v, dtype=np.float32) for k, v in m.items() if k in b.din} for m in maps]
    res = run_bass_kernel_spmd(b.nc, maps, core_ids=cores).results
    f32 = np.float32
    y_p = np.zeros((2, SEQ, D), f32)
    y_s = np.zeros((32, 8, D), f32)
    kvp = [np.zeros((1, 2, w, 2, 8, 64), f32) for (w, d) in GROUPS]
    kvs = [np.zeros((1, 32, w, 2, 8, 64), f32) for (w, d) in GROUPS]
    conv_p = np.zeros((1, 2, 3, D), f32)
    conv_s = np.zeros((1, 32, 3, D), f32)
    C_p = np.zeros((1, 2, 4, 256, 128), f32)
    C_s = np.zeros((1, 32, 4, 256, 128), f32)
    n_p = np.zeros((1, 2, 4, 128), f32)
    n_s = np.zeros((1, 32, 4, 128), f32)
    m_p = np.zeros((1, 2, 4), f32)
    m_s = np.zeros((1, 32, 4), f32)
    for c in cores:
        r = res[c]
        bb, p = c // 4, c % 4
        sq = slice(4 * c, 4 * c + 4)
        y_p[bb, p * SEG:(p + 1) * SEG] = r["y_p"]
        y_s[sq] = np.asarray(r["y_s"]).reshape(4, 8, D)
        for g in range(3):
            kvs[g][0, sq] = np.asarray(r["kvs%d" % g]).reshape(4, -1, 2, 8, 64)
        Cs = np.asarray(r["Cs"])
        C_s[0, sq] = Cs[..., :256].transpose(0, 2, 3, 1)
        n_s[0, sq] = Cs[..., 256].transpose(0, 2, 1)
        m_s[0, sq] = np.asarray(r["ms"])[:, :, 0]
        conv_s[0, sq] = np.asarray(r["convs"]).transpose(0, 3, 2, 1).reshape(4, 3, D)
        if p == 3:
            for g in range(3):
                kvp[g][0, bb] = np.asarray(r["kvp%d" % g]).reshape(-1, 2, 8, 64)
            Cp = np.asarray(r["Cp"])
            C_p[0, bb] = Cp[..., :256].transpose(1, 2, 0)
            n_p[0, bb] = Cp[..., 256].T
            m_p[0, bb] = np.asarray(r["mp"])[:, 0]
            conv_p[0, bb] = np.asarray(r["convp"]).transpose(2, 1, 0).reshape(3, D)
    return (y_p, y_s, kvp[0], kvs[0], kvp[1], kvs[1], kvp[2], kvs[2],
            conv_p, conv_s, C_p, C_s, n_p, n_s, m_p, m_s)


def _mlstm_prefix(self):
    A, P, ps, C, M, hT = self.A, self.P, self.ps, self.C, self.M, self.hT
    NT = 48
    xf = self.inp("xf", [PREFIX - HALO, 1024])
    ident, tri = C["ident"], C["tri"]
    hTp = [A.alloc([8, 128], BF16) for _ in range(2)]
    gt_all = A.alloc([NT, 8])
    lf = A.alloc([NT, 4])
    ie = A.alloc([NT, 4])
    tot = A.alloc([NT, 4])
    incl = A.alloc([NT, 4])
    a_all = A.alloc([NT, 4])
    wk_all = A.alloc([NT, 4])
    negtv = A.alloc([NT])
    pm = A.alloc([4])
    row1 = A.alloc([1])
    dg = A.alloc([4])
    Mg_bc = A.alloc([4])
    cb4 = A.alloc([4, 8, 131])
    hT4 = A.alloc([8, 512], BF16)
    cacc = A.alloc([8, 128])
    cexp = A.alloc([8, 128])
    c_act = [A.alloc([8, 128], BF16) for _ in range(2)]
    vaug = [A.alloc([4, 257], BF16) for _ in range(2)]
    kmw = [A.alloc([4, 128], BF16) for _ in range(2)]
    for i in range(2):
        self.v("pool", "memset", vaug[i][:, :, 256:257], 1.0, w=[("pvaug1", i)])

    def tile_src(i, slot):
        if i < 32:
            hp_ = hTp[i % 2]
            self.norm_tile(xf[i * 128:(i + 1) * 128, :], 128, hp_, "p", slot, ("hTp", i % 2), bank0=5)
            return hp_, [("hTp", i % 2)]
        ti = i - 32
        return hT[:, :, ti * 128:(ti + 1) * 128], [("hT", ti)]

    for i in range(NT):
        hTt, hk = tile_src(i, i)
        bank = 7 if i % 2 == 0 else 4
        gps = ps[:, bank, 0:8]
        for c in range(8):
            self.mm(gps, hTt[:, c, :], M["wg"][:, c, :], start=(c == 0), stop=(c == 7),
                    r=["wg"] + [(k, c) for k in hk], w=[("P", bank)])
        self.v("dve", "tensor_tensor", gt_all[:, i, :], gps, C["b_if_bc"], ALU.add, r=[("P", bank), "b_if_bc"], w=["gt_all"])
    tv48 = C["tvalid"][:, 0:NT]
    self.v("dve", "tensor_scalar", negtv, tv48, -1.0, None, op0=ALU.mult, r=["tvalid"], w=["negtv"])
    self.act(lf, gt_all[:, :, 4:8], AF.Exp, r=["gt_all"], w=["lf"], scale=-1.0)
    self.act(lf, lf, AF.Ln, r=["lf"], w=["lf"], bias=M["one1"])
    self.v("dve", "tensor_tensor", lf, lf, negtv.unsqueeze(2).to_broadcast([128, NT, 4]), ALU.mult, r=["lf", "negtv"], w=["lf"])
    self.v("dve", "tensor_tensor", ie, gt_all[:, :, 0:4], tv48.unsqueeze(2).to_broadcast([128, NT, 4]), ALU.mult,
           r=["gt_all", "tvalid"], w=["ie"])
    self.v("dve", "tensor_tensor", ie, ie, M["negbig"][:, 0:NT].unsqueeze(2).to_broadcast([128, NT, 4]), ALU.add,
           r=["ie", "negbig"], w=["ie"])
    lf2 = lf.rearrange("p t h -> p (t h)")
    self.mm(ps[:, 0, 0:NT * 4], tri, lf2, r=["lf", "tri"], w=[("P", 0)])
    self.mm(ps[:, 1, 0:NT * 4], M["ones"], lf2, r=["lf", "ones"], w=[("P", 1)])
    self.v("dve", "tensor_copy", tot.rearrange("p t h -> p (t h)"), ps[:, 1, 0:NT * 4], r=[("P", 1)], w=["tot"])
    for h in range(4):
        self.v("dve", "tensor_tensor_scan", incl[:, :, h], M["ones"][:, 0:NT], tot[:, :, h], 0.0, op0=ALU.mult, op1=ALU.add,
               r=["tot", "ones"], w=[("incl", h)])
    inclk = [("incl", h) for h in range(4)]
    self.v("dve", "tensor_tensor", a_all, ie, incl, ALU.subtract, r=["ie"] + inclk, w=["a_all"])
    self.v("dve", "tensor_tensor", a_all, a_all, tot, ALU.add, r=["a_all", "tot"], w=["a_all"])
    self.v("dve", "tensor_tensor", a_all.rearrange("p t h -> p (t h)"), a_all.rearrange("p t h -> p (t h)"), ps[:, 0, 0:NT * 4],
           ALU.subtract, r=["a_all", ("P", 0)], w=["a_all"])
    self.v("dve", "tensor_reduce", pm, a_all.rearrange("p t h -> p h t"), AX.X, ALU.max, r=["a_all"], w=["pm"])
    self.mm(ps[0:4, 2, 0:128], pm, ident, r=["pm", "ident"], w=[("P", 2)])
    self.v("dve", "tensor_reduce", row1[0:4], ps[0:4, 2, 0:128], AX.X, ALU.max, r=[("P", 2)], w=["row1"])
    self.v("dve", "tensor_scalar", row1[0:4], row1[0:4], 0.0, None, op0=ALU.max, r=["row1"], w=["row1"])
    self.v("dve", "tensor_tensor", dg[0:4], ident[0:4, 0:4], row1[0:4, 0:1].to_broadcast([4, 4]), ALU.mult, r=["row1", "ident"], w=["pdg"])
    self.mm(ps[:, 2, 128:132], M["ones"][0:4, :], dg[0:4], r=["pdg", "ones"], w=[("P", 2)])
    self.v("dve", "tensor_copy", Mg_bc, ps[:, 2, 128:132], r=[("P", 2)], w=["Mg_bc"])
    self.v("dve", "tensor_tensor", wk_all, a_all, Mg_bc.unsqueeze(1).to_broadcast([128, NT, 4]), ALU.subtract,
           r=["a_all", "Mg_bc"], w=["wk_all"])
    self.act(wk_all, wk_all, AF.Exp, r=["wk_all"], w=["wk_all"])
    self.v("dve", "tensor_tensor", M["m_bc"], incl[:, NT - 1, :], Mg_bc, ALU.add, r=inclk + ["Mg_bc"], w=["m_bc"])
    self.mm(ps[0:4, 2, 136:137], M["m_bc"][0:1, 0:4], M["ones"][0:1, 0:1], r=["m_bc", "ones"], w=[("P", 2)])
    self.v("dve", "tensor_copy", M["m_row"][0:4], ps[0:4, 2, 136:137], r=[("P", 2)], w=["m_row"])
    cw, cbias = C["conv_wT"], C["conv_bT"]
    hist = A.alloc([8, 3])
    self.v("pool", "memset", hist, 0.0, w=["hist"])
    for g0 in range(0, NT, 4):
        if g0 < 32:
            for q in range(4):
                i = g0 + q
                self.norm_tile(xf[i * 128:(i + 1) * 128, :], 128, hT4[:, :, q * 128:(q + 1) * 128], "p", NT + i, ("hT4", q), bank0=5)
            src4 = hT4
            hk4 = [("hT4", q) for q in range(4)]
            tsl = lambda q: hT4[:, :, q * 128:(q + 1) * 128]
        else:
            t0 = g0 - 32
            src4 = hT[:, :, t0 * 128:(t0 + 4) * 128]
            hk4 = [("hT", t0 + q) for q in range(4)]
            tsl = lambda q, t0=t0: hT[:, :, (t0 + q) * 128:(t0 + q + 1) * 128]
        for fb in range(8):
            bank = fb % 2
            for c in range(8):
                self.mm(ps[:, bank, :], M["wxm"][:, c, fb * 128:(fb + 1) * 128], src4[:, c, :], start=(c == 0), stop=(c == 7),
                        r=["wxm"] + [(k, c) for k in hk4], w=[("P", bank)])
            self.act(cb4[:, :, fb, 3:131], ps[:, bank, :].rearrange("p (q t) -> p q t", q=4), AF.Copy,
                     r=[("P", bank)], w=[("pcb", q) for q in range(4)])
        for q in range(4):
            i = g0 + q
            par = i % 2
            hTt, hk = tsl(q), [hk4[q]]
            tv = C["tvalid"][:, i:i + 1]
            cbp = cb4[:, q]
            self.v("pool", "tensor_copy", cbp[:, :, 0:3], hist, r=["hist"], w=[("pcb", q)])
            for half in range(2):
                for c in range(8):
                    self.mm(ps[:, 2 + half, :], hTt[:, c, :], M["wxm"][:, c, half * 512:(half + 1) * 512], start=(c == 0), stop=(c == 7),
                            r=["wxm"] + [(k, c) for k in hk], w=[("P", 2 + half)])
                self.act(vaug[par][:, 2 * half:2 * half + 2, 0:256], ps[:, 2 + half, :].rearrange("p (h v) -> p h v", h=2), AF.Copy,
                         r=[("P", 2 + half), ("pvaug1", par)], w=[("pvaug", par)])
            for fb in range(8):
                self.act(cacc[:, fb, :], cbp[:, fb, 3:131], AF.Identity, r=[("pcb", q), "conv_wT", "conv_bT"], w=[("pcacc", fb)],
                         scale=cw[:, fb, 3:4], bias=cbias[:, fb:fb + 1])
            for j in range(3):
                for fb in range(8):
                    self.v("dve", "scalar_tensor_tensor", cacc[:, fb, :], cbp[:, fb, j:j + 128], cw[:, fb, j:j + 1], cacc[:, fb, :],
                           op0=ALU.mult, op1=ALU.add, r=[("pcb", q), ("pcacc", fb)], w=[("pcacc", fb)])
            self.v("dve", "tensor_scalar", hist, cbp[:, :, 128:131], tv, M["one1"], op0=ALU.mult, op1=ALU.mult,
                   r=[("pcb", q), "tvalid", "one1"], w=["hist"])
            for fb in range(8):
                self.act(c_act[par][:, fb, :], cacc[:, fb, :], AF.Silu, r=[("pcacc", fb)], w=[("pc_act", par, fb)])
            kmps = ps[:, 4, :].rearrange("p (h k) -> p h k", h=4)
            for h in range(4):
                for ec in range(2):
                    self.mm(kmps[:, h, :], c_act[par][:, 2 * h + ec, :], M["wmk"][:, ec, h, :], start=(ec == 0), stop=(ec == 1),
                            r=[("pc_act", par, 2 * h + ec), "wmk"], w=[("P", 4)])
            self.v("dve", "tensor_tensor", kmw[par], kmps, wk_all[:, i, :].unsqueeze(2).to_broadcast([128, 4, 128]), ALU.mult,
                   r=[("P", 4), "wk_all"], w=[("pkmw", par)])
            for h in range(4):
                bk = 2 + (h % 2)
                dps = ps[:, bk, 0:257]
                self.mm(dps, kmw[par][:, h, :], vaug[par][:, h, :], r=[("pkmw", par), ("pvaug", par)], w=[("P", bk)])
                self.v("dve", "tensor_tensor", M["CT"][:, h, :], M["CT"][:, h, :], dps, ALU.add, r=[("P", bk), "CT"], w=["CT"])
    self.v("pool", "tensor_copy", M["convbuf"][:, :, 0:3], hist, r=["hist"], w=[("cb", 0), ("cb", 1)])


Builder.mlstm_prefix = _mlstm_prefix
```

```python
import numpy as np
import concourse.bass as bass
import concourse.mybir as mybir
from concourse.bass_utils import run_bass_kernel_spmd

F32 = mybir.dt.float32
BF16 = mybir.dt.bfloat16
AF = mybir.ActivationFunctionType
ALU = mybir.AluOpType
AX = mybir.AxisListType

NCORES = 8
D = 1024
SEQ = 8192
SEG = 2048
NLT = SEG // 128
HALO = 2048
PREFIX = 6144
TS = 32
PW = 10248
OFF_Q, OFF_K, OFF_V, OFF_ZA, OFF_XM, OFF_ZM, OFF_OM, OFF_I, OFF_F, OFF_GA, OFF_GM = (
    0, 1536, 3072, 4608, 5120, 6144, 7168, 8192, 8196, 8200, 9224)
GROUPS = ((128, 1), (512, 4), (2048, 16))
EPS = 1e-6
NEG = -30000.0
RAW_GAP = 1

DEBUG = {}


class _Op:
    __slots__ = ("eng", "fn", "deps", "stream", "signal", "val", "idx", "pos", "slot")


class Prog:
    ENGS = ("pe", "act", "dve", "pool", "sp")

    def __init__(self, nc):
        self.nc = nc
        self.ops = {e: [] for e in self.ENGS}
        self.lastw = {}
        self.readers = {}
        self.streams = {}
        self.barrier_deps = []
        self.nops = 0

    def add(self, eng, fn, r=(), w=(), dma=None):
        op = _Op()
        op.eng = eng
        op.fn = fn
        op.stream = dma
        op.signal = False
        op.val = None
        op.idx = self.nops
        self.nops += 1
        deps = {}
        for k in r:
            d = self.lastw.get(k)
            if d is not None:
                deps[d.idx] = (d, True)
            if isinstance(k, tuple) and k[0] == "P":
                for d in self.readers.get(k, ()):
                    if d.eng != eng and d.idx not in deps:
                        deps[d.idx] = (d, False)
        for k in w:
            d = self.lastw.get(k)
            if d is not None and d.idx not in deps:
                deps[d.idx] = (d, False)
            for d in self.readers.get(k, ()):
                if d.idx not in deps:
                    deps[d.idx] = (d, False)
        for d in self.barrier_deps:
            if d.idx not in deps:
                deps[d.idx] = (d, True)
        op.deps = list(deps.values())
        for k in w:
            self.lastw[k] = op
            self.readers[k] = []
        for k in r:
            self.readers.setdefault(k, []).append(op)
        op.pos = len(self.ops[eng])
        self.ops[eng].append(op)
        if dma is not None:
            self.streams.setdefault(dma, []).append(op)
        return op

    KSEM = 8
    KSEM_STREAM = {"cpy": 32}
    PERSIST = ("cpy",)

    def kof(self, s):
        return self.KSEM_STREAM.get(s, self.KSEM)

    def barrier(self):
        deps = []
        for e in self.ENGS:
            if self.ops[e]:
                for op in reversed(self.ops[e]):
                    if op.stream is None:
                        deps.append(op)
                        break
        for s, lst in self.streams.items():
            if s in self.PERSIST:
                continue
            deps.extend(lst[-self.kof(s):])
        self.barrier_deps = deps
        self.lastw = {k: v for k, v in self.lastw.items() if v.stream in self.PERSIST}
        self.readers = {}

    def emit(self, final_streams):
        nc = self.nc
        K = self.KSEM
        need = {}
        for e in self.ENGS:
            for op in self.ops[e]:
                lst = []
                for d, raw in op.deps:
                    if d.stream is None and d.eng == e:
                        if e in ("pe", "sp"):
                            continue
                        if not raw:
                            continue
                        if op.pos - d.pos > RAW_GAP:
                            continue
                    d.signal = True
                    lst.append(d)
                need[op.idx] = lst
        for e in self.ENGS:
            cnt = 0
            for op in self.ops[e]:
                if op.stream is None and op.signal:
                    cnt += 1
                    op.val = cnt
        for s, lst in self.streams.items():
            Ks = self.kof(s)
            for i, op in enumerate(lst):
                op.slot = i % Ks
                op.val = 16 * (i // Ks + 1)
        import contextlib
        with contextlib.ExitStack() as st:
            sems = {e: st.enter_context(nc.semaphore("s_" + e)) for e in self.ENGS}
            ssems = {s: [st.enter_context(nc.semaphore("d_%s_%d" % (s, k))) for k in range(min(self.kof(s), len(lst)))]
                     for s, lst in self.streams.items()}
            block = st.enter_context(nc.Block())

            def run(e, eng):
                waited = {}

                def wait(key, v):
                    if v > waited.get(key, 0):
                        waited[key] = v
                        sem = ssems[key[1]][key[2]] if key[0] == "s" else sems[key[1]]
                        eng.wait_ge(sem, v)

                for op in self.ops[e]:
                    w = {}
                    for d in need[op.idx]:
                        key = ("s", d.stream, d.slot) if d.stream is not None else ("e", d.eng)
                        if d.val > w.get(key, 0):
                            w[key] = d.val
                    for key, v in w.items():
                        wait(key, v)
                    if op.stream is not None and op.val > 16:
                        wait(("s", op.stream, op.slot), op.val - 16)
                    ins = op.fn(eng)
                    if op.stream is not None:
                        ins.then_inc(ssems[op.stream][op.slot], 16)
                    elif op.signal:
                        ins.then_inc(sems[e], 1)
                if e == "sp":
                    for s in final_streams:
                        if s in self.streams:
                            for op in self.streams[s][-self.kof(s):]:
                                wait(("s", s, op.slot), op.val)

            block.tensor(lambda t: run("pe", t))
            block.scalar(lambda t: run("act", t))
            block.vector(lambda t: run("dve", t))
            block.gpsimd(lambda t: run("pool", t))
            block.sync(lambda t: run("sp", t))


class Arena:
    def __init__(self, nc, words):
        self.t = nc.alloc_sbuf_tensor("arena", [128, words], F32)
        self.words = words
        self.top = 0
        self.marks = []

    def push(self):
        self.marks.append(self.top)

    def pop(self):
        self.top = self.marks.pop()

    def alloc(self, shape, dt=F32):
        n = int(np.prod(shape))
        words = (n + 1) // 2 if dt == BF16 else n
        words = (words + 7) // 8 * 8
        assert self.top + words <= self.words, ("SBUF arena overflow", self.top, words, self.words)
        ap = self.t[:, self.top:self.top + words]
        self.top += words
        if dt == BF16:
            ap = ap.bitcast(BF16)
        ap = ap[:, 0:n]
        if len(shape) == 2:
            ap = ap.rearrange("p (a b) -> p a b", a=shape[0])
        elif len(shape) == 3:
            ap = ap.rearrange("p (a b c) -> p a b c", a=shape[0], b=shape[1])
        elif len(shape) == 4:
            ap = ap.rearrange("p (a b c d) -> p a b c d", a=shape[0], b=shape[1], c=shape[2])
        return ap


def _t5_bucket(dist):
    n = np.asarray(dist).astype(np.int64)
    max_exact = 16
    nf = np.maximum(n, 1).astype(np.float32)
    large = max_exact + (np.log(nf / max_exact) / np.log(np.float32(2048 / max_exact))
                         * (32 - max_exact)).astype(np.int64)
    large = np.minimum(large, 31)
    return np.where(n < max_exact, n, large).astype(np.int32)


def _consts():
    c = {}
    c["ident"] = np.eye(128, dtype=np.float32)
    c["antiid"] = np.eye(128, dtype=np.float32)[::-1].copy()
    s = np.arange(128)
    c["tri"] = (s[:, None] <= s[None, :]).astype(np.float32)
    c["cmaskT"] = np.where(s[:, None] <= s[None, :], 0.0, NEG).astype(np.float32)
    oh = np.zeros((32, 3, 129), np.float32)
    for g, (_, d) in enumerate(GROUPS):
        b = _t5_bucket(np.arange(129) * d)
        oh[b, g, np.arange(129)] = 1.0
    c["ohd"] = oh
    selp = np.zeros((5, 128), np.float32); selp[0] = 1.0
    sels = np.zeros((5, 128), np.float32)
    for i in range(TS):
        sels[1 + i // 8, i] = 1.0
    c["selp"] = selp
    c["sels"] = sels
    return c


def _fm(v):
    return np.ascontiguousarray(v.reshape(-1, 128).T)


class Builder:
    def __init__(self):
        nc = bass.Bass("TRN2", target_bir_lowering=False)
        self.nc = nc
        self.P = Prog(nc)
        self.A = Arena(nc, 53184)
        self.ps = nc.alloc_psum_tensor("psum", [128, 8, 512], F32)
        self.din = {}
        self.dout = {}
        self.uid = 0

    def inp(self, name, shape, dt=F32):
        t = self.nc.dram_tensor(name, list(shape), dt, kind="ExternalInput").ap()
        self.din[name] = t
        return t

    def outp(self, name, shape, dt=F32):
        t = self.nc.dram_tensor(name, list(shape), dt, kind="ExternalOutput").ap()
        self.dout[name] = t
        return t

    def scratch(self, name, shape, dt=F32):
        return self.nc.dram_tensor(name, list(shape), dt).ap()

    def mm(self, out, lhsT, rhs, start=True, stop=True, r=(), w=()):
        return self.P.add("pe", lambda e: e.matmul(out, lhsT, rhs, start=start, stop=stop), r, w)

    def tr(self, out, in_, ident, r=(), w=()):
        return self.P.add("pe", lambda e: e.transpose(out, in_, ident), r, w)

    def act(self, out, in_, func, r=(), w=(), **kw):
        return self.P.add("act", lambda e: e.activation(out=out, in_=in_, func=func, **kw), r, w)

    def v(self, eng, name, *args, r=(), w=(), **kw):
        return self.P.add(eng, lambda e: getattr(e, name)(*args, **kw), r, w)

    def dma(self, eng, out, in_, stream, r=(), w=(), **kw):
        return self.P.add(eng, lambda e: e.dma_start(out=out, in_=in_, **kw), r, w, dma=stream)

    def dbg(self, name, ap, shape, dt=F32, r=()):
        if not DEBUG.get(name):
            return
        o = self.outp("dbg_" + name, shape, dt)
        self.P.barrier()
        self.dma("sp", o, ap, "dbg", r=r)

    def key(self, base):
        self.uid += 1
        return (base, self.uid)

    def phase0(self):
        A, P, ps = self.A, self.P, self.ps
        C = self.C = {}

        def load(name, shape, parts=128, dt=F32, eng="sp", src=None):
            t = A.alloc(list(shape[1:]), dt)
            src = self.inp(name, shape) if src is None else src
            self.dma(eng, t[0:parts], src, "ld0", w=[name])
            C[name] = t
            return t

        load("ident", [128, 128])
        C["ident_bf"] = A.alloc([128], BF16)
        self.dma("pool", C["ident_bf"], self.din["ident"], "ldc", w=["ident_bf"])
        C["antiid_bf"] = A.alloc([128], BF16)
        self.dma("pool", C["antiid_bf"], self.inp("antiid", [128, 128]), "ldc", w=["antiid_bf"])
        load("tri", [128, 128])
        load("antiid", [128, 128], src=self.din["antiid"])
        C["cmaskT_bf"] = A.alloc([128], BF16)
        self.dma("pool", C["cmaskT_bf"], self.inp("cmaskT", [128, 128]), "ldc", w=["cmaskT_bf"])
        load("ohd", [32, 3, 129], parts=32)
        load("selp", [5, 128], parts=5)
        load("sels", [5, 128], parts=5)
        load("rel_table", [32, 24], parts=32)
        for nm in ("gainT", "conv_bT", "m_normT", "m_skipT"):
            load(nm, [128, 8])
        load("b_adaT", [128, 24])
        load("conv_wT", [128, 8, 4])
        load("b_if_bc", [128, 8])
        load("tvalid", [128, 64])
        load("cT", [128, 8, 5])

        siluT = A.alloc([8, 5], BF16)
        self.act(siluT, C["cT"], AF.Silu, r=["cT"], w=["siluT"])

        ada = A.alloc([24, 5])
        mult = A.alloc([8, 5])
        gate_p = A.alloc([1024])
        gate_s = A.alloc([1024])
        A.push()
        load("b_gate_rows", [5, 1024], parts=5)
        gate_rows = A.alloc([1024])
        w_ada = self.inp("w_ada", [1024, 3072])
        wada = A.alloc([8, 3072], BF16)
        wv = w_ada.rearrange("(c p) n -> p c n", p=128)
        for c in range(8):
            self.dma("pool", wada[:, c, :], wv[:, c, :], "ldw", w=[("wada", c)])
        adaps = ps[:, 0, 0:120].rearrange("p (a b) -> p a b", a=24)
        for cb in range(24):
            for c in range(8):
                self.mm(adaps[:, cb, :], wada[:, c, cb * 128:(cb + 1) * 128], siluT[:, c, :],
                        start=(c == 0), stop=(c == 7), r=[("wada", c), "siluT"], w=[("P", 0)])
        for half in range(2):
            for c in range(8):
                self.mm(ps[0:5, 1 + half, :], siluT[:, c, :], wada[:, c, 2048 + half * 512:2048 + (half + 1) * 512],
                        start=(c == 0), stop=(c == 7), r=[("wada", c), "siluT"], w=[("P", 1 + half)])
        self.v("dve", "tensor_tensor", ada, adaps, C["b_adaT"].unsqueeze(2).to_broadcast([128, 24, 5]), ALU.add,
               r=[("P", 0), "b_adaT"], w=["ada"])
        self.v("dve", "tensor_scalar", mult, ada[:, 8:16, :], 1.0, None, op0=ALU.add, r=["ada"], w=["mult"])
        self.v("dve", "tensor_tensor", mult, mult, C["gainT"].unsqueeze(2).to_broadcast([128, 8, 5]), ALU.mult,
               r=["mult", "gainT"], w=["mult"])
        C["mult"] = mult
        C["shift"] = ada[:, 0:8, :]
        C["ada"] = ada
        for half in range(2):
            self.v("dve", "tensor_tensor", gate_rows[0:5, half * 512:(half + 1) * 512], ps[0:5, 1 + half, :],
                   C["b_gate_rows"][0:5, half * 512:(half + 1) * 512], ALU.add,
                   r=[("P", 1 + half), "b_gate_rows"], w=[("gate_rows", half)])
        for half in range(2):
            sl = slice(half * 512, (half + 1) * 512)
            self.mm(ps[:, 3, :], C["selp"][0:5, :], gate_rows[0:5, sl], r=[("gate_rows", half), "selp"], w=[("P", 3)])
            self.act(gate_p[:, sl], ps[:, 3, :], AF.Copy, r=[("P", 3)], w=[("gate_p", half)])
            self.mm(ps[:, 4, :], C["sels"][0:5, :], gate_rows[0:5, sl], r=[("gate_rows", half), "sels"], w=[("P", 4)])
            self.act(gate_s[:, sl], ps[:, 4, :], AF.Copy, r=[("P", 4)], w=[("gate_s", half)])
        A.pop()
        P.barrier()
        C["gate_p"] = gate_p
        C["gate_s"] = gate_s
        self.dbg("ada", ada, [128, 24, 5], r=["ada"])
        self.dbg("gate_s", gate_s, [128, 1024], r=[("gate_s", 0), ("gate_s", 1)])

    def norm_tile(self, xsrc, ntok, hT_dst, kind, slot, hkey, bank0=6):
        A, P, ps, C = self.A, self.P, self.ps, self.C
        W = self.W1
        i3, i2 = slot % len(W["xt"]), slot % 2
        xt = W["xt"][i3]
        self.dma("sp", xt[0:ntok], xsrc, "ldx", w=[("xt", i3)])
        ss = W["ss"][:, slot % 4:slot % 4 + 1]
        self.act(W["junk"][0:ntok], xt[0:ntok], AF.Square, r=[("xt", i3)], w=["junk", ("ss", slot % 4)],
                 accum_out=ss[0:ntok])
        self.v("dve", "tensor_scalar", ss[0:ntok], ss[0:ntok], 1.0 / D, EPS, op0=ALU.mult, op1=ALU.add,
               r=[("ss", slot % 4)], w=[("ss", slot % 4)])
        self.act(ss[0:ntok], ss[0:ntok], AF.Ln, r=[("ss", slot % 4)], w=[("ss", slot % 4)])
        self.act(ss[0:ntok], ss[0:ntok], AF.Exp, r=[("ss", slot % 4)], w=[("ss", slot % 4)], scale=-0.5)
        xn = W["xn"][i2]
        self.act(xn[0:ntok], xt[0:ntok], AF.Copy, r=[("xt", i3), ("ss", slot % 4)], w=[("xn", i2)], scale=ss[0:ntok])
        if DEBUG.get("stage", 9) < 1:
            return
        bank = bank0 + i2
        pt = ps[:, bank, :].bitcast(BF16).rearrange("p (c t) -> p c t", c=8)
        for c in range(8):
            self.tr(pt[:, c, 0:ntok], xn[0:ntok, c * 128:(c + 1) * 128], C["ident_bf"][0:ntok, 0:ntok],
                    r=[("xn", i2), "ident_bf"], w=[("P", bank0 + i2)])
        if DEBUG.get("stage", 9) < 2:
            return
        for c in range(8):
            if kind == "p":
                if True:
                    self.act(hT_dst[:, c, :], pt[:, c, 0:ntok], AF.Identity, r=[("P", bank0 + i2), "mult", "ada"], w=[(hkey, c)],
                             scale=C["mult"][:, c, 0:1], bias=C["shift"][:, c, 0:1])
                else:
                    self.v("dve", "tensor_scalar", hT_dst[:, c, :], pt[:, c, 0:ntok], C["mult"][:, c, 0:1], C["shift"][:, c, 0:1],
                           op0=ALU.mult, op1=ALU.add, r=[("P", bank0 + i2), "mult", "ada"], w=[(hkey, c)])
            else:
                tmp = W["stmp"]
                tmp0 = W["stmp0"]
                self.act(tmp0, pt[:, c, 0:ntok], AF.Copy, r=[("P", bank0 + i2)], w=["stmp0"])
                self.v("dve", "tensor_tensor", tmp.rearrange("p (s t) -> p s t", s=4),
                       tmp0.rearrange("p (s t) -> p s t", s=4),
                       C["mult"][:, c, 1:5].unsqueeze(2).to_broadcast([128, 4, 8]), ALU.mult,
                       r=["stmp0", "mult"], w=["stmp"])
                self.v("dve", "tensor_tensor", hT_dst[:, c, :].rearrange("p (s t) -> p s t", s=4),
                       tmp.rearrange("p (s t) -> p s t", s=4),
                       C["shift"][:, c, 1:5].unsqueeze(2).to_broadcast([128, 4, 8]), ALU.add,
                       r=["stmp", "ada"], w=[(hkey, c)])

    def phase1(self):
        A = self.A
        self.hT = A.alloc([8, HALO + SEG + TS], BF16)
        A.push()
        self.W1 = {
            "xt": [A.alloc([1024]) for _ in range(3)],
            "ss": A.alloc([4]),
            "junk": A.alloc([1024], BF16),
            "xn": [A.alloc([1024], BF16) for _ in range(2)],
            "stmp": A.alloc([32]),
            "stmp0": A.alloc([32]),
        }
        xh = self.inp("xh", [HALO + SEG, 1024])
        xs = self.inp("xs", [TS, 1024])
        slot = 0
        if not DEBUG.get("skip_s"):
            self.norm_tile(xs, TS, self.hT[:, :, HALO + SEG:HALO + SEG + TS], "s", slot, ("hT", 32))
        slot += 1
        for ti in range(DEBUG.get("ntiles", (HALO + SEG) // 128)):
            self.norm_tile(xh[ti * 128:(ti + 1) * 128, :], 128, self.hT[:, :, ti * 128:(ti + 1) * 128], "p", slot, ("hT", ti))
            slot += 1
        self.dbg("hT", self.hT, [128, 8, HALO + SEG + TS], BF16, r=[])
        self.dbg("xn", self.W1["xn"][1], [128, 1024], BF16, r=[])
        A.pop()


def build_program(upto=99):
    b = Builder()
    b.phase0()
    b.sample_copies()
    b.w_in = b.inp("w_in", [1024, PW])
    if upto >= 1:
        b.P.barrier()
        b.phase1()
    if upto >= 2:
        b.P.barrier()
        b.attT = b.A.alloc([4, SEG + TS], BF16)
        b.C["F"] = b.A.alloc([3, 129])
        b.A.push()
        b.phase_bias()
        b.P.barrier()
        if DEBUG.get("att_stage", 9) >= 1:
            b.phase_attention()
        b.A.pop()
        b.P.barrier()
        if not DEBUG.get("no_sattn"):
            b.phase_sample_attn()
        b.dbg("attT2", b.attT, [128, 4, SEG + TS], BF16)
    if upto >= 3:
        b.P.barrier()
        b.phase_mlstm()
    if upto >= 4:
        b.P.barrier()
        b.phase_out()
    if DEBUG.get("dmult"):
        DEBUG["mult_end"] = True
        b.dbg("mult_end", b.C["mult"], [128, 8, 5])
    b.P.emit(final_streams=list(b.P.streams.keys()))
    return b


def make_in_maps(inp, cores):
    consts = _consts()
    f32 = np.float32
    maps = []
    xp = inp["x_prompt"]
    for c in cores:
        b, p = c // 4, c % 4
        s0 = p * SEG
        m = dict(consts)
        ext = np.zeros((PREFIX + SEG, D), f32)
        lo = s0 - PREFIX
        src_lo = max(lo, 0)
        ext[src_lo - lo:] = xp[b, src_lo:s0 + SEG]
        m["xf"] = np.ascontiguousarray(ext[:PREFIX - HALO])
        m["xh"] = np.ascontiguousarray(ext[PREFIX - HALO:])
        m["xs"] = np.ascontiguousarray(inp["x_sample"][4 * c:4 * c + 4].reshape(TS, D))
        tv = np.zeros(64, f32)
        tv[(src_lo - lo) // 128:] = 1.0
        m["tvalid"] = np.ascontiguousarray(np.broadcast_to(tv, (128, 64)))
        call = np.concatenate([inp["c_prompt"][b:b + 1], inp["c_sample"][4 * c:4 * c + 4]], 0)
        m["cT"] = np.ascontiguousarray(call.T.reshape(8, 128, 5).transpose(1, 0, 2))
        m["w_ada"] = inp["w_ada"][0]
        m["rel_table"] = inp["rel_table"]
        m["gainT"] = _fm(inp["norm_gain"][0])
        m["conv_bT"] = _fm(inp["conv_b"][0])
        m["m_normT"] = _fm(inp["m_norm"][0])
        m["m_skipT"] = _fm(inp["m_skip"][0])
        m["b_adaT"] = _fm(inp["b_ada"][0])
        m["conv_wT"] = np.ascontiguousarray(inp["conv_w"][0].reshape(4, 8, 128).transpose(2, 1, 0))
        m["b_gate_rows"] = np.ascontiguousarray(np.broadcast_to(inp["b_ada"][0][2048:], (5, 1024)))
        m["fgain_bc"] = np.ascontiguousarray(np.broadcast_to(inp["final_gain"], (128, 1024)))
        m["b_if_bc"] = np.ascontiguousarray(np.broadcast_to(inp["b_if"][0], (128, 8)))
        m["w_in"] = inp["w_in"][0]
        st_ = np.zeros((8, 8, 128), f32)
        for t in range(8):
            st_[t, t, :] = 1.0
        m["selt"] = st_
        osl = np.zeros((128, 8, 8), f32)
        for t in range(8):
            osl[:, t, t] = 1.0
        m["onesel"] = osl
        for g, nm in enumerate(("cache_kv_w128", "cache_kv_w512", "cache_kv_w2048")):
            m["cache%d" % g] = np.ascontiguousarray(inp[nm][0, 4 * c:4 * c + 4].reshape(4, -1, 2, 512))
        m["w_pa"] = inp["w_pa"][0]
        m["w_pm"] = inp["w_pm"][0]
        m["w_out"] = inp["w_out"][0]
        m["w_mq"] = inp["w_mq"][0]
        m["w_mk"] = inp["w_mk"][0]
        eh = np.zeros((4, 4, 128), f32)
        for h in range(4):
            eh[h, h, :] = 1.0
        m["ehsel"] = eh
        sq = slice(4 * c, 4 * c + 4)
        Cst = inp["state_C"][0, sq]
        nst = inp["state_n"][0, sq]
        c0 = np.concatenate([Cst.transpose(0, 3, 1, 2), nst.transpose(0, 2, 1)[..., None]], axis=-1)
        m["C0T"] = np.ascontiguousarray(c0)
        mst = inp["state_m"][0, sq]
        m["m0row"] = np.ascontiguousarray(mst[:, :, None])
        m["m0bc"] = np.ascontiguousarray(np.broadcast_to(mst[:, None, :], (4, 128, 4)))
        cvs = inp["state_conv"][0, sq]
        m["conv0"] = np.ascontiguousarray(cvs.reshape(4, 3, 8, 128).transpose(0, 3, 2, 1))
        m["coremask"] = np.full((128, 128), NEG if p == 0 else 0.0, f32)
        sm = np.zeros((128, 4, 128), f32)
        for e in range(64):
            sm[e, 0, e] = 1.0
            sm[e, 1, 64 + e] = 1.0
            sm[64 + e, 2, e] = 1.0
            sm[64 + e, 3, 64 + e] = 1.0
        m["selmats"] = sm
        maps.append(m)
    return maps


def run_cores(inp, cores, upto=99):
    b = build_program(upto)
    maps = make_in_maps(inp, cores)
    maps = [{k: np.ascontiguousarray(v, dtype=np.float32) for k, v in m.items() if k in b.din} for m in maps]
    res = run_bass_kernel_spmd(b.nc, maps, core_ids=list(range(len(cores))))
    return res.results


def _phase_bias(self):
    A, P, ps, C = self.A, self.P, self.ps, self.C
    F_sb = C["F"]
    gv = A.alloc([3, 2, 256])
    self.v("pool", "memset", gv[0:8], NEG, w=["gv"])
    for g in range(3):
        self.mm(ps[0:8, 5, 0:129], C["rel_table"][0:32, g * 8:(g + 1) * 8], C["ohd"][0:32, g, :],
                r=["rel_table", "ohd"], w=[("P", 5)])
        self.act(F_sb[0:8, g, :], ps[0:8, 5, 0:129], AF.Copy, r=[("P", 5)], w=[("F", g)])
        self.v("pool", "tensor_copy", gv[0:8, g, 1, 127:255], F_sb[0:8, g, 0:128], r=[("F", g), "gv"], w=[("gv", g)])
        self.v("pool", "tensor_copy", gv[0:8, g, 0, 0:128], F_sb[0:8, g, 1:129], r=[("F", g), "gv"], w=[("gv", g)])
    gvd = self.scratch("gvd", [8, 3, 2, 256])
    self.dma("sp", gvd, gv[0:8], "gvw", r=[("gv", 0), ("gv", 1), ("gv", 2)], w=["gvd"])
    biasH = A.alloc([24, 256], BF16)
    for g in range(3):
        for h in range(8):
            for kb in range(2):
                off = ((h * 3 + g) * 2 + kb) * 256
                src = bass.AP(tensor=gvd.tensor, offset=off, ap=[[1, 128], [1, 128]])
                self.dma("pool", biasH[:, g * 8 + h, kb * 128:(kb + 1) * 128], src, "ldc", r=["gvd"], w=[("biasH", g)])
    C["biasH"] = biasH
    cm = A.alloc([128], BF16)
    self.dma("pool", cm, self.inp("coremask", [128, 128]), "ldc", w=["coremask"])
    C["coremask"] = cm
    C["selmats"] = A.alloc([4, 128])
    self.dma("sp", C["selmats"], self.inp("selmats", [128, 4, 128]), "ld0", w=["selmats"])


def _phase_attention(self):
    A, P, ps, C, hT = self.A, self.P, self.ps, self.C, self.hT
    w_in = self.w_in
    wv_in = w_in.rearrange("(c p) n -> p c n", p=128)
    kvp = [self.outp("kvp%d" % g, [GROUPS[g][0], 2, 512]) for g in range(3)]
    A.push()
    acc = A.alloc([2, SEG])
    wq = A.alloc([8, 128], BF16)
    wkv = A.alloc([8, 256], BF16)
    wz = A.alloc([8, 128], BF16)
    qT = A.alloc([SEG], BF16)
    kT = A.alloc([4096], BF16)
    vaug = A.alloc([32, 2, 128], BF16)
    pT = [A.alloc([256], BF16) for _ in range(4)]
    stage = [A.alloc([256]) for _ in range(2)]
    rbuf = A.alloc([512])
    att = A.alloc([512])
    sz = A.alloc([512])
    if not DEBUG.get("no_vones"):
        self.v("pool", "memset", vaug[:, :, :, 64:128], 1.0, w=["vones"])
    cnt = 0
    STG = DEBUG.get("att_stage", 9)
    for hp in range(DEBUG.get("att_hp", 4)):
        for g, (win, d) in enumerate(GROUPS):
            if g not in DEBUG.get("att_groups", (0, 1, 2)):
                continue
            U = SEG // d
            U2 = U + 128
            col = g * 512 + hp * 128
            allc = lambda nm: [(nm, c) for c in range(8)]
            self.dma("pool", wq, wv_in[:, :, OFF_Q + col:OFF_Q + col + 128], "ldw", w=allc("wq"))
            self.dma("pool", wkv[:, :, 0:128], wv_in[:, :, OFF_K + col:OFF_K + col + 128], "ldw", w=allc("wkv"))
            self.dma("pool", wkv[:, :, 128:256], wv_in[:, :, OFF_V + col:OFF_V + col + 128], "ldw", w=allc("wkv"))
            hq = [hT[:, c, HALO:HALO + SEG].rearrange("p (u r) -> p r u", r=d) for c in range(8)]
            hk = [hT[:, c, HALO - 128 * d:HALO + SEG].rearrange("p (u r) -> p r u", r=d) for c in range(8)]

            def chunks(Ux, total):
                res = []
                if Ux >= 512:
                    for r in range(d):
                        u0 = 0
                        while u0 < Ux:
                            n = min(512, Ux - u0)
                            res.append((r, 1, u0, n))
                            u0 += n
                else:
                    nr = 512 // Ux
                    for r0 in range(0, d, nr):
                        res.append((r0, nr, 0, Ux))
                return res

            def tile_keys(view_lo, r0, nr, u0, n, c):
                lo = view_lo + r0 + d * u0
                hi = view_lo + r0 + nr - 1 + d * (u0 + n - 1)
                return [(("hT", t), c) for t in range(lo // 128, hi // 128 + 1)]

            for (r0, nr, u0, n) in chunks(U, SEG):
                bank = cnt % 2
                cnt += 1
                pso = ps[:, bank, 0:nr * n]
                pso3 = pso if nr == 1 else pso.rearrange("p (a b) -> p a b", a=nr)
                for c in range(8):
                    rhs = hq[c][:, r0, u0:u0 + n] if nr == 1 else hq[c][:, r0:r0 + nr, :]
                    self.mm(pso3, wq[:, c, :], rhs, start=(c == 0), stop=(c == 7),
                            r=[("wq", c)] + tile_keys(HALO, r0, nr, u0, n, c), w=[("P", bank)])
                f0 = r0 * U + u0
                self.act(qT[:, f0:f0 + nr * n], pso, AF.Copy, r=[("P", bank)], w=["qT"], scale=0.125)
            if STG < 2:
                continue
            for (r0, nr, u0, n) in chunks(U2, U2 * d):
                bank = cnt % 2
                cnt += 1
                pso = ps[:, bank, 0:nr * n]
                pso3 = pso if nr == 1 else pso.rearrange("p (a b) -> p a b", a=nr)
                for c in range(8):
                    rhs = hk[c][:, r0, u0:u0 + n] if nr == 1 else hk[c][:, r0:r0 + nr, :]
                    self.mm(pso3, wkv[:, c, 0:128], rhs, start=(c == 0), stop=(c == 7),
                            r=[("wkv", c)] + tile_keys(HALO - 128 * d, r0, nr, u0, n, c), w=[("P", bank)])
                f0 = r0 * U2 + u0
                self.act(kT[:, f0:f0 + nr * n], pso, AF.Copy, r=[("P", bank)], w=["kT"])
            nm = U2 // 128
            if STG < 3:
                continue
            for r in range(d):
                for m in range(nm):
                    blk = r * nm + m
                    bank = cnt % 2
                    cnt += 1
                    pso = ps[:, bank, 0:256]
                    lastb = (m == nm - 1)
                    for c in range(8):
                        self.mm(pso if lastb else pso[:, 128:256], hk[c][:, r, 128 * m:128 * m + 128],
                                wkv[:, c, :] if lastb else wkv[:, c, 128:256], start=(c == 0), stop=(c == 7),
                                r=[("wkv", c)] + tile_keys(HALO - 128 * d, r, 1, 128 * m, 128, c), w=[("P", bank)])
                    self.act(vaug[:, blk, :, 0:64], pso[:, 128:256].rearrange("p (h e) -> p h e", h=2), AF.Copy,
                             r=[("P", bank), "vones"], w=[("vaug", blk)])
                    if m == nm - 1 and not DEBUG.get("no_kvout"):
                        st = stage[blk % 2]
                        if DEBUG.get("kv_act"):
                            self.act(st, pso, AF.Copy, r=[("P", bank)], w=[("stage", blk % 2)])
                        else:
                            self.v("dve", "tensor_copy", st, pso, r=[("P", bank)], w=[("stage", blk % 2)])
                        dst = kvp[g].rearrange("(i r) k c -> r i k c", r=d)[r, :, :, hp * 128:(hp + 1) * 128]
                        if DEBUG.get("kv_plain"):
                            dst = kvp[g][0:128, :, hp * 128:(hp + 1) * 128]
                        if not DEBUG.get("kv_nodma"):
                            self.dma("sp", dst, st.rearrange("p (k c) -> p k c", k=2), "out", r=[("stage", blk % 2)], w=[])
            if STG < 4:
                continue
            for r in range(d):
                for n in range(U // 128):
                    for h in range(2):
                        gh = g * 8 + hp * 2 + h
                        hs = slice(h * 64, h * 64 + 64)
                        si = cnt % 3
                        cnt += 1
                        sbk, obk = (2, 3, 6)[si], (4, 5, 7)[si]
                        S = ps[:, sbk, 0:256]
                        O = ps[:, obk, 0:128]
                        self.mm(S, C["antiid_bf"], C["biasH"][:, gh, :], start=True, stop=False,
                                r=["antiid_bf", ("biasH", g)], w=[("P", sbk)])
                        if n == 0:
                            self.mm(S[:, 0:128], C["ident_bf"], C["coremask"], start=False, stop=False,
                                    r=["ident_bf", "coremask"], w=[("P", sbk)])
                        q_ap = qT[hs, r * U + 128 * n:r * U + 128 * n + 128]
                        self.mm(S[:, 0:128], kT[hs, r * U2 + 128 * n:r * U2 + 128 * n + 128], q_ap, start=False, stop=False,
                                r=["qT", "kT"], w=[("P", sbk)])
                        self.mm(S[:, 128:256], kT[hs, r * U2 + 128 * (n + 1):r * U2 + 128 * (n + 2)], q_ap, start=False, stop=True,
                                r=["qT", "kT"], w=[("P", sbk)])
                        self.act(pT[si], S, AF.Exp, r=[("P", sbk)], w=[("pT", si)])
                        b0 = r * nm + n
                        self.mm(O, vaug[:, b0, h, :], pT[si][:, 0:128], start=True, stop=False,
                                r=[("vaug", b0), ("pT", si)], w=[("P", obk)])
                        self.mm(O, vaug[:, b0 + 1, h, :], pT[si][:, 128:256], start=False, stop=True,
                                r=[("vaug", b0 + 1), ("pT", si)], w=[("P", obk)])
                        av = acc[:, h, :].rearrange("p (u r) -> p r u", r=d)[:, r, 128 * n:128 * n + 128]
                        if d == 1:
                            ak = [("acc", h, n // 4, rr) for rr in range(16)]
                        elif d == 4:
                            ak = [("acc", h, n, r + 4 * j) for j in range(4)]
                        else:
                            ak = [("acc", h, qq, r) for qq in range(4)]
                        if g == 0:
                            self.v("dve", "tensor_copy", av, O, r=[("P", obk)], w=ak)
                        else:
                            self.v("dve", "tensor_tensor", av, av, O, ALU.add, r=[("P", obk)] + ak, w=ak)
        if STG < 5:
            continue
        P.barrier()
        zc = OFF_ZA + hp * 128
        self.dma("pool", wz, wv_in[:, :, zc:zc + 128], "ldw", w=[("wz", c) for c in range(8)])
        for k in range(4):
            tk = slice(512 * k, 512 * k + 512)
            for j, (bank, sm) in enumerate(((6, (0, 1)), (7, (2, 3)))):
                for h in range(2):
                    self.mm(ps[:, bank, :], C["selmats"][:, sm[h], :], acc[:, h, tk], start=(h == 0), stop=(h == 1),
                            r=["selmats"], w=[("P", 6 + j)])
            self.v("dve", "reciprocal", rbuf, ps[:, 7, :], r=[("P", 7)], w=["rbuf"])
            self.v("dve", "tensor_tensor", att, ps[:, 6, :], rbuf, ALU.mult, r=[("P", 6), "rbuf"], w=["att"])
            for c in range(8):
                self.mm(ps[:, 0, :], wz[:, c, :], hT[:, c, HALO + 512 * k:HALO + 512 * k + 512], start=(c == 0), stop=(c == 7),
                        r=[("wz", c)] + [(("hT", t), c) for t in range(16 + 4 * k, 16 + 4 * k + 4)], w=[("P", 0)])
            self.act(sz, ps[:, 0, :], AF.Silu, r=[("P", 0)], w=["sz"])
            self.v("dve", "tensor_tensor", self.attT[:, hp, tk], att, sz, ALU.mult, r=["att", "sz"], w=[("attT", hp)])
        P.barrier()
    A.pop()
    self.dbg("attT", self.attT, [128, 4, SEG + TS], BF16)


Builder.phase_bias = _phase_bias
Builder.phase_attention = _phase_attention


def _mlstm_setup(self):
    A, P, C = self.A, self.P, self.C
    wv_in = self.w_in.rearrange("(c p) n -> p c n", p=128)
    M = self.M = {}
    M["wxm"] = A.alloc([8, 1024], BF16)
    M["wg"] = A.alloc([8, 8], BF16)
    M["wmq"] = A.alloc([2, 4, 128], BF16)
    M["wmk"] = A.alloc([2, 4, 128], BF16)
    for c in range(8):
        self.dma("pool", M["wxm"][:, c, :], wv_in[:, c, OFF_XM:OFF_XM + 1024], "ldw", w=["wxm"])
        self.dma("pool", M["wg"][:, c, :], wv_in[:, c, OFF_I:OFF_I + 8], "ldw", w=["wg"])
    wq_d = self.inp("w_mq", [4, 256, 128]).rearrange("h (c p) k -> p c h k", p=128)
    wk_d = self.inp("w_mk", [4, 256, 128]).rearrange("h (c p) k -> p c h k", p=128)
    for ec in range(2):
        for h in range(4):
            self.dma("pool", M["wmq"][:, ec, h, :], wq_d[:, ec, h, :], "ldw", w=["wmq"])
            self.dma("pool", M["wmk"][:, ec, h, :], wk_d[:, ec, h, :], "ldw", w=["wmk"])
    M["ones"] = A.alloc([128])
    self.v("pool", "memset", M["ones"], 1.0, w=["ones"])
    M["ehsel"] = A.alloc([4, 128])
    self.dma("sp", M["ehsel"][0:4], self.inp("ehsel", [4, 4, 128]), "ld0", w=["ehsel"])
    M["negbig"] = A.alloc([64])
    self.v("dve", "tensor_scalar", M["negbig"], C["tvalid"], 1.0e4, -1.0e4, op0=ALU.mult, op1=ALU.add,
           r=["tvalid"], w=["negbig"])
    M["one1"] = A.alloc([1])
    M["zero1"] = A.alloc([1])
    self.v("pool", "memset", M["one1"], 1.0, w=["one1"])
    self.v("pool", "memset", M["zero1"], 0.0, w=["zero1"])
    M["neg1"] = A.alloc([1])
    self.v("pool", "memset", M["neg1"], -1.0, w=["neg1"])
    M["CT"] = A.alloc([4, 257], F32)
    M["m_row"] = A.alloc([1], F32)
    M["m_bc"] = A.alloc([4], F32)
    M["convbuf"] = A.alloc([8, 131], F32)


def _mlstm_local_setup(self):
    A, P, C, M = self.A, self.P, self.C, self.M
    wv_in = self.w_in.rearrange("(c p) n -> p c n", p=128)
    M["wzm"] = A.alloc([8, 1024], BF16)
    M["wom"] = A.alloc([8, 1024], BF16)
    for c in range(8):
        self.dma("pool", M["wzm"][:, c, :], wv_in[:, c, OFF_ZM:OFF_ZM + 1024], "ldw", w=["wzm"])
        self.dma("pool", M["wom"][:, c, :], wv_in[:, c, OFF_OM:OFF_OM + 1024], "ldw", w=["wom"])
    for nm, shp, dt in (("cacc", [8, 128], F32), ("c_act", [8, 128], BF16),
                        ("vaug", [4, 257], BF16), ("kmw", [4, 128], BF16), ("qmT", [4, 128], BF16), ("kmT", [4, 128], BF16),
                        ("gt", [8], F32), ("lf", [4], F32), ("ie", [4], F32), ("b_tok", [4], F32), ("a_tok", [4], F32),
                        ("a_row", [128], F32), ("cm_row", [128], F32), ("M_row", [128], F32), ("negM_row", [128], F32),
                        ("AT", [1], F32), ("Mend_row", [1], F32), ("dg", [4], F32), ("Mend_bc", [4], F32),
                        ("tmp4", [4], F32), ("wk", [4], F32), ("wC", [4], F32), ("M_tok", [4], F32), ("emt", [4], F32),
                        ("DT", [128], F32), ("Wbc", [128], F32), ("scT", [128], BF16), ("qtil", [128], BF16),
                        ("CTb", [4, 257], BF16), ("hh", [256], F32), ("hn", [256], BF16), ("hnm", [8, 128], F32),
                        ("st", [8], F32), ("so", [128], F32), ("szm", [128], F32), ("t1", [128], F32), ("t2", [128], F32),
                        ):
        M[nm] = A.alloc(shp, dt)
        if nm in ("DT", "Wbc", "scT", "qtil", "hh", "hn", "st"):
            M[nm + "_b"] = A.alloc(shp, dt)
    self.v("pool", "memset", M["vaug"][:, :, 256:257], 1.0, w=["vaug1"])
    M["so_all"] = A.alloc([8, 512], BF16)
    M["sz_all"] = A.alloc([8, 512], BF16)


def _mlstm_gates4(self, tok0, tiles):
    ps, M, hT = self.ps, self.M, self.hT
    for fb in range(8):
        for (wname, bank, func, dst, dk) in (("wom", 5, AF.Sigmoid, "so_all", "so_all"), ("wzm", 6, AF.Silu, "sz_all", "sz_all")):
            for c in range(8):
                self.mm(ps[:, bank, :], M[wname][:, c, fb * 128:(fb + 1) * 128], hT[:, c, tok0:tok0 + 512], start=(c == 0), stop=(c == 7),
                        r=[wname] + [(("hT", t), c) for t in tiles], w=[("P", bank)])
            self.act(M[dst][:, fb, :], ps[:, bank, :], func, r=[("P", bank)], w=[(dk, fb)])


def _mlstm_tile(self, hTt, hkeys, ntok, tcol, with_out, mout_dst, moutkey, gate_off=None):
    A, P, ps, C, M = self.A, self.P, self.ps, self.C, self.M
    N = ntok
    B = lambda i: ("P", i)
    tv = C["tvalid"][:, tcol:tcol + 1] if tcol is not None else M["one1"]
    nb = M["negbig"][:, tcol:tcol + 1] if tcol is not None else M["zero1"]
    tvk = ["tvalid", "negbig", "one1", "zero1"]
    cb = M["convbuf"]
    xmps = ps[:, 0:2, :].rearrange("p a (b t) -> p (a b) t", t=128)
    for fb in range(8):
        for c in range(8):
            self.mm(xmps[:, fb, 0:N], M["wxm"][:, c, fb * 128:(fb + 1) * 128], hTt[:, c, :], start=(c == 0), stop=(c == 7),
                    r=["wxm"] + [(k, c) for k in hkeys], w=[B(fb // 4)])
    for half in range(2):
        self.act(cb[:, 4 * half:4 * half + 4, 3:3 + N], xmps[:, 4 * half:4 * half + 4, 0:N], AF.Copy,
                 r=[B(half)] + tvk, w=[("cb", half)], scale=tv)
    for half in range(2):
        for c in range(8):
            self.mm(ps[0:N, 2 + half, :], hTt[:, c, :], M["wxm"][:, c, half * 512:(half + 1) * 512], start=(c == 0), stop=(c == 7),
                    r=["wxm"] + [(k, c) for k in hkeys], w=[B(2 + half)])
        self.act(M["vaug"][0:N, 2 * half:2 * half + 2, 0:256], ps[0:N, 2 + half, :].rearrange("p (h v) -> p h v", h=2), AF.Copy,
                 r=[B(2 + half), "vaug1"], w=[("vaug", half)])
    gps = ps[0:N, 7, 0:8]
    for c in range(8):
        self.mm(gps, hTt[:, c, :], M["wg"][:, c, :], start=(c == 0), stop=(c == 7),
                r=["wg"] + [(k, c) for k in hkeys], w=[B(7)])
    gt = M["gt"]
    self.v("dve", "tensor_tensor", gt[0:N], gps, C["b_if_bc"][0:N], ALU.add, r=[B(7), "b_if_bc"], w=["gt"])
    lf = M["lf"]
    self.act(lf[0:N], gt[0:N, 4:8], AF.Exp, r=["gt"], w=["lf"], scale=-1.0)
    self.act(lf[0:N], lf[0:N], AF.Ln, r=["lf"], w=["lf"], bias=M["one1"][0:N])
    self.v("dve", "tensor_scalar", lf[0:N], lf[0:N], tv[0:N], M["neg1"][0:N], op0=ALU.mult, op1=ALU.mult, r=["lf", "neg1"] + tvk, w=["lf"])
    ie = M["ie"]
    self.v("dve", "tensor_scalar", ie[0:N], gt[0:N, 0:4], tv[0:N], nb[0:N], op0=ALU.mult, op1=ALU.add, r=["gt"] + tvk, w=["ie"])
    tri, ident = C["tri"], C["ident"]
    self.mm(ps[0:N, 7, 8:12], tri[0:N, 0:N], lf[0:N], r=["lf", "tri"], w=[B(7)])
    self.mm(ps[:, 7, 12:16], M["ones"][0:N, :], lf[0:N], r=["lf", "ones"], w=[B(7)])
    self.mm(ps[0:4, 7, 144:144 + N], lf[0:N], tri[0:N, 0:N], r=["lf", "tri"], w=[B(7)])
    b_tok, a_tok = M["b_tok"], M["a_tok"]
    self.v("dve", "tensor_copy", b_tok[0:N], ps[0:N, 7, 8:12], r=[B(7)], w=["b_tok"])
    self.v("dve", "tensor_tensor", a_tok[0:N], ie[0:N], b_tok[0:N], ALU.subtract, r=["ie", "b_tok"], w=["a_tok"])
    self.mm(ps[0:4, 7, 16:16 + N], a_tok[0:N], ident[0:N, 0:N], r=["a_tok", "ident"], w=[B(7)])
    a_row = M["a_row"]
    self.v("dve", "tensor_copy", a_row[0:4, 0:N], ps[0:4, 7, 16:16 + N], r=[B(7)], w=["a_row"])
    cw, cbias = C["conv_wT"], C["conv_bT"]
    for fb in range(8):
        self.act(M["cacc"][:, fb, 0:N], cb[:, fb, 3:3 + N], AF.Identity, r=[("cb", fb // 4), "conv_wT", "conv_bT"], w=[("cacc", fb)],
                 scale=cw[:, fb, 3:4], bias=cbias[:, fb:fb + 1])
    for j in range(3):
        for fb in range(8):
            ca = M["cacc"][:, fb, 0:N]
            self.v("dve", "scalar_tensor_tensor", ca, cb[:, fb, j:j + N], cw[:, fb, j:j + 1], ca, op0=ALU.mult, op1=ALU.add,
                   r=[("cb", fb // 4), ("cacc", fb)], w=[("cacc", fb)])
    for fb in range(8):
        self.act(M["c_act"][:, fb, 0:N], M["cacc"][:, fb, 0:N], AF.Silu, r=[("cacc", fb)], w=[("c_act", fb)])
    for half in range(2):
        self.v("pool", "tensor_copy", cb[:, 4 * half:4 * half + 4, 0:3], cb[:, 4 * half:4 * half + 4, N:N + 3],
               r=[("cb", half)], w=[("cb", half)])
    kmps = ps[0:N, 4, :].rearrange("p (h k) -> p h k", h=4)
    for h in range(4):
        for ec in range(2):
            self.mm(kmps[:, h, :], M["c_act"][:, 2 * h + ec, 0:N], M["wmk"][:, ec, h, :], start=(ec == 0), stop=(ec == 1),
                    r=[("c_act", 2 * h + ec), "wmk"], w=[B(4)])
    if with_out:
        qps = ps[:, 5, :].rearrange("p (h t) -> p h t", h=4)
        kps = ps[:, 6, :].rearrange("p (h t) -> p h t", h=4)
        for h in range(4):
            for ec in range(2):
                self.mm(qps[:, h, 0:N], M["wmq"][:, ec, h, :], M["c_act"][:, 2 * h + ec, 0:N], start=(ec == 0), stop=(ec == 1),
                        r=[("c_act", 2 * h + ec), "wmq"], w=[B(5)])
            for ec in range(2):
                self.mm(kps[:, h, 0:N], M["wmk"][:, ec, h, :], M["c_act"][:, 2 * h + ec, 0:N], start=(ec == 0), stop=(ec == 1),
                        r=[("c_act", 2 * h + ec), "wmk"], w=[B(6)])
        self.act(M["qmT"][:, :, 0:N], qps[:, :, 0:N], AF.Copy, r=[B(5)], w=["qmT"], scale=float(128 ** -0.5))
        self.act(M["kmT"][:, :, 0:N], kps[:, :, 0:N], AF.Copy, r=[B(6)], w=["kmT"])
        self.v("pool", "tensor_copy", M["CTb"], M["CT"], r=["CT"], w=["CTb"])
        self.v("dve", "tensor_tensor_scan", M["cm_row"][0:4, 0:N], M["ones"][0:4, 0:N], a_row[0:4, 0:N], -1.0e30,
               op0=ALU.mult, op1=ALU.max, r=["a_row", "ones"], w=["cm_row"])
        self.v("dve", "tensor_tensor", M["M_row"][0:4, 0:N], M["cm_row"][0:4, 0:N], M["m_row"][0:4, 0:1].to_broadcast([4, N]), ALU.max,
               r=["cm_row", "m_row"], w=["M_row"])
        self.v("dve", "tensor_scalar", M["negM_row"][0:4, 0:N], M["M_row"][0:4, 0:N], -1.0, None, op0=ALU.mult,
               r=["M_row"], w=["negM_row"])
        self.mm(ps[0:N, 7, 276:280], M["M_row"][0:4, 0:N], ident[0:4, 0:4], r=["M_row", "ident"], w=[B(7)])
        self.v("dve", "tensor_tensor", M["emt"][0:N], b_tok[0:N], ps[0:N, 7, 276:280], ALU.add, r=["b_tok", B(7)], w=["emt"])
        self.act(M["emt"][0:N], M["emt"][0:N], AF.Exp, r=["emt"], w=["emt"], scale=-1.0)
    self.v("dve", "tensor_reduce", M["AT"][0:4], a_row[0:4, 0:N], AX.X, ALU.max, r=["a_row"], w=["AT"])
    self.v("dve", "tensor_tensor", M["Mend_row"][0:4], M["AT"][0:4], M["m_row"][0:4], ALU.max, r=["AT", "m_row"], w=["Mend_row"])
    self.v("dve", "tensor_tensor", M["dg"][0:4], ident[0:4, 0:4], M["Mend_row"][0:4, 0:1].to_broadcast([4, 4]), ALU.mult,
           r=["Mend_row", "ident"], w=["dg"])
    self.mm(ps[:, 7, 272:276], M["ones"][0:4, :], M["dg"][0:4], r=["dg", "ones"], w=[B(7)])
    self.v("dve", "tensor_copy", M["Mend_bc"], ps[:, 7, 272:276], r=[B(7)], w=["Mend_bc"])
    self.v("dve", "tensor_tensor", M["wk"][0:N], a_tok[0:N], M["Mend_bc"][0:N], ALU.subtract, r=["a_tok", "Mend_bc"], w=["wk"])
    self.act(M["wk"][0:N], M["wk"][0:N], AF.Exp, r=["wk"], w=["wk"])
    self.v("dve", "tensor_tensor", M["wC"], M["m_bc"], M["Mend_bc"], ALU.subtract, r=["m_bc", "Mend_bc"], w=["wC"])
    self.act(M["wC"], M["wC"], AF.Exp, r=["wC"], w=["wC"])
    self.v("dve", "tensor_tensor", M["kmw"][0:N], kmps, M["wk"][0:N].unsqueeze(2).to_broadcast([N, 4, 128]), ALU.mult,
           r=[B(4), "wk"], w=["kmw"])
    if with_out:
        def head_body(h):
            hsl = slice(h, h + 1)
            par = h % 2
            sfx = "" if par == 0 else "_b"
            hb0, hb1 = (0, 1) if par == 0 else (5, 6)
            pl, pm, pst = ps[:, hb0, 0:N], ps[:, hb0, 128:128 + N], ps[:, hb0, 256:256 + N]
            self.mm(pl, M["ehsel"][0:4, h, :], M["negM_row"][0:4, 0:N], r=["ehsel", "negM_row"], w=[B(hb0)])
            yield
            self.mm(pm, M["ehsel"][0:4, h, :], M["negM_row"][0:4, 0:N], start=True, stop=False, r=["ehsel", "negM_row"], w=[B(hb0)])
            yield
            self.mm(pm[0:N], C["ident_bf"][0:N, 0:N], C["cmaskT_bf"][0:N, 0:N], start=False, stop=True,
                    r=["ident_bf", "cmaskT_bf"], w=[B(hb0)])
            yield
            self.mm(pst[0:N], M["kmT"][:, h, 0:N], M["qmT"][:, h, 0:N], r=["kmT", "qmT"], w=[B(hb0)])
            yield
            self.act(M["DT" + sfx][0:N, 0:N], pm[0:N], AF.Exp, r=[B(hb0), "a_tok"], w=["DT" + sfx], bias=a_tok[0:N, hsl])
            yield
            self.act(M["Wbc" + sfx][:, 0:N], pl, AF.Exp, r=[B(hb0), "m_bc"], w=["Wbc" + sfx], bias=M["m_bc"][:, hsl])
            yield
            self.v("dve", "tensor_tensor", M["scT" + sfx][0:N, 0:N], pst[0:N], M["DT" + sfx][0:N, 0:N], ALU.mult, r=[B(hb0), "DT" + sfx], w=["scT" + sfx])
            yield
            self.v("pool", "tensor_tensor", M["qtil" + sfx][:, 0:N], M["qmT"][:, h, 0:N], M["Wbc" + sfx][:, 0:N], ALU.mult,
                   r=["qmT", "Wbc" + sfx], w=["qtil" + sfx])
            yield
            nd = ps[0:N, hb1, 0:257]
            self.mm(nd, M["scT" + sfx][0:N, 0:N], M["vaug"][0:N, h, :], start=True, stop=False, r=["scT" + sfx, ("vaug", h // 2), "vaug1"], w=[B(hb1)])
            yield
            self.mm(nd, M["qtil" + sfx][:, 0:N], M["CTb"][:, h, :], start=False, stop=True, r=["qtil" + sfx, "CTb"], w=[B(hb1)])
            yield
            st = M["st" + sfx]
            self.act(st[0:N, 0:1], nd[:, 256:257], AF.Abs, r=[B(hb1)], w=["st" + sfx])
            yield
            self.v("dve", "tensor_tensor", st[0:N, 0:1], st[0:N, 0:1], M["emt"][0:N, hsl], ALU.max, r=["st" + sfx, "emt"], w=["st" + sfx])
            yield
            self.v("dve", "reciprocal", st[0:N, 0:1], st[0:N, 0:1], r=["st" + sfx], w=["st" + sfx])
            yield
            self.act(M["hh" + sfx][0:N], nd[:, 0:256], AF.Copy, r=[B(hb1), "st" + sfx], w=["hh" + sfx, "st1" + sfx], scale=st[0:N, 0:1], accum_out=st[0:N, 1:2])
            yield
            self.act(M["hn" + sfx][0:N], M["hh" + sfx][0:N], AF.Square, r=["hh" + sfx], w=["hn" + sfx, "st2" + sfx], accum_out=st[0:N, 2:3])
            yield
            self.v("dve", "tensor_scalar", st[0:N, 3:4], st[0:N, 1:2], 1.0 / 256, None, op0=ALU.mult, r=["st1" + sfx], w=["st3" + sfx])
            yield
            self.v("dve", "tensor_tensor", st[0:N, 4:5], st[0:N, 3:4], st[0:N, 3:4], ALU.mult, r=["st3" + sfx], w=["st4" + sfx])
            yield
            self.v("dve", "scalar_tensor_tensor", st[0:N, 5:6], st[0:N, 2:3], 1.0 / 256, st[0:N, 4:5], op0=ALU.mult, op1=ALU.subtract,
                   r=["st2" + sfx, "st4" + sfx], w=["st5" + sfx])
            yield
            self.v("dve", "tensor_scalar", st[0:N, 5:6], st[0:N, 5:6], EPS, None, op0=ALU.add, r=["st5" + sfx], w=["st5" + sfx])
            yield
            self.act(st[0:N, 5:6], st[0:N, 5:6], AF.Ln, r=["st5" + sfx], w=["st5" + sfx])
            yield
            self.act(st[0:N, 5:6], st[0:N, 5:6], AF.Exp, r=["st5" + sfx], w=["st5" + sfx], scale=-0.5)
            yield
            self.v("dve", "scalar_tensor_tensor", st[0:N, 6:7], st[0:N, 3:4], -1.0, st[0:N, 5:6], op0=ALU.mult, op1=ALU.mult,
                   r=["st3" + sfx, "st5" + sfx], w=["st6" + sfx])
            yield
            self.act(M["hn" + sfx][0:N], M["hh" + sfx][0:N], AF.Identity, r=["hh" + sfx, "st5" + sfx, "st6" + sfx], w=["hn" + sfx], scale=st[0:N, 5:6], bias=st[0:N, 6:7])
            yield
            pt = ps[:, 4, 128 * par:128 * par + 128].bitcast(BF16).rearrange("p (b t) -> p b t", b=2)
            for vb in range(2):
                self.tr(pt[:, vb, 0:N], M["hn" + sfx][0:N, vb * 128:(vb + 1) * 128], C["ident_bf"][0:N, 0:N], r=["hn" + sfx, "ident_bf"], w=[B(4)])
            yield
            for vb in range(2):
                fb = 2 * h + vb
                self.act(M["hnm"][:, fb, 0:N], pt[:, vb, 0:N], AF.Copy, r=[B(4), "m_normT"], w=[("hnm", fb)], scale=C["m_normT"][:, fb:fb + 1])
            yield
        for pair in ((0, 1), (2, 3)):
            gens = [head_body(h) for h in pair]
            live = list(gens)
            while live:
                for gen in list(live):
                    try:
                        next(gen)
                    except StopIteration:
                        live.remove(gen)
    for h in range(4):
        bk = 2 + (h % 2)
        dps = ps[:, bk, 0:257]
        self.mm(dps, M["kmw"][0:N, h, :], M["vaug"][0:N, h, :], r=["kmw", ("vaug", h // 2), "vaug1"], w=[B(bk)])
        self.v("dve", "scalar_tensor_tensor", M["CT"][:, h, :], M["CT"][:, h, :], M["wC"][:, h:h + 1], dps, op0=ALU.mult, op1=ALU.add,
               r=["wC", B(bk), "CT", "CTb"], w=["CT"])
    self.v("dve", "tensor_tensor", M["m_row"][0:4], M["Mend_row"][0:4], ps[0:4, 7, 144 + N - 1:144 + N], ALU.add,
           r=["Mend_row", B(7)], w=["m_row"])
    self.v("dve", "tensor_tensor", M["m_bc"], M["Mend_bc"], ps[:, 7, 12:16], ALU.add, r=["Mend_bc", B(7)], w=["m_bc"])
    if with_out:
        for fb in range(8):
            if gate_off is None:
                for (wname, bank) in (("wom", 5), ("wzm", 6)):
                    for c in range(8):
                        self.mm(ps[:, bank, 0:N], M[wname][:, c, fb * 128:(fb + 1) * 128], hTt[:, c, :], start=(c == 0), stop=(c == 7),
                                r=[wname] + [(k, c) for k in hkeys], w=[B(bank)])
                self.act(M["so"][:, 0:N], ps[:, 5, 0:N], AF.Sigmoid, r=[B(5)], w=["so"])
                self.act(M["szm"][:, 0:N], ps[:, 6, 0:N], AF.Silu, r=[B(6)], w=["szm"])
                so_ap, sz_ap, sok, szk = M["so"][:, 0:N], M["szm"][:, 0:N], "so", "szm"
            else:
                so_ap, sz_ap = M["so_all"][:, fb, gate_off:gate_off + N], M["sz_all"][:, fb, gate_off:gate_off + N]
                sok, szk = ("so_all", fb), ("sz_all", fb)
            self.v("dve", "tensor_tensor", M["t1"][:, 0:N], so_ap, M["hnm"][:, fb, 0:N], ALU.mult, r=[sok, ("hnm", fb)], w=["t1"])
            self.v("dve", "scalar_tensor_tensor", M["t2"][:, 0:N], M["c_act"][:, fb, 0:N], C["m_skipT"][:, fb:fb + 1], M["t1"][:, 0:N],
                   op0=ALU.mult, op1=ALU.add, r=[("c_act", fb), "t1", "m_skipT"], w=["t2"])
            self.v("dve", "tensor_tensor", mout_dst[:, fb, :], M["t2"][:, 0:N], sz_ap, ALU.mult, r=["t2", szk], w=[(moutkey, fb)])


Builder.mlstm_setup = _mlstm_setup
Builder.mlstm_local_setup = _mlstm_local_setup
Builder.mlstm_tile = _mlstm_tile
Builder.mlstm_gates4 = _mlstm_gates4


def _phase_mlstm(self):
    A, P, ps, C, hT = self.A, self.P, self.ps, self.C, self.hT
    self.moutS = A.alloc([8, TS], BF16)
    A.push()
    self.mlstm_setup()
    M = self.M
    self.v("pool", "memset", M["CT"], 0.0, w=["CT"])
    A.push()
    self.W1 = {
        "xt": [A.alloc([1024]) for _ in range(2)],
        "ss": A.alloc([4]),
        "junk": A.alloc([1024], BF16),
        "xn": [A.alloc([1024], BF16) for _ in range(2)],
    }
    self.mlstm_prefix()
    if DEBUG.get("sbuf"):
        print("mlstm prefix sbuf top", A.top)
    A.pop()
    P.barrier()
    self.mlstm_local_setup()
    if DEBUG.get("sbuf"):
        print("mlstm local sbuf top", A.top)
    for j in range(DEBUG.get("nloc", 16)):
        if j % 4 == 0:
            self.mlstm_gates4(HALO + j * 128, [16 + j + q for q in range(4)])
        self.mlstm_tile(hT[:, :, HALO + j * 128:HALO + (j + 1) * 128], [("hT", 16 + j)], 128, 48 + j, True,
                        hT[:, :, j * 128:(j + 1) * 128], ("hT", j), gate_off=(j % 4) * 128)
    o_conv = self.outp("convp", [128, 8, 3])
    o_C = self.outp("Cp", [128, 4, 257])
    o_m = self.outp("mp", [4, 1])
    self.dma("sp", o_conv, M["convbuf"][:, :, 0:3], "out", r=[("cb", 0), ("cb", 1)])
    self.dma("sp", o_C, M["CT"], "out", r=["CT"])
    self.dma("sp", o_m, M["m_row"][0:4], "out", r=["m_row"])
    C0 = self.inp("C0T", [4, 128, 4, 257])
    m0r = self.inp("m0row", [4, 4, 1])
    m0b = self.inp("m0bc", [4, 128, 4])
    cv0 = self.inp("conv0", [4, 128, 8, 3])
    o_convs = self.outp("convs", [4, 128, 8, 3])
    o_Cs = self.outp("Cs", [4, 128, 4, 257])
    o_ms = self.outp("ms", [4, 4, 1])
    for j in range(4 if not DEBUG.get("no_smp") else 0):
        self.dma("sp", M["CT"], C0[j], "ldx", w=["CT"])
        self.dma("sp", M["m_row"][0:4], m0r[j], "ldx", w=["m_row"])
        self.dma("sp", M["m_bc"], m0b[j], "ldx", w=["m_bc"])
        self.dma("sp", M["convbuf"][:, :, 0:3], cv0[j], "ldx", w=[("cb", 0), ("cb", 1)])
        self.mlstm_tile(hT[:, :, HALO + SEG + 8 * j:HALO + SEG + 8 * j + 8], [("hT", 32)], 8, None, True,
                        self.moutS[:, :, 8 * j:8 * j + 8], ("moutS", j))
        self.dma("sp", o_convs[j], M["convbuf"][:, :, 0:3], "out", r=[("cb", 0), ("cb", 1)])
        self.dma("sp", o_Cs[j], M["CT"], "out", r=["CT"])
        self.dma("sp", o_ms[j], M["m_row"][0:4], "out", r=["m_row"])
    if DEBUG.get("mdump"):
        for nm, shp, dt in (("convbuf", [128, 8, 131], F32), ("c_act", [128, 8, 128], BF16), ("vaug", [128, 4, 257], BF16),
                            ("kmw", [128, 4, 128], BF16), ("gt", [128, 8], F32), ("lf", [128, 4], F32), ("ie", [128, 4], F32),
                            ("b_tok", [128, 4], F32), ("a_tok", [128, 4], F32), ("a_row", [128, 128], F32),
                            ("Mend_bc", [128, 4], F32), ("wk", [128, 4], F32), ("wC", [128, 4], F32), ("CT", [128, 4, 257], F32),
                            ("m_bc", [128, 4], F32), ("hh", [128, 256], F32), ("hnm", [128, 8, 128], F32), ("st", [128, 8], F32),
                            ("emt", [128, 4], F32), ("qmT", [128, 4, 128], BF16), ("kmT", [128, 4, 128], BF16), ("DT", [128, 128], F32),
                            ("Wbc", [128, 128], F32), ("M_row", [128, 128], F32)):
            DEBUG["md_" + nm] = True
            self.dbg("md_" + nm, M[nm], shp, dt)
        for nm, ap, shp, dt in (("hTp1", M["hTp"][1], [128, 8, 128], BF16), ("xn1", self.W1["xn"][1], [128, 1024], BF16),
                                ("xt1", self.W1["xt"][1], [128, 1024], F32), ("ss", self.W1["ss"], [128, 4], F32),
                                ("wg", M["wg"], [128, 8, 8], BF16), ("mult", C["mult"], [128, 8, 5], F32)):
            DEBUG["md_" + nm] = True
            self.dbg("md_" + nm, ap, shp, dt)
    A.pop()
    self.P.barrier()
    self.dbg("mout", self.hT[:, :, 0:SEG], [128, 8, SEG], BF16)
    self.dbg("moutS", self.moutS, [128, 8, TS], BF16)


Builder.phase_mlstm = _phase_mlstm


def _phase_out(self):
    A, P, ps, C, hT = self.A, self.P, self.ps, self.C, self.hT
    wv_in = self.w_in.rearrange("(c p) n -> p c n", p=128)
    A.push()
    wpa = A.alloc([4, 1024], BF16)
    wpm = A.alloc([8, 1024], BF16)
    wout = A.alloc([8, 1024], BF16)
    wga = A.alloc([8, 128], BF16)
    wgm = A.alloc([8, 128], BF16)
    merged = A.alloc([8, SEG + TS], BF16)
    fgain = A.alloc([1024])
    sga, sgm, t1, t2 = (A.alloc([512]) for _ in range(4))
    xt = [A.alloc([1024])] * 2
    yt = [A.alloc([1024]) for _ in range(2)]
    ss = A.alloc([4])
    self.dma("sp", fgain, self.inp("fgain_bc", [128, 1024]), "ld0", w=["fgain"])
    wpa_d = self.inp("w_pa", [512, 1024]).rearrange("(c p) n -> p c n", p=128)
    wpm_d = self.inp("w_pm", [1024, 1024]).rearrange("(c p) n -> p c n", p=128)
    wout_d = self.inp("w_out", [1024, 1024]).rearrange("(c p) n -> p c n", p=128)
    for c in range(4):
        self.dma("pool", wpa[:, c, :], wpa_d[:, c, :], "ldw", w=["wpa"])
    for c in range(8):
        self.dma("pool", wpm[:, c, :], wpm_d[:, c, :], "ldw", w=["wpm"])
        self.dma("pool", wout[:, c, :], wout_d[:, c, :], "ldw", w=["wout"])
    chunks = []
    for k in range(4):
        chunks.append(dict(n=512, m0=512 * k,
                           att=lambda hp, k=k: self.attT[:, hp, 512 * k:512 * k + 512],
                           mout=lambda fb, k=k: hT[:, fb, 512 * k:512 * k + 512],
                           hs=lambda c, k=k: hT[:, c, HALO + 512 * k:HALO + 512 * k + 512]))
    chunks.append(dict(n=TS, m0=SEG,
                       att=lambda hp: self.attT[:, hp, SEG:SEG + TS],
                       mout=lambda fb: self.moutS[:, fb, :],
                       hs=lambda c: hT[:, c, HALO + SEG:HALO + SEG + TS]))
    for cb in range(8):
        cs = slice(cb * 128, (cb + 1) * 128)
        self.dma("pool", wga, wv_in[:, :, OFF_GA + cb * 128:OFF_GA + (cb + 1) * 128], "ldw", w=["wga"])
        self.dma("pool", wgm, wv_in[:, :, OFF_GM + cb * 128:OFF_GM + (cb + 1) * 128], "ldw", w=["wgm"])
        for ch in chunks:
            n = ch["n"]
            for hp in range(4):
                self.mm(ps[:, 0, 0:n], wpa[:, hp, cs], ch["att"](hp), start=(hp == 0), stop=(hp == 3), r=["wpa"], w=[("P", 0)])
            for fb in range(8):
                self.mm(ps[:, 1, 0:n], wpm[:, fb, cs], ch["mout"](fb), start=(fb == 0), stop=(fb == 7), r=["wpm"], w=[("P", 1)])
            for c in range(8):
                self.mm(ps[:, 2, 0:n], wga[:, c, :], ch["hs"](c), start=(c == 0), stop=(c == 7), r=["wga"], w=[("P", 2)])
            for c in range(8):
                self.mm(ps[:, 3, 0:n], wgm[:, c, :], ch["hs"](c), start=(c == 0), stop=(c == 7), r=["wgm"], w=[("P", 3)])
            self.act(sga[:, 0:n], ps[:, 2, 0:n], AF.Sigmoid, r=[("P", 2)], w=["sga"])
            self.act(sgm[:, 0:n], ps[:, 3, 0:n], AF.Sigmoid, r=[("P", 3)], w=["sgm"])
            self.v("dve", "tensor_tensor", t1[:, 0:n], ps[:, 0, 0:n], sga[:, 0:n], ALU.mult, r=[("P", 0), "sga"], w=["t1"])
            self.v("dve", "tensor_tensor", t2[:, 0:n], ps[:, 1, 0:n], sgm[:, 0:n], ALU.mult, r=[("P", 1), "sgm"], w=["t2"])
            self.v("pool", "tensor_tensor", merged[:, cb, ch["m0"]:ch["m0"] + n], t1[:, 0:n], t2[:, 0:n], ALU.add,
                   r=["t1", "t2"], w=[("merged", cb)])
    self.dbg("merged", merged, [128, 8, SEG + TS], BF16)
    xh = self.din["xh"]
    xs = self.din["xs"]
    y_p = self.outp("y_p", [SEG, 1024])
    y_s = self.outp("y_s", [TS, 1024])
    tiles = [(xh[HALO + j * 128:HALO + (j + 1) * 128, :], y_p[j * 128:(j + 1) * 128, :], 128, j * 128, C["gate_p"]) for j in range(NLT)]
    tiles.append((xs, y_s, TS, SEG, C["gate_s"]))
    for i, (xsrc, ydst, n, m0, gate) in enumerate(tiles):
        i2 = i % 2
        self.dma("sp", xt[i2][0:n], xsrc, "ldx", w=[("xt", 0)])
        for half in range(2):
            hs_ = slice(half * 512, (half + 1) * 512)
            for cb in range(8):
                self.mm(ps[0:n, 4 + half, :], merged[:, cb, m0:m0 + n], wout[:, cb, hs_], start=(cb == 0), stop=(cb == 7),
                        r=["wout", ("merged", cb)], w=[("P", 4 + half)])
            self.v("dve", "tensor_tensor", yt[i2][0:n, hs_], ps[0:n, 4 + half, :], gate[0:n, hs_], ALU.mult,
                   r=[("P", 4 + half), ("gate_p", half), ("gate_s", half)], w=[("yt", i2, half)])
            self.v("pool", "tensor_tensor", yt[i2][0:n, hs_], yt[i2][0:n, hs_], xt[i2][0:n, hs_], ALU.add,
                   r=[("yt", i2, half), ("xt", 0)], w=[("yt", i2, half)])
        sl = ss[:, i % 4:i % 4 + 1]
        sk = ("ss", i % 4)
        self.act(xt[0][0:n], yt[i2][0:n], AF.Square, r=[("yt", i2, 0), ("yt", i2, 1)], w=[("xt", 0), sk], accum_out=sl[0:n])
        self.v("dve", "tensor_scalar", sl[0:n], sl[0:n], 1.0 / D, EPS, op0=ALU.mult, op1=ALU.add, r=[sk], w=[sk])
        self.act(sl[0:n], sl[0:n], AF.Ln, r=[sk], w=[sk])
        self.act(sl[0:n], sl[0:n], AF.Exp, r=[sk], w=[sk], scale=-0.5)
        self.act(yt[i2][0:n], yt[i2][0:n], AF.Copy, r=[("yt", i2, 0), ("yt", i2, 1), sk], w=[("yt", i2, 0), ("yt", i2, 1)], scale=sl[0:n])
        self.v("pool", "tensor_tensor", yt[i2][0:n], yt[i2][0:n], fgain[0:n], ALU.mult,
               r=[("yt", i2, 0), ("yt", i2, 1), "fgain"], w=[("yt", i2, 0), ("yt", i2, 1)])
        self.dma("sp", ydst, yt[i2][0:n], "out", r=[("yt", i2, 0), ("yt", i2, 1)])
    A.pop()


Builder.phase_out = _phase_out


def _sample_copies(self):
    LB = [w for (w, d) in GROUPS]
    self.s_cache = cache = [self.inp("cache%d" % g, [4, LB[g], 2, 512]) for g in range(3)]
    self.s_kvs = kvs = [self.outp("kvs%d" % g, [4, LB[g], 2, 512]) for g in range(3)]
    self.s_cat = cat = [self.scratch("cat%d" % g, [4, LB[g] + 8, 2, 512]) for g in range(2)]
    for g in range(3):
        for j in range(4):
            nsp = 4 if g == 2 else 1
            rows = LB[g] - 8
            step = (rows + nsp - 1) // nsp
            for a in range(0, rows, step):
                b_ = min(rows, a + step)
                self.dma("sp", kvs[g][j, a:b_], cache[g][j, 8 + a:8 + b_], "cpy", w=[("kvs", g, j, a)])
            if g < 2:
                self.dma("sp", cat[g][j, 0:LB[g]], cache[g][j], "cpy", w=[("catb", g, j)])


def _phase_sample_attn(self):
    A, P, ps, C, hT = self.A, self.P, self.ps, self.C, self.hT
    wv_in = self.w_in.rearrange("(c p) n -> p c n", p=128)
    LB = [w for (w, d) in GROUPS]
    cache, kvs, cat = self.s_cache, self.s_kvs, self.s_cat
    A.push()
    ident, F_sb = C["ident"], C["F"]
    ones = A.alloc([128])
    self.v("pool", "memset", ones, 1.0, w=["s_ones"])
    z8 = A.alloc([8])
    self.v("pool", "memset", z8, 0.0, w=["z8"])
    onesel = A.alloc([8, 8])
    self.dma("sp", onesel, self.inp("onesel", [128, 8, 8]), "ld0", w=["onesel"])
    selt = A.alloc([8, 128])
    self.dma("sp", selt[0:8], self.inp("selt", [8, 8, 128]), "ld0", w=["selt"])
    biasS = A.alloc([3, 8])
    bold = A.alloc([3, 8])
    tmpF = A.alloc([8])
    dgF = A.alloc([8])
    for g in range(3):
        self.mm(ps[:, 0, 0:8], F_sb[0:8, g, 0:128], ident[0:8, 0:8], r=[("F", g), "ident"], w=[("P", 0)])
        self.act(tmpF, ps[:, 0, 0:8], AF.Copy, r=[("P", 0)], w=["tmpF"])
        self.mm(ps[:, 0, 8:16], C["antiid"], tmpF, r=["tmpF", "antiid"], w=[("P", 0)])
        self.act(biasS[:, g, :], ps[:, 0, 8:16], AF.Copy, r=[("P", 0)], w=[("biasS", g)])
        self.v("dve", "tensor_tensor", dgF[0:8], ident[0:8, 0:8], F_sb[0:8, g, 128:129].to_broadcast([8, 8]), ALU.mult,
               r=[("F", g), "ident"], w=["dgF"])
        self.mm(ps[0:8, 0, 16:24], ones[0:8, 0:8], dgF[0:8], r=["dgF", "s_ones"], w=[("P", 0)])
        self.act(bold[0:8, g, :], ps[0:8, 0, 16:24], AF.Copy, r=[("P", 0)], w=[("bold", g)])
    wq = [A.alloc([8, 512], BF16) for _ in range(2)]
    qj = [A.alloc([1536]) for _ in range(4)]
    kj = A.alloc([1536])
    vj = A.alloc([1536])
    oldkv = [A.alloc([3, 2, 512]) for _ in range(1)][0]
    wcnt = 0
    for kind, off in (("q", OFF_Q), ("k", OFF_K), ("v", OFF_V)):
        for g in range(3):
            wb = wq[wcnt % 2]
            wk_ = ("swq", wcnt % 2)
            wcnt += 1
            self.dma("pool", wb, wv_in[:, :, off + g * 512:off + (g + 1) * 512], "ldw", w=[wk_])
            for j in range(4):
                bank = 1 + (j % 2)
                for c in range(8):
                    self.mm(ps[0:8, bank, :], hT[:, c, HALO + SEG + 8 * j:HALO + SEG + 8 * j + 8], wb[:, c, :],
                            start=(c == 0), stop=(c == 7), r=[wk_, (("hT", 32), c)], w=[("P", bank)])
                if kind == "q":
                    self.act(qj[j][0:8, g * 512:(g + 1) * 512], ps[0:8, bank, :], AF.Copy, r=[("P", bank)], w=[("qj", j, g)], scale=0.125)
                else:
                    st = kj if kind == "k" else vj
                    sk = ("kvst", kind, j % 3)
                    sl_ = st[0:8, (j % 3) * 512:(j % 3 + 1) * 512]
                    self.act(sl_, ps[0:8, bank, :], AF.Copy, r=[("P", bank)], w=[sk])
                    kvi = 0 if kind == "k" else 1
                    self.dma("sp", kvs[g][j, LB[g] - 8:LB[g], kvi, :], sl_, "out", r=[sk], w=[("kvsn", g, j, kvi)])
                    if g < 2:
                        self.dma("sp", cat[g][j, LB[g]:LB[g] + 8, kvi, :], sl_, "out", r=[sk], w=[("catn", g, j, kvi)])
    szs = A.alloc([4, TS])
    wz = wq[0]
    self.dma("pool", wz, wv_in[:, :, OFF_ZA:OFF_ZA + 512], "ldw", w=[("swq", 0)])
    for hp in range(4):
        for c in range(8):
            self.mm(ps[:, 3, 0:TS], wz[:, c, hp * 128:(hp + 1) * 128], hT[:, c, HALO + SEG:HALO + SEG + TS], start=(c == 0), stop=(c == 7),
                    r=[("swq", 0), (("hT", 32), c)], w=[("P", 3)])
        self.act(szs[:, hp, :], ps[:, 3, 0:TS], AF.Silu, r=[("P", 3)], w=[("szs", hp)])
    NKG = 6
    Kg = [A.alloc([2, 512]) for _ in range(NKG)]
    prod = A.alloc([512])
    lg = [A.alloc([8]) for _ in range(4)]
    Pz = [A.alloc([8, 8]) for _ in range(NKG)]
    numS = A.alloc([512])
    denS = A.alloc([8])
    pold = A.alloc([8])
    tmpo = A.alloc([512])
    oj = A.alloc([512])
    u = 0
    for j in range(4):
        for g in range(3):
            self.dma("sp", oldkv[0:8, g], cache[g][j, 0:8], "gat", w=[("old", g)])
        self.mm(ps[0:8, 5, :], z8[0:8, 0:8], qj[j][0:8, 0:512], start=True, stop=False, r=["z8", ("qj", j, 0)], w=[("P", 5)])
        self.mm(ps[0:8, 6, 0:8], z8[0:8, 0:8], qj[j][0:8, 0:8], start=True, stop=False, r=["z8", ("qj", j, 0)], w=[("P", 6)])
        first = False
        for g, (win, d) in enumerate(GROUPS):
            for t in range(8):
                kb = Kg[u % NKG]
                kk = ("Kg", u % NKG)
                pz = Pz[u % NKG]
                pk = ("Pz", u % NKG)
                lgt = lg[u % 4]
                lk = ("lg", u % 4)
                u += 1
                a0 = d + t
                if g < 2:
                    src = cat[g][j, a0:a0 + 127 * d + 1:d]
                    deps = [("catb", g, j), ("catn", g, j, 0), ("catn", g, j, 1)]
                else:
                    src = kvs[2][j, a0 - 8:a0 - 8 + 127 * d + 1:d]
                    deps = [("kvs", 2, j, a) for a in range(0, LB[2] - 8, (LB[2] - 8 + 3) // 4)] + [("kvsn", 2, j, 0), ("kvsn", 2, j, 1)]
                self.dma("sp", kb, src, "gat", r=deps, w=[kk])
                self.mm(ps[:, 4, :], selt[0:8, t, :], qj[j][0:8, g * 512:(g + 1) * 512], r=["selt", ("qj", j, g)], w=[("P", 4)])
                self.v("dve", "tensor_tensor", prod, kb[:, 0, :], ps[:, 4, :], ALU.mult, r=[kk, ("P", 4)], w=["prod"])
                self.v("dve", "tensor_reduce", lgt, prod.rearrange("p (h e) -> p h e", h=8), AX.X, ALU.add, r=["prod"], w=[lk])
                self.v("dve", "tensor_tensor", lgt, lgt, biasS[:, g, :], ALU.add, r=[lk, ("biasS", g)], w=[lk])
                pe_ = pz[:, 0, :]
                self.act(pe_, lgt, AF.Exp, r=[lk], w=[pk])
                last = (g == 2 and t == 7)
                kv3 = kb[:, 1, :].rearrange("p (h e) -> p h e", h=8)
                self.v("dve", "tensor_tensor", kv3, kv3, pe_.unsqueeze(2).to_broadcast([128, 8, 64]), ALU.mult, r=[pk, kk], w=[kk])
                self.mm(ps[0:8, 5, :], onesel[:, t, :], kb[:, 1, :], start=False, stop=last, r=["onesel", kk], w=[("P", 5)])
                self.mm(ps[0:8, 6, 0:8], onesel[:, t, :], pe_, start=False, stop=last, r=["onesel", pk], w=[("P", 6)])
        self.v("dve", "tensor_copy", numS[0:8], ps[0:8, 5, :], r=[("P", 5)], w=["numS"])
        self.v("dve", "tensor_copy", denS[0:8], ps[0:8, 6, 0:8], r=[("P", 6)], w=["denS"])
        for g in range(3):
            self.v("dve", "tensor_tensor", tmpo[0:8], qj[j][0:8, g * 512:(g + 1) * 512], oldkv[0:8, g, 0, :], ALU.mult,
                   r=[("qj", j, g), ("old", g)], w=["tmpo"])
            self.v("dve", "tensor_reduce", pold[0:8], tmpo[0:8].rearrange("p (h e) -> p h e", h=8), AX.X, ALU.add, r=["tmpo"], w=["pold"])
            self.v("dve", "tensor_tensor", pold[0:8], pold[0:8], bold[0:8, g, :], ALU.add, r=["pold", ("bold", g)], w=["pold"])
            self.act(pold[0:8], pold[0:8], AF.Exp, r=["pold"], w=["pold"])
            self.v("dve", "tensor_tensor", denS[0:8], denS[0:8], pold[0:8], ALU.add, r=["denS", "pold"], w=["denS"])
            self.v("dve", "tensor_tensor", tmpo[0:8].rearrange("p (h e) -> p h e", h=8),
                   oldkv[0:8, g, 1, :].rearrange("p (h e) -> p h e", h=8), pold[0:8].unsqueeze(2).to_broadcast([8, 8, 64]), ALU.mult,
                   r=[("old", g), "pold"], w=["tmpo"])
            self.v("dve", "tensor_tensor", numS[0:8], numS[0:8], tmpo[0:8], ALU.add, r=["numS", "tmpo"], w=["numS"])
        self.v("dve", "reciprocal", denS[0:8], denS[0:8], r=["denS"], w=["denS"])
        self.v("dve", "tensor_tensor", oj[0:8].rearrange("p (h e) -> p h e", h=8), numS[0:8].rearrange("p (h e) -> p h e", h=8),
               denS[0:8].unsqueeze(2).to_broadcast([8, 8, 64]), ALU.mult, r=["numS", "denS"], w=["oj"])
        for hp in range(4):
            self.tr(ps[:, 7, 0:8], oj[0:8, hp * 128:(hp + 1) * 128], ident[0:8, 0:8], r=["oj", "ident"], w=[("P", 7)])
            self.v("dve", "tensor_tensor", self.attT[:, hp, SEG + 8 * j:SEG + 8 * j + 8], ps[:, 7, 0:8], szs[:, hp, 8 * j:8 * j + 8], ALU.mult,
                   r=[("P", 7), ("szs", hp)], w=[("attTs", hp, j)])
    A.pop()


Builder.phase_sample_attn = _phase_sample_attn
Builder.sample_copies = _sample_copies


_PROG_CACHE = {}


def kernel(**inputs):
    inp = {k: np.asarray(v) for k, v in inputs.items()}
    if "prog" not in _PROG_CACHE:
        _PROG_CACHE["prog"] = build_program(99)
    b = _PROG_CACHE["prog"]
    cores = list(range(NCORES))
    maps = make_in_maps(inp, cores)
    maps = [{k: np.ascontiguousarray(v, dtype=np.float32) for k, v in m.items() if k in b.din} for m in maps]
    res = run_bass_kernel_spmd(b.nc, maps, core_ids=cores).results
    f32 = np.float32
    y_p = np.zeros((2, SEQ, D), f32)
    y_s = np.zeros((32, 8, D), f32)
    kvp = [np.zeros((1, 2, w, 2, 8, 64), f32) for (w, d) in GROUPS]
    kvs = [np.zeros((1, 32, w, 2, 8, 64), f32) for (w, d) in GROUPS]
    conv_p = np.zeros((1, 2, 3, D), f32)
    conv_s = np.zeros((1, 32, 3, D), f32)
    C_p = np.zeros((1, 2, 4, 256, 128), f32)
    C_s = np.zeros((1, 32, 4, 256, 128), f32)
    n_p = np.zeros((1, 2, 4, 128), f32)
    n_s = np.zeros((1, 32, 4, 128), f32)
    m_p = np.zeros((1, 2, 4), f32)
    m_s = np.zeros((1, 32, 4), f32)
    for c in cores:
        r = res[c]
        bb, p = c // 4, c % 4
        sq = slice(4 * c, 4 * c + 4)
        y_p[bb, p * SEG:(p + 1) * SEG] = r["y_p"]
        y_s[sq] = np.asarray(r["y_s"]).reshape(4, 8, D)
        for g in range(3):
            kvs[g][0, sq] = np.asarray(r["kvs%d" % g]).reshape(4, -1, 2, 8, 64)
        Cs = np.asarray(r["Cs"])
        C_s[0, sq] = Cs[..., :256].transpose(0, 2, 3, 1)
        n_s[0, sq] = Cs[..., 256].transpose(0, 2, 1)
        m_s[0, sq] = np.asarray(r["ms"])[:, :, 0]
        conv_s[0, sq] = np.asarray(r["convs"]).transpose(0, 3, 2, 1).reshape(4, 3, D)
        if p == 3:
            for g in range(3):
                kvp[g][0, bb] = np.asarray(r["kvp%d" % g]).reshape(-1, 2, 8, 64)
            Cp = np.asarray(r["Cp"])
            C_p[0, bb] = Cp[..., :256].transpose(1, 2, 0)
            n_p[0, bb] = Cp[..., 256].T
            m_p[0, bb] = np.asarray(r["mp"])[:, 0]
            conv_p[0, bb] = np.asarray(r["convp"]).transpose(2, 1, 0).reshape(3, D)
    return (y_p, y_s, kvp[0], kvs[0], kvp[1], kvs[1], kvp[2], kvs[2],
            conv_p, conv_s, C_p, C_s, n_p, n_s, m_p, m_s)


def _mlstm_prefix(self):
    A, P, ps, C, M, hT = self.A, self.P, self.ps, self.C, self.M, self.hT
    NT = 48
    xf = self.inp("xf", [PREFIX - HALO, 1024])
    ident, tri = C["ident"], C["tri"]
    hTp = [A.alloc([8, 128], BF16) for _ in range(2)]
    gt_all = A.alloc([NT, 8])
    lf = A.alloc([NT, 4])
    ie = A.alloc([NT, 4])
    tot = A.alloc([NT, 4])
    incl = A.alloc([NT, 4])
    a_all = A.alloc([NT, 4])
    wk_all = A.alloc([NT, 4])
    negtv = A.alloc([NT])
    pm = A.alloc([4])
    row1 = A.alloc([1])
    dg = A.alloc([4])
    Mg_bc = A.alloc([4])
    cb4 = A.alloc([4, 8, 131])
    hT4 = A.alloc([8, 512], BF16)
    cacc = A.alloc([8, 128])
    cexp = A.alloc([8, 128])
    c_act = [A.alloc([8, 128], BF16) for _ in range(2)]
    vaug = [A.alloc([4, 257], BF16) for _ in range(2)]
    kmw = [A.alloc([4, 128], BF16) for _ in range(2)]
    for i in range(2):
        self.v("pool", "memset", vaug[i][:, :, 256:257], 1.0, w=[("pvaug1", i)])

    def tile_src(i, slot):
        if i < 32:
            hp_ = hTp[i % 2]
            self.norm_tile(xf[i * 128:(i + 1) * 128, :], 128, hp_, "p", slot, ("hTp", i % 2), bank0=5)
            return hp_, [("hTp", i % 2)]
        ti = i - 32
        return hT[:, :, ti * 128:(ti + 1) * 128], [("hT", ti)]

    for i in range(NT):
        hTt, hk = tile_src(i, i)
        bank = 7 if i % 2 == 0 else 4
        gps = ps[:, bank, 0:8]
        for c in range(8):
            self.mm(gps, hTt[:, c, :], M["wg"][:, c, :], start=(c == 0), stop=(c == 7),
                    r=["wg"] + [(k, c) for k in hk], w=[("P", bank)])
        self.v("dve", "tensor_tensor", gt_all[:, i, :], gps, C["b_if_bc"], ALU.add, r=[("P", bank), "b_if_bc"], w=["gt_all"])
    tv48 = C["tvalid"][:, 0:NT]
    self.v("dve", "tensor_scalar", negtv, tv48, -1.0, None, op0=ALU.mult, r=["tvalid"], w=["negtv"])
    self.act(lf, gt_all[:, :, 4:8], AF.Exp, r=["gt_all"], w=["lf"], scale=-1.0)
    self.act(lf, lf, AF.Ln, r=["lf"], w=["lf"], bias=M["one1"])
    self.v("dve", "tensor_tensor", lf, lf, negtv.unsqueeze(2).to_broadcast([128, NT, 4]), ALU.mult, r=["lf", "negtv"], w=["lf"])
    self.v("dve", "tensor_tensor", ie, gt_all[:, :, 0:4], tv48.unsqueeze(2).to_broadcast([128, NT, 4]), ALU.mult,
           r=["gt_all", "tvalid"], w=["ie"])
    self.v("dve", "tensor_tensor", ie, ie, M["negbig"][:, 0:NT].unsqueeze(2).to_broadcast([128, NT, 4]), ALU.add,
           r=["ie", "negbig"], w=["ie"])
    lf2 = lf.rearrange("p t h -> p (t h)")
    self.mm(ps[:, 0, 0:NT * 4], tri, lf2, r=["lf", "tri"], w=[("P", 0)])
    self.mm(ps[:, 1, 0:NT * 4], M["ones"], lf2, r=["lf", "ones"], w=[("P", 1)])
    self.v("dve", "tensor_copy", tot.rearrange("p t h -> p (t h)"), ps[:, 1, 0:NT * 4], r=[("P", 1)], w=["tot"])
    for h in range(4):
        self.v("dve", "tensor_tensor_scan", incl[:, :, h], M["ones"][:, 0:NT], tot[:, :, h], 0.0, op0=ALU.mult, op1=ALU.add,
               r=["tot", "ones"], w=[("incl", h)])
    inclk = [("incl", h) for h in range(4)]
    self.v("dve", "tensor_tensor", a_all, ie, incl, ALU.subtract, r=["ie"] + inclk, w=["a_all"])
    self.v("dve", "tensor_tensor", a_all, a_all, tot, ALU.add, r=["a_all", "tot"], w=["a_all"])
    self.v("dve", "tensor_tensor", a_all.rearrange("p t h -> p (t h)"), a_all.rearrange("p t h -> p (t h)"), ps[:, 0, 0:NT * 4],
           ALU.subtract, r=["a_all", ("P", 0)], w=["a_all"])
    self.v("dve", "tensor_reduce", pm, a_all.rearrange("p t h -> p h t"), AX.X, ALU.max, r=["a_all"], w=["pm"])
    self.mm(ps[0:4, 2, 0:128], pm, ident, r=["pm", "ident"], w=[("P", 2)])
    self.v("dve", "tensor_reduce", row1[0:4], ps[0:4, 2, 0:128], AX.X, ALU.max, r=[("P", 2)], w=["row1"])
    self.v("dve", "tensor_scalar", row1[0:4], row1[0:4], 0.0, None, op0=ALU.max, r=["row1"], w=["row1"])
    self.v("dve", "tensor_tensor", dg[0:4], ident[0:4, 0:4], row1[0:4, 0:1].to_broadcast([4, 4]), ALU.mult, r=["row1", "ident"], w=["pdg"])
    self.mm(ps[:, 2, 128:132], M["ones"][0:4, :], dg[0:4], r=["pdg", "ones"], w=[("P", 2)])
    self.v("dve", "tensor_copy", Mg_bc, ps[:, 2, 128:132], r=[("P", 2)], w=["Mg_bc"])
    self.v("dve", "tensor_tensor", wk_all, a_all, Mg_bc.unsqueeze(1).to_broadcast([128, NT, 4]), ALU.subtract,
           r=["a_all", "Mg_bc"], w=["wk_all"])
    self.act(wk_all, wk_all, AF.Exp, r=["wk_all"], w=["wk_all"])
    self.v("dve", "tensor_tensor", M["m_bc"], incl[:, NT - 1, :], Mg_bc, ALU.add, r=inclk + ["Mg_bc"], w=["m_bc"])
    self.mm(ps[0:4, 2, 136:137], M["m_bc"][0:1, 0:4], M["ones"][0:1, 0:1], r=["m_bc", "ones"], w=[("P", 2)])
    self.v("dve", "tensor_copy", M["m_row"][0:4], ps[0:4, 2, 136:137], r=[("P", 2)], w=["m_row"])
    cw, cbias = C["conv_wT"], C["conv_bT"]
    hist = A.alloc([8, 3])
    self.v("pool", "memset", hist, 0.0, w=["hist"])
    for g0 in range(0, NT, 4):
        if g0 < 32:
            for q in range(4):
                i = g0 + q
                self.norm_tile(xf[i * 128:(i + 1) * 128, :], 128, hT4[:, :, q * 128:(q + 1) * 128], "p", NT + i, ("hT4", q), bank0=5)
            src4 = hT4
            hk4 = [("hT4", q) for q in range(4)]
            tsl = lambda q: hT4[:, :, q * 128:(q + 1) * 128]
        else:
            t0 = g0 - 32
            src4 = hT[:, :, t0 * 128:(t0 + 4) * 128]
            hk4 = [("hT", t0 + q) for q in range(4)]
            tsl = lambda q, t0=t0: hT[:, :, (t0 + q) * 128:(t0 + q + 1) * 128]
        for fb in range(8):
            bank = fb % 2
            for c in range(8):
                self.mm(ps[:, bank, :], M["wxm"][:, c, fb * 128:(fb + 1) * 128], src4[:, c, :], start=(c == 0), stop=(c == 7),
                        r=["wxm"] + [(k, c) for k in hk4], w=[("P", bank)])
            self.act(cb4[:, :, fb, 3:131], ps[:, bank, :].rearrange("p (q t) -> p q t", q=4), AF.Copy,
                     r=[("P", bank)], w=[("pcb", q) for q in range(4)])
        for q in range(4):
            i = g0 + q
            par = i % 2
            hTt, hk = tsl(q), [hk4[q]]
            tv = C["tvalid"][:, i:i + 1]
            cbp = cb4[:, q]
            self.v("pool", "tensor_copy", cbp[:, :, 0:3], hist, r=["hist"], w=[("pcb", q)])
            for half in range(2):
                for c in range(8):
                    self.mm(ps[:, 2 + half, :], hTt[:, c, :], M["wxm"][:, c, half * 512:(half + 1) * 512], start=(c == 0), stop=(c == 7),
                            r=["wxm"] + [(k, c) for k in hk], w=[("P", 2 + half)])
                self.act(vaug[par][:, 2 * half:2 * half + 2, 0:256], ps[:, 2 + half, :].rearrange("p (h v) -> p h v", h=2), AF.Copy,
                         r=[("P", 2 + half), ("pvaug1", par)], w=[("pvaug", par)])
            for fb in range(8):
                self.act(cacc[:, fb, :], cbp[:, fb, 3:131], AF.Identity, r=[("pcb", q), "conv_wT", "conv_bT"], w=[("pcacc", fb)],
                         scale=cw[:, fb, 3:4], bias=cbias[:, fb:fb + 1])
            for j in range(3):
                for fb in range(8):
                    self.v("dve", "scalar_tensor_tensor", cacc[:, fb, :], cbp[:, fb, j:j + 128], cw[:, fb, j:j + 1], cacc[:, fb, :],
                           op0=ALU.mult, op1=ALU.add, r=[("pcb", q), ("pcacc", fb)], w=[("pcacc", fb)])
            self.v("dve", "tensor_scalar", hist, cbp[:, :, 128:131], tv, M["one1"], op0=ALU.mult, op1=ALU.mult,
                   r=[("pcb", q), "tvalid", "one1"], w=["hist"])
            for fb in range(8):
                self.act(c_act[par][:, fb, :], cacc[:, fb, :], AF.Silu, r=[("pcacc", fb)], w=[("pc_act", par, fb)])
            kmps = ps[:, 4, :].rearrange("p (h k) -> p h k", h=4)
            for h in range(4):
                for ec in range(2):
                    self.mm(kmps[:, h, :], c_act[par][:, 2 * h + ec, :], M["wmk"][:, ec, h, :], start=(ec == 0), stop=(ec == 1),
                            r=[("pc_act", par, 2 * h + ec), "wmk"], w=[("P", 4)])
            self.v("dve", "tensor_tensor", kmw[par], kmps, wk_all[:, i, :].unsqueeze(2).to_broadcast([128, 4, 128]), ALU.mult,
                   r=[("P", 4), "wk_all"], w=[("pkmw", par)])
            for h in range(4):
                bk = 2 + (h % 2)
                dps = ps[:, bk, 0:257]
                self.mm(dps, kmw[par][:, h, :], vaug[par][:, h, :], r=[("pkmw", par), ("pvaug", par)], w=[("P", bk)])
                self.v("dve", "tensor_tensor", M["CT"][:, h, :], M["CT"][:, h, :], dps, ALU.add, r=[("P", bk), "CT"], w=["CT"])
    self.v("pool", "tensor_copy", M["convbuf"][:, :, 0:3], hist, r=["hist"], w=[("cb", 0), ("cb", 1)])


Builder.mlstm_prefix = _mlstm_prefix
```

```python
import numpy as np
import concourse.bass as bass
import concourse.mybir as mybir
from concourse.bass_utils import run_bass_kernel_spmd

F32 = mybir.dt.float32
BF16 = mybir.dt.bfloat16
AF = mybir.ActivationFunctionType
ALU = mybir.AluOpType
AX = mybir.AxisListType

NCORES = 8
D = 1024
SEQ = 8192
SEG = 2048
NLT = SEG // 128
HALO = 2048
PREFIX = 6144
TS = 32
PW = 10248
OFF_Q, OFF_K, OFF_V, OFF_ZA, OFF_XM, OFF_ZM, OFF_OM, OFF_I, OFF_F, OFF_GA, OFF_GM = (
    0, 1536, 3072, 4608, 5120, 6144, 7168, 8192, 8196, 8200, 9224)
GROUPS = ((128, 1), (512, 4), (2048, 16))
EPS = 1e-6
NEG = -30000.0
RAW_GAP = 1

DEBUG = {}


class _Op:
    __slots__ = ("eng", "fn", "deps", "stream", "signal", "val", "idx", "pos", "slot")


class Prog:
    ENGS = ("pe", "act", "dve", "pool", "sp")

    def __init__(self, nc):
        self.nc = nc
        self.ops = {e: [] for e in self.ENGS}
        self.lastw = {}
        self.readers = {}
        self.streams = {}
        self.barrier_deps = []
        self.nops = 0

    def add(self, eng, fn, r=(), w=(), dma=None):
        op = _Op()
        op.eng = eng
        op.fn = fn
        op.stream = dma
        op.signal = False
        op.val = None
        op.idx = self.nops
        self.nops += 1
        deps = {}
        for k in r:
            d = self.lastw.get(k)
            if d is not None:
                deps[d.idx] = (d, True)
            if isinstance(k, tuple) and k[0] == "P":
                for d in self.readers.get(k, ()):
                    if d.eng != eng and d.idx not in deps:
                        deps[d.idx] = (d, False)
        for k in w:
            d = self.lastw.get(k)
            if d is not None and d.idx not in deps:
                deps[d.idx] = (d, False)
            for d in self.readers.get(k, ()):
                if d.idx not in deps:
                    deps[d.idx] = (d, False)
        for d in self.barrier_deps:
            if d.idx not in deps:
                deps[d.idx] = (d, True)
        op.deps = list(deps.values())
        for k in w:
            self.lastw[k] = op
            self.readers[k] = []
        for k in r:
            self.readers.setdefault(k, []).append(op)
        op.pos = len(self.ops[eng])
        self.ops[eng].append(op)
        if dma is not None:
            self.streams.setdefault(dma, []).append(op)
        return op

    KSEM = 8
    KSEM_STREAM = {"cpy": 32}
    PERSIST = ("cpy",)

    def kof(self, s):
        return self.KSEM_STREAM.get(s, self.KSEM)

    def barrier(self):
        deps = []
        for e in self.ENGS:
            if self.ops[e]:
                for op in reversed(self.ops[e]):
                    if op.stream is None:
                        deps.append(op)
                        break
        for s, lst in self.streams.items():
            if s in self.PERSIST:
                continue
            deps.extend(lst[-self.kof(s):])
        self.barrier_deps = deps
        self.lastw = {k: v for k, v in self.lastw.items() if v.stream in self.PERSIST}
        self.readers = {}

    def emit(self, final_streams):
        nc = self.nc
        K = self.KSEM
        need = {}
        for e in self.ENGS:
            for op in self.ops[e]:
                lst = []
                for d, raw in op.deps:
                    if d.stream is None and d.eng == e:
                        if e in ("pe", "sp"):
                            continue
                        if not raw:
                            continue
                        if op.pos - d.pos > RAW_GAP:
                            continue
                    d.signal = True
                    lst.append(d)
                need[op.idx] = lst
        for e in self.ENGS:
            cnt = 0
            for op in self.ops[e]:
                if op.stream is None and op.signal:
                    cnt += 1
                    op.val = cnt
        for s, lst in self.streams.items():
            Ks = self.kof(s)
            for i, op in enumerate(lst):
                op.slot = i % Ks
                op.val = 16 * (i // Ks + 1)
        import contextlib
        with contextlib.ExitStack() as st:
            sems = {e: st.enter_context(nc.semaphore("s_" + e)) for e in self.ENGS}
            ssems = {s: [st.enter_context(nc.semaphore("d_%s_%d" % (s, k))) for k in range(min(self.kof(s), len(lst)))]
                     for s, lst in self.streams.items()}
            block = st.enter_context(nc.Block())

            def run(e, eng):
                waited = {}

                def wait(key, v):
                    if v > waited.get(key, 0):
                        waited[key] = v
                        sem = ssems[key[1]][key[2]] if key[0] == "s" else sems[key[1]]
                        eng.wait_ge(sem, v)

                for op in self.ops[e]:
                    w = {}
                    for d in need[op.idx]:
                        key = ("s", d.stream, d.slot) if d.stream is not None else ("e", d.eng)
                        if d.val > w.get(key, 0):
                            w[key] = d.val
                    for key, v in w.items():
                        wait(key, v)
                    if op.stream is not None and op.val > 16:
                        wait(("s", op.stream, op.slot), op.val - 16)
                    ins = op.fn(eng)
                    if op.stream is not None:
                        ins.then_inc(ssems[op.stream][op.slot], 16)
                    elif op.signal:
                        ins.then_inc(sems[e], 1)
                if e == "sp":
                    for s in final_streams:
                        if s in self.streams:
                            for op in self.streams[s][-self.kof(s):]:
                                wait(("s", s, op.slot), op.val)

            block.tensor(lambda t: run("pe", t))
            block.scalar(lambda t: run("act", t))
            block.vector(lambda t: run("dve", t))
            block.gpsimd(lambda t: run("pool", t))
            block.sync(lambda t: run("sp", t))


class Arena:
    def __init__(self, nc, words):
        self.t = nc.alloc_sbuf_tensor("arena", [128, words], F32)
        self.words = words
        self.top = 0
        self.marks = []

    def push(self):
        self.marks.append(self.top)

    def pop(self):
        self.top = self.marks.pop()

    def alloc(self, shape, dt=F32):
        n = int(np.prod(shape))
        words = (n + 1) // 2 if dt == BF16 else n
        words = (words + 7) // 8 * 8
        assert self.top + words <= self.words, ("SBUF arena overflow", self.top, words, self.words)
        ap = self.t[:, self.top:self.top + words]
        self.top += words
        if dt == BF16:
            ap = ap.bitcast(BF16)
        ap = ap[:, 0:n]
        if len(shape) == 2:
            ap = ap.rearrange("p (a b) -> p a b", a=shape[0])
        elif len(shape) == 3:
            ap = ap.rearrange("p (a b c) -> p a b c", a=shape[0], b=shape[1])
        elif len(shape) == 4:
            ap = ap.rearrange("p (a b c d) -> p a b c d", a=shape[0], b=shape[1], c=shape[2])
        return ap


def _t5_bucket(dist):
    n = np.asarray(dist).astype(np.int64)
    max_exact = 16
    nf = np.maximum(n, 1).astype(np.float32)
    large = max_exact + (np.log(nf / max_exact) / np.log(np.float32(2048 / max_exact))
                         * (32 - max_exact)).astype(np.int64)
    large = np.minimum(large, 31)
    return np.where(n < max_exact, n, large).astype(np.int32)


def _consts():
    c = {}
    c["ident"] = np.eye(128, dtype=np.float32)
    c["antiid"] = np.eye(128, dtype=np.float32)[::-1].copy()
    s = np.arange(128)
    c["tri"] = (s[:, None] <= s[None, :]).astype(np.float32)
    c["cmaskT"] = np.where(s[:, None] <= s[None, :], 0.0, NEG).astype(np.float32)
    oh = np.zeros((32, 3, 129), np.float32)
    for g, (_, d) in enumerate(GROUPS):
        b = _t5_bucket(np.arange(129) * d)
        oh[b, g, np.arange(129)] = 1.0
    c["ohd"] = oh
    selp = np.zeros((5, 128), np.float32); selp[0] = 1.0
    sels = np.zeros((5, 128), np.float32)
    for i in range(TS):
        sels[1 + i // 8, i] = 1.0
    c["selp"] = selp
    c["sels"] = sels
    return c


def _fm(v):
    return np.ascontiguousarray(v.reshape(-1, 128).T)


class Builder:
    def __init__(self):
        nc = bass.Bass("TRN2", target_bir_lowering=False)
        self.nc = nc
        self.P = Prog(nc)
        self.A = Arena(nc, 53184)
        self.ps = nc.alloc_psum_tensor("psum", [128, 8, 512], F32)
        self.din = {}
        self.dout = {}
        self.uid = 0

    def inp(self, name, shape, dt=F32):
        t = self.nc.dram_tensor(name, list(shape), dt, kind="ExternalInput").ap()
        self.din[name] = t
        return t

    def outp(self, name, shape, dt=F32):
        t = self.nc.dram_tensor(name, list(shape), dt, kind="ExternalOutput").ap()
        self.dout[name] = t
        return t

    def scratch(self, name, shape, dt=F32):
        return self.nc.dram_tensor(name, list(shape), dt).ap()

    def mm(self, out, lhsT, rhs, start=True, stop=True, r=(), w=()):
        return self.P.add("pe", lambda e: e.matmul(out, lhsT, rhs, start=start, stop=stop), r, w)

    def tr(self, out, in_, ident, r=(), w=()):
        return self.P.add("pe", lambda e: e.transpose(out, in_, ident), r, w)

    def act(self, out, in_, func, r=(), w=(), **kw):
        return self.P.add("act", lambda e: e.activation(out=out, in_=in_, func=func, **kw), r, w)

    def v(self, eng, name, *args, r=(), w=(), **kw):
        return self.P.add(eng, lambda e: getattr(e, name)(*args, **kw), r, w)

    def dma(self, eng, out, in_, stream, r=(), w=(), **kw):
        return self.P.add(eng, lambda e: e.dma_start(out=out, in_=in_, **kw), r, w, dma=stream)

    def dbg(self, name, ap, shape, dt=F32, r=()):
        if not DEBUG.get(name):
            return
        o = self.outp("dbg_" + name, shape, dt)
        self.P.barrier()
        self.dma("sp", o, ap, "dbg", r=r)

    def key(self, base):
        self.uid += 1
        return (base, self.uid)

    def phase0(self):
        A, P, ps = self.A, self.P, self.ps
        C = self.C = {}

        def load(name, shape, parts=128, dt=F32, eng="sp", src=None):
            t = A.alloc(list(shape[1:]), dt)
            src = self.inp(name, shape) if src is None else src
            self.dma(eng, t[0:parts], src, "ld0", w=[name])
            C[name] = t
            return t

        load("ident", [128, 128])
        C["ident_bf"] = A.alloc([128], BF16)
        self.dma("pool", C["ident_bf"], self.din["ident"], "ldc", w=["ident_bf"])
        C["antiid_bf"] = A.alloc([128], BF16)
        self.dma("pool", C["antiid_bf"], self.inp("antiid", [128, 128]), "ldc", w=["antiid_bf"])
        load("tri", [128, 128])
        load("antiid", [128, 128], src=self.din["antiid"])
        C["cmaskT_bf"] = A.alloc([128], BF16)
        self.dma("pool", C["cmaskT_bf"], self.inp("cmaskT", [128, 128]), "ldc", w=["cmaskT_bf"])
        load("ohd", [32, 3, 129], parts=32)
        load("selp", [5, 128], parts=5)
        load("sels", [5, 128], parts=5)
        load("rel_table", [32, 24], parts=32)
        for nm in ("gainT", "conv_bT", "m_normT", "m_skipT"):
            load(nm, [128, 8])
        load("b_adaT", [128, 24])
        load("conv_wT", [128, 8, 4])
        load("b_if_bc", [128, 8])
        load("tvalid", [128, 64])
        load("cT", [128, 8, 5])

        siluT = A.alloc([8, 5], BF16)
        self.act(siluT, C["cT"], AF.Silu, r=["cT"], w=["siluT"])

        ada = A.alloc([24, 5])
        mult = A.alloc([8, 5])
        gate_p = A.alloc([1024])
        gate_s = A.alloc([1024])
        A.push()
        load("b_gate_rows", [5, 1024], parts=5)
        gate_rows = A.alloc([1024])
        w_ada = self.inp("w_ada", [1024, 3072])
        wada = A.alloc([8, 3072], BF16)
        wv = w_ada.rearrange("(c p) n -> p c n", p=128)
        for c in range(8):
            self.dma("pool", wada[:, c, :], wv[:, c, :], "ldw", w=[("wada", c)])
        adaps = ps[:, 0, 0:120].rearrange("p (a b) -> p a b", a=24)
        for cb in range(24):
            for c in range(8):
                self.mm(adaps[:, cb, :], wada[:, c, cb * 128:(cb + 1) * 128], siluT[:, c, :],
                        start=(c == 0), stop=(c == 7), r=[("wada", c), "siluT"], w=[("P", 0)])
        for half in range(2):
            for c in range(8):
                self.mm(ps[0:5, 1 + half, :], siluT[:, c, :], wada[:, c, 2048 + half * 512:2048 + (half + 1) * 512],
                        start=(c == 0), stop=(c == 7), r=[("wada", c), "siluT"], w=[("P", 1 + half)])
        self.v("dve", "tensor_tensor", ada, adaps, C["b_adaT"].unsqueeze(2).to_broadcast([128, 24, 5]), ALU.add,
               r=[("P", 0), "b_adaT"], w=["ada"])
        self.v("dve", "tensor_scalar", mult, ada[:, 8:16, :], 1.0, None, op0=ALU.add, r=["ada"], w=["mult"])
        self.v("dve", "tensor_tensor", mult, mult, C["gainT"].unsqueeze(2).to_broadcast([128, 8, 5]), ALU.mult,
               r=["mult", "gainT"], w=["mult"])
        C["mult"] = mult
        C["shift"] = ada[:, 0:8, :]
        C["ada"] = ada
        for half in range(2):
            self.v("dve", "tensor_tensor", gate_rows[0:5, half * 512:(half + 1) * 512], ps[0:5, 1 + half, :],
                   C["b_gate_rows"][0:5, half * 512:(half + 1) * 512], ALU.add,
                   r=[("P", 1 + half), "b_gate_rows"], w=[("gate_rows", half)])
        for half in range(2):
            sl = slice(half * 512, (half + 1) * 512)
            self.mm(ps[:, 3, :], C["selp"][0:5, :], gate_rows[0:5, sl], r=[("gate_rows", half), "selp"], w=[("P", 3)])
            self.act(gate_p[:, sl], ps[:, 3, :], AF.Copy, r=[("P", 3)], w=[("gate_p", half)])
            self.mm(ps[:, 4, :], C["sels"][0:5, :], gate_rows[0:5, sl], r=[("gate_rows", half), "sels"], w=[("P", 4)])
            self.act(gate_s[:, sl], ps[:, 4, :], AF.Copy, r=[("P", 4)], w=[("gate_s", half)])
        A.pop()
        P.barrier()
        C["gate_p"] = gate_p
        C["gate_s"] = gate_s
        self.dbg("ada", ada, [128, 24, 5], r=["ada"])
        self.dbg("gate_s", gate_s, [128, 1024], r=[("gate_s", 0), ("gate_s", 1)])

    def norm_tile(self, xsrc, ntok, hT_dst, kind, slot, hkey, bank0=6):
        A, P, ps, C = self.A, self.P, self.ps, self.C
        W = self.W1
        i3, i2 = slot % len(W["xt"]), slot % 2
        xt = W["xt"][i3]
        self.dma("sp", xt[0:ntok], xsrc, "ldx", w=[("xt", i3)])
        ss = W["ss"][:, slot % 4:slot % 4 + 1]
        self.act(W["junk"][0:ntok], xt[0:ntok], AF.Square, r=[("xt", i3)], w=["junk", ("ss", slot % 4)],
                 accum_out=ss[0:ntok])
        self.v("dve", "tensor_scalar", ss[0:ntok], ss[0:ntok], 1.0 / D, EPS, op0=ALU.mult, op1=ALU.add,
               r=[("ss", slot % 4)], w=[("ss", slot % 4)])
        self.act(ss[0:ntok], ss[0:ntok], AF.Ln, r=[("ss", slot % 4)], w=[("ss", slot % 4)])
        self.act(ss[0:ntok], ss[0:ntok], AF.Exp, r=[("ss", slot % 4)], w=[("ss", slot % 4)], scale=-0.5)
        xn = W["xn"][i2]
        self.act(xn[0:ntok], xt[0:ntok], AF.Copy, r=[("xt", i3), ("ss", slot % 4)], w=[("xn", i2)], scale=ss[0:ntok])
        if DEBUG.get("stage", 9) < 1:
            return
        bank = bank0 + i2
        pt = ps[:, bank, :].bitcast(BF16).rearrange("p (c t) -> p c t", c=8)
        for c in range(8):
            self.tr(pt[:, c, 0:ntok], xn[0:ntok, c * 128:(c + 1) * 128], C["ident_bf"][0:ntok, 0:ntok],
                    r=[("xn", i2), "ident_bf"], w=[("P", bank0 + i2)])
        if DEBUG.get("stage", 9) < 2:
            return
        for c in range(8):
            if kind == "p":
                if True:
                    self.act(hT_dst[:, c, :], pt[:, c, 0:ntok], AF.Identity, r=[("P", bank0 + i2), "mult", "ada"], w=[(hkey, c)],
                             scale=C["mult"][:, c, 0:1], bias=C["shift"][:, c, 0:1])
                else:
                    self.v("dve", "tensor_scalar", hT_dst[:, c, :], pt[:, c, 0:ntok], C["mult"][:, c, 0:1], C["shift"][:, c, 0:1],
                           op0=ALU.mult, op1=ALU.add, r=[("P", bank0 + i2), "mult", "ada"], w=[(hkey, c)])
            else:
                tmp = W["stmp"]
                tmp0 = W["stmp0"]
                self.act(tmp0, pt[:, c, 0:ntok], AF.Copy, r=[("P", bank0 + i2)], w=["stmp0"])
                self.v("dve", "tensor_tensor", tmp.rearrange("p (s t) -> p s t", s=4),
                       tmp0.rearrange("p (s t) -> p s t", s=4),
                       C["mult"][:, c, 1:5].unsqueeze(2).to_broadcast([128, 4, 8]), ALU.mult,
                       r=["stmp0", "mult"], w=["stmp"])
                self.v("dve", "tensor_tensor", hT_dst[:, c, :].rearrange("p (s t) -> p s t", s=4),
                       tmp.rearrange("p (s t) -> p s t", s=4),
                       C["shift"][:, c, 1:5].unsqueeze(2).to_broadcast([128, 4, 8]), ALU.add,
                       r=["stmp", "ada"], w=[(hkey, c)])

    def phase1(self):
        A = self.A
        self.hT = A.alloc([8, HALO + SEG + TS], BF16)
        A.push()
        self.W1 = {
            "xt": [A.alloc([1024]) for _ in range(3)],
            "ss": A.alloc([4]),
            "junk": A.alloc([1024], BF16),
            "xn": [A.alloc([1024], BF16) for _ in range(2)],
            "stmp": A.alloc([32]),
            "stmp0": A.alloc([32]),
        }
        xh = self.inp("xh", [HALO + SEG, 1024])
        xs = self.inp("xs", [TS, 1024])
        slot = 0
        if not DEBUG.get("skip_s"):
            self.norm_tile(xs, TS, self.hT[:, :, HALO + SEG:HALO + SEG + TS], "s", slot, ("hT", 32))
        slot += 1
        for ti in range(DEBUG.get("ntiles", (HALO + SEG) // 128)):
            self.norm_tile(xh[ti * 128:(ti + 1) * 128, :], 128, self.hT[:, :, ti * 128:(ti + 1) * 128], "p", slot, ("hT", ti))
            slot += 1
        self.dbg("hT", self.hT, [128, 8, HALO + SEG + TS], BF16, r=[])
        self.dbg("xn", self.W1["xn"][1], [128, 1024], BF16, r=[])
        A.pop()


def build_program(upto=99):
    b = Builder()
    b.phase0()
    b.sample_copies()
    b.w_in = b.inp("w_in", [1024, PW])
    if upto >= 1:
        b.P.barrier()
        b.phase1()
    if upto >= 2:
        b.P.barrier()
        b.attT = b.A.alloc([4, SEG + TS], BF16)
        b.C["F"] = b.A.alloc([3, 129])
        b.A.push()
        b.phase_bias()
        b.P.barrier()
        if DEBUG.get("att_stage", 9) >= 1:
            b.phase_attention()
        b.A.pop()
        b.P.barrier()
        if not DEBUG.get("no_sattn"):
            b.phase_sample_attn()
        b.dbg("attT2", b.attT, [128, 4, SEG + TS], BF16)
    if upto >= 3:
        b.P.barrier()
        b.phase_mlstm()
    if upto >= 4:
        b.P.barrier()
        b.phase_out()
    if DEBUG.get("dmult"):
        DEBUG["mult_end"] = True
        b.dbg("mult_end", b.C["mult"], [128, 8, 5])
    b.P.emit(final_streams=list(b.P.streams.keys()))
    return b


def make_in_maps(inp, cores):
    consts = _consts()
    f32 = np.float32
    maps = []
    xp = inp["x_prompt"]
    for c in cores:
        b, p = c // 4, c % 4
        s0 = p * SEG
        m = dict(consts)
        ext = np.zeros((PREFIX + SEG, D), f32)
        lo = s0 - PREFIX
        src_lo = max(lo, 0)
        ext[src_lo - lo:] = xp[b, src_lo:s0 + SEG]
        m["xf"] = np.ascontiguousarray(ext[:PREFIX - HALO])
        m["xh"] = np.ascontiguousarray(ext[PREFIX - HALO:])
        m["xs"] = np.ascontiguousarray(inp["x_sample"][4 * c:4 * c + 4].reshape(TS, D))
        tv = np.zeros(64, f32)
        tv[(src_lo - lo) // 128:] = 1.0
        m["tvalid"] = np.ascontiguousarray(np.broadcast_to(tv, (128, 64)))
        call = np.concatenate([inp["c_prompt"][b:b + 1], inp["c_sample"][4 * c:4 * c + 4]], 0)
        m["cT"] = np.ascontiguousarray(call.T.reshape(8, 128, 5).transpose(1, 0, 2))
        m["w_ada"] = inp["w_ada"][0]
        m["rel_table"] = inp["rel_table"]
        m["gainT"] = _fm(inp["norm_gain"][0])
        m["conv_bT"] = _fm(inp["conv_b"][0])
        m["m_normT"] = _fm(inp["m_norm"][0])
        m["m_skipT"] = _fm(inp["m_skip"][0])
        m["b_adaT"] = _fm(inp["b_ada"][0])
        m["conv_wT"] = np.ascontiguousarray(inp["conv_w"][0].reshape(4, 8, 128).transpose(2, 1, 0))
        m["b_gate_rows"] = np.ascontiguousarray(np.broadcast_to(inp["b_ada"][0][2048:], (5, 1024)))
        m["fgain_bc"] = np.ascontiguousarray(np.broadcast_to(inp["final_gain"], (128, 1024)))
        m["b_if_bc"] = np.ascontiguousarray(np.broadcast_to(inp["b_if"][0], (128, 8)))
        m["w_in"] = inp["w_in"][0]
        st_ = np.zeros((8, 8, 128), f32)
        for t in range(8):
            st_[t, t, :] = 1.0
        m["selt"] = st_
        osl = np.zeros((128, 8, 8), f32)
        for t in range(8):
            osl[:, t, t] = 1.0
        m["onesel"] = osl
        for g, nm in enumerate(("cache_kv_w128", "cache_kv_w512", "cache_kv_w2048")):
            m["cache%d" % g] = np.ascontiguousarray(inp[nm][0, 4 * c:4 * c + 4].reshape(4, -1, 2, 512))
        m["w_pa"] = inp["w_pa"][0]
        m["w_pm"] = inp["w_pm"][0]
        m["w_out"] = inp["w_out"][0]
        m["w_mq"] = inp["w_mq"][0]
        m["w_mk"] = inp["w_mk"][0]
        eh = np.zeros((4, 4, 128), f32)
        for h in range(4):
            eh[h, h, :] = 1.0
        m["ehsel"] = eh
        sq = slice(4 * c, 4 * c + 4)
        Cst = inp["state_C"][0, sq]
        nst = inp["state_n"][0, sq]
        c0 = np.concatenate([Cst.transpose(0, 3, 1, 2), nst.transpose(0, 2, 1)[..., None]], axis=-1)
        m["C0T"] = np.ascontiguousarray(c0)
        mst = inp["state_m"][0, sq]
        m["m0row"] = np.ascontiguousarray(mst[:, :, None])
        m["m0bc"] = np.ascontiguousarray(np.broadcast_to(mst[:, None, :], (4, 128, 4)))
        cvs = inp["state_conv"][0, sq]
        m["conv0"] = np.ascontiguousarray(cvs.reshape(4, 3, 8, 128).transpose(0, 3, 2, 1))
        m["coremask"] = np.full((128, 128), NEG if p == 0 else 0.0, f32)
        sm = np.zeros((128, 4, 128), f32)
        for e in range(64):
            sm[e, 0, e] = 1.0
            sm[e, 1, 64 + e] = 1.0
            sm[64 + e, 2, e] = 1.0
            sm[64 + e, 3, 64 + e] = 1.0
        m["selmats"] = sm
        maps.append(m)
    return maps


def run_cores(inp, cores, upto=99):
    b = build_program(upto)
    maps = make_in_maps(inp, cores)
    maps = [{k: np.ascontiguousarray(v, dtype=np.float32) for k, v in m.items() if k in b.din} for m in maps]
    res = run_bass_kernel_spmd(b.nc, maps, core_ids=list(range(len(cores))))
    return res.results


def _phase_bias(self):
    A, P, ps, C = self.A, self.P, self.ps, self.C
    F_sb = C["F"]
    gv = A.alloc([3, 2, 256])
    self.v("pool", "memset", gv[0:8], NEG, w=["gv"])
    for g in range(3):
        self.mm(ps[0:8, 5, 0:129], C["rel_table"][0:32, g * 8:(g + 1) * 8], C["ohd"][0:32, g, :],
                r=["rel_table", "ohd"], w=[("P", 5)])
        self.act(F_sb[0:8, g, :], ps[0:8, 5, 0:129], AF.Copy, r=[("P", 5)], w=[("F", g)])
        self.v("pool", "tensor_copy", gv[0:8, g, 1, 127:255], F_sb[0:8, g, 0:128], r=[("F", g), "gv"], w=[("gv", g)])
        self.v("pool", "tensor_copy", gv[0:8, g, 0, 0:128], F_sb[0:8, g, 1:129], r=[("F", g), "gv"], w=[("gv", g)])
    gvd = self.scratch("gvd", [8, 3, 2, 256])
    self.dma("sp", gvd, gv[0:8], "gvw", r=[("gv", 0), ("gv", 1), ("gv", 2)], w=["gvd"])
    biasH = A.alloc([24, 256], BF16)
    for g in range(3):
        for h in range(8):
            for kb in range(2):
                off = ((h * 3 + g) * 2 + kb) * 256
                src = bass.AP(tensor=gvd.tensor, offset=off, ap=[[1, 128], [1, 128]])
                self.dma("pool", biasH[:, g * 8 + h, kb * 128:(kb + 1) * 128], src, "ldc", r=["gvd"], w=[("biasH", g)])
    C["biasH"] = biasH
    cm = A.alloc([128], BF16)
    self.dma("pool", cm, self.inp("coremask", [128, 128]), "ldc", w=["coremask"])
    C["coremask"] = cm
    C["selmats"] = A.alloc([4, 128])
    self.dma("sp", C["selmats"], self.inp("selmats", [128, 4, 128]), "ld0", w=["selmats"])


def _phase_attention(self):
    A, P, ps, C, hT = self.A, self.P, self.ps, self.C, self.hT
    w_in = self.w_in
    wv_in = w_in.rearrange("(c p) n -> p c n", p=128)
    kvp = [self.outp("kvp%d" % g, [GROUPS[g][0], 2, 512]) for g in range(3)]
    A.push()
    acc = A.alloc([2, SEG])
    wq = A.alloc([8, 128], BF16)
    wkv = A.alloc([8, 256], BF16)
    wz = A.alloc([8, 128], BF16)
    qT = A.alloc([SEG], BF16)
    kT = A.alloc([4096], BF16)
    vaug = A.alloc([32, 2, 128], BF16)
    pT = [A.alloc([256], BF16) for _ in range(4)]
    stage = [A.alloc([256]) for _ in range(2)]
    rbuf = A.alloc([512])
    att = A.alloc([512])
    sz = A.alloc([512])
    if not DEBUG.get("no_vones"):
        self.v("pool", "memset", vaug[:, :, :, 64:128], 1.0, w=["vones"])
    cnt = 0
    STG = DEBUG.get("att_stage", 9)
    for hp in range(DEBUG.get("att_hp", 4)):
        for g, (win, d) in enumerate(GROUPS):
            if g not in DEBUG.get("att_groups", (0, 1, 2)):
                continue
            U = SEG // d
            U2 = U + 128
            col = g * 512 + hp * 128
            allc = lambda nm: [(nm, c) for c in range(8)]
            self.dma("pool", wq, wv_in[:, :, OFF_Q + col:OFF_Q + col + 128], "ldw", w=allc("wq"))
            self.dma("pool", wkv[:, :, 0:128], wv_in[:, :, OFF_K + col:OFF_K + col + 128], "ldw", w=allc("wkv"))
            self.dma("pool", wkv[:, :, 128:256], wv_in[:, :, OFF_V + col:OFF_V + col + 128], "ldw", w=allc("wkv"))
            hq = [hT[:, c, HALO:HALO + SEG].rearrange("p (u r) -> p r u", r=d) for c in range(8)]
            hk = [hT[:, c, HALO - 128 * d:HALO + SEG].rearrange("p (u r) -> p r u", r=d) for c in range(8)]

            def chunks(Ux, total):
                res = []
                if Ux >= 512:
                    for r in range(d):
                        u0 = 0
                        while u0 < Ux:
                            n = min(512, Ux - u0)
                            res.append((r, 1, u0, n))
                            u0 += n
                else:
                    nr = 512 // Ux
                    for r0 in range(0, d, nr):
                        res.append((r0, nr, 0, Ux))
                return res

            def tile_keys(view_lo, r0, nr, u0, n, c):
                lo = view_lo + r0 + d * u0
                hi = view_lo + r0 + nr - 1 + d * (u0 + n - 1)
                return [(("hT", t), c) for t in range(lo // 128, hi // 128 + 1)]

            for (r0, nr, u0, n) in chunks(U, SEG):
                bank = cnt % 2
                cnt += 1
                pso = ps[:, bank, 0:nr * n]
                pso3 = pso if nr == 1 else pso.rearrange("p (a b) -> p a b", a=nr)
                for c in range(8):
                    rhs = hq[c][:, r0, u0:u0 + n] if nr == 1 else hq[c][:, r0:r0 + nr, :]
                    self.mm(pso3, wq[:, c, :], rhs, start=(c == 0), stop=(c == 7),
                            r=[("wq", c)] + tile_keys(HALO, r0, nr, u0, n, c), w=[("P", bank)])
                f0 = r0 * U + u0
                self.act(qT[:, f0:f0 + nr * n], pso, AF.Copy, r=[("P", bank)], w=["qT"], scale=0.125)
            if STG < 2:
                continue
            for (r0, nr, u0, n) in chunks(U2, U2 * d):
                bank = cnt % 2
                cnt += 1
                pso = ps[:, bank, 0:nr * n]
                pso3 = pso if nr == 1 else pso.rearrange("p (a b) -> p a b", a=nr)
                for c in range(8):
                    rhs = hk[c][:, r0, u0:u0 + n] if nr == 1 else hk[c][:, r0:r0 + nr, :]
                    self.mm(pso3, wkv[:, c, 0:128], rhs, start=(c == 0), stop=(c == 7),
                            r=[("wkv", c)] + tile_keys(HALO - 128 * d, r0, nr, u0, n, c), w=[("P", bank)])
                f0 = r0 * U2 + u0
                self.act(kT[:, f0:f0 + nr * n], pso, AF.Copy, r=[("P", bank)], w=["kT"])
            nm = U2 // 128
            if STG < 3:
                continue
            for r in range(d):
                for m in range(nm):
                    blk = r * nm + m
                    bank = cnt % 2
                    cnt += 1
                    pso = ps[:, bank, 0:256]
                    lastb = (m == nm - 1)
                    for c in range(8):
                        self.mm(pso if lastb else pso[:, 128:256], hk[c][:, r, 128 * m:128 * m + 128],
                                wkv[:, c, :] if lastb else wkv[:, c, 128:256], start=(c == 0), stop=(c == 7),
                                r=[("wkv", c)] + tile_keys(HALO - 128 * d, r, 1, 128 * m, 128, c), w=[("P", bank)])
                    self.act(vaug[:, blk, :, 0:64], pso[:, 128:256].rearrange("p (h e) -> p h e", h=2), AF.Copy,
                             r=[("P", bank), "vones"], w=[("vaug", blk)])
                    if m == nm - 1 and not DEBUG.get("no_kvout"):
                        st = stage[blk % 2]
                        if DEBUG.get("kv_act"):
                            self.act(st, pso, AF.Copy, r=[("P", bank)], w=[("stage", blk % 2)])
                        else:
                            self.v("dve", "tensor_copy", st, pso, r=[("P", bank)], w=[("stage", blk % 2)])
                        dst = kvp[g].rearrange("(i r) k c -> r i k c", r=d)[r, :, :, hp * 128:(hp + 1) * 128]
                        if DEBUG.get("kv_plain"):
                            dst = kvp[g][0:128, :, hp * 128:(hp + 1) * 128]
                        if not DEBUG.get("kv_nodma"):
                            self.dma("sp", dst, st.rearrange("p (k c) -> p k c", k=2), "out", r=[("stage", blk % 2)], w=[])
            if STG < 4:
                continue
            for r in range(d):
                for n in range(U // 128):
                    for h in range(2):
                        gh = g * 8 + hp * 2 + h
                        hs = slice(h * 64, h * 64 + 64)
                        si = cnt % 3
                        cnt += 1
                        sbk, obk = (2, 3, 6)[si], (4, 5, 7)[si]
                        S = ps[:, sbk, 0:256]
                        O = ps[:, obk, 0:128]
                        self.mm(S, C["antiid_bf"], C["biasH"][:, gh, :], start=True, stop=False,
                                r=["antiid_bf", ("biasH", g)], w=[("P", sbk)])
                        if n == 0:
                            self.mm(S[:, 0:128], C["ident_bf"], C["coremask"], start=False, stop=False,
                                    r=["ident_bf", "coremask"], w=[("P", sbk)])
                        q_ap = qT[hs, r * U + 128 * n:r * U + 128 * n + 128]
                        self.mm(S[:, 0:128], kT[hs, r * U2 + 128 * n:r * U2 + 128 * n + 128], q_ap, start=False, stop=False,
                                r=["qT", "kT"], w=[("P", sbk)])
                        self.mm(S[:, 128:256], kT[hs, r * U2 + 128 * (n + 1):r * U2 + 128 * (n + 2)], q_ap, start=False, stop=True,
                                r=["qT", "kT"], w=[("P", sbk)])
                        self.act(pT[si], S, AF.Exp, r=[("P", sbk)], w=[("pT", si)])
                        b0 = r * nm + n
                        self.mm(O, vaug[:, b0, h, :], pT[si][:, 0:128], start=True, stop=False,
                                r=[("vaug", b0), ("pT", si)], w=[("P", obk)])
                        self.mm(O, vaug[:, b0 + 1, h, :], pT[si][:, 128:256], start=False, stop=True,
                                r=[("vaug", b0 + 1), ("pT", si)], w=[("P", obk)])
                        av = acc[:, h, :].rearrange("p (u r) -> p r u", r=d)[:, r, 128 * n:128 * n + 128]
                        if d == 1:
                            ak = [("acc", h, n // 4, rr) for rr in range(16)]
                        elif d == 4:
                            ak = [("acc", h, n, r + 4 * j) for j in range(4)]
                        else:
                            ak = [("acc", h, qq, r) for qq in range(4)]
                        if g == 0:
                            self.v("dve", "tensor_copy", av, O, r=[("P", obk)], w=ak)
                        else:
                            self.v("dve", "tensor_tensor", av, av, O, ALU.add, r=[("P", obk)] + ak, w=ak)
        if STG < 5:
            continue
        P.barrier()
        zc = OFF_ZA + hp * 128
        self.dma("pool", wz, wv_in[:, :, zc:zc + 128], "ldw", w=[("wz", c) for c in range(8)])
        for k in range(4):
            tk = slice(512 * k, 512 * k + 512)
            for j, (bank, sm) in enumerate(((6, (0, 1)), (7, (2, 3)))):
                for h in range(2):
                    self.mm(ps[:, bank, :], C["selmats"][:, sm[h], :], acc[:, h, tk], start=(h == 0), stop=(h == 1),
                            r=["selmats"], w=[("P", 6 + j)])
            self.v("dve", "reciprocal", rbuf, ps[:, 7, :], r=[("P", 7)], w=["rbuf"])
            self.v("dve", "tensor_tensor", att, ps[:, 6, :], rbuf, ALU.mult, r=[("P", 6), "rbuf"], w=["att"])
            for c in range(8):
                self.mm(ps[:, 0, :], wz[:, c, :], hT[:, c, HALO + 512 * k:HALO + 512 * k + 512], start=(c == 0), stop=(c == 7),
                        r=[("wz", c)] + [(("hT", t), c) for t in range(16 + 4 * k, 16 + 4 * k + 4)], w=[("P", 0)])
            self.act(sz, ps[:, 0, :], AF.Silu, r=[("P", 0)], w=["sz"])
            self.v("dve", "tensor_tensor", self.attT[:, hp, tk], att, sz, ALU.mult, r=["att", "sz"], w=[("attT", hp)])
        P.barrier()
    A.pop()
    self.dbg("attT", self.attT, [128, 4, SEG + TS], BF16)


Builder.phase_bias = _phase_bias
Builder.phase_attention = _phase_attention


def _mlstm_setup(self):
    A, P, C = self.A, self.P, self.C
    wv_in = self.w_in.rearrange("(c p) n -> p c n", p=128)
    M = self.M = {}
    M["wxm"] = A.alloc([8, 1024], BF16)
    M["wg"] = A.alloc([8, 8], BF16)
    M["wmq"] = A.alloc([2, 4, 128], BF16)
    M["wmk"] = A.alloc([2, 4, 128], BF16)
    for c in range(8):
        self.dma("pool", M["wxm"][:, c, :], wv_in[:, c, OFF_XM:OFF_XM + 1024], "ldw", w=["wxm"])
        self.dma("pool", M["wg"][:, c, :], wv_in[:, c, OFF_I:OFF_I + 8], "ldw", w=["wg"])
    wq_d = self.inp("w_mq", [4, 256, 128]).rearrange("h (c p) k -> p c h k", p=128)
    wk_d = self.inp("w_mk", [4, 256, 128]).rearrange("h (c p) k -> p c h k", p=128)
    for ec in range(2):
        for h in range(4):
            self.dma("pool", M["wmq"][:, ec, h, :], wq_d[:, ec, h, :], "ldw", w=["wmq"])
            self.dma("pool", M["wmk"][:, ec, h, :], wk_d[:, ec, h, :], "ldw", w=["wmk"])
    M["ones"] = A.alloc([128])
    self.v("pool", "memset", M["ones"], 1.0, w=["ones"])
    M["ehsel"] = A.alloc([4, 128])
    self.dma("sp", M["ehsel"][0:4], self.inp("ehsel", [4, 4, 128]), "ld0", w=["ehsel"])
    M["negbig"] = A.alloc([64])
    self.v("dve", "tensor_scalar", M["negbig"], C["tvalid"], 1.0e4, -1.0e4, op0=ALU.mult, op1=ALU.add,
           r=["tvalid"], w=["negbig"])
    M["one1"] = A.alloc([1])
    M["zero1"] = A.alloc([1])
    self.v("pool", "memset", M["one1"], 1.0, w=["one1"])
    self.v("pool", "memset", M["zero1"], 0.0, w=["zero1"])
    M["neg1"] = A.alloc([1])
    self.v("pool", "memset", M["neg1"], -1.0, w=["neg1"])
    M["CT"] = A.alloc([4, 257], F32)
    M["m_row"] = A.alloc([1], F32)
    M["m_bc"] = A.alloc([4], F32)
    M["convbuf"] = A.alloc([8, 131], F32)


def _mlstm_local_setup(self):
    A, P, C, M = self.A, self.P, self.C, self.M
    wv_in = self.w_in.rearrange("(c p) n -> p c n", p=128)
    M["wzm"] = A.alloc([8, 1024], BF16)
    M["wom"] = A.alloc([8, 1024], BF16)
    for c in range(8):
        self.dma("pool", M["wzm"][:, c, :], wv_in[:, c, OFF_ZM:OFF_ZM + 1024], "ldw", w=["wzm"])
        self.dma("pool", M["wom"][:, c, :], wv_in[:, c, OFF_OM:OFF_OM + 1024], "ldw", w=["wom"])
    for nm, shp, dt in (("cacc", [8, 128], F32), ("c_act", [8, 128], BF16),
                        ("vaug", [4, 257], BF16), ("kmw", [4, 128], BF16), ("qmT", [4, 128], BF16), ("kmT", [4, 128], BF16),
                        ("gt", [8], F32), ("lf", [4], F32), ("ie", [4], F32), ("b_tok", [4], F32), ("a_tok", [4], F32),
                        ("a_row", [128], F32), ("cm_row", [128], F32), ("M_row", [128], F32), ("negM_row", [128], F32),
                        ("AT", [1], F32), ("Mend_row", [1], F32), ("dg", [4], F32), ("Mend_bc", [4], F32),
                        ("tmp4", [4], F32), ("wk", [4], F32), ("wC", [4], F32), ("M_tok", [4], F32), ("emt", [4], F32),
                        ("DT", [128], F32), ("Wbc", [128], F32), ("scT", [128], BF16), ("qtil", [128], BF16),
                        ("CTb", [4, 257], BF16), ("hh", [256], F32), ("hn", [256], BF16), ("hnm", [8, 128], F32),
                        ("st", [8], F32), ("so", [128], F32), ("szm", [128], F32), ("t1", [128], F32), ("t2", [128], F32),
                        ):
        M[nm] = A.alloc(shp, dt)
        if nm in ("DT", "Wbc", "scT", "qtil", "hh", "hn", "st"):
            M[nm + "_b"] = A.alloc(shp, dt)
    self.v("pool", "memset", M["vaug"][:, :, 256:257], 1.0, w=["vaug1"])
    M["so_all"] = A.alloc([8, 512], BF16)
    M["sz_all"] = A.alloc([8, 512], BF16)


def _mlstm_gates4(self, tok0, tiles):
    ps, M, hT = self.ps, self.M, self.hT
    for fb in range(8):
        for (wname, bank, func, dst, dk) in (("wom", 5, AF.Sigmoid, "so_all", "so_all"), ("wzm", 6, AF.Silu, "sz_all", "sz_all")):
            for c in range(8):
                self.mm(ps[:, bank, :], M[wname][:, c, fb * 128:(fb + 1) * 128], hT[:, c, tok0:tok0 + 512], start=(c == 0), stop=(c == 7),
                        r=[wname] + [(("hT", t), c) for t in tiles], w=[("P", bank)])
            self.act(M[dst][:, fb, :], ps[:, bank, :], func, r=[("P", bank)], w=[(dk, fb)])


def _mlstm_tile(self, hTt, hkeys, ntok, tcol, with_out, mout_dst, moutkey, gate_off=None):
    A, P, ps, C, M = self.A, self.P, self.ps, self.C, self.M
    N = ntok
    B = lambda i: ("P", i)
    tv = C["tvalid"][:, tcol:tcol + 1] if tcol is not None else M["one1"]
    nb = M["negbig"][:, tcol:tcol + 1] if tcol is not None else M["zero1"]
    tvk = ["tvalid", "negbig", "one1", "zero1"]
    cb = M["convbuf"]
    xmps = ps[:, 0:2, :].rearrange("p a (b t) -> p (a b) t", t=128)
    for fb in range(8):
        for c in range(8):
            self.mm(xmps[:, fb, 0:N], M["wxm"][:, c, fb * 128:(fb + 1) * 128], hTt[:, c, :], start=(c == 0), stop=(c == 7),
                    r=["wxm"] + [(k, c) for k in hkeys], w=[B(fb // 4)])
    for half in range(2):
        self.act(cb[:, 4 * half:4 * half + 4, 3:3 + N], xmps[:, 4 * half:4 * half + 4, 0:N], AF.Copy,
                 r=[B(half)] + tvk, w=[("cb", half)], scale=tv)
    for half in range(2):
        for c in range(8):
            self.mm(ps[0:N, 2 + half, :], hTt[:, c, :], M["wxm"][:, c, half * 512:(half + 1) * 512], start=(c == 0), stop=(c == 7),
                    r=["wxm"] + [(k, c) for k in hkeys], w=[B(2 + half)])
        self.act(M["vaug"][0:N, 2 * half:2 * half + 2, 0:256], ps[0:N, 2 + half, :].rearrange("p (h v) -> p h v", h=2), AF.Copy,
                 r=[B(2 + half), "vaug1"], w=[("vaug", half)])
    gps = ps[0:N, 7, 0:8]
    for c in range(8):
        self.mm(gps, hTt[:, c, :], M["wg"][:, c, :], start=(c == 0), stop=(c == 7),
                r=["wg"] + [(k, c) for k in hkeys], w=[B(7)])
    gt = M["gt"]
    self.v("dve", "tensor_tensor", gt[0:N], gps, C["b_if_bc"][0:N], ALU.add, r=[B(7), "b_if_bc"], w=["gt"])
    lf = M["lf"]
    self.act(lf[0:N], gt[0:N, 4:8], AF.Exp, r=["gt"], w=["lf"], scale=-1.0)
    self.act(lf[0:N], lf[0:N], AF.Ln, r=["lf"], w=["lf"], bias=M["one1"][0:N])
    self.v("dve", "tensor_scalar", lf[0:N], lf[0:N], tv[0:N], M["neg1"][0:N], op0=ALU.mult, op1=ALU.mult, r=["lf", "neg1"] + tvk, w=["lf"])
    ie = M["ie"]
    self.v("dve", "tensor_scalar", ie[0:N], gt[0:N, 0:4], tv[0:N], nb[0:N], op0=ALU.mult, op1=ALU.add, r=["gt"] + tvk, w=["ie"])
    tri, ident = C["tri"], C["ident"]
    self.mm(ps[0:N, 7, 8:12], tri[0:N, 0:N], lf[0:N], r=["lf", "tri"], w=[B(7)])
    self.mm(ps[:, 7, 12:16], M["ones"][0:N, :], lf[0:N], r=["lf", "ones"], w=[B(7)])
    self.mm(ps[0:4, 7, 144:144 + N], lf[0:N], tri[0:N, 0:N], r=["lf", "tri"], w=[B(7)])
    b_tok, a_tok = M["b_tok"], M["a_tok"]
    self.v("dve", "tensor_copy", b_tok[0:N], ps[0:N, 7, 8:12], r=[B(7)], w=["b_tok"])
    self.v("dve", "tensor_tensor", a_tok[0:N], ie[0:N], b_tok[0:N], ALU.subtract, r=["ie", "b_tok"], w=["a_tok"])
    self.mm(ps[0:4, 7, 16:16 + N], a_tok[0:N], ident[0:N, 0:N], r=["a_tok", "ident"], w=[B(7)])
    a_row = M["a_row"]
    self.v("dve", "tensor_copy", a_row[0:4, 0:N], ps[0:4, 7, 16:16 + N], r=[B(7)], w=["a_row"])
    cw, cbias = C["conv_wT"], C["conv_bT"]
    for fb in range(8):
        self.act(M["cacc"][:, fb, 0:N], cb[:, fb, 3:3 + N], AF.Identity, r=[("cb", fb // 4), "conv_wT", "conv_bT"], w=[("cacc", fb)],
                 scale=cw[:, fb, 3:4], bias=cbias[:, fb:fb + 1])
    for j in range(3):
        for fb in range(8):
            ca = M["cacc"][:, fb, 0:N]
            self.v("dve", "scalar_tensor_tensor", ca, cb[:, fb, j:j + N], cw[:, fb, j:j + 1], ca, op0=ALU.mult, op1=ALU.add,
                   r=[("cb", fb // 4), ("cacc", fb)], w=[("cacc", fb)])
    for fb in range(8):
        self.act(M["c_act"][:, fb, 0:N], M["cacc"][:, fb, 0:N], AF.Silu, r=[("cacc", fb)], w=[("c_act", fb)])
    for half in range(2):
        self.v("pool", "tensor_copy", cb[:, 4 * half:4 * half + 4, 0:3], cb[:, 4 * half:4 * half + 4, N:N + 3],
               r=[("cb", half)], w=[("cb", half)])
    kmps = ps[0:N, 4, :].rearrange("p (h k) -> p h k", h=4)
    for h in range(4):
        for ec in range(2):
            self.mm(kmps[:, h, :], M["c_act"][:, 2 * h + ec, 0:N], M["wmk"][:, ec, h, :], start=(ec == 0), stop=(ec == 1),
                    r=[("c_act", 2 * h + ec), "wmk"], w=[B(4)])
    if with_out:
        qps = ps[:, 5, :].rearrange("p (h t) -> p h t", h=4)
        kps = ps[:, 6, :].rearrange("p (h t) -> p h t", h=4)
        for h in range(4):
            for ec in range(2):
                self.mm(qps[:, h, 0:N], M["wmq"][:, ec, h, :], M["c_act"][:, 2 * h + ec, 0:N], start=(ec == 0), stop=(ec == 1),
                        r=[("c_act", 2 * h + ec), "wmq"], w=[B(5)])
            for ec in range(2):
                self.mm(kps[:, h, 0:N], M["wmk"][:, ec, h, :], M["c_act"][:, 2 * h + ec, 0:N], start=(ec == 0), stop=(ec == 1),
                        r=[("c_act", 2 * h + ec), "wmk"], w=[B(6)])
        self.act(M["qmT"][:, :, 0:N], qps[:, :, 0:N], AF.Copy, r=[B(5)], w=["qmT"], scale=float(128 ** -0.5))
        self.act(M["kmT"][:, :, 0:N], kps[:, :, 0:N], AF.Copy, r=[B(6)], w=["kmT"])
        self.v("pool", "tensor_copy", M["CTb"], M["CT"], r=["CT"], w=["CTb"])
        self.v("dve", "tensor_tensor_scan", M["cm_row"][0:4, 0:N], M["ones"][0:4, 0:N], a_row[0:4, 0:N], -1.0e30,
               op0=ALU.mult, op1=ALU.max, r=["a_row", "ones"], w=["cm_row"])
        self.v("dve", "tensor_tensor", M["M_row"][0:4, 0:N], M["cm_row"][0:4, 0:N], M["m_row"][0:4, 0:1].to_broadcast([4, N]), ALU.max,
               r=["cm_row", "m_row"], w=["M_row"])
        self.v("dve", "tensor_scalar", M["negM_row"][0:4, 0:N], M["M_row"][0:4, 0:N], -1.0, None, op0=ALU.mult,
               r=["M_row"], w=["negM_row"])
        self.mm(ps[0:N, 7, 276:280], M["M_row"][0:4, 0:N], ident[0:4, 0:4], r=["M_row", "ident"], w=[B(7)])
        self.v("dve", "tensor_tensor", M["emt"][0:N], b_tok[0:N], ps[0:N, 7, 276:280], ALU.add, r=["b_tok", B(7)], w=["emt"])
        self.act(M["emt"][0:N], M["emt"][0:N], AF.Exp, r=["emt"], w=["emt"], scale=-1.0)
    self.v("dve", "tensor_reduce", M["AT"][0:4], a_row[0:4, 0:N], AX.X, ALU.max, r=["a_row"], w=["AT"])
    self.v("dve", "tensor_tensor", M["Mend_row"][0:4], M["AT"][0:4], M["m_row"][0:4], ALU.max, r=["AT", "m_row"], w=["Mend_row"])
    self.v("dve", "tensor_tensor", M["dg"][0:4], ident[0:4, 0:4], M["Mend_row"][0:4, 0:1].to_broadcast([4, 4]), ALU.mult,
           r=["Mend_row", "ident"], w=["dg"])
    self.mm(ps[:, 7, 272:276], M["ones"][0:4, :], M["dg"][0:4], r=["dg", "ones"], w=[B(7)])
    self.v("dve", "tensor_copy", M["Mend_bc"], ps[:, 7, 272:276], r=[B(7)], w=["Mend_bc"])
    self.v("dve", "tensor_tensor", M["wk"][0:N], a_tok[0:N], M["Mend_bc"][0:N], ALU.subtract, r=["a_tok", "Mend_bc"], w=["wk"])
    self.act(M["wk"][0:N], M["wk"][0:N], AF.Exp, r=["wk"], w=["wk"])
    self.v("dve", "tensor_tensor", M["wC"], M["m_bc"], M["Mend_bc"], ALU.subtract, r=["m_bc", "Mend_bc"], w=["wC"])
    self.act(M["wC"], M["wC"], AF.Exp, r=["wC"], w=["wC"])
    self.v("dve", "tensor_tensor", M["kmw"][0:N], kmps, M["wk"][0:N].unsqueeze(2).to_broadcast([N, 4, 128]), ALU.mult,
           r=[B(4), "wk"], w=["kmw"])
    if with_out:
        def head_body(h):
            hsl = slice(h, h + 1)
            par = h % 2
            sfx = "" if par == 0 else "_b"
            hb0, hb1 = (0, 1) if par == 0 else (5, 6)
            pl, pm, pst = ps[:, hb0, 0:N], ps[:, hb0, 128:128 + N], ps[:, hb0, 256:256 + N]
            self.mm(pl, M["ehsel"][0:4, h, :], M["negM_row"][0:4, 0:N], r=["ehsel", "negM_row"], w=[B(hb0)])
            yield
            self.mm(pm, M["ehsel"][0:4, h, :], M["negM_row"][0:4, 0:N], start=True, stop=False, r=["ehsel", "negM_row"], w=[B(hb0)])
            yield
            self.mm(pm[0:N], C["ident_bf"][0:N, 0:N], C["cmaskT_bf"][0:N, 0:N], start=False, stop=True,
                    r=["ident_bf", "cmaskT_bf"], w=[B(hb0)])
            yield
            self.mm(pst[0:N], M["kmT"][:, h, 0:N], M["qmT"][:, h, 0:N], r=["kmT", "qmT"], w=[B(hb0)])
            yield
            self.act(M["DT" + sfx][0:N, 0:N], pm[0:N], AF.Exp, r=[B(hb0), "a_tok"], w=["DT" + sfx], bias=a_tok[0:N, hsl])
            yield
            self.act(M["Wbc" + sfx][:, 0:N], pl, AF.Exp, r=[B(hb0), "m_bc"], w=["Wbc" + sfx], bias=M["m_bc"][:, hsl])
            yield
            self.v("dve", "tensor_tensor", M["scT" + sfx][0:N, 0:N], pst[0:N], M["DT" + sfx][0:N, 0:N], ALU.mult, r=[B(hb0), "DT" + sfx], w=["scT" + sfx])
            yield
            self.v("pool", "tensor_tensor", M["qtil" + sfx][:, 0:N], M["qmT"][:, h, 0:N], M["Wbc" + sfx][:, 0:N], ALU.mult,
                   r=["qmT", "Wbc" + sfx], w=["qtil" + sfx])
            yield
            nd = ps[0:N, hb1, 0:257]
            self.mm(nd, M["scT" + sfx][0:N, 0:N], M["vaug"][0:N, h, :], start=True, stop=False, r=["scT" + sfx, ("vaug", h // 2), "vaug1"], w=[B(hb1)])
            yield
            self.mm(nd, M["qtil" + sfx][:, 0:N], M["CTb"][:, h, :], start=False, stop=True, r=["qtil" + sfx, "CTb"], w=[B(hb1)])
            yield
            st = M["st" + sfx]
            self.act(st[0:N, 0:1], nd[:, 256:257], AF.Abs, r=[B(hb1)], w=["st" + sfx])
            yield
            self.v("dve", "tensor_tensor", st[0:N, 0:1], st[0:N, 0:1], M["emt"][0:N, hsl], ALU.max, r=["st" + sfx, "emt"], w=["st" + sfx])
            yield
            self.v("dve", "reciprocal", st[0:N, 0:1], st[0:N, 0:1], r=["st" + sfx], w=["st" + sfx])
            yield
            self.act(M["hh" + sfx][0:N], nd[:, 0:256], AF.Copy, r=[B(hb1), "st" + sfx], w=["hh" + sfx, "st1" + sfx], scale=st[0:N, 0:1], accum_out=st[0:N, 1:2])
            yield
            self.act(M["hn" + sfx][0:N], M["hh" + sfx][0:N], AF.Square, r=["hh" + sfx], w=["hn" + sfx, "st2" + sfx], accum_out=st[0:N, 2:3])
            yield
            self.v("dve", "tensor_scalar", st[0:N, 3:4], st[0:N, 1:2], 1.0 / 256, None, op0=ALU.mult, r=["st1" + sfx], w=["st3" + sfx])
            yield
            self.v("dve", "tensor_tensor", st[0:N, 4:5], st[0:N, 3:4], st[0:N, 3:4], ALU.mult, r=["st3" + sfx], w=["st4" + sfx])
            yield
            self.v("dve", "scalar_tensor_tensor", st[0:N, 5:6], st[0:N, 2:3], 1.0 / 256, st[0:N, 4:5], op0=ALU.mult, op1=ALU.subtract,
                   r=["st2" + sfx, "st4" + sfx], w=["st5" + sfx])
            yield
            self.v("dve", "tensor_scalar", st[0:N, 5:6], st[0:N, 5:6], EPS, None, op0=ALU.add, r=["st5" + sfx], w=["st5" + sfx])
            yield
            self.act(st[0:N, 5:6], st[0:N, 5:6], AF.Ln, r=["st5" + sfx], w=["st5" + sfx])
            yield
            self.act(st[0:N, 5:6], st[0:N, 5:6], AF.Exp, r=["st5" + sfx], w=["st5" + sfx], scale=-0.5)
            yield
            self.v("dve", "scalar_tensor_tensor", st[0:N, 6:7], st[0:N, 3:4], -1.0, st[0:N, 5:6], op0=ALU.mult, op1=ALU.mult,
                   r=["st3" + sfx, "st5" + sfx], w=["st6" + sfx])
            yield
            self.act(M["hn" + sfx][0:N], M["hh" + sfx][0:N], AF.Identity, r=["hh" + sfx, "st5" + sfx, "st6" + sfx], w=["hn" + sfx], scale=st[0:N, 5:6], bias=st[0:N, 6:7])
            yield
            pt = ps[:, 4, 128 * par:128 * par + 128].bitcast(BF16).rearrange("p (b t) -> p b t", b=2)
            for vb in range(2):
                self.tr(pt[:, vb, 0:N], M["hn" + sfx][0:N, vb * 128:(vb + 1) * 128], C["ident_bf"][0:N, 0:N], r=["hn" + sfx, "ident_bf"], w=[B(4)])
            yield
            for vb in range(2):
                fb = 2 * h + vb
                self.act(M["hnm"][:, fb, 0:N], pt[:, vb, 0:N], AF.Copy, r=[B(4), "m_normT"], w=[("hnm", fb)], scale=C["m_normT"][:, fb:fb + 1])
            yield
        for pair in ((0, 1), (2, 3)):
            gens = [head_body(h) for h in pair]
            live = list(gens)
            while live:
                for gen in list(live):
                    try:
                        next(gen)
                    except StopIteration:
                        live.remove(gen)
    for h in range(4):
        bk = 2 + (h % 2)
        dps = ps[:, bk, 0:257]
        self.mm(dps, M["kmw"][0:N, h, :], M["vaug"][0:N, h, :], r=["kmw", ("vaug", h // 2), "vaug1"], w=[B(bk)])
        self.v("dve", "scalar_tensor_tensor", M["CT"][:, h, :], M["CT"][:, h, :], M["wC"][:, h:h + 1], dps, op0=ALU.mult, op1=ALU.add,
               r=["wC", B(bk), "CT", "CTb"], w=["CT"])
    self.v("dve", "tensor_tensor", M["m_row"][0:4], M["Mend_row"][0:4], ps[0:4, 7, 144 + N - 1:144 + N], ALU.add,
           r=["Mend_row", B(7)], w=["m_row"])
    self.v("dve", "tensor_tensor", M["m_bc"], M["Mend_bc"], ps[:, 7, 12:16], ALU.add, r=["Mend_bc", B(7)], w=["m_bc"])
    if with_out:
        for fb in range(8):
            if gate_off is None:
                for (wname, bank) in (("wom", 5), ("wzm", 6)):
                    for c in range(8):
                        self.mm(ps[:, bank, 0:N], M[wname][:, c, fb * 128:(fb + 1) * 128], hTt[:, c, :], start=(c == 0), stop=(c == 7),
                                r=[wname] + [(k, c) for k in hkeys], w=[B(bank)])
                self.act(M["so"][:, 0:N], ps[:, 5, 0:N], AF.Sigmoid, r=[B(5)], w=["so"])
                self.act(M["szm"][:, 0:N], ps[:, 6, 0:N], AF.Silu, r=[B(6)], w=["szm"])
                so_ap, sz_ap, sok, szk = M["so"][:, 0:N], M["szm"][:, 0:N], "so", "szm"
            else:
                so_ap, sz_ap = M["so_all"][:, fb, gate_off:gate_off + N], M["sz_all"][:, fb, gate_off:gate_off + N]
                sok, szk = ("so_all", fb), ("sz_all", fb)
            self.v("dve", "tensor_tensor", M["t1"][:, 0:N], so_ap, M["hnm"][:, fb, 0:N], ALU.mult, r=[sok, ("hnm", fb)], w=["t1"])
            self.v("dve", "scalar_tensor_tensor", M["t2"][:, 0:N], M["c_act"][:, fb, 0:N], C["m_skipT"][:, fb:fb + 1], M["t1"][:, 0:N],
                   op0=ALU.mult, op1=ALU.add, r=[("c_act", fb), "t1", "m_skipT"], w=["t2"])
            self.v("dve", "tensor_tensor", mout_dst[:, fb, :], M["t2"][:, 0:N], sz_ap, ALU.mult, r=["t2", szk], w=[(moutkey, fb)])


Builder.mlstm_setup = _mlstm_setup
Builder.mlstm_local_setup = _mlstm_local_setup
Builder.mlstm_tile = _mlstm_tile
Builder.mlstm_gates4 = _mlstm_gates4


def _phase_mlstm(self):
    A, P, ps, C, hT = self.A, self.P, self.ps, self.C, self.hT
    self.moutS = A.alloc([8, TS], BF16)
    A.push()
    self.mlstm_setup()
    M = self.M
    self.v("pool", "memset", M["CT"], 0.0, w=["CT"])
    A.push()
    self.W1 = {
        "xt": [A.alloc([1024]) for _ in range(2)],
        "ss": A.alloc([4]),
        "junk": A.alloc([1024], BF16),
        "xn": [A.alloc([1024], BF16) for _ in range(2)],
    }
    self.mlstm_prefix()
    if DEBUG.get("sbuf"):
        print("mlstm prefix sbuf top", A.top)
    A.pop()
    P.barrier()
    self.mlstm_local_setup()
    if DEBUG.get("sbuf"):
        print("mlstm local sbuf top", A.top)
    for j in range(DEBUG.get("nloc", 16)):
        if j % 4 == 0:
            self.mlstm_gates4(HALO + j * 128, [16 + j + q for q in range(4)])
        self.mlstm_tile(hT[:, :, HALO + j * 128:HALO + (j + 1) * 128], [("hT", 16 + j)], 128, 48 + j, True,
                        hT[:, :, j * 128:(j + 1) * 128], ("hT", j), gate_off=(j % 4) * 128)
    o_conv = self.outp("convp", [128, 8, 3])
    o_C = self.outp("Cp", [128, 4, 257])
    o_m = self.outp("mp", [4, 1])
    self.dma("sp", o_conv, M["convbuf"][:, :, 0:3], "out", r=[("cb", 0), ("cb", 1)])
    self.dma("sp", o_C, M["CT"], "out", r=["CT"])
    self.dma("sp", o_m, M["m_row"][0:4], "out", r=["m_row"])
    C0 = self.inp("C0T", [4, 128, 4, 257])
    m0r = self.inp("m0row", [4, 4, 1])
    m0b = self.inp("m0bc", [4, 128, 4])
    cv0 = self.inp("conv0", [4, 128, 8, 3])
    o_convs = self.outp("convs", [4, 128, 8, 3])
    o_Cs = self.outp("Cs", [4, 128, 4, 257])
    o_ms = self.outp("ms", [4, 4, 1])
    for j in range(4 if not DEBUG.get("no_smp") else 0):
        self.dma("sp", M["CT"], C0[j], "ldx", w=["CT"])
        self.dma("sp", M["m_row"][0:4], m0r[j], "ldx", w=["m_row"])
        self.dma("sp", M["m_bc"], m0b[j], "ldx", w=["m_bc"])
        self.dma("sp", M["convbuf"][:, :, 0:3], cv0[j], "ldx", w=[("cb", 0), ("cb", 1)])
        self.mlstm_tile(hT[:, :, HALO + SEG + 8 * j:HALO + SEG + 8 * j + 8], [("hT", 32)], 8, None, True,
                        self.moutS[:, :, 8 * j:8 * j + 8], ("moutS", j))
        self.dma("sp", o_convs[j], M["convbuf"][:, :, 0:3], "out", r=[("cb", 0), ("cb", 1)])
        self.dma("sp", o_Cs[j], M["CT"], "out", r=["CT"])
        self.dma("sp", o_ms[j], M["m_row"][0:4], "out", r=["m_row"])
    if DEBUG.get("mdump"):
        for nm, shp, dt in (("convbuf", [128, 8, 131], F32), ("c_act", [128, 8, 128], BF16), ("vaug", [128, 4, 257], BF16),
                            ("kmw", [128, 4, 128], BF16), ("gt", [128, 8], F32), ("lf", [128, 4], F32), ("ie", [128, 4], F32),
                            ("b_tok", [128, 4], F32), ("a_tok", [128, 4], F32), ("a_row", [128, 128], F32),
                            ("Mend_bc", [128, 4], F32), ("wk", [128, 4], F32), ("wC", [128, 4], F32), ("CT", [128, 4, 257], F32),
                            ("m_bc", [128, 4], F32), ("hh", [128, 256], F32), ("hnm", [128, 8, 128], F32), ("st", [128, 8], F32),
                            ("emt", [128, 4], F32), ("qmT", [128, 4, 128], BF16), ("kmT", [128, 4, 128], BF16), ("DT", [128, 128], F32),
                            ("Wbc", [128, 128], F32), ("M_row", [128, 128], F32)):
            DEBUG["md_" + nm] = True
            self.dbg("md_" + nm, M[nm], shp, dt)
        for nm, ap, shp, dt in (("hTp1", M["hTp"][1], [128, 8, 128], BF16), ("xn1", self.W1["xn"][1], [128, 1024], BF16),
                                ("xt1", self.W1["xt"][1], [128, 1024], F32), ("ss", self.W1["ss"], [128, 4], F32),
                                ("wg", M["wg"], [128, 8, 8], BF16), ("mult", C["mult"], [128, 8, 5], F32)):
            DEBUG["md_" + nm] = True
            self.dbg("md_" + nm, ap, shp, dt)
    A.pop()
    self.P.barrier()
    self.dbg("mout", self.hT[:, :, 0:SEG], [128, 8, SEG], BF16)
    self.dbg("moutS", self.moutS, [128, 8, TS], BF16)


Builder.phase_mlstm = _phase_mlstm


def _phase_out(self):
    A, P, ps, C, hT = self.A, self.P, self.ps, self.C, self.hT
    wv_in = self.w_in.rearrange("(c p) n -> p c n", p=128)
    A.push()
    wpa = A.alloc([4, 1024], BF16)
    wpm = A.alloc([8, 1024], BF16)
    wout = A.alloc([8, 1024], BF16)
    wga = A.alloc([8, 128], BF16)
    wgm = A.alloc([8, 128], BF16)
    merged = A.alloc([8, SEG + TS], BF16)
    fgain = A.alloc([1024])
    sga, sgm, t1, t2 = (A.alloc([512]) for _ in range(4))
    xt = [A.alloc([1024])] * 2
    yt = [A.alloc([1024]) for _ in range(2)]
    ss = A.alloc([4])
    self.dma("sp", fgain, self.inp("fgain_bc", [128, 1024]), "ld0", w=["fgain"])
    wpa_d = self.inp("w_pa", [512, 1024]).rearrange("(c p) n -> p c n", p=128)
    wpm_d = self.inp("w_pm", [1024, 1024]).rearrange("(c p) n -> p c n", p=128)
    wout_d = self.inp("w_out", [1024, 1024]).rearrange("(c p) n -> p c n", p=128)
    for c in range(4):
        self.dma("pool", wpa[:, c, :], wpa_d[:, c, :], "ldw", w=["wpa"])
    for c in range(8):
        self.dma("pool", wpm[:, c, :], wpm_d[:, c, :], "ldw", w=["wpm"])
        self.dma("pool", wout[:, c, :], wout_d[:, c, :], "ldw", w=["wout"])
    chunks = []
    for k in range(4):
        chunks.append(dict(n=512, m0=512 * k,
                           att=lambda hp, k=k: self.attT[:, hp, 512 * k:512 * k + 512],
                           mout=lambda fb, k=k: hT[:, fb, 512 * k:512 * k + 512],
                           hs=lambda c, k=k: hT[:, c, HALO + 512 * k:HALO + 512 * k + 512]))
    chunks.append(dict(n=TS, m0=SEG,
                       att=lambda hp: self.attT[:, hp, SEG:SEG + TS],
                       mout=lambda fb: self.moutS[:, fb, :],
                       hs=lambda c: hT[:, c, HALO + SEG:HALO + SEG + TS]))
    for cb in range(8):
        cs = slice(cb * 128, (cb + 1) * 128)
        self.dma("pool", wga, wv_in[:, :, OFF_GA + cb * 128:OFF_GA + (cb + 1) * 128], "ldw", w=["wga"])
        self.dma("pool", wgm, wv_in[:, :, OFF_GM + cb * 128:OFF_GM + (cb + 1) * 128], "ldw", w=["wgm"])
        for ch in chunks:
            n = ch["n"]
            for hp in range(4):
                self.mm(ps[:, 0, 0:n], wpa[:, hp, cs], ch["att"](hp), start=(hp == 0), stop=(hp == 3), r=["wpa"], w=[("P", 0)])
            for fb in range(8):
                self.mm(ps[:, 1, 0:n], wpm[:, fb, cs], ch["mout"](fb), start=(fb == 0), stop=(fb == 7), r=["wpm"], w=[("P", 1)])
            for c in range(8):
                self.mm(ps[:, 2, 0:n], wga[:, c, :], ch["hs"](c), start=(c == 0), stop=(c == 7), r=["wga"], w=[("P", 2)])
            for c in range(8):
                self.mm(ps[:, 3, 0:n], wgm[:, c, :], ch["hs"](c), start=(c == 0), stop=(c == 7), r=["wgm"], w=[("P", 3)])
            self.act(sga[:, 0:n], ps[:, 2, 0:n], AF.Sigmoid, r=[("P", 2)], w=["sga"])
            self.act(sgm[:, 0:n], ps[:, 3, 0:n], AF.Sigmoid, r=[("P", 3)], w=["sgm"])
            self.v("dve", "tensor_tensor", t1[:, 0:n], ps[:, 0, 0:n], sga[:, 0:n], ALU.mult, r=[("P", 0), "sga"], w=["t1"])
            self.v("dve", "tensor_tensor", t2[:, 0:n], ps[:, 1, 0:n], sgm[:, 0:n], ALU.mult, r=[("P", 1), "sgm"], w=["t2"])
            self.v("pool", "tensor_tensor", merged[:, cb, ch["m0"]:ch["m0"] + n], t1[:, 0:n], t2[:, 0:n], ALU.add,
                   r=["t1", "t2"], w=[("merged", cb)])
    self.dbg("merged", merged, [128, 8, SEG + TS], BF16)
    xh = self.din["xh"]
    xs = self.din["xs"]
    y_p = self.outp("y_p", [SEG, 1024])
    y_s = self.outp("y_s", [TS, 1024])
    tiles = [(xh[HALO + j * 128:HALO + (j + 1) * 128, :], y_p[j * 128:(j + 1) * 128, :], 128, j * 128, C["gate_p"]) for j in range(NLT)]
    tiles.append((xs, y_s, TS, SEG, C["gate_s"]))
    for i, (xsrc, ydst, n, m0, gate) in enumerate(tiles):
        i2 = i % 2
        self.dma("sp", xt[i2][0:n], xsrc, "ldx", w=[("xt", 0)])
        for half in range(2):
            hs_ = slice(half * 512, (half + 1) * 512)
            for cb in range(8):
                self.mm(ps[0:n, 4 + half, :], merged[:, cb, m0:m0 + n], wout[:, cb, hs_], start=(cb == 0), stop=(cb == 7),
                        r=["wout", ("merged", cb)], w=[("P", 4 + half)])
            self.v("dve", "tensor_tensor", yt[i2][0:n, hs_], ps[0:n, 4 + half, :], gate[0:n, hs_], ALU.mult,
                   r=[("P", 4 + half), ("gate_p", half), ("gate_s", half)], w=[("yt", i2, half)])
            self.v("pool", "tensor_tensor", yt[i2][0:n, hs_], yt[i2][0:n, hs_], xt[i2][0:n, hs_], ALU.add,
                   r=[("yt", i2, half), ("xt", 0)], w=[("yt", i2, half)])
        sl = ss[:, i % 4:i % 4 + 1]
        sk = ("ss", i % 4)
        self.act(xt[0][0:n], yt[i2][0:n], AF.Square, r=[("yt", i2, 0), ("yt", i2, 1)], w=[("xt", 0), sk], accum_out=sl[0:n])
        self.v("dve", "tensor_scalar", sl[0:n], sl[0:n], 1.0 / D, EPS, op0=ALU.mult, op1=ALU.add, r=[sk], w=[sk])
        self.act(sl[0:n], sl[0:n], AF.Ln, r=[sk], w=[sk])
        self.act(sl[0:n], sl[0:n], AF.Exp, r=[sk], w=[sk], scale=-0.5)
        self.act(yt[i2][0:n], yt[i2][0:n], AF.Copy, r=[("yt", i2, 0), ("yt", i2, 1), sk], w=[("yt", i2, 0), ("yt", i2, 1)], scale=sl[0:n])
        self.v("pool", "tensor_tensor", yt[i2][0:n], yt[i2][0:n], fgain[0:n], ALU.mult,
               r=[("yt", i2, 0), ("yt", i2, 1), "fgain"], w=[("yt", i2, 0), ("yt", i2, 1)])
        self.dma("sp", ydst, yt[i2][0:n], "out", r=[("yt", i2, 0), ("yt", i2, 1)])
    A.pop()


Builder.phase_out = _phase_out


def _sample_copies(self):
    LB = [w for (w, d) in GROUPS]
    self.s_cache = cache = [self.inp("cache%d" % g, [4, LB[g], 2, 512]) for g in range(3)]
    self.s_kvs = kvs = [self.outp("kvs%d" % g, [4, LB[g], 2, 512]) for g in range(3)]
    self.s_cat = cat = [self.scratch("cat%d" % g, [4, LB[g] + 8, 2, 512]) for g in range(2)]
    for g in range(3):
        for j in range(4):
            nsp = 4 if g == 2 else 1
            rows = LB[g] - 8
            step = (rows + nsp - 1) // nsp
            for a in range(0, rows, step):
                b_ = min(rows, a + step)
                self.dma("sp", kvs[g][j, a:b_], cache[g][j, 8 + a:8 + b_], "cpy", w=[("kvs", g, j, a)])
            if g < 2:
                self.dma("sp", cat[g][j, 0:LB[g]], cache[g][j], "cpy", w=[("catb", g, j)])


def _phase_sample_attn(self):
    A, P, ps, C, hT = self.A, self.P, self.ps, self.C, self.hT
    wv_in = self.w_in.rearrange("(c p) n -> p c n", p=128)
    LB = [w for (w, d) in GROUPS]
    cache, kvs, cat = self.s_cache, self.s_kvs, self.s_cat
    A.push()
    ident, F_sb = C["ident"], C["F"]
    ones = A.alloc([128])
    self.v("pool", "memset", ones, 1.0, w=["s_ones"])
    z8 = A.alloc([8])
    self.v("pool", "memset", z8, 0.0, w=["z8"])
    onesel = A.alloc([8, 8])
    self.dma("sp", onesel, self.inp("onesel", [128, 8, 8]), "ld0", w=["onesel"])
    selt = A.alloc([8, 128])
    self.dma("sp", selt[0:8], self.inp("selt", [8, 8, 128]), "ld0", w=["selt"])
    biasS = A.alloc([3, 8])
    bold = A.alloc([3, 8])
    tmpF = A.alloc([8])
    dgF = A.alloc([8])
    for g in range(3):
        self.mm(ps[:, 0, 0:8], F_sb[0:8, g, 0:128], ident[0:8, 0:8], r=[("F", g), "ident"], w=[("P", 0)])
        self.act(tmpF, ps[:, 0, 0:8], AF.Copy, r=[("P", 0)], w=["tmpF"])
        self.mm(ps[:, 0, 8:16], C["antiid"], tmpF, r=["tmpF", "antiid"], w=[("P", 0)])
        self.act(biasS[:, g, :], ps[:, 0, 8:16], AF.Copy, r=[("P", 0)], w=[("biasS", g)])
        self.v("dve", "tensor_tensor", dgF[0:8], ident[0:8, 0:8], F_sb[0:8, g, 128:129].to_broadcast([8, 8]), ALU.mult,
               r=[("F", g), "ident"], w=["dgF"])
        self.mm(ps[0:8, 0, 16:24], ones[0:8, 0:8], dgF[0:8], r=["dgF", "s_ones"], w=[("P", 0)])
        self.act(bold[0:8, g, :], ps[0:8, 0, 16:24], AF.Copy, r=[("P", 0)], w=[("bold", g)])
    wq = [A.alloc([8, 512], BF16) for _ in range(2)]
    qj = [A.alloc([1536]) for _ in range(4)]
    kj = A.alloc([1536])
    vj = A.alloc([1536])
    oldkv = [A.alloc([3, 2, 512]) for _ in range(1)][0]
    wcnt = 0
    for kind, off in (("q", OFF_Q), ("k", OFF_K), ("v", OFF_V)):
        for g in range(3):
            wb = wq[wcnt % 2]
            wk_ = ("swq", wcnt % 2)
            wcnt += 1
            self.dma("pool", wb, wv_in[:, :, off + g * 512:off + (g + 1) * 512], "ldw", w=[wk_])
            for j in range(4):
                bank = 1 + (j % 2)
                for c in range(8):
                    self.mm(ps[0:8, bank, :], hT[:, c, HALO + SEG + 8 * j:HALO + SEG + 8 * j + 8], wb[:, c, :],
                            start=(c == 0), stop=(c == 7), r=[wk_, (("hT", 32), c)], w=[("P", bank)])
                if kind == "q":
                    self.act(qj[j][0:8, g * 512:(g + 1) * 512], ps[0:8, bank, :], AF.Copy, r=[("P", bank)], w=[("qj", j, g)], scale=0.125)
                else:
                    st = kj if kind == "k" else vj
                    sk = ("kvst", kind, j % 3)
                    sl_ = st[0:8, (j % 3) * 512:(j % 3 + 1) * 512]
                    self.act(sl_, ps[0:8, bank, :], AF.Copy, r=[("P", bank)], w=[sk])
                    kvi = 0 if kind == "k" else 1
                    self.dma("sp", kvs[g][j, LB[g] - 8:LB[g], kvi, :], sl_, "out", r=[sk], w=[("kvsn", g, j, kvi)])
                    if g < 2:
                        self.dma("sp", cat[g][j, LB[g]:LB[g] + 8, kvi, :], sl_, "out", r=[sk], w=[("catn", g, j, kvi)])
    szs = A.alloc([4, TS])
    wz = wq[0]
    self.dma("pool", wz, wv_in[:, :, OFF_ZA:OFF_ZA + 512], "ldw", w=[("swq", 0)])
    for hp in range(4):
        for c in range(8):
            self.mm(ps[:, 3, 0:TS], wz[:, c, hp * 128:(hp + 1) * 128], hT[:, c, HALO + SEG:HALO + SEG + TS], start=(c == 0), stop=(c == 7),
                    r=[("swq", 0), (("hT", 32), c)], w=[("P", 3)])
        self.act(szs[:, hp, :], ps[:, 3, 0:TS], AF.Silu, r=[("P", 3)], w=[("szs", hp)])
    NKG = 6
    Kg = [A.alloc([2, 512]) for _ in range(NKG)]
    prod = A.alloc([512])
    lg = [A.alloc([8]) for _ in range(4)]
    Pz = [A.alloc([8, 8]) for _ in range(NKG)]
    numS = A.alloc([512])
    denS = A.alloc([8])
    pold = A.alloc([8])
    tmpo = A.alloc([512])
    oj = A.alloc([512])
    u = 0
    for j in range(4):
        for g in range(3):
            self.dma("sp", oldkv[0:8, g], cache[g][j, 0:8], "gat", w=[("old", g)])
        self.mm(ps[0:8, 5, :], z8[0:8, 0:8], qj[j][0:8, 0:512], start=True, stop=False, r=["z8", ("qj", j, 0)], w=[("P", 5)])
        self.mm(ps[0:8, 6, 0:8], z8[0:8, 0:8], qj[j][0:8, 0:8], start=True, stop=False, r=["z8", ("qj", j, 0)], w=[("P", 6)])
        first = False
        for g, (win, d) in enumerate(GROUPS):
            for t in range(8):
                kb = Kg[u % NKG]
                kk = ("Kg", u % NKG)
                pz = Pz[u % NKG]
                pk = ("Pz", u % NKG)
                lgt = lg[u % 4]
                lk = ("lg", u % 4)
                u += 1
                a0 = d + t
                if g < 2:
                    src = cat[g][j, a0:a0 + 127 * d + 1:d]
                    deps = [("catb", g, j), ("catn", g, j, 0), ("catn", g, j, 1)]
                else:
                    src = kvs[2][j, a0 - 8:a0 - 8 + 127 * d + 1:d]
                    deps = [("kvs", 2, j, a) for a in range(0, LB[2] - 8, (LB[2] - 8 + 3) // 4)] + [("kvsn", 2, j, 0), ("kvsn", 2, j, 1)]
                self.dma("sp", kb, src, "gat", r=deps, w=[kk])
                self.mm(ps[:, 4, :], selt[0:8, t, :], qj[j][0:8, g * 512:(g + 1) * 512], r=["selt", ("qj", j, g)], w=[("P", 4)])
                self.v("dve", "tensor_tensor", prod, kb[:, 0, :], ps[:, 4, :], ALU.mult, r=[kk, ("P", 4)], w=["prod"])
                self.v("dve", "tensor_reduce", lgt, prod.rearrange("p (h e) -> p h e", h=8), AX.X, ALU.add, r=["prod"], w=[lk])
                self.v("dve", "tensor_tensor", lgt, lgt, biasS[:, g, :], ALU.add, r=[lk, ("biasS", g)], w=[lk])
                pe_ = pz[:, 0, :]
                self.act(pe_, lgt, AF.Exp, r=[lk], w=[pk])
                last = (g == 2 and t == 7)
                kv3 = kb[:, 1, :].rearrange("p (h e) -> p h e", h=8)
                self.v("dve", "tensor_tensor", kv3, kv3, pe_.unsqueeze(2).to_broadcast([128, 8, 64]), ALU.mult, r=[pk, kk], w=[kk])
                self.mm(ps[0:8, 5, :], onesel[:, t, :], kb[:, 1, :], start=False, stop=last, r=["onesel", kk], w=[("P", 5)])
                self.mm(ps[0:8, 6, 0:8], onesel[:, t, :], pe_, start=False, stop=last, r=["onesel", pk], w=[("P", 6)])
        self.v("dve", "tensor_copy", numS[0:8], ps[0:8, 5, :], r=[("P", 5)], w=["numS"])
        self.v("dve", "tensor_copy", denS[0:8], ps[0:8, 6, 0:8], r=[("P", 6)], w=["denS"])
        for g in range(3):
            self.v("dve", "tensor_tensor", tmpo[0:8], qj[j][0:8, g * 512:(g + 1) * 512], oldkv[0:8, g, 0, :], ALU.mult,
                   r=[("qj", j, g), ("old", g)], w=["tmpo"])
            self.v("dve", "tensor_reduce", pold[0:8], tmpo[0:8].rearrange("p (h e) -> p h e", h=8), AX.X, ALU.add, r=["tmpo"], w=["pold"])
            self.v("dve", "tensor_tensor", pold[0:8], pold[0:8], bold[0:8, g, :], ALU.add, r=["pold", ("bold", g)], w=["pold"])
            self.act(pold[0:8], pold[0:8], AF.Exp, r=["pold"], w=["pold"])
            self.v("dve", "tensor_tensor", denS[0:8], denS[0:8], pold[0:8], ALU.add, r=["denS", "pold"], w=["denS"])
            self.v("dve", "tensor_tensor", tmpo[0:8].rearrange("p (h e) -> p h e", h=8),
                   oldkv[0:8, g, 1, :].rearrange("p (h e) -> p h e", h=8), pold[0:8].unsqueeze(2).to_broadcast([8, 8, 64]), ALU.mult,
                   r=[("old", g), "pold"], w=["tmpo"])
            self.v("dve", "tensor_tensor", numS[0:8], numS[0:8], tmpo[0:8], ALU.add, r=["numS", "tmpo"], w=["numS"])
        self.v("dve", "reciprocal", denS[0:8], denS[0:8], r=["denS"], w=["denS"])
        self.v("dve", "tensor_tensor", oj[0:8].rearrange("p (h e) -> p h e", h=8), numS[0:8].rearrange("p (h e) -> p h e", h=8),
               denS[0:8].unsqueeze(2).to_broadcast([8, 8, 64]), ALU.mult, r=["numS", "denS"], w=["oj"])
        for hp in range(4):
            self.tr(ps[:, 7, 0:8], oj[0:8, hp * 128:(hp + 1) * 128], ident[0:8, 0:8], r=["oj", "ident"], w=[("P", 7)])
            self.v("dve", "tensor_tensor", self.attT[:, hp, SEG + 8 * j:SEG + 8 * j + 8], ps[:, 7, 0:8], szs[:, hp, 8 * j:8 * j + 8], ALU.mult,
                   r=[("P", 7), ("szs", hp)], w=[("attTs", hp, j)])
    A.pop()


Builder.phase_sample_attn = _phase_sample_attn
Builder.sample_copies = _sample_copies


_PROG_CACHE = {}


def kernel(**inputs):
    inp = {k: np.asarray(v) for k, v in inputs.items()}
    if "prog" not in _PROG_CACHE:
        _PROG_CACHE["prog"] = build_program(99)
    b = _PROG_CACHE["prog"]
    cores = list(range(NCORES))
    maps = make_in_maps(inp, cores)
    maps = [{k: np.ascontiguousarray(v, dtype=np.float32) for k, v in m.items() if k in b.din} for m in maps]
    res = run_bass_kernel_spmd(b.nc, maps, core_ids=cores).results
    f32 = np.float32
    y_p = np.zeros((2, SEQ, D), f32)
    y_s = np.zeros((32, 8, D), f32)
    kvp = [np.zeros((1, 2, w, 2, 8, 64), f32) for (w, d) in GROUPS]
    kvs = [np.zeros((1, 32, w, 2, 8, 64), f32) for (w, d) in GROUPS]
    conv_p = np.zeros((1, 2, 3, D), f32)
    conv_s = np.zeros((1, 32, 3, D), f32)
    C_p = np.zeros((1, 2, 4, 256, 128), f32)
    C_s = np.zeros((1, 32, 4, 256, 128), f32)
    n_p = np.zeros((1, 2, 4, 128), f32)
    n_s = np.zeros((1, 32, 4, 128), f32)
    m_p = np.zeros((1, 2, 4), f32)
    m_s = np.zeros((1, 32, 4), f32)
    for c in cores:
        r = res[c]
        bb, p = c // 4, c % 4
        sq = slice(4 * c, 4 * c + 4)
        y_p[bb, p * SEG:(p + 1) * SEG] = r["y_p"]
        y_s[sq] = np.asarray(r["y_s"]).reshape(4, 8, D)
        for g in range(3):
            kvs[g][0, sq] = np.asarray(r["kvs%d" % g]).reshape(4, -1, 2, 8, 64)
        Cs = np.asarray(r["Cs"])
        C_s[0, sq] = Cs[..., :256].transpose(0, 2, 3, 1)
        n_s[0, sq] = Cs[..., 256].transpose(0, 2, 1)
        m_s[0, sq] = np.asarray(r["ms"])[:, :, 0]
        conv_s[0, sq] = np.asarray(r["convs"]).transpose(0, 3, 2, 1).reshape(4, 3, D)
        if p == 3:
            for g in range(3):
                kvp[g][0, bb] = np.asarray(r["kvp%d" % g]).reshape(-1, 2, 8, 64)
            Cp = np.asarray(r["Cp"])
            C_p[0, bb] = Cp[..., :256].transpose(1, 2, 0)
            n_p[0, bb] = Cp[..., 256].T
            m_p[0, bb] = np.asarray(r["mp"])[:, 0]
            conv_p[0, bb] = np.asarray(r["convp"]).transpose(2, 1, 0).reshape(3, D)
    return (y_p, y_s, kvp[0], kvs[0], kvp[1], kvs[1], kvp[2], kvs[2],
            conv_p, conv_s, C_p, C_s, n_p, n_s, m_p, m_s)


def _mlstm_prefix(self):
    A, P, ps, C, M, hT = self.A, self.P, self.ps, self.C, self.M, self.hT
    NT = 48
    xf = self.inp("xf", [PREFIX - HALO, 1024])
    ident, tri = C["ident"], C["tri"]
    hTp = [A.alloc([8, 128], BF16) for _ in range(2)]
    gt_all = A.alloc([NT, 8])
    lf = A.alloc([NT, 4])
    ie = A.alloc([NT, 4])
    tot = A.alloc([NT, 4])
    incl = A.alloc([NT, 4])
    a_all = A.alloc([NT, 4])
    wk_all = A.alloc([NT, 4])
    negtv = A.alloc([NT])
    pm = A.alloc([4])
    row1 = A.alloc([1])
    dg = A.alloc([4])
    Mg_bc = A.alloc([4])
    cb4 = A.alloc([4, 8, 131])
    hT4 = A.alloc([8, 512], BF16)
    caccs = [A.alloc([8, 128]) for _ in range(2)]
    cexp = A.alloc([8, 128])
    c_act = [A.alloc([8, 128], BF16) for _ in range(2)]
    vaug = [A.alloc([4, 257], BF16) for _ in range(2)]
    kmw = [A.alloc([4, 128], BF16) for _ in range(2)]
    for i in range(2):
        self.v("pool", "memset", vaug[i][:, :, 256:257], 1.0, w=[("pvaug1", i)])

    def tile_src(i, slot):
        if i < 32:
            hp_ = hTp[i % 2]
            self.norm_tile(xf[i * 128:(i + 1) * 128, :], 128, hp_, "p", slot, ("hTp", i % 2), bank0=5)
            return hp_, [("hTp", i % 2)]
        ti = i - 32
        return hT[:, :, ti * 128:(ti + 1) * 128], [("hT", ti)]

    for i in range(NT):
        hTt, hk = tile_src(i, i)
        bank = 7 if i % 2 == 0 else 4
        gps = ps[:, bank, 0:8]
        for c in range(8):
            self.mm(gps, hTt[:, c, :], M["wg"][:, c, :], start=(c == 0), stop=(c == 7),
                    r=["wg"] + [(k, c) for k in hk], w=[("P", bank)])
        self.v("dve", "tensor_tensor", gt_all[:, i, :], gps, C["b_if_bc"], ALU.add, r=[("P", bank), "b_if_bc"], w=["gt_all"])
    tv48 = C["tvalid"][:, 0:NT]
    self.v("dve", "tensor_scalar", negtv, tv48, -1.0, None, op0=ALU.mult, r=["tvalid"], w=["negtv"])
    self.act(lf, gt_all[:, :, 4:8], AF.Exp, r=["gt_all"], w=["lf"], scale=-1.0)
    self.act(lf, lf, AF.Ln, r=["lf"], w=["lf"], bias=M["one1"])
    self.v("dve", "tensor_tensor", lf, lf, negtv.unsqueeze(2).to_broadcast([128, NT, 4]), ALU.mult, r=["lf", "negtv"], w=["lf"])
    self.v("dve", "tensor_tensor", ie, gt_all[:, :, 0:4], tv48.unsqueeze(2).to_broadcast([128, NT, 4]), ALU.mult,
           r=["gt_all", "tvalid"], w=["ie"])
    self.v("dve", "tensor_tensor", ie, ie, M["negbig"][:, 0:NT].unsqueeze(2).to_broadcast([128, NT, 4]), ALU.add,
           r=["ie", "negbig"], w=["ie"])
    lf2 = lf.rearrange("p t h -> p (t h)")
    self.mm(ps[:, 0, 0:NT * 4], tri, lf2, r=["lf", "tri"], w=[("P", 0)])
    self.mm(ps[:, 1, 0:NT * 4], M["ones"], lf2, r=["lf", "ones"], w=[("P", 1)])
    self.v("dve", "tensor_copy", tot.rearrange("p t h -> p (t h)"), ps[:, 1, 0:NT * 4], r=[("P", 1)], w=["tot"])
    for h in range(4):
        self.v("dve", "tensor_tensor_scan", incl[:, :, h], M["ones"][:, 0:NT], tot[:, :, h], 0.0, op0=ALU.mult, op1=ALU.add,
               r=["tot", "ones"], w=[("incl", h)])
    inclk = [("incl", h) for h in range(4)]
    self.v("dve", "tensor_tensor", a_all, ie, incl, ALU.subtract, r=["ie"] + inclk, w=["a_all"])
    self.v("dve", "tensor_tensor", a_all, a_all, tot, ALU.add, r=["a_all", "tot"], w=["a_all"])
    self.v("dve", "tensor_tensor", a_all.rearrange("p t h -> p (t h)"), a_all.rearrange("p t h -> p (t h)"), ps[:, 0, 0:NT * 4],
           ALU.subtract, r=["a_all", ("P", 0)], w=["a_all"])
    self.v("dve", "tensor_reduce", pm, a_all.rearrange("p t h -> p h t"), AX.X, ALU.max, r=["a_all"], w=["pm"])
    self.mm(ps[0:4, 2, 0:128], pm, ident, r=["pm", "ident"], w=[("P", 2)])
    self.v("dve", "tensor_reduce", row1[0:4], ps[0:4, 2, 0:128], AX.X, ALU.max, r=[("P", 2)], w=["row1"])
    self.v("dve", "tensor_scalar", row1[0:4], row1[0:4], 0.0, None, op0=ALU.max, r=["row1"], w=["row1"])
    self.v("dve", "tensor_tensor", dg[0:4], ident[0:4, 0:4], row1[0:4, 0:1].to_broadcast([4, 4]), ALU.mult, r=["row1", "ident"], w=["pdg"])
    self.mm(ps[:, 2, 128:132], M["ones"][0:4, :], dg[0:4], r=["pdg", "ones"], w=[("P", 2)])
    self.v("dve", "tensor_copy", Mg_bc, ps[:, 2, 128:132], r=[("P", 2)], w=["Mg_bc"])
    self.v("dve", "tensor_tensor", wk_all, a_all, Mg_bc.unsqueeze(1).to_broadcast([128, NT, 4]), ALU.subtract,
           r=["a_all", "Mg_bc"], w=["wk_all"])
    self.act(wk_all, wk_all, AF.Exp, r=["wk_all"], w=["wk_all"])
    self.v("dve", "tensor_tensor", M["m_bc"], incl[:, NT - 1, :], Mg_bc, ALU.add, r=inclk + ["Mg_bc"], w=["m_bc"])
    self.mm(ps[0:4, 2, 136:137], M["m_bc"][0:1, 0:4], M["ones"][0:1, 0:1], r=["m_bc", "ones"], w=[("P", 2)])
    self.v("dve", "tensor_copy", M["m_row"][0:4], ps[0:4, 2, 136:137], r=[("P", 2)], w=["m_row"])
    cw, cbias = C["conv_wT"], C["conv_bT"]
    hist = A.alloc([8, 3])
    self.v("pool", "memset", hist, 0.0, w=["hist"])
    for g0 in range(0, NT, 4):
        if g0 < 32:
            for q in range(4):
                i = g0 + q
                self.norm_tile(xf[i * 128:(i + 1) * 128, :], 128, hT4[:, :, q * 128:(q + 1) * 128], "p", NT + i, ("hT4", q), bank0=5)
            src4 = hT4
            hk4 = [("hT4", q) for q in range(4)]
            tsl = lambda q: hT4[:, :, q * 128:(q + 1) * 128]
        else:
            t0 = g0 - 32
            src4 = hT[:, :, t0 * 128:(t0 + 4) * 128]
            hk4 = [("hT", t0 + q) for q in range(4)]
            tsl = lambda q, t0=t0: hT[:, :, (t0 + q) * 128:(t0 + q + 1) * 128]
        for fb in range(8):
            bank = fb % 2
            for c in range(8):
                self.mm(ps[:, bank, :], M["wxm"][:, c, fb * 128:(fb + 1) * 128], src4[:, c, :], start=(c == 0), stop=(c == 7),
                        r=["wxm"] + [(k, c) for k in hk4], w=[("P", bank)])
            self.act(cb4[:, :, fb, 3:131], ps[:, bank, :].rearrange("p (q t) -> p q t", q=4), AF.Copy,
                     r=[("P", bank)], w=[("pcb", q) for q in range(4)])
        def tile_body(q):
            i = g0 + q
            par = i % 2
            cacc = caccs[par]
            vb0 = 2 if par == 0 else 0
            kmb = 4 if par == 0 else 7
            hTt, hk = tsl(q), [hk4[q]]
            tv = C["tvalid"][:, i:i + 1]
            cbp = cb4[:, q]
            for half in range(2):
                for c in range(8):
                    self.mm(ps[:, vb0 + half, :], hTt[:, c, :], M["wxm"][:, c, half * 512:(half + 1) * 512], start=(c == 0), stop=(c == 7),
                            r=["wxm"] + [(k, c) for k in hk], w=[("P", vb0 + half)])
                self.act(vaug[par][:, 2 * half:2 * half + 2, 0:256], ps[:, vb0 + half, :].rearrange("p (h v) -> p h v", h=2), AF.Copy,
                         r=[("P", vb0 + half), ("pvaug1", par)], w=[("pvaug", par)])
                yield
            for fb in range(8):
                self.act(cacc[:, fb, :], cbp[:, fb, 3:131], AF.Identity, r=[("pcb", q), "conv_wT", "conv_bT"], w=[("pcacc", par, fb)],
                         scale=cw[:, fb, 3:4], bias=cbias[:, fb:fb + 1])
            yield
            for j in range(3):
                for fb in range(8):
                    self.v("dve", "scalar_tensor_tensor", cacc[:, fb, :], cbp[:, fb, j:j + 128], cw[:, fb, j:j + 1], cacc[:, fb, :],
                           op0=ALU.mult, op1=ALU.add, r=[("pcb", q), ("pcacc", par, fb)], w=[("pcacc", par, fb)])
                yield
            for fb in range(8):
                self.act(c_act[par][:, fb, :], cacc[:, fb, :], AF.Silu, r=[("pcacc", par, fb)], w=[("pc_act", par, fb)])
            yield
            kmps = ps[:, kmb, :].rearrange("p (h k) -> p h k", h=4)
            for h in range(4):
                for ec in range(2):
                    self.mm(kmps[:, h, :], c_act[par][:, 2 * h + ec, :], M["wmk"][:, ec, h, :], start=(ec == 0), stop=(ec == 1),
                            r=[("pc_act", par, 2 * h + ec), "wmk"], w=[("P", kmb)])
            yield
            self.v("dve", "tensor_tensor", kmw[par], kmps, wk_all[:, i, :].unsqueeze(2).to_broadcast([128, 4, 128]), ALU.mult,
                   r=[("P", kmb), "wk_all"], w=[("pkmw", par)])
            yield
            for h in range(4):
                bk = vb0 + (h % 2)
                dps = ps[:, bk, 0:257]
                self.mm(dps, kmw[par][:, h, :], vaug[par][:, h, :], r=[("pkmw", par), ("pvaug", par)], w=[("P", bk)])
                self.v("dve", "tensor_tensor", M["CT"][:, h, :], M["CT"][:, h, :], dps, ALU.add, r=[("P", bk), "CT"], w=["CT"])
                yield

        for q in range(4):
            i = g0 + q
            tv = C["tvalid"][:, i:i + 1]
            cbp = cb4[:, q]
            self.v("pool", "tensor_copy", cbp[:, :, 0:3], hist, r=["hist"], w=[("pcb", q)])
            self.v("dve", "tensor_scalar", hist, cbp[:, :, 128:131], tv, M["one1"], op0=ALU.mult, op1=ALU.mult,
                   r=[("pcb", q), "tvalid", "one1"], w=["hist"])
        for pair in ((0, 1), (2, 3)):
            live = [tile_body(q) for q in pair]
            while live:
                for gen in list(live):
                    try:
                        next(gen)
                    except StopIteration:
                        live.remove(gen)
    self.v("pool", "tensor_copy", M["convbuf"][:, :, 0:3], hist, r=["hist"], w=[("cb", 0), ("cb", 1)])


Builder.mlstm_prefix = _mlstm_prefix
```

```python
import numpy as np
import concourse.bass as bass
import concourse.mybir as mybir
from concourse.bass_utils import run_bass_kernel_spmd

F32 = mybir.dt.float32
BF16 = mybir.dt.bfloat16
AF = mybir.ActivationFunctionType
ALU = mybir.AluOpType
AX = mybir.AxisListType

NCORES = 8
D = 1024
SEQ = 8192
SEG = 2048
NLT = SEG // 128
HALO = 2048
PREFIX = 6144
TS = 32
PW = 10248
OFF_Q, OFF_K, OFF_V, OFF_ZA, OFF_XM, OFF_ZM, OFF_OM, OFF_I, OFF_F, OFF_GA, OFF_GM = (
    0, 1536, 3072, 4608, 5120, 6144, 7168, 8192, 8196, 8200, 9224)
GROUPS = ((128, 1), (512, 4), (2048, 16))
EPS = 1e-6
NEG = -30000.0
RAW_GAP = 1

DEBUG = {}


class _Op:
    __slots__ = ("eng", "fn", "deps", "stream", "signal", "val", "idx", "pos", "slot")


class Prog:
    ENGS = ("pe", "act", "dve", "pool", "sp")

    def __init__(self, nc):
        self.nc = nc
        self.ops = {e: [] for e in self.ENGS}
        self.lastw = {}
        self.readers = {}
        self.streams = {}
        self.barrier_deps = []
        self.nops = 0

    def add(self, eng, fn, r=(), w=(), dma=None):
        op = _Op()
        op.eng = eng
        op.fn = fn
        op.stream = dma
        op.signal = False
        op.val = None
        op.idx = self.nops
        self.nops += 1
        deps = {}
        for k in r:
            d = self.lastw.get(k)
            if d is not None:
                deps[d.idx] = (d, True)
            if isinstance(k, tuple) and k[0] == "P":
                for d in self.readers.get(k, ()):
                    if d.eng != eng and d.idx not in deps:
                        deps[d.idx] = (d, False)
        for k in w:
            d = self.lastw.get(k)
            if d is not None and d.idx not in deps:
                deps[d.idx] = (d, False)
            for d in self.readers.get(k, ()):
                if d.idx not in deps:
                    deps[d.idx] = (d, False)
        for d in self.barrier_deps:
            if d.idx not in deps:
                deps[d.idx] = (d, True)
        op.deps = list(deps.values())
        for k in w:
            self.lastw[k] = op
            self.readers[k] = []
        for k in r:
            self.readers.setdefault(k, []).append(op)
        op.pos = len(self.ops[eng])
        self.ops[eng].append(op)
        if dma is not None:
            self.streams.setdefault(dma, []).append(op)
        return op

    KSEM = 8
    KSEM_STREAM = {"cpy": 32}
    PERSIST = ("cpy",)

    def kof(self, s):
        return self.KSEM_STREAM.get(s, self.KSEM)

    def barrier(self):
        deps = []
        for e in self.ENGS:
            if self.ops[e]:
                for op in reversed(self.ops[e]):
                    if op.stream is None:
                        deps.append(op)
                        break
        for s, lst in self.streams.items():
            if s in self.PERSIST:
                continue
            deps.extend(lst[-self.kof(s):])
        self.barrier_deps = deps
        self.lastw = {k: v for k, v in self.lastw.items() if v.stream in self.PERSIST}
        self.readers = {}

    def emit(self, final_streams):
        nc = self.nc
        K = self.KSEM
        need = {}
        for e in self.ENGS:
            for op in self.ops[e]:
                lst = []
                for d, raw in op.deps:
                    if d.stream is None and d.eng == e:
                        if e in ("pe", "sp"):
                            continue
                        if not raw:
                            continue
                        if op.pos - d.pos > RAW_GAP:
                            continue
                    d.signal = True
                    lst.append(d)
                need[op.idx] = lst
        for e in self.ENGS:
            cnt = 0
            for op in self.ops[e]:
                if op.stream is None and op.signal:
                    cnt += 1
                    op.val = cnt
        for s, lst in self.streams.items():
            Ks = self.kof(s)
            for i, op in enumerate(lst):
                op.slot = i % Ks
                op.val = 16 * (i // Ks + 1)
        import contextlib
        with contextlib.ExitStack() as st:
            sems = {e: st.enter_context(nc.semaphore("s_" + e)) for e in self.ENGS}
            ssems = {s: [st.enter_context(nc.semaphore("d_%s_%d" % (s, k))) for k in range(min(self.kof(s), len(lst)))]
                     for s, lst in self.streams.items()}
            block = st.enter_context(nc.Block())

            def run(e, eng):
                waited = {}

                def wait(key, v):
                    if v > waited.get(key, 0):
                        waited[key] = v
                        sem = ssems[key[1]][key[2]] if key[0] == "s" else sems[key[1]]
                        eng.wait_ge(sem, v)

                for op in self.ops[e]:
                    w = {}
                    for d in need[op.idx]:
                        key = ("s", d.stream, d.slot) if d.stream is not None else ("e", d.eng)
                        if d.val > w.get(key, 0):
                            w[key] = d.val
                    for key, v in w.items():
                        wait(key, v)
                    if op.stream is not None and op.val > 16:
                        wait(("s", op.stream, op.slot), op.val - 16)
                    ins = op.fn(eng)
                    if op.stream is not None:
                        ins.then_inc(ssems[op.stream][op.slot], 16)
                    elif op.signal:
                        ins.then_inc(sems[e], 1)
                if e == "sp":
                    for s in final_streams:
                        if s in self.streams:
                            for op in self.streams[s][-self.kof(s):]:
                                wait(("s", s, op.slot), op.val)

            block.tensor(lambda t: run("pe", t))
            block.scalar(lambda t: run("act", t))
            block.vector(lambda t: run("dve", t))
            block.gpsimd(lambda t: run("pool", t))
            block.sync(lambda t: run("sp", t))


class Arena:
    def __init__(self, nc, words):
        self.t = nc.alloc_sbuf_tensor("arena", [128, words], F32)
        self.words = words
        self.top = 0
        self.marks = []

    def push(self):
        self.marks.append(self.top)

    def pop(self):
        self.top = self.marks.pop()

    def alloc(self, shape, dt=F32):
        n = int(np.prod(shape))
        words = (n + 1) // 2 if dt == BF16 else n
        words = (words + 7) // 8 * 8
        assert self.top + words <= self.words, ("SBUF arena overflow", self.top, words, self.words)
        ap = self.t[:, self.top:self.top + words]
        self.top += words
        if dt == BF16:
            ap = ap.bitcast(BF16)
        ap = ap[:, 0:n]
        if len(shape) == 2:
            ap = ap.rearrange("p (a b) -> p a b", a=shape[0])
        elif len(shape) == 3:
            ap = ap.rearrange("p (a b c) -> p a b c", a=shape[0], b=shape[1])
        elif len(shape) == 4:
            ap = ap.rearrange("p (a b c d) -> p a b c d", a=shape[0], b=shape[1], c=shape[2])
        return ap


def _t5_bucket(dist):
    n = np.asarray(dist).astype(np.int64)
    max_exact = 16
    nf = np.maximum(n, 1).astype(np.float32)
    large = max_exact + (np.log(nf / max_exact) / np.log(np.float32(2048 / max_exact))
                         * (32 - max_exact)).astype(np.int64)
    large = np.minimum(large, 31)
    return np.where(n < max_exact, n, large).astype(np.int32)


def _consts():
    c = {}
    c["ident"] = np.eye(128, dtype=np.float32)
    c["antiid"] = np.eye(128, dtype=np.float32)[::-1].copy()
    s = np.arange(128)
    c["tri"] = (s[:, None] <= s[None, :]).astype(np.float32)
    c["cmaskT"] = np.where(s[:, None] <= s[None, :], 0.0, NEG).astype(np.float32)
    oh = np.zeros((32, 3, 129), np.float32)
    for g, (_, d) in enumerate(GROUPS):
        b = _t5_bucket(np.arange(129) * d)
        oh[b, g, np.arange(129)] = 1.0
    c["ohd"] = oh
    selp = np.zeros((5, 128), np.float32); selp[0] = 1.0
    sels = np.zeros((5, 128), np.float32)
    for i in range(TS):
        sels[1 + i // 8, i] = 1.0
    c["selp"] = selp
    c["sels"] = sels
    return c


def _fm(v):
    return np.ascontiguousarray(v.reshape(-1, 128).T)


class Builder:
    def __init__(self):
        nc = bass.Bass("TRN2", target_bir_lowering=False)
        self.nc = nc
        self.P = Prog(nc)
        self.A = Arena(nc, 53184)
        self.ps = nc.alloc_psum_tensor("psum", [128, 8, 512], F32)
        self.din = {}
        self.dout = {}
        self.uid = 0

    def inp(self, name, shape, dt=F32):
        t = self.nc.dram_tensor(name, list(shape), dt, kind="ExternalInput").ap()
        self.din[name] = t
        return t

    def outp(self, name, shape, dt=F32):
        t = self.nc.dram_tensor(name, list(shape), dt, kind="ExternalOutput").ap()
        self.dout[name] = t
        return t

    def scratch(self, name, shape, dt=F32):
        return self.nc.dram_tensor(name, list(shape), dt).ap()

    def mm(self, out, lhsT, rhs, start=True, stop=True, r=(), w=()):
        return self.P.add("pe", lambda e: e.matmul(out, lhsT, rhs, start=start, stop=stop), r, w)

    def tr(self, out, in_, ident, r=(), w=()):
        return self.P.add("pe", lambda e: e.transpose(out, in_, ident), r, w)

    def act(self, out, in_, func, r=(), w=(), **kw):
        return self.P.add("act", lambda e: e.activation(out=out, in_=in_, func=func, **kw), r, w)

    def v(self, eng, name, *args, r=(), w=(), **kw):
        return self.P.add(eng, lambda e: getattr(e, name)(*args, **kw), r, w)

    def dma(self, eng, out, in_, stream, r=(), w=(), **kw):
        return self.P.add(eng, lambda e: e.dma_start(out=out, in_=in_, **kw), r, w, dma=stream)

    def dbg(self, name, ap, shape, dt=F32, r=()):
        if not DEBUG.get(name):
            return
        o = self.outp("dbg_" + name, shape, dt)
        self.P.barrier()
        self.dma("sp", o, ap, "dbg", r=r)

    def key(self, base):
        self.uid += 1
        return (base, self.uid)

    def phase0(self):
        A, P, ps = self.A, self.P, self.ps
        C = self.C = {}

        def load(name, shape, parts=128, dt=F32, eng="sp", src=None):
            t = A.alloc(list(shape[1:]), dt)
            src = self.inp(name, shape) if src is None else src
            self.dma(eng, t[0:parts], src, "ld0", w=[name])
            C[name] = t
            return t

        load("ident", [128, 128])
        C["ident_bf"] = A.alloc([128], BF16)
        self.dma("pool", C["ident_bf"], self.din["ident"], "ldc", w=["ident_bf"])
        C["antiid_bf"] = A.alloc([128], BF16)
        self.dma("pool", C["antiid_bf"], self.inp("antiid", [128, 128]), "ldc", w=["antiid_bf"])
        load("tri", [128, 128])
        load("antiid", [128, 128], src=self.din["antiid"])
        C["cmaskT_bf"] = A.alloc([128], BF16)
        self.dma("pool", C["cmaskT_bf"], self.inp("cmaskT", [128, 128]), "ldc", w=["cmaskT_bf"])
        load("ohd", [32, 3, 129], parts=32)
        load("selp", [5, 128], parts=5)
        load("sels", [5, 128], parts=5)
        load("rel_table", [32, 24], parts=32)
        for nm in ("gainT", "conv_bT", "m_normT", "m_skipT"):
            load(nm, [128, 8])
        load("b_adaT", [128, 24])
        load("conv_wT", [128, 8, 4])
        load("b_if_bc", [128, 8])
        load("tvalid", [128, 64])
        load("cT", [128, 8, 5])

        siluT = A.alloc([8, 5], BF16)
        self.act(siluT, C["cT"], AF.Silu, r=["cT"], w=["siluT"])

        ada = A.alloc([24, 5])
        mult = A.alloc([8, 5])
        gate_p = A.alloc([1024])
        gate_s = A.alloc([1024])
        A.push()
        load("b_gate_rows", [5, 1024], parts=5)
        gate_rows = A.alloc([1024])
        w_ada = self.inp("w_ada", [1024, 3072])
        wada = A.alloc([8, 3072], BF16)
        wv = w_ada.rearrange("(c p) n -> p c n", p=128)
        for c in range(8):
            self.dma("pool", wada[:, c, :], wv[:, c, :], "ldw", w=[("wada", c)])
        adaps = ps[:, 0, 0:120].rearrange("p (a b) -> p a b", a=24)
        for cb in range(24):
            for c in range(8):
                self.mm(adaps[:, cb, :], wada[:, c, cb * 128:(cb + 1) * 128], siluT[:, c, :],
                        start=(c == 0), stop=(c == 7), r=[("wada", c), "siluT"], w=[("P", 0)])
        for half in range(2):
            for c in range(8):
                self.mm(ps[0:5, 1 + half, :], siluT[:, c, :], wada[:, c, 2048 + half * 512:2048 + (half + 1) * 512],
                        start=(c == 0), stop=(c == 7), r=[("wada", c), "siluT"], w=[("P", 1 + half)])
        self.v("dve", "tensor_tensor", ada, adaps, C["b_adaT"].unsqueeze(2).to_broadcast([128, 24, 5]), ALU.add,
               r=[("P", 0), "b_adaT"], w=["ada"])
        self.v("dve", "tensor_scalar", mult, ada[:, 8:16, :], 1.0, None, op0=ALU.add, r=["ada"], w=["mult"])
        self.v("dve", "tensor_tensor", mult, mult, C["gainT"].unsqueeze(2).to_broadcast([128, 8, 5]), ALU.mult,
               r=["mult", "gainT"], w=["mult"])
        C["mult"] = mult
        C["shift"] = ada[:, 0:8, :]
        C["ada"] = ada
        for half in range(2):
            self.v("dve", "tensor_tensor", gate_rows[0:5, half * 512:(half + 1) * 512], ps[0:5, 1 + half, :],
                   C["b_gate_rows"][0:5, half * 512:(half + 1) * 512], ALU.add,
                   r=[("P", 1 + half), "b_gate_rows"], w=[("gate_rows", half)])
        for half in range(2):
            sl = slice(half * 512, (half + 1) * 512)
            self.mm(ps[:, 3, :], C["selp"][0:5, :], gate_rows[0:5, sl], r=[("gate_rows", half), "selp"], w=[("P", 3)])
            self.act(gate_p[:, sl], ps[:, 3, :], AF.Copy, r=[("P", 3)], w=[("gate_p", half)])
            self.mm(ps[:, 4, :], C["sels"][0:5, :], gate_rows[0:5, sl], r=[("gate_rows", half), "sels"], w=[("P", 4)])
            self.act(gate_s[:, sl], ps[:, 4, :], AF.Copy, r=[("P", 4)], w=[("gate_s", half)])
        A.pop()
        P.barrier()
        C["gate_p"] = gate_p
        C["gate_s"] = gate_s
        self.dbg("ada", ada, [128, 24, 5], r=["ada"])
        self.dbg("gate_s", gate_s, [128, 1024], r=[("gate_s", 0), ("gate_s", 1)])

    def norm_tile(self, xsrc, ntok, hT_dst, kind, slot, hkey, bank0=6):
        A, P, ps, C = self.A, self.P, self.ps, self.C
        W = self.W1
        i3, i2 = slot % len(W["xt"]), slot % 2
        xt = W["xt"][i3]
        self.dma("sp", xt[0:ntok], xsrc, "ldx", w=[("xt", i3)])
        ss = W["ss"][:, slot % 4:slot % 4 + 1]
        self.act(W["junk"][0:ntok], xt[0:ntok], AF.Square, r=[("xt", i3)], w=["junk", ("ss", slot % 4)],
                 accum_out=ss[0:ntok])
        self.v("dve", "tensor_scalar", ss[0:ntok], ss[0:ntok], 1.0 / D, EPS, op0=ALU.mult, op1=ALU.add,
               r=[("ss", slot % 4)], w=[("ss", slot % 4)])
        self.act(ss[0:ntok], ss[0:ntok], AF.Ln, r=[("ss", slot % 4)], w=[("ss", slot % 4)])
        self.act(ss[0:ntok], ss[0:ntok], AF.Exp, r=[("ss", slot % 4)], w=[("ss", slot % 4)], scale=-0.5)
        xn = W["xn"][i2]
        self.act(xn[0:ntok], xt[0:ntok], AF.Copy, r=[("xt", i3), ("ss", slot % 4)], w=[("xn", i2)], scale=ss[0:ntok])
        if DEBUG.get("stage", 9) < 1:
            return
        bank = bank0 + i2
        pt = ps[:, bank, :].bitcast(BF16).rearrange("p (c t) -> p c t", c=8)
        for c in range(8):
            self.tr(pt[:, c, 0:ntok], xn[0:ntok, c * 128:(c + 1) * 128], C["ident_bf"][0:ntok, 0:ntok],
                    r=[("xn", i2), "ident_bf"], w=[("P", bank0 + i2)])
        if DEBUG.get("stage", 9) < 2:
            return
        for c in range(8):
            if kind == "p":
                if True:
                    self.act(hT_dst[:, c, :], pt[:, c, 0:ntok], AF.Identity, r=[("P", bank0 + i2), "mult", "ada"], w=[(hkey, c)],
                             scale=C["mult"][:, c, 0:1], bias=C["shift"][:, c, 0:1])
                else:
                    self.v("dve", "tensor_scalar", hT_dst[:, c, :], pt[:, c, 0:ntok], C["mult"][:, c, 0:1], C["shift"][:, c, 0:1],
                           op0=ALU.mult, op1=ALU.add, r=[("P", bank0 + i2), "mult", "ada"], w=[(hkey, c)])
            else:
                tmp = W["stmp"]
                tmp0 = W["stmp0"]
                self.act(tmp0, pt[:, c, 0:ntok], AF.Copy, r=[("P", bank0 + i2)], w=["stmp0"])
                self.v("dve", "tensor_tensor", tmp.rearrange("p (s t) -> p s t", s=4),
                       tmp0.rearrange("p (s t) -> p s t", s=4),
                       C["mult"][:, c, 1:5].unsqueeze(2).to_broadcast([128, 4, 8]), ALU.mult,
                       r=["stmp0", "mult"], w=["stmp"])
                self.v("dve", "tensor_tensor", hT_dst[:, c, :].rearrange("p (s t) -> p s t", s=4),
                       tmp.rearrange("p (s t) -> p s t", s=4),
                       C["shift"][:, c, 1:5].unsqueeze(2).to_broadcast([128, 4, 8]), ALU.add,
                       r=["stmp", "ada"], w=[(hkey, c)])

    def phase1(self):
        A = self.A
        self.hT = A.alloc([8, HALO + SEG + TS], BF16)
        A.push()
        self.W1 = {
            "xt": [A.alloc([1024]) for _ in range(3)],
            "ss": A.alloc([4]),
            "junk": A.alloc([1024], BF16),
            "xn": [A.alloc([1024], BF16) for _ in range(2)],
            "stmp": A.alloc([32]),
            "stmp0": A.alloc([32]),
        }
        xh = self.inp("xh", [HALO + SEG, 1024])
        xs = self.inp("xs", [TS, 1024])
        slot = 0
        if not DEBUG.get("skip_s"):
            self.norm_tile(xs, TS, self.hT[:, :, HALO + SEG:HALO + SEG + TS], "s", slot, ("hT", 32))
        slot += 1
        for ti in range(DEBUG.get("ntiles", (HALO + SEG) // 128)):
            self.norm_tile(xh[ti * 128:(ti + 1) * 128, :], 128, self.hT[:, :, ti * 128:(ti + 1) * 128], "p", slot, ("hT", ti))
            slot += 1
        self.dbg("hT", self.hT, [128, 8, HALO + SEG + TS], BF16, r=[])
        self.dbg("xn", self.W1["xn"][1], [128, 1024], BF16, r=[])
        A.pop()


def build_program(upto=99):
    b = Builder()
    b.phase0()
    b.sample_copies()
    b.w_in = b.inp("w_in", [1024, PW])
    if upto >= 1:
        b.P.barrier()
        b.phase1()
    if upto >= 2:
        b.P.barrier()
        b.attT = b.A.alloc([4, SEG + TS], BF16)
        b.C["F"] = b.A.alloc([3, 129])
        b.A.push()
        b.phase_bias()
        b.P.barrier()
        if DEBUG.get("att_stage", 9) >= 1:
            b.phase_attention()
        b.A.pop()
        b.P.barrier()
        if not DEBUG.get("no_sattn"):
            b.phase_sample_attn()
        b.dbg("attT2", b.attT, [128, 4, SEG + TS], BF16)
    if upto >= 3:
        b.P.barrier()
        b.phase_mlstm()
    if upto >= 4:
        b.P.barrier()
        b.phase_out()
    if DEBUG.get("dmult"):
        DEBUG["mult_end"] = True
        b.dbg("mult_end", b.C["mult"], [128, 8, 5])
    b.P.emit(final_streams=list(b.P.streams.keys()))
    return b


def make_in_maps(inp, cores):
    consts = _consts()
    f32 = np.float32
    maps = []
    xp = inp["x_prompt"]
    for c in cores:
        b, p = c // 4, c % 4
        s0 = p * SEG
        m = dict(consts)
        ext = np.zeros((PREFIX + SEG, D), f32)
        lo = s0 - PREFIX
        src_lo = max(lo, 0)
        ext[src_lo - lo:] = xp[b, src_lo:s0 + SEG]
        m["xf"] = np.ascontiguousarray(ext[:PREFIX - HALO])
        m["xh"] = np.ascontiguousarray(ext[PREFIX - HALO:])
        m["xs"] = np.ascontiguousarray(inp["x_sample"][4 * c:4 * c + 4].reshape(TS, D))
        tv = np.zeros(64, f32)
        tv[(src_lo - lo) // 128:] = 1.0
        m["tvalid"] = np.ascontiguousarray(np.broadcast_to(tv, (128, 64)))
        call = np.concatenate([inp["c_prompt"][b:b + 1], inp["c_sample"][4 * c:4 * c + 4]], 0)
        m["cT"] = np.ascontiguousarray(call.T.reshape(8, 128, 5).transpose(1, 0, 2))
        m["w_ada"] = inp["w_ada"][0]
        m["rel_table"] = inp["rel_table"]
        m["gainT"] = _fm(inp["norm_gain"][0])
        m["conv_bT"] = _fm(inp["conv_b"][0])
        m["m_normT"] = _fm(inp["m_norm"][0])
        m["m_skipT"] = _fm(inp["m_skip"][0])
        m["b_adaT"] = _fm(inp["b_ada"][0])
        m["conv_wT"] = np.ascontiguousarray(inp["conv_w"][0].reshape(4, 8, 128).transpose(2, 1, 0))
        m["b_gate_rows"] = np.ascontiguousarray(np.broadcast_to(inp["b_ada"][0][2048:], (5, 1024)))
        m["fgain_bc"] = np.ascontiguousarray(np.broadcast_to(inp["final_gain"], (128, 1024)))
        m["b_if_bc"] = np.ascontiguousarray(np.broadcast_to(inp["b_if"][0], (128, 8)))
        m["w_in"] = inp["w_in"][0]
        st_ = np.zeros((8, 8, 128), f32)
        for t in range(8):
            st_[t, t, :] = 1.0
        m["selt"] = st_
        osl = np.zeros((128, 8, 8), f32)
        for t in range(8):
            osl[:, t, t] = 1.0
        m["onesel"] = osl
        for g, nm in enumerate(("cache_kv_w128", "cache_kv_w512", "cache_kv_w2048")):
            m["cache%d" % g] = np.ascontiguousarray(inp[nm][0, 4 * c:4 * c + 4].reshape(4, -1, 2, 512))
        m["w_pa"] = inp["w_pa"][0]
        m["w_pm"] = inp["w_pm"][0]
        m["w_out"] = inp["w_out"][0]
        m["w_mq"] = inp["w_mq"][0]
        m["w_mk"] = inp["w_mk"][0]
        eh = np.zeros((4, 4, 128), f32)
        for h in range(4):
            eh[h, h, :] = 1.0
        m["ehsel"] = eh
        sq = slice(4 * c, 4 * c + 4)
        Cst = inp["state_C"][0, sq]
        nst = inp["state_n"][0, sq]
        c0 = np.concatenate([Cst.transpose(0, 3, 1, 2), nst.transpose(0, 2, 1)[..., None]], axis=-1)
        m["C0T"] = np.ascontiguousarray(c0)
        mst = inp["state_m"][0, sq]
        m["m0row"] = np.ascontiguousarray(mst[:, :, None])
        m["m0bc"] = np.ascontiguousarray(np.broadcast_to(mst[:, None, :], (4, 128, 4)))
        cvs = inp["state_conv"][0, sq]
        m["conv0"] = np.ascontiguousarray(cvs.reshape(4, 3, 8, 128).transpose(0, 3, 2, 1))
        m["coremask"] = np.full((128, 128), NEG if p == 0 else 0.0, f32)
        sm = np.zeros((128, 4, 128), f32)
        for e in range(64):
            sm[e, 0, e] = 1.0
            sm[e, 1, 64 + e] = 1.0
            sm[64 + e, 2, e] = 1.0
            sm[64 + e, 3, 64 + e] = 1.0
        m["selmats"] = sm
        maps.append(m)
    return maps


def run_cores(inp, cores, upto=99):
    b = build_program(upto)
    maps = make_in_maps(inp, cores)
    maps = [{k: np.ascontiguousarray(v, dtype=np.float32) for k, v in m.items() if k in b.din} for m in maps]
    res = run_bass_kernel_spmd(b.nc, maps, core_ids=list(range(len(cores))))
    return res.results


def _phase_bias(self):
    A, P, ps, C = self.A, self.P, self.ps, self.C
    F_sb = C["F"]
    gv = A.alloc([3, 2, 256])
    self.v("pool", "memset", gv[0:8], NEG, w=["gv"])
    for g in range(3):
        self.mm(ps[0:8, 5, 0:129], C["rel_table"][0:32, g * 8:(g + 1) * 8], C["ohd"][0:32, g, :],
                r=["rel_table", "ohd"], w=[("P", 5)])
        self.act(F_sb[0:8, g, :], ps[0:8, 5, 0:129], AF.Copy, r=[("P", 5)], w=[("F", g)])
        self.v("pool", "tensor_copy", gv[0:8, g, 1, 127:255], F_sb[0:8, g, 0:128], r=[("F", g), "gv"], w=[("gv", g)])
        self.v("pool", "tensor_copy", gv[0:8, g, 0, 0:128], F_sb[0:8, g, 1:129], r=[("F", g), "gv"], w=[("gv", g)])
    gvd = self.scratch("gvd", [8, 3, 2, 256])
    self.dma("sp", gvd, gv[0:8], "gvw", r=[("gv", 0), ("gv", 1), ("gv", 2)], w=["gvd"])
    biasH = A.alloc([24, 256], BF16)
    for g in range(3):
        for h in range(8):
            for kb in range(2):
                off = ((h * 3 + g) * 2 + kb) * 256
                src = bass.AP(tensor=gvd.tensor, offset=off, ap=[[1, 128], [1, 128]])
                self.dma("pool", biasH[:, g * 8 + h, kb * 128:(kb + 1) * 128], src, "ldc", r=["gvd"], w=[("biasH", g)])
    C["biasH"] = biasH
    cm = A.alloc([128], BF16)
    self.dma("pool", cm, self.inp("coremask", [128, 128]), "ldc", w=["coremask"])
    C["coremask"] = cm
    C["selmats"] = A.alloc([4, 128])
    self.dma("sp", C["selmats"], self.inp("selmats", [128, 4, 128]), "ld0", w=["selmats"])


def _phase_attention(self):
    A, P, ps, C, hT = self.A, self.P, self.ps, self.C, self.hT
    w_in = self.w_in
    wv_in = w_in.rearrange("(c p) n -> p c n", p=128)
    kvp = [self.outp("kvp%d" % g, [GROUPS[g][0], 2, 512]) for g in range(3)]
    A.push()
    acc = A.alloc([2, SEG])
    wq = A.alloc([8, 128], BF16)
    wkv = A.alloc([8, 256], BF16)
    wz = A.alloc([8, 128], BF16)
    qT = A.alloc([SEG], BF16)
    kT = A.alloc([4096], BF16)
    vaug = A.alloc([32, 2, 128], BF16)
    pT = [A.alloc([256], BF16) for _ in range(4)]
    stage = [A.alloc([256]) for _ in range(2)]
    rbuf = A.alloc([512])
    att = A.alloc([512])
    sz = A.alloc([512])
    if not DEBUG.get("no_vones"):
        self.v("pool", "memset", vaug[:, :, :, 64:128], 1.0, w=["vones"])
    cnt = 0
    STG = DEBUG.get("att_stage", 9)
    for hp in range(DEBUG.get("att_hp", 4)):
        for g, (win, d) in enumerate(GROUPS):
            if g not in DEBUG.get("att_groups", (0, 1, 2)):
                continue
            U = SEG // d
            U2 = U + 128
            col = g * 512 + hp * 128
            allc = lambda nm: [(nm, c) for c in range(8)]
            self.dma("pool", wq, wv_in[:, :, OFF_Q + col:OFF_Q + col + 128], "ldw", w=allc("wq"))
            self.dma("pool", wkv[:, :, 0:128], wv_in[:, :, OFF_K + col:OFF_K + col + 128], "ldw", w=allc("wkv"))
            self.dma("pool", wkv[:, :, 128:256], wv_in[:, :, OFF_V + col:OFF_V + col + 128], "ldw", w=allc("wkv"))
            hq = [hT[:, c, HALO:HALO + SEG].rearrange("p (u r) -> p r u", r=d) for c in range(8)]
            hk = [hT[:, c, HALO - 128 * d:HALO + SEG].rearrange("p (u r) -> p r u", r=d) for c in range(8)]

            def chunks(Ux, total):
                res = []
                if Ux >= 512:
                    for r in range(d):
                        u0 = 0
                        while u0 < Ux:
                            n = min(512, Ux - u0)
                            res.append((r, 1, u0, n))
                            u0 += n
                else:
                    nr = 512 // Ux
                    for r0 in range(0, d, nr):
                        res.append((r0, nr, 0, Ux))
                return res

            def tile_keys(view_lo, r0, nr, u0, n, c):
                lo = view_lo + r0 + d * u0
                hi = view_lo + r0 + nr - 1 + d * (u0 + n - 1)
                return [(("hT", t), c) for t in range(lo // 128, hi // 128 + 1)]

            for (r0, nr, u0, n) in chunks(U, SEG):
                bank = cnt % 2
                cnt += 1
                pso = ps[:, bank, 0:nr * n]
                pso3 = pso if nr == 1 else pso.rearrange("p (a b) -> p a b", a=nr)
                for c in range(8):
                    rhs = hq[c][:, r0, u0:u0 + n] if nr == 1 else hq[c][:, r0:r0 + nr, :]
                    self.mm(pso3, wq[:, c, :], rhs, start=(c == 0), stop=(c == 7),
                            r=[("wq", c)] + tile_keys(HALO, r0, nr, u0, n, c), w=[("P", bank)])
                f0 = r0 * U + u0
                self.act(qT[:, f0:f0 + nr * n], pso, AF.Copy, r=[("P", bank)], w=["qT"], scale=0.125)
            if STG < 2:
                continue
            for (r0, nr, u0, n) in chunks(U2, U2 * d):
                bank = cnt % 2
                cnt += 1
                pso = ps[:, bank, 0:nr * n]
                pso3 = pso if nr == 1 else pso.rearrange("p (a b) -> p a b", a=nr)
                for c in range(8):
                    rhs = hk[c][:, r0, u0:u0 + n] if nr == 1 else hk[c][:, r0:r0 + nr, :]
                    self.mm(pso3, wkv[:, c, 0:128], rhs, start=(c == 0), stop=(c == 7),
                            r=[("wkv", c)] + tile_keys(HALO - 128 * d, r0, nr, u0, n, c), w=[("P", bank)])
                f0 = r0 * U2 + u0
                self.act(kT[:, f0:f0 + nr * n], pso, AF.Copy, r=[("P", bank)], w=["kT"])
            nm = U2 // 128
            if STG < 3:
                continue
            for r in range(d):
                for m in range(nm):
                    blk = r * nm + m
                    bank = cnt % 2
                    cnt += 1
                    pso = ps[:, bank, 0:256]
                    lastb = (m == nm - 1)
                    for c in range(8):
                        self.mm(pso if lastb else pso[:, 128:256], hk[c][:, r, 128 * m:128 * m + 128],
                                wkv[:, c, :] if lastb else wkv[:, c, 128:256], start=(c == 0), stop=(c == 7),
                                r=[("wkv", c)] + tile_keys(HALO - 128 * d, r, 1, 128 * m, 128, c), w=[("P", bank)])
                    self.act(vaug[:, blk, :, 0:64], pso[:, 128:256].rearrange("p (h e) -> p h e", h=2), AF.Copy,
                             r=[("P", bank), "vones"], w=[("vaug", blk)])
                    if m == nm - 1 and not DEBUG.get("no_kvout"):
                        st = stage[blk % 2]
                        if DEBUG.get("kv_act"):
                            self.act(st, pso, AF.Copy, r=[("P", bank)], w=[("stage", blk % 2)])
                        else:
                            self.v("dve", "tensor_copy", st, pso, r=[("P", bank)], w=[("stage", blk % 2)])
                        dst = kvp[g].rearrange("(i r) k c -> r i k c", r=d)[r, :, :, hp * 128:(hp + 1) * 128]
                        if DEBUG.get("kv_plain"):
                            dst = kvp[g][0:128, :, hp * 128:(hp + 1) * 128]
                        if not DEBUG.get("kv_nodma"):
                            self.dma("sp", dst, st.rearrange("p (k c) -> p k c", k=2), "out", r=[("stage", blk % 2)], w=[])
            if STG < 4:
                continue
            for r in range(d):
                for n in range(U // 128):
                    for h in range(2):
                        gh = g * 8 + hp * 2 + h
                        hs = slice(h * 64, h * 64 + 64)
                        si = cnt % 3
                        cnt += 1
                        sbk, obk = (2, 3, 6)[si], (4, 5, 7)[si]
                        S = ps[:, sbk, 0:256]
                        O = ps[:, obk, 0:128]
                        self.mm(S, C["antiid_bf"], C["biasH"][:, gh, :], start=True, stop=False,
                                r=["antiid_bf", ("biasH", g)], w=[("P", sbk)])
                        if n == 0:
                            self.mm(S[:, 0:128], C["ident_bf"], C["coremask"], start=False, stop=False,
                                    r=["ident_bf", "coremask"], w=[("P", sbk)])
                        q_ap = qT[hs, r * U + 128 * n:r * U + 128 * n + 128]
                        self.mm(S[:, 0:128], kT[hs, r * U2 + 128 * n:r * U2 + 128 * n + 128], q_ap, start=False, stop=False,
                                r=["qT", "kT"], w=[("P", sbk)])
                        self.mm(S[:, 128:256], kT[hs, r * U2 + 128 * (n + 1):r * U2 + 128 * (n + 2)], q_ap, start=False, stop=True,
                                r=["qT", "kT"], w=[("P", sbk)])
                        self.act(pT[si], S, AF.Exp, r=[("P", sbk)], w=[("pT", si)])
                        b0 = r * nm + n
                        self.mm(O, vaug[:, b0, h, :], pT[si][:, 0:128], start=True, stop=False,
                                r=[("vaug", b0), ("pT", si)], w=[("P", obk)])
                        self.mm(O, vaug[:, b0 + 1, h, :], pT[si][:, 128:256], start=False, stop=True,
                                r=[("vaug", b0 + 1), ("pT", si)], w=[("P", obk)])
                        av = acc[:, h, :].rearrange("p (u r) -> p r u", r=d)[:, r, 128 * n:128 * n + 128]
                        if d == 1:
                            ak = [("acc", h, n // 4, rr) for rr in range(16)]
                        elif d == 4:
                            ak = [("acc", h, n, r + 4 * j) for j in range(4)]
                        else:
                            ak = [("acc", h, qq, r) for qq in range(4)]
                        if g == 0:
                            self.v("dve", "tensor_copy", av, O, r=[("P", obk)], w=ak)
                        else:
                            self.v("dve", "tensor_tensor", av, av, O, ALU.add, r=[("P", obk)] + ak, w=ak)
        if STG < 5:
            continue
        P.barrier()
        zc = OFF_ZA + hp * 128
        self.dma("pool", wz, wv_in[:, :, zc:zc + 128], "ldw", w=[("wz", c) for c in range(8)])
        for k in range(4):
            tk = slice(512 * k, 512 * k + 512)
            for j, (bank, sm) in enumerate(((6, (0, 1)), (7, (2, 3)))):
                for h in range(2):
                    self.mm(ps[:, bank, :], C["selmats"][:, sm[h], :], acc[:, h, tk], start=(h == 0), stop=(h == 1),
                            r=["selmats"], w=[("P", 6 + j)])
            self.v("dve", "reciprocal", rbuf, ps[:, 7, :], r=[("P", 7)], w=["rbuf"])
            self.v("dve", "tensor_tensor", att, ps[:, 6, :], rbuf, ALU.mult, r=[("P", 6), "rbuf"], w=["att"])
            for c in range(8):
                self.mm(ps[:, 0, :], wz[:, c, :], hT[:, c, HALO + 512 * k:HALO + 512 * k + 512], start=(c == 0), stop=(c == 7),
                        r=[("wz", c)] + [(("hT", t), c) for t in range(16 + 4 * k, 16 + 4 * k + 4)], w=[("P", 0)])
            self.act(sz, ps[:, 0, :], AF.Silu, r=[("P", 0)], w=["sz"])
            self.v("dve", "tensor_tensor", self.attT[:, hp, tk], att, sz, ALU.mult, r=["att", "sz"], w=[("attT", hp)])
        P.barrier()
    A.pop()
    self.dbg("attT", self.attT, [128, 4, SEG + TS], BF16)


Builder.phase_bias = _phase_bias
Builder.phase_attention = _phase_attention


def _mlstm_setup(self):
    A, P, C = self.A, self.P, self.C
    wv_in = self.w_in.rearrange("(c p) n -> p c n", p=128)
    M = self.M = {}
    M["wxm"] = A.alloc([8, 1024], BF16)
    M["wg"] = A.alloc([8, 8], BF16)
    M["wmq"] = A.alloc([2, 4, 128], BF16)
    M["wmk"] = A.alloc([2, 4, 128], BF16)
    for c in range(8):
        self.dma("pool", M["wxm"][:, c, :], wv_in[:, c, OFF_XM:OFF_XM + 1024], "ldw", w=["wxm"])
        self.dma("pool", M["wg"][:, c, :], wv_in[:, c, OFF_I:OFF_I + 8], "ldw", w=["wg"])
    wq_d = self.inp("w_mq", [4, 256, 128]).rearrange("h (c p) k -> p c h k", p=128)
    wk_d = self.inp("w_mk", [4, 256, 128]).rearrange("h (c p) k -> p c h k", p=128)
    for ec in range(2):
        for h in range(4):
            self.dma("pool", M["wmq"][:, ec, h, :], wq_d[:, ec, h, :], "ldw", w=["wmq"])
            self.dma("pool", M["wmk"][:, ec, h, :], wk_d[:, ec, h, :], "ldw", w=["wmk"])
    M["ones"] = A.alloc([128])
    self.v("pool", "memset", M["ones"], 1.0, w=["ones"])
    M["ehsel"] = A.alloc([4, 128])
    self.dma("sp", M["ehsel"][0:4], self.inp("ehsel", [4, 4, 128]), "ld0", w=["ehsel"])
    M["negbig"] = A.alloc([64])
    self.v("dve", "tensor_scalar", M["negbig"], C["tvalid"], 1.0e4, -1.0e4, op0=ALU.mult, op1=ALU.add,
           r=["tvalid"], w=["negbig"])
    M["one1"] = A.alloc([1])
    M["zero1"] = A.alloc([1])
    self.v("pool", "memset", M["one1"], 1.0, w=["one1"])
    self.v("pool", "memset", M["zero1"], 0.0, w=["zero1"])
    M["neg1"] = A.alloc([1])
    self.v("pool", "memset", M["neg1"], -1.0, w=["neg1"])
    M["CT"] = A.alloc([4, 257], F32)
    M["m_row"] = A.alloc([1], F32)
    M["m_bc"] = A.alloc([4], F32)
    M["convbuf"] = A.alloc([8, 131], F32)


def _mlstm_local_setup(self):
    A, P, C, M = self.A, self.P, self.C, self.M
    wv_in = self.w_in.rearrange("(c p) n -> p c n", p=128)
    M["wzm"] = A.alloc([8, 1024], BF16)
    M["wom"] = A.alloc([8, 1024], BF16)
    for c in range(8):
        self.dma("pool", M["wzm"][:, c, :], wv_in[:, c, OFF_ZM:OFF_ZM + 1024], "ldw", w=["wzm"])
        self.dma("pool", M["wom"][:, c, :], wv_in[:, c, OFF_OM:OFF_OM + 1024], "ldw", w=["wom"])
    for nm, shp, dt in (("cacc", [8, 128], F32), ("c_act", [8, 128], BF16),
                        ("vaug", [4, 257], BF16), ("kmw", [4, 128], BF16), ("qmT", [4, 128], BF16), ("kmT", [4, 128], BF16),
                        ("gt", [8], F32), ("lf", [4], F32), ("ie", [4], F32), ("b_tok", [4], F32), ("a_tok", [4], F32),
                        ("a_row", [128], F32), ("cm_row", [128], F32), ("M_row", [128], F32), ("negM_row", [128], F32),
                        ("AT", [1], F32), ("Mend_row", [1], F32), ("dg", [4], F32), ("Mend_bc", [4], F32),
                        ("tmp4", [4], F32), ("wk", [4], F32), ("wC", [4], F32), ("M_tok", [4], F32), ("emt", [4], F32),
                        ("DT", [128], F32), ("Wbc", [128], F32), ("scT", [128], BF16), ("qtil", [128], BF16),
                        ("CTb", [4, 257], BF16), ("hh", [256], F32), ("hn", [256], BF16), ("hnm", [8, 128], F32),
                        ("st", [8], F32), ("so", [128], F32), ("szm", [128], F32), ("t1", [128], F32), ("t2", [128], F32),
                        ):
        M[nm] = A.alloc(shp, dt)
        if nm in ("DT", "Wbc", "scT", "qtil", "hh", "hn", "st"):
            M[nm + "_b"] = A.alloc(shp, dt)
    self.v("pool", "memset", M["vaug"][:, :, 256:257], 1.0, w=["vaug1"])
    M["so_all"] = A.alloc([8, 512], BF16)
    M["sz_all"] = A.alloc([8, 512], BF16)


def _mlstm_gates4(self, tok0, tiles, n=512):
    ps, M, hT = self.ps, self.M, self.hT
    for fb in range(8):
        for (wname, bank, func, dst, dk) in (("wom", 5, AF.Sigmoid, "so_all", "so_all"), ("wzm", 6, AF.Silu, "sz_all", "sz_all")):
            for c in range(8):
                self.mm(ps[:, bank, 0:n], M[wname][:, c, fb * 128:(fb + 1) * 128], hT[:, c, tok0:tok0 + n], start=(c == 0), stop=(c == 7),
                        r=[wname] + [(("hT", t), c) for t in tiles], w=[("P", bank)])
            self.act(M[dst][:, fb, 0:n], ps[:, bank, 0:n], func, r=[("P", bank)], w=[(dk, fb)])


def _mlstm_tile(self, hTt, hkeys, ntok, tcol, with_out, mout_dst, moutkey, gate_off=None):
    A, P, ps, C, M = self.A, self.P, self.ps, self.C, self.M
    N = ntok
    B = lambda i: ("P", i)
    tv = C["tvalid"][:, tcol:tcol + 1] if tcol is not None else M["one1"]
    nb = M["negbig"][:, tcol:tcol + 1] if tcol is not None else M["zero1"]
    tvk = ["tvalid", "negbig", "one1", "zero1"]
    cb = M["convbuf"]
    xmps = ps[:, 0:2, :].rearrange("p a (b t) -> p (a b) t", t=128)
    for fb in range(8):
        for c in range(8):
            self.mm(xmps[:, fb, 0:N], M["wxm"][:, c, fb * 128:(fb + 1) * 128], hTt[:, c, :], start=(c == 0), stop=(c == 7),
                    r=["wxm"] + [(k, c) for k in hkeys], w=[B(fb // 4)])
    for half in range(2):
        self.act(cb[:, 4 * half:4 * half + 4, 3:3 + N], xmps[:, 4 * half:4 * half + 4, 0:N], AF.Copy,
                 r=[B(half)] + tvk, w=[("cb", half)], scale=tv)
    for half in range(2):
        for c in range(8):
            self.mm(ps[0:N, 2 + half, :], hTt[:, c, :], M["wxm"][:, c, half * 512:(half + 1) * 512], start=(c == 0), stop=(c == 7),
                    r=["wxm"] + [(k, c) for k in hkeys], w=[B(2 + half)])
        self.act(M["vaug"][0:N, 2 * half:2 * half + 2, 0:256], ps[0:N, 2 + half, :].rearrange("p (h v) -> p h v", h=2), AF.Copy,
                 r=[B(2 + half), "vaug1"], w=[("vaug", half)])
    gps = ps[0:N, 7, 0:8]
    for c in range(8):
        self.mm(gps, hTt[:, c, :], M["wg"][:, c, :], start=(c == 0), stop=(c == 7),
                r=["wg"] + [(k, c) for k in hkeys], w=[B(7)])
    gt = M["gt"]
    self.v("dve", "tensor_tensor", gt[0:N], gps, C["b_if_bc"][0:N], ALU.add, r=[B(7), "b_if_bc"], w=["gt"])
    lf = M["lf"]
    self.act(lf[0:N], gt[0:N, 4:8], AF.Exp, r=["gt"], w=["lf"], scale=-1.0)
    self.act(lf[0:N], lf[0:N], AF.Ln, r=["lf"], w=["lf"], bias=M["one1"][0:N])
    self.v("dve", "tensor_scalar", lf[0:N], lf[0:N], tv[0:N], M["neg1"][0:N], op0=ALU.mult, op1=ALU.mult, r=["lf", "neg1"] + tvk, w=["lf"])
    ie = M["ie"]
    self.v("dve", "tensor_scalar", ie[0:N], gt[0:N, 0:4], tv[0:N], nb[0:N], op0=ALU.mult, op1=ALU.add, r=["gt"] + tvk, w=["ie"])
    tri, ident = C["tri"], C["ident"]
    self.mm(ps[0:N, 7, 8:12], tri[0:N, 0:N], lf[0:N], r=["lf", "tri"], w=[B(7)])
    self.mm(ps[:, 7, 12:16], M["ones"][0:N, :], lf[0:N], r=["lf", "ones"], w=[B(7)])
    self.mm(ps[0:4, 7, 144:144 + N], lf[0:N], tri[0:N, 0:N], r=["lf", "tri"], w=[B(7)])
    b_tok, a_tok = M["b_tok"], M["a_tok"]
    self.v("dve", "tensor_copy", b_tok[0:N], ps[0:N, 7, 8:12], r=[B(7)], w=["b_tok"])
    self.v("dve", "tensor_tensor", a_tok[0:N], ie[0:N], b_tok[0:N], ALU.subtract, r=["ie", "b_tok"], w=["a_tok"])
    self.mm(ps[0:4, 7, 16:16 + N], a_tok[0:N], ident[0:N, 0:N], r=["a_tok", "ident"], w=[B(7)])
    a_row = M["a_row"]
    self.v("dve", "tensor_copy", a_row[0:4, 0:N], ps[0:4, 7, 16:16 + N], r=[B(7)], w=["a_row"])
    cw, cbias = C["conv_wT"], C["conv_bT"]
    for fb in range(8):
        self.act(M["cacc"][:, fb, 0:N], cb[:, fb, 3:3 + N], AF.Identity, r=[("cb", fb // 4), "conv_wT", "conv_bT"], w=[("cacc", fb)],
                 scale=cw[:, fb, 3:4], bias=cbias[:, fb:fb + 1])
    for j in range(3):
        for fb in range(8):
            ca = M["cacc"][:, fb, 0:N]
            self.v("dve", "scalar_tensor_tensor", ca, cb[:, fb, j:j + N], cw[:, fb, j:j + 1], ca, op0=ALU.mult, op1=ALU.add,
                   r=[("cb", fb // 4), ("cacc", fb)], w=[("cacc", fb)])
    for fb in range(8):
        self.act(M["c_act"][:, fb, 0:N], M["cacc"][:, fb, 0:N], AF.Silu, r=[("cacc", fb)], w=[("c_act", fb)])
    for half in range(2):
        self.v("pool", "tensor_copy", cb[:, 4 * half:4 * half + 4, 0:3], cb[:, 4 * half:4 * half + 4, N:N + 3],
               r=[("cb", half)], w=[("cb", half)])
    kmps = ps[0:N, 4, :].rearrange("p (h k) -> p h k", h=4)
    for h in range(4):
        for ec in range(2):
            self.mm(kmps[:, h, :], M["c_act"][:, 2 * h + ec, 0:N], M["wmk"][:, ec, h, :], start=(ec == 0), stop=(ec == 1),
                    r=[("c_act", 2 * h + ec), "wmk"], w=[B(4)])
    if with_out:
        qps = ps[:, 5, :].rearrange("p (h t) -> p h t", h=4)
        kps = ps[:, 6, :].rearrange("p (h t) -> p h t", h=4)
        for h in range(4):
            for ec in range(2):
                self.mm(qps[:, h, 0:N], M["wmq"][:, ec, h, :], M["c_act"][:, 2 * h + ec, 0:N], start=(ec == 0), stop=(ec == 1),
                        r=[("c_act", 2 * h + ec), "wmq"], w=[B(5)])
            for ec in range(2):
                self.mm(kps[:, h, 0:N], M["wmk"][:, ec, h, :], M["c_act"][:, 2 * h + ec, 0:N], start=(ec == 0), stop=(ec == 1),
                        r=[("c_act", 2 * h + ec), "wmk"], w=[B(6)])
        self.act(M["qmT"][:, :, 0:N], qps[:, :, 0:N], AF.Copy, r=[B(5)], w=["qmT"], scale=float(128 ** -0.5))
        self.act(M["kmT"][:, :, 0:N], kps[:, :, 0:N], AF.Copy, r=[B(6)], w=["kmT"])
        self.v("pool", "tensor_copy", M["CTb"], M["CT"], r=["CT"], w=["CTb"])
        self.v("dve", "tensor_tensor_scan", M["cm_row"][0:4, 0:N], M["ones"][0:4, 0:N], a_row[0:4, 0:N], -1.0e30,
               op0=ALU.mult, op1=ALU.max, r=["a_row", "ones"], w=["cm_row"])
        self.v("dve", "tensor_tensor", M["M_row"][0:4, 0:N], M["cm_row"][0:4, 0:N], M["m_row"][0:4, 0:1].to_broadcast([4, N]), ALU.max,
               r=["cm_row", "m_row"], w=["M_row"])
        self.v("dve", "tensor_scalar", M["negM_row"][0:4, 0:N], M["M_row"][0:4, 0:N], -1.0, None, op0=ALU.mult,
               r=["M_row"], w=["negM_row"])
        self.mm(ps[0:N, 7, 276:280], M["M_row"][0:4, 0:N], ident[0:4, 0:4], r=["M_row", "ident"], w=[B(7)])
        self.v("dve", "tensor_tensor", M["emt"][0:N], b_tok[0:N], ps[0:N, 7, 276:280], ALU.add, r=["b_tok", B(7)], w=["emt"])
        self.act(M["emt"][0:N], M["emt"][0:N], AF.Exp, r=["emt"], w=["emt"], scale=-1.0)
    self.v("dve", "tensor_reduce", M["AT"][0:4], a_row[0:4, 0:N], AX.X, ALU.max, r=["a_row"], w=["AT"])
    self.v("dve", "tensor_tensor", M["Mend_row"][0:4], M["AT"][0:4], M["m_row"][0:4], ALU.max, r=["AT", "m_row"], w=["Mend_row"])
    self.v("dve", "tensor_tensor", M["dg"][0:4], ident[0:4, 0:4], M["Mend_row"][0:4, 0:1].to_broadcast([4, 4]), ALU.mult,
           r=["Mend_row", "ident"], w=["dg"])
    self.mm(ps[:, 7, 272:276], M["ones"][0:4, :], M["dg"][0:4], r=["dg", "ones"], w=[B(7)])
    self.v("dve", "tensor_copy", M["Mend_bc"], ps[:, 7, 272:276], r=[B(7)], w=["Mend_bc"])
    self.v("dve", "tensor_tensor", M["wk"][0:N], a_tok[0:N], M["Mend_bc"][0:N], ALU.subtract, r=["a_tok", "Mend_bc"], w=["wk"])
    self.act(M["wk"][0:N], M["wk"][0:N], AF.Exp, r=["wk"], w=["wk"])
    self.v("dve", "tensor_tensor", M["wC"], M["m_bc"], M["Mend_bc"], ALU.subtract, r=["m_bc", "Mend_bc"], w=["wC"])
    self.act(M["wC"], M["wC"], AF.Exp, r=["wC"], w=["wC"])
    self.v("dve", "tensor_tensor", M["kmw"][0:N], kmps, M["wk"][0:N].unsqueeze(2).to_broadcast([N, 4, 128]), ALU.mult,
           r=[B(4), "wk"], w=["kmw"])
    if with_out:
        def head_body(h):
            hsl = slice(h, h + 1)
            par = h % 2
            sfx = "" if par == 0 else "_b"
            hb0, hb1 = (0, 1) if par == 0 else (5, 6)
            pl, pm, pst = ps[:, hb0, 0:N], ps[:, hb0, 128:128 + N], ps[:, hb0, 256:256 + N]
            self.mm(pl, M["ehsel"][0:4, h, :], M["negM_row"][0:4, 0:N], r=["ehsel", "negM_row"], w=[B(hb0)])
            yield
            self.mm(pm, M["ehsel"][0:4, h, :], M["negM_row"][0:4, 0:N], start=True, stop=False, r=["ehsel", "negM_row"], w=[B(hb0)])
            yield
            self.mm(pm[0:N], C["ident_bf"][0:N, 0:N], C["cmaskT_bf"][0:N, 0:N], start=False, stop=True,
                    r=["ident_bf", "cmaskT_bf"], w=[B(hb0)])
            yield
            self.mm(pst[0:N], M["kmT"][:, h, 0:N], M["qmT"][:, h, 0:N], r=["kmT", "qmT"], w=[B(hb0)])
            yield
            self.act(M["DT" + sfx][0:N, 0:N], pm[0:N], AF.Exp, r=[B(hb0), "a_tok"], w=["DT" + sfx], bias=a_tok[0:N, hsl])
            yield
            self.act(M["Wbc" + sfx][:, 0:N], pl, AF.Exp, r=[B(hb0), "m_bc"], w=["Wbc" + sfx], bias=M["m_bc"][:, hsl])
            yield
            self.v("dve", "tensor_tensor", M["scT" + sfx][0:N, 0:N], pst[0:N], M["DT" + sfx][0:N, 0:N], ALU.mult, r=[B(hb0), "DT" + sfx], w=["scT" + sfx])
            yield
            self.v("pool", "tensor_tensor", M["qtil" + sfx][:, 0:N], M["qmT"][:, h, 0:N], M["Wbc" + sfx][:, 0:N], ALU.mult,
                   r=["qmT", "Wbc" + sfx], w=["qtil" + sfx])
            yield
            nd = ps[0:N, hb1, 0:257]
            self.mm(nd, M["scT" + sfx][0:N, 0:N], M["vaug"][0:N, h, :], start=True, stop=False, r=["scT" + sfx, ("vaug", h // 2), "vaug1"], w=[B(hb1)])
            yield
            self.mm(nd, M["qtil" + sfx][:, 0:N], M["CTb"][:, h, :], start=False, stop=True, r=["qtil" + sfx, "CTb"], w=[B(hb1)])
            yield
            st = M["st" + sfx]
            self.act(st[0:N, 0:1], nd[:, 256:257], AF.Abs, r=[B(hb1)], w=["st" + sfx])
            yield
            self.v("dve", "tensor_tensor", st[0:N, 0:1], st[0:N, 0:1], M["emt"][0:N, hsl], ALU.max, r=["st" + sfx, "emt"], w=["st" + sfx])
            yield
            self.v("dve", "reciprocal", st[0:N, 0:1], st[0:N, 0:1], r=["st" + sfx], w=["st" + sfx])
            yield
            self.act(M["hh" + sfx][0:N], nd[:, 0:256], AF.Copy, r=[B(hb1), "st" + sfx], w=["hh" + sfx, "st1" + sfx], scale=st[0:N, 0:1], accum_out=st[0:N, 1:2])
            yield
            self.act(M["hn" + sfx][0:N], M["hh" + sfx][0:N], AF.Square, r=["hh" + sfx], w=["hn" + sfx, "st2" + sfx], accum_out=st[0:N, 2:3])
            yield
            self.v("dve", "tensor_scalar", st[0:N, 3:4], st[0:N, 1:2], 1.0 / 256, None, op0=ALU.mult, r=["st1" + sfx], w=["st3" + sfx])
            yield
            self.v("dve", "tensor_tensor", st[0:N, 4:5], st[0:N, 3:4], st[0:N, 3:4], ALU.mult, r=["st3" + sfx], w=["st4" + sfx])
            yield
            self.v("dve", "scalar_tensor_tensor", st[0:N, 5:6], st[0:N, 2:3], 1.0 / 256, st[0:N, 4:5], op0=ALU.mult, op1=ALU.subtract,
                   r=["st2" + sfx, "st4" + sfx], w=["st5" + sfx])
            yield
            self.v("dve", "tensor_scalar", st[0:N, 5:6], st[0:N, 5:6], EPS, None, op0=ALU.add, r=["st5" + sfx], w=["st5" + sfx])
            yield
            self.act(st[0:N, 5:6], st[0:N, 5:6], AF.Ln, r=["st5" + sfx], w=["st5" + sfx])
            yield
            self.act(st[0:N, 5:6], st[0:N, 5:6], AF.Exp, r=["st5" + sfx], w=["st5" + sfx], scale=-0.5)
            yield
            self.v("dve", "scalar_tensor_tensor", st[0:N, 6:7], st[0:N, 3:4], -1.0, st[0:N, 5:6], op0=ALU.mult, op1=ALU.mult,
                   r=["st3" + sfx, "st5" + sfx], w=["st6" + sfx])
            yield
            self.act(M["hn" + sfx][0:N], M["hh" + sfx][0:N], AF.Identity, r=["hh" + sfx, "st5" + sfx, "st6" + sfx], w=["hn" + sfx], scale=st[0:N, 5:6], bias=st[0:N, 6:7])
            yield
            pt = ps[:, 4, 128 * par:128 * par + 128].bitcast(BF16).rearrange("p (b t) -> p b t", b=2)
            for vb in range(2):
                self.tr(pt[:, vb, 0:N], M["hn" + sfx][0:N, vb * 128:(vb + 1) * 128], C["ident_bf"][0:N, 0:N], r=["hn" + sfx, "ident_bf"], w=[B(4)])
            yield
            for vb in range(2):
                fb = 2 * h + vb
                self.act(M["hnm"][:, fb, 0:N], pt[:, vb, 0:N], AF.Copy, r=[B(4), "m_normT"], w=[("hnm", fb)], scale=C["m_normT"][:, fb:fb + 1])
            yield
        for pair in ((0, 1), (2, 3)):
            gens = [head_body(h) for h in pair]
            live = list(gens)
            while live:
                for gen in list(live):
                    try:
                        next(gen)
                    except StopIteration:
                        live.remove(gen)
    for h in range(4):
        bk = 2 + (h % 2)
        dps = ps[:, bk, 0:257]
        self.mm(dps, M["kmw"][0:N, h, :], M["vaug"][0:N, h, :], r=["kmw", ("vaug", h // 2), "vaug1"], w=[B(bk)])
        self.v("dve", "scalar_tensor_tensor", M["CT"][:, h, :], M["CT"][:, h, :], M["wC"][:, h:h + 1], dps, op0=ALU.mult, op1=ALU.add,
               r=["wC", B(bk), "CT", "CTb"], w=["CT"])
    self.v("dve", "tensor_tensor", M["m_row"][0:4], M["Mend_row"][0:4], ps[0:4, 7, 144 + N - 1:144 + N], ALU.add,
           r=["Mend_row", B(7)], w=["m_row"])
    self.v("dve", "tensor_tensor", M["m_bc"], M["Mend_bc"], ps[:, 7, 12:16], ALU.add, r=["Mend_bc", B(7)], w=["m_bc"])
    if with_out:
        for fb in range(8):
            if gate_off is None:
                for (wname, bank) in (("wom", 5), ("wzm", 6)):
                    for c in range(8):
                        self.mm(ps[:, bank, 0:N], M[wname][:, c, fb * 128:(fb + 1) * 128], hTt[:, c, :], start=(c == 0), stop=(c == 7),
                                r=[wname] + [(k, c) for k in hkeys], w=[B(bank)])
                self.act(M["so"][:, 0:N], ps[:, 5, 0:N], AF.Sigmoid, r=[B(5)], w=["so"])
                self.act(M["szm"][:, 0:N], ps[:, 6, 0:N], AF.Silu, r=[B(6)], w=["szm"])
                so_ap, sz_ap, sok, szk = M["so"][:, 0:N], M["szm"][:, 0:N], "so", "szm"
            else:
                so_ap, sz_ap = M["so_all"][:, fb, gate_off:gate_off + N], M["sz_all"][:, fb, gate_off:gate_off + N]
                sok, szk = ("so_all", fb), ("sz_all", fb)
            self.v("dve", "tensor_tensor", M["t1"][:, 0:N], so_ap, M["hnm"][:, fb, 0:N], ALU.mult, r=[sok, ("hnm", fb)], w=["t1"])
            self.v("dve", "scalar_tensor_tensor", M["t2"][:, 0:N], M["c_act"][:, fb, 0:N], C["m_skipT"][:, fb:fb + 1], M["t1"][:, 0:N],
                   op0=ALU.mult, op1=ALU.add, r=[("c_act", fb), "t1", "m_skipT"], w=["t2"])
            self.v("dve", "tensor_tensor", mout_dst[:, fb, :], M["t2"][:, 0:N], sz_ap, ALU.mult, r=["t2", szk], w=[(moutkey, fb)])


Builder.mlstm_setup = _mlstm_setup
Builder.mlstm_local_setup = _mlstm_local_setup
Builder.mlstm_tile = _mlstm_tile
Builder.mlstm_gates4 = _mlstm_gates4


def _phase_mlstm(self):
    A, P, ps, C, hT = self.A, self.P, self.ps, self.C, self.hT
    self.moutS = A.alloc([8, TS], BF16)
    A.push()
    self.mlstm_setup()
    M = self.M
    self.v("pool", "memset", M["CT"], 0.0, w=["CT"])
    A.push()
    self.W1 = {
        "xt": [A.alloc([1024]) for _ in range(2)],
        "ss": A.alloc([4]),
        "junk": A.alloc([1024], BF16),
        "xn": [A.alloc([1024], BF16) for _ in range(2)],
    }
    self.mlstm_prefix()
    if DEBUG.get("sbuf"):
        print("mlstm prefix sbuf top", A.top)
    A.pop()
    P.barrier()
    self.mlstm_local_setup()
    if DEBUG.get("sbuf"):
        print("mlstm local sbuf top", A.top)
    for j in range(DEBUG.get("nloc", 16)):
        if j % 4 == 0:
            self.mlstm_gates4(HALO + j * 128, [16 + j + q for q in range(4)])
        self.mlstm_tile(hT[:, :, HALO + j * 128:HALO + (j + 1) * 128], [("hT", 16 + j)], 128, 48 + j, True,
                        hT[:, :, j * 128:(j + 1) * 128], ("hT", j), gate_off=(j % 4) * 128)
    o_conv = self.outp("convp", [128, 8, 3])
    o_C = self.outp("Cp", [128, 4, 257])
    o_m = self.outp("mp", [4, 1])
    self.dma("sp", o_conv, M["convbuf"][:, :, 0:3], "out", r=[("cb", 0), ("cb", 1)])
    self.dma("sp", o_C, M["CT"], "out", r=["CT"])
    self.dma("sp", o_m, M["m_row"][0:4], "out", r=["m_row"])
    C0 = self.inp("C0T", [4, 128, 4, 257])
    m0r = self.inp("m0row", [4, 4, 1])
    m0b = self.inp("m0bc", [4, 128, 4])
    cv0 = self.inp("conv0", [4, 128, 8, 3])
    o_convs = self.outp("convs", [4, 128, 8, 3])
    o_Cs = self.outp("Cs", [4, 128, 4, 257])
    o_ms = self.outp("ms", [4, 4, 1])
    if not DEBUG.get("no_smp"):
        self.mlstm_gates4(HALO + SEG, [32], n=TS)
    for j in range(4 if not DEBUG.get("no_smp") else 0):
        self.dma("sp", M["CT"], C0[j], "ldx", w=["CT"])
        self.dma("sp", M["m_row"][0:4], m0r[j], "ldx", w=["m_row"])
        self.dma("sp", M["m_bc"], m0b[j], "ldx", w=["m_bc"])
        self.dma("sp", M["convbuf"][:, :, 0:3], cv0[j], "ldx", w=[("cb", 0), ("cb", 1)])
        self.mlstm_tile(hT[:, :, HALO + SEG + 8 * j:HALO + SEG + 8 * j + 8], [("hT", 32)], 8, None, True,
                        self.moutS[:, :, 8 * j:8 * j + 8], ("moutS", j), gate_off=8 * j)
        self.dma("sp", o_convs[j], M["convbuf"][:, :, 0:3], "out", r=[("cb", 0), ("cb", 1)])
        self.dma("sp", o_Cs[j], M["CT"], "out", r=["CT"])
        self.dma("sp", o_ms[j], M["m_row"][0:4], "out", r=["m_row"])
    if DEBUG.get("mdump"):
        for nm, shp, dt in (("convbuf", [128, 8, 131], F32), ("c_act", [128, 8, 128], BF16), ("vaug", [128, 4, 257], BF16),
                            ("kmw", [128, 4, 128], BF16), ("gt", [128, 8], F32), ("lf", [128, 4], F32), ("ie", [128, 4], F32),
                            ("b_tok", [128, 4], F32), ("a_tok", [128, 4], F32), ("a_row", [128, 128], F32),
                            ("Mend_bc", [128, 4], F32), ("wk", [128, 4], F32), ("wC", [128, 4], F32), ("CT", [128, 4, 257], F32),
                            ("m_bc", [128, 4], F32), ("hh", [128, 256], F32), ("hnm", [128, 8, 128], F32), ("st", [128, 8], F32),
                            ("emt", [128, 4], F32), ("qmT", [128, 4, 128], BF16), ("kmT", [128, 4, 128], BF16), ("DT", [128, 128], F32),
                            ("Wbc", [128, 128], F32), ("M_row", [128, 128], F32)):
            DEBUG["md_" + nm] = True
            self.dbg("md_" + nm, M[nm], shp, dt)
        for nm, ap, shp, dt in (("hTp1", M["hTp"][1], [128, 8, 128], BF16), ("xn1", self.W1["xn"][1], [128, 1024], BF16),
                                ("xt1", self.W1["xt"][1], [128, 1024], F32), ("ss", self.W1["ss"], [128, 4], F32),
                                ("wg", M["wg"], [128, 8, 8], BF16), ("mult", C["mult"], [128, 8, 5], F32)):
            DEBUG["md_" + nm] = True
            self.dbg("md_" + nm, ap, shp, dt)
    A.pop()
    self.P.barrier()
    self.dbg("mout", self.hT[:, :, 0:SEG], [128, 8, SEG], BF16)
    self.dbg("moutS", self.moutS, [128, 8, TS], BF16)


Builder.phase_mlstm = _phase_mlstm


def _phase_out(self):
    A, P, ps, C, hT = self.A, self.P, self.ps, self.C, self.hT
    wv_in = self.w_in.rearrange("(c p) n -> p c n", p=128)
    A.push()
    wpa = A.alloc([4, 1024], BF16)
    wpm = A.alloc([8, 1024], BF16)
    wout = A.alloc([8, 1024], BF16)
    wga = A.alloc([8, 128], BF16)
    wgm = A.alloc([8, 128], BF16)
    merged = A.alloc([8, SEG + TS], BF16)
    fgain = A.alloc([1024])
    sga, sgm, t1, t2 = (A.alloc([512]) for _ in range(4))
    xt = [A.alloc([1024])] * 2
    yt = [A.alloc([1024]) for _ in range(2)]
    ss = A.alloc([4])
    self.dma("sp", fgain, self.inp("fgain_bc", [128, 1024]), "ld0", w=["fgain"])
    wpa_d = self.inp("w_pa", [512, 1024]).rearrange("(c p) n -> p c n", p=128)
    wpm_d = self.inp("w_pm", [1024, 1024]).rearrange("(c p) n -> p c n", p=128)
    wout_d = self.inp("w_out", [1024, 1024]).rearrange("(c p) n -> p c n", p=128)
    for c in range(4):
        self.dma("pool", wpa[:, c, :], wpa_d[:, c, :], "ldw", w=["wpa"])
    for c in range(8):
        self.dma("pool", wpm[:, c, :], wpm_d[:, c, :], "ldw", w=["wpm"])
        self.dma("pool", wout[:, c, :], wout_d[:, c, :], "ldw", w=["wout"])
    chunks = []
    for k in range(4):
        chunks.append(dict(n=512, m0=512 * k,
                           att=lambda hp, k=k: self.attT[:, hp, 512 * k:512 * k + 512],
                           mout=lambda fb, k=k: hT[:, fb, 512 * k:512 * k + 512],
                           hs=lambda c, k=k: hT[:, c, HALO + 512 * k:HALO + 512 * k + 512]))
    chunks.append(dict(n=TS, m0=SEG,
                       att=lambda hp: self.attT[:, hp, SEG:SEG + TS],
                       mout=lambda fb: self.moutS[:, fb, :],
                       hs=lambda c: hT[:, c, HALO + SEG:HALO + SEG + TS]))
    for cb in range(8):
        cs = slice(cb * 128, (cb + 1) * 128)
        self.dma("pool", wga, wv_in[:, :, OFF_GA + cb * 128:OFF_GA + (cb + 1) * 128], "ldw", w=["wga"])
        self.dma("pool", wgm, wv_in[:, :, OFF_GM + cb * 128:OFF_GM + (cb + 1) * 128], "ldw", w=["wgm"])
        for ch in chunks:
            n = ch["n"]
            for hp in range(4):
                self.mm(ps[:, 0, 0:n], wpa[:, hp, cs], ch["att"](hp), start=(hp == 0), stop=(hp == 3), r=["wpa"], w=[("P", 0)])
            for fb in range(8):
                self.mm(ps[:, 1, 0:n], wpm[:, fb, cs], ch["mout"](fb), start=(fb == 0), stop=(fb == 7), r=["wpm"], w=[("P", 1)])
            for c in range(8):
                self.mm(ps[:, 2, 0:n], wga[:, c, :], ch["hs"](c), start=(c == 0), stop=(c == 7), r=["wga"], w=[("P", 2)])
            for c in range(8):
                self.mm(ps[:, 3, 0:n], wgm[:, c, :], ch["hs"](c), start=(c == 0), stop=(c == 7), r=["wgm"], w=[("P", 3)])
            self.act(sga[:, 0:n], ps[:, 2, 0:n], AF.Sigmoid, r=[("P", 2)], w=["sga"])
            self.act(sgm[:, 0:n], ps[:, 3, 0:n], AF.Sigmoid, r=[("P", 3)], w=["sgm"])
            self.v("dve", "tensor_tensor", t1[:, 0:n], ps[:, 0, 0:n], sga[:, 0:n], ALU.mult, r=[("P", 0), "sga"], w=["t1"])
            self.v("dve", "tensor_tensor", t2[:, 0:n], ps[:, 1, 0:n], sgm[:, 0:n], ALU.mult, r=[("P", 1), "sgm"], w=["t2"])
            self.v("pool", "tensor_tensor", merged[:, cb, ch["m0"]:ch["m0"] + n], t1[:, 0:n], t2[:, 0:n], ALU.add,
                   r=["t1", "t2"], w=[("merged", cb)])
    self.dbg("merged", merged, [128, 8, SEG + TS], BF16)
    xh = self.din["xh"]
    xs = self.din["xs"]
    y_p = self.outp("y_p", [SEG, 1024])
    y_s = self.outp("y_s", [TS, 1024])
    tiles = [(xh[HALO + j * 128:HALO + (j + 1) * 128, :], y_p[j * 128:(j + 1) * 128, :], 128, j * 128, C["gate_p"]) for j in range(NLT)]
    tiles.append((xs, y_s, TS, SEG, C["gate_s"]))
    for i, (xsrc, ydst, n, m0, gate) in enumerate(tiles):
        i2 = i % 2
        self.dma("sp", xt[i2][0:n], xsrc, "ldx", w=[("xt", 0)])
        for half in range(2):
            hs_ = slice(half * 512, (half + 1) * 512)
            for cb in range(8):
                self.mm(ps[0:n, 4 + half, :], merged[:, cb, m0:m0 + n], wout[:, cb, hs_], start=(cb == 0), stop=(cb == 7),
                        r=["wout", ("merged", cb)], w=[("P", 4 + half)])
            self.v("dve", "tensor_tensor", yt[i2][0:n, hs_], ps[0:n, 4 + half, :], gate[0:n, hs_], ALU.mult,
                   r=[("P", 4 + half), ("gate_p", half), ("gate_s", half)], w=[("yt", i2, half)])
            self.v("pool", "tensor_tensor", yt[i2][0:n, hs_], yt[i2][0:n, hs_], xt[i2][0:n, hs_], ALU.add,
                   r=[("yt", i2, half), ("xt", 0)], w=[("yt", i2, half)])
        sl = ss[:, i % 4:i % 4 + 1]
        sk = ("ss", i % 4)
        self.act(xt[0][0:n], yt[i2][0:n], AF.Square, r=[("yt", i2, 0), ("yt", i2, 1)], w=[("xt", 0), sk], accum_out=sl[0:n])
        self.v("dve", "tensor_scalar", sl[0:n], sl[0:n], 1.0 / D, EPS, op0=ALU.mult, op1=ALU.add, r=[sk], w=[sk])
        self.act(sl[0:n], sl[0:n], AF.Ln, r=[sk], w=[sk])
        self.act(sl[0:n], sl[0:n], AF.Exp, r=[sk], w=[sk], scale=-0.5)
        self.act(yt[i2][0:n], yt[i2][0:n], AF.Copy, r=[("yt", i2, 0), ("yt", i2, 1), sk], w=[("yt", i2, 0), ("yt", i2, 1)], scale=sl[0:n])
        self.v("pool", "tensor_tensor", yt[i2][0:n], yt[i2][0:n], fgain[0:n], ALU.mult,
               r=[("yt", i2, 0), ("yt", i2, 1), "fgain"], w=[("yt", i2, 0), ("yt", i2, 1)])
        self.dma("sp", ydst, yt[i2][0:n], "out", r=[("yt", i2, 0), ("yt", i2, 1)])
    A.pop()


Builder.phase_out = _phase_out


def _sample_copies(self):
    LB = [w for (w, d) in GROUPS]
    self.s_cache = cache = [self.inp("cache%d" % g, [4, LB[g], 2, 512]) for g in range(3)]
    self.s_kvs = kvs = [self.outp("kvs%d" % g, [4, LB[g], 2, 512]) for g in range(3)]
    self.s_cat = cat = [self.scratch("cat%d" % g, [4, LB[g] + 8, 2, 512]) for g in range(2)]
    for g in range(3):
        for j in range(4):
            nsp = 4 if g == 2 else 1
            rows = LB[g] - 8
            step = (rows + nsp - 1) // nsp
            for a in range(0, rows, step):
                b_ = min(rows, a + step)
                self.dma("sp", kvs[g][j, a:b_], cache[g][j, 8 + a:8 + b_], "cpy", w=[("kvs", g, j, a)])
            if g < 2:
                self.dma("sp", cat[g][j, 0:LB[g]], cache[g][j], "cpy", w=[("catb", g, j)])


def _phase_sample_attn(self):
    A, P, ps, C, hT = self.A, self.P, self.ps, self.C, self.hT
    wv_in = self.w_in.rearrange("(c p) n -> p c n", p=128)
    LB = [w for (w, d) in GROUPS]
    cache, kvs, cat = self.s_cache, self.s_kvs, self.s_cat
    A.push()
    ident, F_sb = C["ident"], C["F"]
    ones = A.alloc([128])
    self.v("pool", "memset", ones, 1.0, w=["s_ones"])
    z8 = A.alloc([8])
    self.v("pool", "memset", z8, 0.0, w=["z8"])
    onesel = A.alloc([8, 8])
    self.dma("sp", onesel, self.inp("onesel", [128, 8, 8]), "ld0", w=["onesel"])
    selt = A.alloc([8, 128])
    self.dma("sp", selt[0:8], self.inp("selt", [8, 8, 128]), "ld0", w=["selt"])
    biasS = A.alloc([3, 8])
    bold = A.alloc([3, 8])
    tmpF = A.alloc([8])
    dgF = A.alloc([8])
    for g in range(3):
        self.mm(ps[:, 0, 0:8], F_sb[0:8, g, 0:128], ident[0:8, 0:8], r=[("F", g), "ident"], w=[("P", 0)])
        self.act(tmpF, ps[:, 0, 0:8], AF.Copy, r=[("P", 0)], w=["tmpF"])
        self.mm(ps[:, 0, 8:16], C["antiid"], tmpF, r=["tmpF", "antiid"], w=[("P", 0)])
        self.act(biasS[:, g, :], ps[:, 0, 8:16], AF.Copy, r=[("P", 0)], w=[("biasS", g)])
        self.v("dve", "tensor_tensor", dgF[0:8], ident[0:8, 0:8], F_sb[0:8, g, 128:129].to_broadcast([8, 8]), ALU.mult,
               r=[("F", g), "ident"], w=["dgF"])
        self.mm(ps[0:8, 0, 16:24], ones[0:8, 0:8], dgF[0:8], r=["dgF", "s_ones"], w=[("P", 0)])
        self.act(bold[0:8, g, :], ps[0:8, 0, 16:24], AF.Copy, r=[("P", 0)], w=[("bold", g)])
    wq = [A.alloc([8, 512], BF16) for _ in range(2)]
    qj = [A.alloc([1536]) for _ in range(4)]
    kj = A.alloc([1536])
    vj = A.alloc([1536])
    oldkv = [A.alloc([3, 2, 512]) for _ in range(1)][0]
    wcnt = 0
    for kind, off in (("q", OFF_Q), ("k", OFF_K), ("v", OFF_V)):
        for g in range(3):
            wb = wq[wcnt % 2]
            wk_ = ("swq", wcnt % 2)
            wcnt += 1
            self.dma("pool", wb, wv_in[:, :, off + g * 512:off + (g + 1) * 512], "ldw", w=[wk_])
            for j in range(4):
                bank = 1 + (j % 2)
                for c in range(8):
                    self.mm(ps[0:8, bank, :], hT[:, c, HALO + SEG + 8 * j:HALO + SEG + 8 * j + 8], wb[:, c, :],
                            start=(c == 0), stop=(c == 7), r=[wk_, (("hT", 32), c)], w=[("P", bank)])
                if kind == "q":
                    self.act(qj[j][0:8, g * 512:(g + 1) * 512], ps[0:8, bank, :], AF.Copy, r=[("P", bank)], w=[("qj", j, g)], scale=0.125)
                else:
                    st = kj if kind == "k" else vj
                    sk = ("kvst", kind, j % 3)
                    sl_ = st[0:8, (j % 3) * 512:(j % 3 + 1) * 512]
                    self.act(sl_, ps[0:8, bank, :], AF.Copy, r=[("P", bank)], w=[sk])
                    kvi = 0 if kind == "k" else 1
                    self.dma("sp", kvs[g][j, LB[g] - 8:LB[g], kvi, :], sl_, "out", r=[sk], w=[("kvsn", g, j, kvi)])
                    if g < 2:
                        self.dma("sp", cat[g][j, LB[g]:LB[g] + 8, kvi, :], sl_, "out", r=[sk], w=[("catn", g, j, kvi)])
    szs = A.alloc([4, TS])
    wz = wq[0]
    self.dma("pool", wz, wv_in[:, :, OFF_ZA:OFF_ZA + 512], "ldw", w=[("swq", 0)])
    for hp in range(4):
        for c in range(8):
            self.mm(ps[:, 3, 0:TS], wz[:, c, hp * 128:(hp + 1) * 128], hT[:, c, HALO + SEG:HALO + SEG + TS], start=(c == 0), stop=(c == 7),
                    r=[("swq", 0), (("hT", 32), c)], w=[("P", 3)])
        self.act(szs[:, hp, :], ps[:, 3, 0:TS], AF.Silu, r=[("P", 3)], w=[("szs", hp)])
    NKG = 6
    Kg = [A.alloc([2, 512]) for _ in range(NKG)]
    prod = A.alloc([512])
    lg = [A.alloc([8]) for _ in range(4)]
    Pz = [A.alloc([8, 8]) for _ in range(NKG)]
    numS = A.alloc([512])
    denS = A.alloc([8])
    pold = A.alloc([8])
    tmpo = A.alloc([512])
    oj = A.alloc([512])
    u = 0
    for j in range(4):
        for g in range(3):
            self.dma("sp", oldkv[0:8, g], cache[g][j, 0:8], "gat", w=[("old", g)])
        self.mm(ps[0:8, 5, :], z8[0:8, 0:8], qj[j][0:8, 0:512], start=True, stop=False, r=["z8", ("qj", j, 0)], w=[("P", 5)])
        self.mm(ps[0:8, 6, 0:8], z8[0:8, 0:8], qj[j][0:8, 0:8], start=True, stop=False, r=["z8", ("qj", j, 0)], w=[("P", 6)])
        first = False
        for g, (win, d) in enumerate(GROUPS):
            for t in range(8):
                kb = Kg[u % NKG]
                kk = ("Kg", u % NKG)
                pz = Pz[u % NKG]
                pk = ("Pz", u % NKG)
                lgt = lg[u % 4]
                lk = ("lg", u % 4)
                u += 1
                a0 = d + t
                if g < 2:
                    src = cat[g][j, a0:a0 + 127 * d + 1:d]
                    deps = [("catb", g, j), ("catn", g, j, 0), ("catn", g, j, 1)]
                else:
                    src = kvs[2][j, a0 - 8:a0 - 8 + 127 * d + 1:d]
                    deps = [("kvs", 2, j, a) for a in range(0, LB[2] - 8, (LB[2] - 8 + 3) // 4)] + [("kvsn", 2, j, 0), ("kvsn", 2, j, 1)]
                self.dma("sp", kb, src, "gat", r=deps, w=[kk])
                self.mm(ps[:, 4, :], selt[0:8, t, :], qj[j][0:8, g * 512:(g + 1) * 512], r=["selt", ("qj", j, g)], w=[("P", 4)])
                self.v("dve", "tensor_tensor", prod, kb[:, 0, :], ps[:, 4, :], ALU.mult, r=[kk, ("P", 4)], w=["prod"])
                self.v("dve", "tensor_reduce", lgt, prod.rearrange("p (h e) -> p h e", h=8), AX.X, ALU.add, r=["prod"], w=[lk])
                self.v("dve", "tensor_tensor", lgt, lgt, biasS[:, g, :], ALU.add, r=[lk, ("biasS", g)], w=[lk])
                pe_ = pz[:, 0, :]
                self.act(pe_, lgt, AF.Exp, r=[lk], w=[pk])
                last = (g == 2 and t == 7)
                kv3 = kb[:, 1, :].rearrange("p (h e) -> p h e", h=8)
                self.v("dve", "tensor_tensor", kv3, kv3, pe_.unsqueeze(2).to_broadcast([128, 8, 64]), ALU.mult, r=[pk, kk], w=[kk])
                self.mm(ps[0:8, 5, :], onesel[:, t, :], kb[:, 1, :], start=False, stop=last, r=["onesel", kk], w=[("P", 5)])
                self.mm(ps[0:8, 6, 0:8], onesel[:, t, :], pe_, start=False, stop=last, r=["onesel", pk], w=[("P", 6)])
        self.v("dve", "tensor_copy", numS[0:8], ps[0:8, 5, :], r=[("P", 5)], w=["numS"])
        self.v("dve", "tensor_copy", denS[0:8], ps[0:8, 6, 0:8], r=[("P", 6)], w=["denS"])
        for g in range(3):
            self.v("dve", "tensor_tensor", tmpo[0:8], qj[j][0:8, g * 512:(g + 1) * 512], oldkv[0:8, g, 0, :], ALU.mult,
                   r=[("qj", j, g), ("old", g)], w=["tmpo"])
            self.v("dve", "tensor_reduce", pold[0:8], tmpo[0:8].rearrange("p (h e) -> p h e", h=8), AX.X, ALU.add, r=["tmpo"], w=["pold"])
            self.v("dve", "tensor_tensor", pold[0:8], pold[0:8], bold[0:8, g, :], ALU.add, r=["pold", ("bold", g)], w=["pold"])
            self.act(pold[0:8], pold[0:8], AF.Exp, r=["pold"], w=["pold"])
            self.v("dve", "tensor_tensor", denS[0:8], denS[0:8], pold[0:8], ALU.add, r=["denS", "pold"], w=["denS"])
            self.v("dve", "tensor_tensor", tmpo[0:8].rearrange("p (h e) -> p h e", h=8),
                   oldkv[0:8, g, 1, :].rearrange("p (h e) -> p h e", h=8), pold[0:8].unsqueeze(2).to_broadcast([8, 8, 64]), ALU.mult,
                   r=[("old", g), "pold"], w=["tmpo"])
            self.v("dve", "tensor_tensor", numS[0:8], numS[0:8], tmpo[0:8], ALU.add, r=["numS", "tmpo"], w=["numS"])
        self.v("dve", "reciprocal", denS[0:8], denS[0:8], r=["denS"], w=["denS"])
        self.v("dve", "tensor_tensor", oj[0:8].rearrange("p (h e) -> p h e", h=8), numS[0:8].rearrange("p (h e) -> p h e", h=8),
               denS[0:8].unsqueeze(2).to_broadcast([8, 8, 64]), ALU.mult, r=["numS", "denS"], w=["oj"])
        for hp in range(4):
            self.tr(ps[:, 7, 0:8], oj[0:8, hp * 128:(hp + 1) * 128], ident[0:8, 0:8], r=["oj", "ident"], w=[("P", 7)])
            self.v("dve", "tensor_tensor", self.attT[:, hp, SEG + 8 * j:SEG + 8 * j + 8], ps[:, 7, 0:8], szs[:, hp, 8 * j:8 * j + 8], ALU.mult,
                   r=[("P", 7), ("szs", hp)], w=[("attTs", hp, j)])
    A.pop()


Builder.phase_sample_attn = _phase_sample_attn
Builder.sample_copies = _sample_copies


_PROG_CACHE = {}


def kernel(**inputs):
    inp = {k: np.asarray(v) for k, v in inputs.items()}
    if "prog" not in _PROG_CACHE:
        _PROG_CACHE["prog"] = build_program(99)
    b = _PROG_CACHE["prog"]
    cores = list(range(NCORES))
    maps = make_in_maps(inp, cores)
    maps = [{k: np.ascontiguousarray(v, dtype=np.float32) for k, v in m.items() if k in b.din} for m in maps]
    res = run_bass_kernel_spmd(b.nc, maps, core_ids=cores).results
    f32 = np.float32
    y_p = np.zeros((2, SEQ, D), f32)
    y_s = np.zeros((32, 8, D), f32)
    kvp = [np.zeros((1, 2, w, 2, 8, 64), f32) for (w, d) in GROUPS]
    kvs = [np.zeros((1, 32, w, 2, 8, 64), f32) for (w, d) in GROUPS]
    conv_p = np.zeros((1, 2, 3, D), f32)
    conv_s = np.zeros((1, 32, 3, D), f32)
    C_p = np.zeros((1, 2, 4, 256, 128), f32)
    C_s = np.zeros((1, 32, 4, 256, 128), f32)
    n_p = np.zeros((1, 2, 4, 128), f32)
    n_s = np.zeros((1, 32, 4, 128), f32)
    m_p = np.zeros((1, 2, 4), f32)
    m_s = np.zeros((1, 32, 4), f32)
    for c in cores:
        r = res[c]
        bb, p = c // 4, c % 4
        sq = slice(4 * c, 4 * c + 4)
        y_p[bb, p * SEG:(p + 1) * SEG] = r["y_p"]
        y_s[sq] = np.asarray(r["y_s"]).reshape(4, 8, D)
        for g in range(3):
            kvs[g][0, sq] = np.asarray(r["kvs%d" % g]).reshape(4, -1, 2, 8, 64)
        Cs = np.asarray(r["Cs"])
        C_s[0, sq] = Cs[..., :256].transpose(0, 2, 3, 1)
        n_s[0, sq] = Cs[..., 256].transpose(0, 2, 1)
        m_s[0, sq] = np.asarray(r["ms"])[:, :, 0]
        conv_s[0, sq] = np.asarray(r["convs"]).transpose(0, 3, 2, 1).reshape(4, 3, D)
        if p == 3:
            for g in range(3):
                kvp[g][0, bb] = np.asarray(r["kvp%d" % g]).reshape(-1, 2, 8, 64)
            Cp = np.asarray(r["Cp"])
            C_p[0, bb] = Cp[..., :256].transpose(1, 2, 0)
            n_p[0, bb] = Cp[..., 256].T
            m_p[0, bb] = np.asarray(r["mp"])[:, 0]
            conv_p[0, bb] = np.asarray(r["convp"]).transpose(2, 1, 0).reshape(3, D)
    return (y_p, y_s, kvp[0], kvs[0], kvp[1], kvs[1], kvp[2], kvs[2],
            conv_p, conv_s, C_p, C_s, n_p, n_s, m_p, m_s)


def _mlstm_prefix(self):
    A, P, ps, C, M, hT = self.A, self.P, self.ps, self.C, self.M, self.hT
    NT = 48
    xf = self.inp("xf", [PREFIX - HALO, 1024])
    ident, tri = C["ident"], C["tri"]
    hTp = [A.alloc([8, 128], BF16) for _ in range(2)]
    gt_all = A.alloc([NT, 8])
    lf = A.alloc([NT, 4])
    ie = A.alloc([NT, 4])
    tot = A.alloc([NT, 4])
    incl = A.alloc([NT, 4])
    a_all = A.alloc([NT, 4])
    wk_all = A.alloc([NT, 4])
    negtv = A.alloc([NT])
    pm = A.alloc([4])
    row1 = A.alloc([1])
    dg = A.alloc([4])
    Mg_bc = A.alloc([4])
    cb4 = A.alloc([4, 8, 131])
    hT4 = A.alloc([8, 512], BF16)
    caccs = [A.alloc([8, 128]) for _ in range(2)]
    cexp = A.alloc([8, 128])
    c_act = [A.alloc([8, 128], BF16) for _ in range(2)]
    vaug = [A.alloc([4, 257], BF16) for _ in range(2)]
    kmw = [A.alloc([4, 128], BF16) for _ in range(2)]
    for i in range(2):
        self.v("pool", "memset", vaug[i][:, :, 256:257], 1.0, w=[("pvaug1", i)])

    def tile_src(i, slot):
        if i < 32:
            hp_ = hTp[i % 2]
            self.norm_tile(xf[i * 128:(i + 1) * 128, :], 128, hp_, "p", slot, ("hTp", i % 2), bank0=5)
            return hp_, [("hTp", i % 2)]
        ti = i - 32
        return hT[:, :, ti * 128:(ti + 1) * 128], [("hT", ti)]

    for i in range(NT):
        hTt, hk = tile_src(i, i)
        bank = 7 if i % 2 == 0 else 4
        gps = ps[:, bank, 0:8]
        for c in range(8):
            self.mm(gps, hTt[:, c, :], M["wg"][:, c, :], start=(c == 0), stop=(c == 7),
                    r=["wg"] + [(k, c) for k in hk], w=[("P", bank)])
        self.v("dve", "tensor_tensor", gt_all[:, i, :], gps, C["b_if_bc"], ALU.add, r=[("P", bank), "b_if_bc"], w=["gt_all"])
    tv48 = C["tvalid"][:, 0:NT]
    self.v("dve", "tensor_scalar", negtv, tv48, -1.0, None, op0=ALU.mult, r=["tvalid"], w=["negtv"])
    self.act(lf, gt_all[:, :, 4:8], AF.Exp, r=["gt_all"], w=["lf"], scale=-1.0)
    self.act(lf, lf, AF.Ln, r=["lf"], w=["lf"], bias=M["one1"])
    self.v("dve", "tensor_tensor", lf, lf, negtv.unsqueeze(2).to_broadcast([128, NT, 4]), ALU.mult, r=["lf", "negtv"], w=["lf"])
    self.v("dve", "tensor_tensor", ie, gt_all[:, :, 0:4], tv48.unsqueeze(2).to_broadcast([128, NT, 4]), ALU.mult,
           r=["gt_all", "tvalid"], w=["ie"])
    self.v("dve", "tensor_tensor", ie, ie, M["negbig"][:, 0:NT].unsqueeze(2).to_broadcast([128, NT, 4]), ALU.add,
           r=["ie", "negbig"], w=["ie"])
    lf2 = lf.rearrange("p t h -> p (t h)")
    self.mm(ps[:, 0, 0:NT * 4], tri, lf2, r=["lf", "tri"], w=[("P", 0)])
    self.mm(ps[:, 1, 0:NT * 4], M["ones"], lf2, r=["lf", "ones"], w=[("P", 1)])
    self.v("dve", "tensor_copy", tot.rearrange("p t h -> p (t h)"), ps[:, 1, 0:NT * 4], r=[("P", 1)], w=["tot"])
    for h in range(4):
        self.v("dve", "tensor_tensor_scan", incl[:, :, h], M["ones"][:, 0:NT], tot[:, :, h], 0.0, op0=ALU.mult, op1=ALU.add,
               r=["tot", "ones"], w=[("incl", h)])
    inclk = [("incl", h) for h in range(4)]
    self.v("dve", "tensor_tensor", a_all, ie, incl, ALU.subtract, r=["ie"] + inclk, w=["a_all"])
    self.v("dve", "tensor_tensor", a_all, a_all, tot, ALU.add, r=["a_all", "tot"], w=["a_all"])
    self.v("dve", "tensor_tensor", a_all.rearrange("p t h -> p (t h)"), a_all.rearrange("p t h -> p (t h)"), ps[:, 0, 0:NT * 4],
           ALU.subtract, r=["a_all", ("P", 0)], w=["a_all"])
    self.v("dve", "tensor_reduce", pm, a_all.rearrange("p t h -> p h t"), AX.X, ALU.max, r=["a_all"], w=["pm"])
    self.mm(ps[0:4, 2, 0:128], pm, ident, r=["pm", "ident"], w=[("P", 2)])
    self.v("dve", "tensor_reduce", row1[0:4], ps[0:4, 2, 0:128], AX.X, ALU.max, r=[("P", 2)], w=["row1"])
    self.v("dve", "tensor_scalar", row1[0:4], row1[0:4], 0.0, None, op0=ALU.max, r=["row1"], w=["row1"])
    self.v("dve", "tensor_tensor", dg[0:4], ident[0:4, 0:4], row1[0:4, 0:1].to_broadcast([4, 4]), ALU.mult, r=["row1", "ident"], w=["pdg"])
    self.mm(ps[:, 2, 128:132], M["ones"][0:4, :], dg[0:4], r=["pdg", "ones"], w=[("P", 2)])
    self.v("dve", "tensor_copy", Mg_bc, ps[:, 2, 128:132], r=[("P", 2)], w=["Mg_bc"])
    self.v("dve", "tensor_tensor", wk_all, a_all, Mg_bc.unsqueeze(1).to_broadcast([128, NT, 4]), ALU.subtract,
           r=["a_all", "Mg_bc"], w=["wk_all"])
    self.act(wk_all, wk_all, AF.Exp, r=["wk_all"], w=["wk_all"])
    self.v("dve", "tensor_tensor", M["m_bc"], incl[:, NT - 1, :], Mg_bc, ALU.add, r=inclk + ["Mg_bc"], w=["m_bc"])
    self.mm(ps[0:4, 2, 136:137], M["m_bc"][0:1, 0:4], M["ones"][0:1, 0:1], r=["m_bc", "ones"], w=[("P", 2)])
    self.v("dve", "tensor_copy", M["m_row"][0:4], ps[0:4, 2, 136:137], r=[("P", 2)], w=["m_row"])
    cw, cbias = C["conv_wT"], C["conv_bT"]
    hist = A.alloc([8, 3])
    self.v("pool", "memset", hist, 0.0, w=["hist"])
    for g0 in range(0, NT, 4):
        if g0 < 32:
            for q in range(4):
                i = g0 + q
                self.norm_tile(xf[i * 128:(i + 1) * 128, :], 128, hT4[:, :, q * 128:(q + 1) * 128], "p", NT + i, ("hT4", q), bank0=5)
            src4 = hT4
            hk4 = [("hT4", q) for q in range(4)]
            tsl = lambda q: hT4[:, :, q * 128:(q + 1) * 128]
        else:
            t0 = g0 - 32
            src4 = hT[:, :, t0 * 128:(t0 + 4) * 128]
            hk4 = [("hT", t0 + q) for q in range(4)]
            tsl = lambda q, t0=t0: hT[:, :, (t0 + q) * 128:(t0 + q + 1) * 128]
        for fb in range(8):
            bank = fb % 2
            for c in range(8):
                self.mm(ps[:, bank, :], M["wxm"][:, c, fb * 128:(fb + 1) * 128], src4[:, c, :], start=(c == 0), stop=(c == 7),
                        r=["wxm"] + [(k, c) for k in hk4], w=[("P", bank)])
            self.act(cb4[:, :, fb, 3:131], ps[:, bank, :].rearrange("p (q t) -> p q t", q=4), AF.Copy,
                     r=[("P", bank)], w=[("pcb", q) for q in range(4)])
        def tile_body(q):
            i = g0 + q
            par = i % 2
            cacc = caccs[par]
            vb0 = 2 if par == 0 else 0
            kmb = 4 if par == 0 else 7
            hTt, hk = tsl(q), [hk4[q]]
            tv = C["tvalid"][:, i:i + 1]
            cbp = cb4[:, q]
            for half in range(2):
                for c in range(8):
                    self.mm(ps[:, vb0 + half, :], hTt[:, c, :], M["wxm"][:, c, half * 512:(half + 1) * 512], start=(c == 0), stop=(c == 7),
                            r=["wxm"] + [(k, c) for k in hk], w=[("P", vb0 + half)])
                self.act(vaug[par][:, 2 * half:2 * half + 2, 0:256], ps[:, vb0 + half, :].rearrange("p (h v) -> p h v", h=2), AF.Copy,
                         r=[("P", vb0 + half), ("pvaug1", par)], w=[("pvaug", par)])
                yield
            for fb in range(8):
                self.act(cacc[:, fb, :], cbp[:, fb, 3:131], AF.Identity, r=[("pcb", q), "conv_wT", "conv_bT"], w=[("pcacc", par, fb)],
                         scale=cw[:, fb, 3:4], bias=cbias[:, fb:fb + 1])
            yield
            for j in range(3):
                for fb in range(8):
                    self.v("dve", "scalar_tensor_tensor", cacc[:, fb, :], cbp[:, fb, j:j + 128], cw[:, fb, j:j + 1], cacc[:, fb, :],
                           op0=ALU.mult, op1=ALU.add, r=[("pcb", q), ("pcacc", par, fb)], w=[("pcacc", par, fb)])
                yield
            for fb in range(8):
                self.act(c_act[par][:, fb, :], cacc[:, fb, :], AF.Silu, r=[("pcacc", par, fb)], w=[("pc_act", par, fb)])
            yield
            kmps = ps[:, kmb, :].rearrange("p (h k) -> p h k", h=4)
            for h in range(4):
                for ec in range(2):
                    self.mm(kmps[:, h, :], c_act[par][:, 2 * h + ec, :], M["wmk"][:, ec, h, :], start=(ec == 0), stop=(ec == 1),
                            r=[("pc_act", par, 2 * h + ec), "wmk"], w=[("P", kmb)])
            yield
            self.v("dve", "tensor_tensor", kmw[par], kmps, wk_all[:, i, :].unsqueeze(2).to_broadcast([128, 4, 128]), ALU.mult,
                   r=[("P", kmb), "wk_all"], w=[("pkmw", par)])
            yield
            for h in range(4):
                bk = vb0 + (h % 2)
                dps = ps[:, bk, 0:257]
                self.mm(dps, kmw[par][:, h, :], vaug[par][:, h, :], r=[("pkmw", par), ("pvaug", par)], w=[("P", bk)])
                self.v("dve", "tensor_tensor", M["CT"][:, h, :], M["CT"][:, h, :], dps, ALU.add, r=[("P", bk), "CT"], w=["CT"])
                yield

        for q in range(4):
            i = g0 + q
            tv = C["tvalid"][:, i:i + 1]
            cbp = cb4[:, q]
            self.v("pool", "tensor_copy", cbp[:, :, 0:3], hist, r=["hist"], w=[("pcb", q)])
            self.v("dve", "tensor_scalar", hist, cbp[:, :, 128:131], tv, M["one1"], op0=ALU.mult, op1=ALU.mult,
                   r=[("pcb", q), "tvalid", "one1"], w=["hist"])
        for pair in ((0, 1), (2, 3)):
            live = [tile_body(q) for q in pair]
            while live:
                for gen in list(live):
                    try:
                        next(gen)
                    except StopIteration:
                        live.remove(gen)
    self.v("pool", "tensor_copy", M["convbuf"][:, :, 0:3], hist, r=["hist"], w=[("cb", 0), ("cb", 1)])


Builder.mlstm_prefix = _mlstm_prefix
```

```python
import numpy as np
import concourse.bass as bass
import concourse.mybir as mybir
from concourse.bass_utils import run_bass_kernel_spmd

F32 = mybir.dt.float32
BF16 = mybir.dt.bfloat16
AF = mybir.ActivationFunctionType
ALU = mybir.AluOpType
AX = mybir.AxisListType

NCORES = 8
D = 1024
SEQ = 8192
SEG = 2048
NLT = SEG // 128
HALO = 2048
PREFIX = 6144
TS = 32
PW = 10248
OFF_Q, OFF_K, OFF_V, OFF_ZA, OFF_XM, OFF_ZM, OFF_OM, OFF_I, OFF_F, OFF_GA, OFF_GM = (
    0, 1536, 3072, 4608, 5120, 6144, 7168, 8192, 8196, 8200, 9224)
GROUPS = ((128, 1), (512, 4), (2048, 16))
EPS = 1e-6
NEG = -30000.0
RAW_GAP = 1

DEBUG = {}


class _Op:
    __slots__ = ("eng", "fn", "deps", "stream", "signal", "val", "idx", "pos", "slot")


class Prog:
    ENGS = ("pe", "act", "dve", "pool", "sp")

    def __init__(self, nc):
        self.nc = nc
        self.ops = {e: [] for e in self.ENGS}
        self.lastw = {}
        self.readers = {}
        self.streams = {}
        self.barrier_deps = []
        self.nops = 0

    def add(self, eng, fn, r=(), w=(), dma=None):
        op = _Op()
        op.eng = eng
        op.fn = fn
        op.stream = dma
        op.signal = False
        op.val = None
        op.idx = self.nops
        self.nops += 1
        deps = {}
        for k in r:
            d = self.lastw.get(k)
            if d is not None:
                deps[d.idx] = (d, True)
            if isinstance(k, tuple) and k[0] == "P":
                for d in self.readers.get(k, ()):
                    if d.eng != eng and d.idx not in deps:
                        deps[d.idx] = (d, False)
        for k in w:
            d = self.lastw.get(k)
            if d is not None and d.idx not in deps:
                deps[d.idx] = (d, False)
            for d in self.readers.get(k, ()):
                if d.idx not in deps:
                    deps[d.idx] = (d, False)
        for d in self.barrier_deps:
            if d.idx not in deps:
                deps[d.idx] = (d, True)
        op.deps = list(deps.values())
        for k in w:
            self.lastw[k] = op
            self.readers[k] = []
        for k in r:
            self.readers.setdefault(k, []).append(op)
        op.pos = len(self.ops[eng])
        self.ops[eng].append(op)
        if dma is not None:
            self.streams.setdefault(dma, []).append(op)
        return op

    KSEM = 8
    KSEM_STREAM = {"cpy": 32}
    PERSIST = ("cpy",)

    def kof(self, s):
        return self.KSEM_STREAM.get(s, self.KSEM)

    def barrier(self):
        deps = []
        for e in self.ENGS:
            if self.ops[e]:
                for op in reversed(self.ops[e]):
                    if op.stream is None:
                        deps.append(op)
                        break
        for s, lst in self.streams.items():
            if s in self.PERSIST:
                continue
            deps.extend(lst[-self.kof(s):])
        self.barrier_deps = deps
        self.lastw = {k: v for k, v in self.lastw.items() if v.stream in self.PERSIST}
        self.readers = {}

    def emit(self, final_streams):
        nc = self.nc
        K = self.KSEM
        need = {}
        for e in self.ENGS:
            for op in self.ops[e]:
                lst = []
                for d, raw in op.deps:
                    if d.stream is None and d.eng == e:
                        if e in ("pe", "sp"):
                            continue
                        if not raw:
                            continue
                        if op.pos - d.pos > RAW_GAP:
                            continue
                    d.signal = True
                    lst.append(d)
                need[op.idx] = lst
        for e in self.ENGS:
            cnt = 0
            for op in self.ops[e]:
                if op.stream is None and op.signal:
                    cnt += 1
                    op.val = cnt
        for s, lst in self.streams.items():
            Ks = self.kof(s)
            for i, op in enumerate(lst):
                op.slot = i % Ks
                op.val = 16 * (i // Ks + 1)
        import contextlib
        with contextlib.ExitStack() as st:
            sems = {e: st.enter_context(nc.semaphore("s_" + e)) for e in self.ENGS}
            ssems = {s: [st.enter_context(nc.semaphore("d_%s_%d" % (s, k))) for k in range(min(self.kof(s), len(lst)))]
                     for s, lst in self.streams.items()}
            block = st.enter_context(nc.Block())

            def run(e, eng):
                waited = {}

                def wait(key, v):
                    if v > waited.get(key, 0):
                        waited[key] = v
                        sem = ssems[key[1]][key[2]] if key[0] == "s" else sems[key[1]]
                        eng.wait_ge(sem, v)

                for op in self.ops[e]:
                    w = {}
                    for d in need[op.idx]:
                        key = ("s", d.stream, d.slot) if d.stream is not None else ("e", d.eng)
                        if d.val > w.get(key, 0):
                            w[key] = d.val
                    for key, v in w.items():
                        wait(key, v)
                    if op.stream is not None and op.val > 16:
                        wait(("s", op.stream, op.slot), op.val - 16)
                    ins = op.fn(eng)
                    if op.stream is not None:
                        ins.then_inc(ssems[op.stream][op.slot], 16)
                    elif op.signal:
                        ins.then_inc(sems[e], 1)
                if e == "sp":
                    for s in final_streams:
                        if s in self.streams:
                            for op in self.streams[s][-self.kof(s):]:
                                wait(("s", s, op.slot), op.val)

            block.tensor(lambda t: run("pe", t))
            block.scalar(lambda t: run("act", t))
            block.vector(lambda t: run("dve", t))
            block.gpsimd(lambda t: run("pool", t))
            block.sync(lambda t: run("sp", t))


class Arena:
    def __init__(self, nc, words):
        self.t = nc.alloc_sbuf_tensor("arena", [128, words], F32)
        self.words = words
        self.top = 0
        self.marks = []

    def push(self):
        self.marks.append(self.top)

    def pop(self):
        self.top = self.marks.pop()

    def alloc(self, shape, dt=F32):
        n = int(np.prod(shape))
        words = (n + 1) // 2 if dt == BF16 else n
        words = (words + 7) // 8 * 8
        assert self.top + words <= self.words, ("SBUF arena overflow", self.top, words, self.words)
        ap = self.t[:, self.top:self.top + words]
        self.top += words
        if dt == BF16:
            ap = ap.bitcast(BF16)
        ap = ap[:, 0:n]
        if len(shape) == 2:
            ap = ap.rearrange("p (a b) -> p a b", a=shape[0])
        elif len(shape) == 3:
            ap = ap.rearrange("p (a b c) -> p a b c", a=shape[0], b=shape[1])
        elif len(shape) == 4:
            ap = ap.rearrange("p (a b c d) -> p a b c d", a=shape[0], b=shape[1], c=shape[2])
        return ap


def _t5_bucket(dist):
    n = np.asarray(dist).astype(np.int64)
    max_exact = 16
    nf = np.maximum(n, 1).astype(np.float32)
    large = max_exact + (np.log(nf / max_exact) / np.log(np.float32(2048 / max_exact))
                         * (32 - max_exact)).astype(np.int64)
    large = np.minimum(large, 31)
    return np.where(n < max_exact, n, large).astype(np.int32)


def _consts():
    c = {}
    c["ident"] = np.eye(128, dtype=np.float32)
    c["antiid"] = np.eye(128, dtype=np.float32)[::-1].copy()
    s = np.arange(128)
    c["tri"] = (s[:, None] <= s[None, :]).astype(np.float32)
    c["cmaskT"] = np.where(s[:, None] <= s[None, :], 0.0, NEG).astype(np.float32)
    oh = np.zeros((32, 3, 129), np.float32)
    for g, (_, d) in enumerate(GROUPS):
        b = _t5_bucket(np.arange(129) * d)
        oh[b, g, np.arange(129)] = 1.0
    c["ohd"] = oh
    selp = np.zeros((5, 128), np.float32); selp[0] = 1.0
    sels = np.zeros((5, 128), np.float32)
    for i in range(TS):
        sels[1 + i // 8, i] = 1.0
    c["selp"] = selp
    c["sels"] = sels
    return c


def _fm(v):
    return np.ascontiguousarray(v.reshape(-1, 128).T)


class Builder:
    def __init__(self):
        nc = bass.Bass("TRN2", target_bir_lowering=False)
        self.nc = nc
        self.P = Prog(nc)
        self.A = Arena(nc, 53184)
        self.ps = nc.alloc_psum_tensor("psum", [128, 8, 512], F32)
        self.din = {}
        self.dout = {}
        self.uid = 0

    def inp(self, name, shape, dt=F32):
        t = self.nc.dram_tensor(name, list(shape), dt, kind="ExternalInput").ap()
        self.din[name] = t
        return t

    def outp(self, name, shape, dt=F32):
        t = self.nc.dram_tensor(name, list(shape), dt, kind="ExternalOutput").ap()
        self.dout[name] = t
        return t

    def scratch(self, name, shape, dt=F32):
        return self.nc.dram_tensor(name, list(shape), dt).ap()

    def mm(self, out, lhsT, rhs, start=True, stop=True, r=(), w=()):
        return self.P.add("pe", lambda e: e.matmul(out, lhsT, rhs, start=start, stop=stop), r, w)

    def tr(self, out, in_, ident, r=(), w=()):
        return self.P.add("pe", lambda e: e.transpose(out, in_, ident), r, w)

    def act(self, out, in_, func, r=(), w=(), **kw):
        return self.P.add("act", lambda e: e.activation(out=out, in_=in_, func=func, **kw), r, w)

    def v(self, eng, name, *args, r=(), w=(), **kw):
        return self.P.add(eng, lambda e: getattr(e, name)(*args, **kw), r, w)

    def dma(self, eng, out, in_, stream, r=(), w=(), **kw):
        return self.P.add(eng, lambda e: e.dma_start(out=out, in_=in_, **kw), r, w, dma=stream)

    def dbg(self, name, ap, shape, dt=F32, r=()):
        if not DEBUG.get(name):
            return
        o = self.outp("dbg_" + name, shape, dt)
        self.P.barrier()
        self.dma("sp", o, ap, "dbg", r=r)

    def key(self, base):
        self.uid += 1
        return (base, self.uid)

    def phase0(self):
        A, P, ps = self.A, self.P, self.ps
        C = self.C = {}

        def load(name, shape, parts=128, dt=F32, eng="sp", src=None):
            t = A.alloc(list(shape[1:]), dt)
            src = self.inp(name, shape) if src is None else src
            self.dma(eng, t[0:parts], src, "ld0", w=[name])
            C[name] = t
            return t

        load("ident", [128, 128])
        C["ident_bf"] = A.alloc([128], BF16)
        self.dma("pool", C["ident_bf"], self.din["ident"], "ldc", w=["ident_bf"])
        C["antiid_bf"] = A.alloc([128], BF16)
        self.dma("pool", C["antiid_bf"], self.inp("antiid", [128, 128]), "ldc", w=["antiid_bf"])
        load("tri", [128, 128])
        load("antiid", [128, 128], src=self.din["antiid"])
        C["cmaskT_bf"] = A.alloc([128], BF16)
        self.dma("pool", C["cmaskT_bf"], self.inp("cmaskT", [128, 128]), "ldc", w=["cmaskT_bf"])
        load("ohd", [32, 3, 129], parts=32)
        load("selp", [5, 128], parts=5)
        load("sels", [5, 128], parts=5)
        load("rel_table", [32, 24], parts=32)
        for nm in ("gainT", "conv_bT", "m_normT", "m_skipT"):
            load(nm, [128, 8])
        load("b_adaT", [128, 24])
        load("conv_wT", [128, 8, 4])
        load("b_if_bc", [128, 8])
        load("tvalid", [128, 64])
        load("cT", [128, 8, 5])

        siluT = A.alloc([8, 5], BF16)
        self.act(siluT, C["cT"], AF.Silu, r=["cT"], w=["siluT"])

        ada = A.alloc([24, 5])
        mult = A.alloc([8, 5])
        gate_p = A.alloc([1024])
        gate_s = A.alloc([1024])
        A.push()
        load("b_gate_rows", [5, 1024], parts=5)
        gate_rows = A.alloc([1024])
        w_ada = self.inp("w_ada", [1024, 3072])
        wada = A.alloc([8, 3072], BF16)
        wv = w_ada.rearrange("(c p) n -> p c n", p=128)
        for c in range(8):
            self.dma("pool", wada[:, c, :], wv[:, c, :], "ldw", w=[("wada", c)])
        adaps = ps[:, 0, 0:120].rearrange("p (a b) -> p a b", a=24)
        for cb in range(24):
            for c in range(8):
                self.mm(adaps[:, cb, :], wada[:, c, cb * 128:(cb + 1) * 128], siluT[:, c, :],
                        start=(c == 0), stop=(c == 7), r=[("wada", c), "siluT"], w=[("P", 0)])
        for half in range(2):
            for c in range(8):
                self.mm(ps[0:5, 1 + half, :], siluT[:, c, :], wada[:, c, 2048 + half * 512:2048 + (half + 1) * 512],
                        start=(c == 0), stop=(c == 7), r=[("wada", c), "siluT"], w=[("P", 1 + half)])
        self.v("dve", "tensor_tensor", ada, adaps, C["b_adaT"].unsqueeze(2).to_broadcast([128, 24, 5]), ALU.add,
               r=[("P", 0), "b_adaT"], w=["ada"])
        self.v("dve", "tensor_scalar", mult, ada[:, 8:16, :], 1.0, None, op0=ALU.add, r=["ada"], w=["mult"])
        self.v("dve", "tensor_tensor", mult, mult, C["gainT"].unsqueeze(2).to_broadcast([128, 8, 5]), ALU.mult,
               r=["mult", "gainT"], w=["mult"])
        C["mult"] = mult
        C["shift"] = ada[:, 0:8, :]
        C["ada"] = ada
        for half in range(2):
            self.v("dve", "tensor_tensor", gate_rows[0:5, half * 512:(half + 1) * 512], ps[0:5, 1 + half, :],
                   C["b_gate_rows"][0:5, half * 512:(half + 1) * 512], ALU.add,
                   r=[("P", 1 + half), "b_gate_rows"], w=[("gate_rows", half)])
        for half in range(2):
            sl = slice(half * 512, (half + 1) * 512)
            self.mm(ps[:, 3, :], C["selp"][0:5, :], gate_rows[0:5, sl], r=[("gate_rows", half), "selp"], w=[("P", 3)])
            self.act(gate_p[:, sl], ps[:, 3, :], AF.Copy, r=[("P", 3)], w=[("gate_p", half)])
            self.mm(ps[:, 4, :], C["sels"][0:5, :], gate_rows[0:5, sl], r=[("gate_rows", half), "sels"], w=[("P", 4)])
            self.act(gate_s[:, sl], ps[:, 4, :], AF.Copy, r=[("P", 4)], w=[("gate_s", half)])
        A.pop()
        P.barrier()
        C["gate_p"] = gate_p
        C["gate_s"] = gate_s
        self.dbg("ada", ada, [128, 24, 5], r=["ada"])
        self.dbg("gate_s", gate_s, [128, 1024], r=[("gate_s", 0), ("gate_s", 1)])

    def norm_tile(self, xsrc, ntok, hT_dst, kind, slot, hkey, bank0=6):
        A, P, ps, C = self.A, self.P, self.ps, self.C
        W = self.W1
        i3, i2 = slot % len(W["xt"]), slot % 2
        xt = W["xt"][i3]
        self.dma("sp", xt[0:ntok], xsrc, "ldx", w=[("xt", i3)])
        ss = W["ss"][:, slot % 4:slot % 4 + 1]
        self.act(W["junk"][0:ntok], xt[0:ntok], AF.Square, r=[("xt", i3)], w=["junk", ("ss", slot % 4)],
                 accum_out=ss[0:ntok])
        self.v("dve", "tensor_scalar", ss[0:ntok], ss[0:ntok], 1.0 / D, EPS, op0=ALU.mult, op1=ALU.add,
               r=[("ss", slot % 4)], w=[("ss", slot % 4)])
        self.act(ss[0:ntok], ss[0:ntok], AF.Ln, r=[("ss", slot % 4)], w=[("ss", slot % 4)])
        self.act(ss[0:ntok], ss[0:ntok], AF.Exp, r=[("ss", slot % 4)], w=[("ss", slot % 4)], scale=-0.5)
        xn = W["xn"][i2]
        self.act(xn[0:ntok], xt[0:ntok], AF.Copy, r=[("xt", i3), ("ss", slot % 4)], w=[("xn", i2)], scale=ss[0:ntok])
        if DEBUG.get("stage", 9) < 1:
            return
        bank = bank0 + i2
        pt = ps[:, bank, :].bitcast(BF16).rearrange("p (c t) -> p c t", c=8)
        for c in range(8):
            self.tr(pt[:, c, 0:ntok], xn[0:ntok, c * 128:(c + 1) * 128], C["ident_bf"][0:ntok, 0:ntok],
                    r=[("xn", i2), "ident_bf"], w=[("P", bank0 + i2)])
        if DEBUG.get("stage", 9) < 2:
            return
        for c in range(8):
            if kind == "p":
                if True:
                    self.act(hT_dst[:, c, :], pt[:, c, 0:ntok], AF.Identity, r=[("P", bank0 + i2), "mult", "ada"], w=[(hkey, c)],
                             scale=C["mult"][:, c, 0:1], bias=C["shift"][:, c, 0:1])
                else:
                    self.v("dve", "tensor_scalar", hT_dst[:, c, :], pt[:, c, 0:ntok], C["mult"][:, c, 0:1], C["shift"][:, c, 0:1],
                           op0=ALU.mult, op1=ALU.add, r=[("P", bank0 + i2), "mult", "ada"], w=[(hkey, c)])
            else:
                tmp = W["stmp"]
                tmp0 = W["stmp0"]
                self.act(tmp0, pt[:, c, 0:ntok], AF.Copy, r=[("P", bank0 + i2)], w=["stmp0"])
                self.v("dve", "tensor_tensor", tmp.rearrange("p (s t) -> p s t", s=4),
                       tmp0.rearrange("p (s t) -> p s t", s=4),
                       C["mult"][:, c, 1:5].unsqueeze(2).to_broadcast([128, 4, 8]), ALU.mult,
                       r=["stmp0", "mult"], w=["stmp"])
                self.v("dve", "tensor_tensor", hT_dst[:, c, :].rearrange("p (s t) -> p s t", s=4),
                       tmp.rearrange("p (s t) -> p s t", s=4),
                       C["shift"][:, c, 1:5].unsqueeze(2).to_broadcast([128, 4, 8]), ALU.add,
                       r=["stmp", "ada"], w=[(hkey, c)])

    def phase1(self):
        A = self.A
        self.hT = A.alloc([8, HALO + SEG + TS], BF16)
        A.push()
        self.W1 = {
            "xt": [A.alloc([1024]) for _ in range(3)],
            "ss": A.alloc([4]),
            "junk": A.alloc([1024], BF16),
            "xn": [A.alloc([1024], BF16) for _ in range(2)],
            "stmp": A.alloc([32]),
            "stmp0": A.alloc([32]),
        }
        xh = self.inp("xh", [HALO + SEG, 1024])
        xs = self.inp("xs", [TS, 1024])
        slot = 0
        if not DEBUG.get("skip_s"):
            self.norm_tile(xs, TS, self.hT[:, :, HALO + SEG:HALO + SEG + TS], "s", slot, ("hT", 32))
        slot += 1
        for ti in range(DEBUG.get("ntiles", (HALO + SEG) // 128)):
            self.norm_tile(xh[ti * 128:(ti + 1) * 128, :], 128, self.hT[:, :, ti * 128:(ti + 1) * 128], "p", slot, ("hT", ti))
            slot += 1
        self.dbg("hT", self.hT, [128, 8, HALO + SEG + TS], BF16, r=[])
        self.dbg("xn", self.W1["xn"][1], [128, 1024], BF16, r=[])
        A.pop()


def build_program(upto=99):
    b = Builder()
    b.phase0()
    b.sample_copies()
    b.w_in = b.inp("w_in", [1024, PW])
    if upto >= 1:
        b.P.barrier()
        b.phase1()
    if upto >= 2:
        b.P.barrier()
        b.attT = b.A.alloc([4, SEG + TS], BF16)
        b.C["F"] = b.A.alloc([3, 129])
        b.A.push()
        b.phase_bias()
        b.P.barrier()
        if DEBUG.get("att_stage", 9) >= 1:
            b.phase_attention()
        b.A.pop()
        b.P.barrier()
        if not DEBUG.get("no_sattn"):
            b.phase_sample_attn()
        b.dbg("attT2", b.attT, [128, 4, SEG + TS], BF16)
    if upto >= 3:
        b.P.barrier()
        b.phase_mlstm()
    if upto >= 4:
        b.P.barrier()
        b.phase_out()
    if DEBUG.get("dmult"):
        DEBUG["mult_end"] = True
        b.dbg("mult_end", b.C["mult"], [128, 8, 5])
    b.P.emit(final_streams=list(b.P.streams.keys()))
    return b


def make_in_maps(inp, cores):
    consts = _consts()
    f32 = np.float32
    maps = []
    xp = inp["x_prompt"]
    for c in cores:
        b, p = c // 4, c % 4
        s0 = p * SEG
        m = dict(consts)
        ext = np.zeros((PREFIX + SEG, D), f32)
        lo = s0 - PREFIX
        src_lo = max(lo, 0)
        ext[src_lo - lo:] = xp[b, src_lo:s0 + SEG]
        m["xf"] = np.ascontiguousarray(ext[:PREFIX - HALO])
        m["xh"] = np.ascontiguousarray(ext[PREFIX - HALO:])
        m["xs"] = np.ascontiguousarray(inp["x_sample"][4 * c:4 * c + 4].reshape(TS, D))
        tv = np.zeros(64, f32)
        tv[(src_lo - lo) // 128:] = 1.0
        m["tvalid"] = np.ascontiguousarray(np.broadcast_to(tv, (128, 64)))
        call = np.concatenate([inp["c_prompt"][b:b + 1], inp["c_sample"][4 * c:4 * c + 4]], 0)
        m["cT"] = np.ascontiguousarray(call.T.reshape(8, 128, 5).transpose(1, 0, 2))
        m["w_ada"] = inp["w_ada"][0]
        m["rel_table"] = inp["rel_table"]
        m["gainT"] = _fm(inp["norm_gain"][0])
        m["conv_bT"] = _fm(inp["conv_b"][0])
        m["m_normT"] = _fm(inp["m_norm"][0])
        m["m_skipT"] = _fm(inp["m_skip"][0])
        m["b_adaT"] = _fm(inp["b_ada"][0])
        m["conv_wT"] = np.ascontiguousarray(inp["conv_w"][0].reshape(4, 8, 128).transpose(2, 1, 0))
        m["b_gate_rows"] = np.ascontiguousarray(np.broadcast_to(inp["b_ada"][0][2048:], (5, 1024)))
        m["fgain_bc"] = np.ascontiguousarray(np.broadcast_to(inp["final_gain"], (128, 1024)))
        m["b_if_bc"] = np.ascontiguousarray(np.broadcast_to(inp["b_if"][0], (128, 8)))
        m["w_in"] = inp["w_in"][0]
        st_ = np.zeros((8, 8, 128), f32)
        for t in range(8):
            st_[t, t, :] = 1.0
        m["selt"] = st_
        osl = np.zeros((128, 8, 8), f32)
        for t in range(8):
            osl[:, t, t] = 1.0
        m["onesel"] = osl
        for g, nm in enumerate(("cache_kv_w128", "cache_kv_w512", "cache_kv_w2048")):
            m["cache%d" % g] = np.ascontiguousarray(inp[nm][0, 4 * c:4 * c + 4].reshape(4, -1, 2, 512))
        m["w_pa"] = inp["w_pa"][0]
        m["w_pm"] = inp["w_pm"][0]
        m["w_out"] = inp["w_out"][0]
        m["w_mq"] = inp["w_mq"][0]
        m["w_mk"] = inp["w_mk"][0]
        eh = np.zeros((4, 4, 128), f32)
        for h in range(4):
            eh[h, h, :] = 1.0
        m["ehsel"] = eh
        sq = slice(4 * c, 4 * c + 4)
        Cst = inp["state_C"][0, sq]
        nst = inp["state_n"][0, sq]
        c0 = np.concatenate([Cst.transpose(0, 3, 1, 2), nst.transpose(0, 2, 1)[..., None]], axis=-1)
        m["C0T"] = np.ascontiguousarray(c0)
        mst = inp["state_m"][0, sq]
        m["m0row"] = np.ascontiguousarray(mst[:, :, None])
        m["m0bc"] = np.ascontiguousarray(np.broadcast_to(mst[:, None, :], (4, 128, 4)))
        cvs = inp["state_conv"][0, sq]
        m["conv0"] = np.ascontiguousarray(cvs.reshape(4, 3, 8, 128).transpose(0, 3, 2, 1))
        m["coremask"] = np.full((128, 128), NEG if p == 0 else 0.0, f32)
        sm = np.zeros((128, 4, 128), f32)
        for e in range(64):
            sm[e, 0, e] = 1.0
            sm[e, 1, 64 + e] = 1.0
            sm[64 + e, 2, e] = 1.0
            sm[64 + e, 3, 64 + e] = 1.0
        m["selmats"] = sm
        maps.append(m)
    return maps


def run_cores(inp, cores, upto=99):
    b = build_program(upto)
    maps = make_in_maps(inp, cores)
    maps = [{k: np.ascontiguousarray(v, dtype=np.float32) for k, v in m.items() if k in b.din} for m in maps]
    res = run_bass_kernel_spmd(b.nc, maps, core_ids=list(range(len(cores))))
    return res.results


def _phase_bias(self):
    A, P, ps, C = self.A, self.P, self.ps, self.C
    F_sb = C["F"]
    gv = A.alloc([3, 2, 256])
    self.v("pool", "memset", gv[0:8], NEG, w=["gv"])
    for g in range(3):
        self.mm(ps[0:8, 5, 0:129], C["rel_table"][0:32, g * 8:(g + 1) * 8], C["ohd"][0:32, g, :],
                r=["rel_table", "ohd"], w=[("P", 5)])
        self.act(F_sb[0:8, g, :], ps[0:8, 5, 0:129], AF.Copy, r=[("P", 5)], w=[("F", g)])
        self.v("pool", "tensor_copy", gv[0:8, g, 1, 127:255], F_sb[0:8, g, 0:128], r=[("F", g), "gv"], w=[("gv", g)])
        self.v("pool", "tensor_copy", gv[0:8, g, 0, 0:128], F_sb[0:8, g, 1:129], r=[("F", g), "gv"], w=[("gv", g)])
    gvd = self.scratch("gvd", [8, 3, 2, 256])
    self.dma("sp", gvd, gv[0:8], "gvw", r=[("gv", 0), ("gv", 1), ("gv", 2)], w=["gvd"])
    biasH = A.alloc([24, 256], BF16)
    for g in range(3):
        for h in range(8):
            for kb in range(2):
                off = ((h * 3 + g) * 2 + kb) * 256
                src = bass.AP(tensor=gvd.tensor, offset=off, ap=[[1, 128], [1, 128]])
                self.dma("pool", biasH[:, g * 8 + h, kb * 128:(kb + 1) * 128], src, "ldc", r=["gvd"], w=[("biasH", g)])
    C["biasH"] = biasH
    cm = A.alloc([128], BF16)
    self.dma("pool", cm, self.inp("coremask", [128, 128]), "ldc", w=["coremask"])
    C["coremask"] = cm
    C["selmats"] = A.alloc([4, 128])
    self.dma("sp", C["selmats"], self.inp("selmats", [128, 4, 128]), "ld0", w=["selmats"])


def _phase_attention(self):
    A, P, ps, C, hT = self.A, self.P, self.ps, self.C, self.hT
    w_in = self.w_in
    wv_in = w_in.rearrange("(c p) n -> p c n", p=128)
    kvp = [self.outp("kvp%d" % g, [GROUPS[g][0], 2, 512]) for g in range(3)]
    A.push()
    acc = A.alloc([2, SEG])
    wq = A.alloc([8, 128], BF16)
    wkv = A.alloc([8, 256], BF16)
    wz = A.alloc([8, 128], BF16)
    qT = A.alloc([SEG], BF16)
    kT = A.alloc([4096], BF16)
    vaug = A.alloc([32, 2, 128], BF16)
    pT = [A.alloc([256], BF16) for _ in range(4)]
    stage = [A.alloc([256]) for _ in range(2)]
    rbuf = A.alloc([512])
    att = A.alloc([512])
    sz = A.alloc([512])
    if not DEBUG.get("no_vones"):
        self.v("pool", "memset", vaug[:, :, :, 64:128], 1.0, w=["vones"])
    cnt = 0
    STG = DEBUG.get("att_stage", 9)
    for hp in range(DEBUG.get("att_hp", 4)):
        for g, (win, d) in enumerate(GROUPS):
            if g not in DEBUG.get("att_groups", (0, 1, 2)):
                continue
            U = SEG // d
            U2 = U + 128
            col = g * 512 + hp * 128
            allc = lambda nm: [(nm, c) for c in range(8)]
            self.dma("pool", wq, wv_in[:, :, OFF_Q + col:OFF_Q + col + 128], "ldw", w=allc("wq"))
            self.dma("pool", wkv[:, :, 0:128], wv_in[:, :, OFF_K + col:OFF_K + col + 128], "ldw", w=allc("wkv"))
            self.dma("pool", wkv[:, :, 128:256], wv_in[:, :, OFF_V + col:OFF_V + col + 128], "ldw", w=allc("wkv"))
            hq = [hT[:, c, HALO:HALO + SEG].rearrange("p (u r) -> p r u", r=d) for c in range(8)]
            hk = [hT[:, c, HALO - 128 * d:HALO + SEG].rearrange("p (u r) -> p r u", r=d) for c in range(8)]

            def chunks(Ux, total):
                res = []
                if Ux >= 512:
                    for r in range(d):
                        u0 = 0
                        while u0 < Ux:
                            n = min(512, Ux - u0)
                            res.append((r, 1, u0, n))
                            u0 += n
                else:
                    nr = 512 // Ux
                    for r0 in range(0, d, nr):
                        res.append((r0, nr, 0, Ux))
                return res

            def tile_keys(view_lo, r0, nr, u0, n, c):
                lo = view_lo + r0 + d * u0
                hi = view_lo + r0 + nr - 1 + d * (u0 + n - 1)
                return [(("hT", t), c) for t in range(lo // 128, hi // 128 + 1)]

            for (r0, nr, u0, n) in chunks(U, SEG):
                bank = cnt % 2
                cnt += 1
                pso = ps[:, bank, 0:nr * n]
                pso3 = pso if nr == 1 else pso.rearrange("p (a b) -> p a b", a=nr)
                for c in range(8):
                    rhs = hq[c][:, r0, u0:u0 + n] if nr == 1 else hq[c][:, r0:r0 + nr, :]
                    self.mm(pso3, wq[:, c, :], rhs, start=(c == 0), stop=(c == 7),
                            r=[("wq", c)] + tile_keys(HALO, r0, nr, u0, n, c), w=[("P", bank)])
                f0 = r0 * U + u0
                self.act(qT[:, f0:f0 + nr * n], pso, AF.Copy, r=[("P", bank)], w=["qT"], scale=0.125)
            if STG < 2:
                continue
            for (r0, nr, u0, n) in chunks(U2, U2 * d):
                bank = cnt % 2
                cnt += 1
                pso = ps[:, bank, 0:nr * n]
                pso3 = pso if nr == 1 else pso.rearrange("p (a b) -> p a b", a=nr)
                for c in range(8):
                    rhs = hk[c][:, r0, u0:u0 + n] if nr == 1 else hk[c][:, r0:r0 + nr, :]
                    self.mm(pso3, wkv[:, c, 0:128], rhs, start=(c == 0), stop=(c == 7),
                            r=[("wkv", c)] + tile_keys(HALO - 128 * d, r0, nr, u0, n, c), w=[("P", bank)])
                f0 = r0 * U2 + u0
                self.act(kT[:, f0:f0 + nr * n], pso, AF.Copy, r=[("P", bank)], w=["kT"])
            nm = U2 // 128
            if STG < 3:
                continue
            for r in range(d):
                for m in range(nm):
                    blk = r * nm + m
                    bank = cnt % 2
                    cnt += 1
                    pso = ps[:, bank, 0:256]
                    lastb = (m == nm - 1)
                    for c in range(8):
                        self.mm(pso if lastb else pso[:, 128:256], hk[c][:, r, 128 * m:128 * m + 128],
                                wkv[:, c, :] if lastb else wkv[:, c, 128:256], start=(c == 0), stop=(c == 7),
                                r=[("wkv", c)] + tile_keys(HALO - 128 * d, r, 1, 128 * m, 128, c), w=[("P", bank)])
                    self.act(vaug[:, blk, :, 0:64], pso[:, 128:256].rearrange("p (h e) -> p h e", h=2), AF.Copy,
                             r=[("P", bank), "vones"], w=[("vaug", blk)])
                    if m == nm - 1 and not DEBUG.get("no_kvout"):
                        st = stage[blk % 2]
                        if DEBUG.get("kv_act"):
                            self.act(st, pso, AF.Copy, r=[("P", bank)], w=[("stage", blk % 2)])
                        else:
                            self.v("dve", "tensor_copy", st, pso, r=[("P", bank)], w=[("stage", blk % 2)])
                        dst = kvp[g].rearrange("(i r) k c -> r i k c", r=d)[r, :, :, hp * 128:(hp + 1) * 128]
                        if DEBUG.get("kv_plain"):
                            dst = kvp[g][0:128, :, hp * 128:(hp + 1) * 128]
                        if not DEBUG.get("kv_nodma"):
                            self.dma("sp", dst, st.rearrange("p (k c) -> p k c", k=2), "out", r=[("stage", blk % 2)], w=[])
            if STG < 4:
                continue
            for r in range(d):
                for n in range(U // 128):
                    for h in range(2):
                        gh = g * 8 + hp * 2 + h
                        hs = slice(h * 64, h * 64 + 64)
                        si = cnt % 3
                        cnt += 1
                        sbk, obk = (2, 3, 6)[si], (4, 5, 7)[si]
                        S = ps[:, sbk, 0:256]
                        O = ps[:, obk, 0:128]
                        self.mm(S, C["antiid_bf"], C["biasH"][:, gh, :], start=True, stop=False,
                                r=["antiid_bf", ("biasH", g)], w=[("P", sbk)])
                        if n == 0:
                            self.mm(S[:, 0:128], C["ident_bf"], C["coremask"], start=False, stop=False,
                                    r=["ident_bf", "coremask"], w=[("P", sbk)])
                        q_ap = qT[hs, r * U + 128 * n:r * U + 128 * n + 128]
                        self.mm(S[:, 0:128], kT[hs, r * U2 + 128 * n:r * U2 + 128 * n + 128], q_ap, start=False, stop=False,
                                r=["qT", "kT"], w=[("P", sbk)])
                        self.mm(S[:, 128:256], kT[hs, r * U2 + 128 * (n + 1):r * U2 + 128 * (n + 2)], q_ap, start=False, stop=True,
                                r=["qT", "kT"], w=[("P", sbk)])
                        self.act(pT[si], S, AF.Exp, r=[("P", sbk)], w=[("pT", si)])
                        b0 = r * nm + n
                        self.mm(O, vaug[:, b0, h, :], pT[si][:, 0:128], start=True, stop=False,
                                r=[("vaug", b0), ("pT", si)], w=[("P", obk)])
                        self.mm(O, vaug[:, b0 + 1, h, :], pT[si][:, 128:256], start=False, stop=True,
                                r=[("vaug", b0 + 1), ("pT", si)], w=[("P", obk)])
                        av = acc[:, h, :].rearrange("p (u r) -> p r u", r=d)[:, r, 128 * n:128 * n + 128]
                        if d == 1:
                            ak = [("acc", h, n // 4, rr) for rr in range(16)]
                        elif d == 4:
                            ak = [("acc", h, n, r + 4 * j) for j in range(4)]
                        else:
                            ak = [("acc", h, qq, r) for qq in range(4)]
                        if g == 0:
                            self.v("dve", "tensor_copy", av, O, r=[("P", obk)], w=ak)
                        else:
                            self.v("dve", "tensor_tensor", av, av, O, ALU.add, r=[("P", obk)] + ak, w=ak)
        if STG < 5:
            continue
        P.barrier()
        zc = OFF_ZA + hp * 128
        self.dma("pool", wz, wv_in[:, :, zc:zc + 128], "ldw", w=[("wz", c) for c in range(8)])
        for k in range(4):
            tk = slice(512 * k, 512 * k + 512)
            for j, (bank, sm) in enumerate(((6, (0, 1)), (7, (2, 3)))):
                for h in range(2):
                    self.mm(ps[:, bank, :], C["selmats"][:, sm[h], :], acc[:, h, tk], start=(h == 0), stop=(h == 1),
                            r=["selmats"], w=[("P", 6 + j)])
            self.v("dve", "reciprocal", rbuf, ps[:, 7, :], r=[("P", 7)], w=["rbuf"])
            self.v("dve", "tensor_tensor", att, ps[:, 6, :], rbuf, ALU.mult, r=[("P", 6), "rbuf"], w=["att"])
            for c in range(8):
                self.mm(ps[:, 0, :], wz[:, c, :], hT[:, c, HALO + 512 * k:HALO + 512 * k + 512], start=(c == 0), stop=(c == 7),
                        r=[("wz", c)] + [(("hT", t), c) for t in range(16 + 4 * k, 16 + 4 * k + 4)], w=[("P", 0)])
            self.act(sz, ps[:, 0, :], AF.Silu, r=[("P", 0)], w=["sz"])
            self.v("dve", "tensor_tensor", self.attT[:, hp, tk], att, sz, ALU.mult, r=["att", "sz"], w=[("attT", hp)])
        P.barrier()
    A.pop()
    self.dbg("attT", self.attT, [128, 4, SEG + TS], BF16)


Builder.phase_bias = _phase_bias
Builder.phase_attention = _phase_attention


def _mlstm_setup(self):
    A, P, C = self.A, self.P, self.C
    wv_in = self.w_in.rearrange("(c p) n -> p c n", p=128)
    M = self.M = {}
    M["wxm"] = A.alloc([8, 1024], BF16)
    M["wg"] = A.alloc([8, 8], BF16)
    M["wmq"] = A.alloc([2, 4, 128], BF16)
    M["wmk"] = A.alloc([2, 4, 128], BF16)
    for c in range(8):
        self.dma("pool", M["wxm"][:, c, :], wv_in[:, c, OFF_XM:OFF_XM + 1024], "ldw", w=["wxm"])
        self.dma("pool", M["wg"][:, c, :], wv_in[:, c, OFF_I:OFF_I + 8], "ldw", w=["wg"])
    wq_d = self.inp("w_mq", [4, 256, 128]).rearrange("h (c p) k -> p c h k", p=128)
    wk_d = self.inp("w_mk", [4, 256, 128]).rearrange("h (c p) k -> p c h k", p=128)
    for ec in range(2):
        for h in range(4):
            self.dma("pool", M["wmq"][:, ec, h, :], wq_d[:, ec, h, :], "ldw", w=["wmq"])
            self.dma("pool", M["wmk"][:, ec, h, :], wk_d[:, ec, h, :], "ldw", w=["wmk"])
    M["ones"] = A.alloc([128])
    self.v("pool", "memset", M["ones"], 1.0, w=["ones"])
    M["ehsel"] = A.alloc([4, 128])
    self.dma("sp", M["ehsel"][0:4], self.inp("ehsel", [4, 4, 128]), "ld0", w=["ehsel"])
    M["negbig"] = A.alloc([64])
    self.v("dve", "tensor_scalar", M["negbig"], C["tvalid"], 1.0e4, -1.0e4, op0=ALU.mult, op1=ALU.add,
           r=["tvalid"], w=["negbig"])
    M["one1"] = A.alloc([1])
    M["zero1"] = A.alloc([1])
    self.v("pool", "memset", M["one1"], 1.0, w=["one1"])
    self.v("pool", "memset", M["zero1"], 0.0, w=["zero1"])
    M["neg1"] = A.alloc([1])
    self.v("pool", "memset", M["neg1"], -1.0, w=["neg1"])
    M["CT"] = A.alloc([4, 257], F32)
    M["m_row"] = A.alloc([1], F32)
    M["m_bc"] = A.alloc([4], F32)
    M["convbuf"] = A.alloc([8, 131], F32)


def _mlstm_local_setup(self):
    A, P, C, M = self.A, self.P, self.C, self.M
    wv_in = self.w_in.rearrange("(c p) n -> p c n", p=128)
    M["wzm"] = A.alloc([8, 1024], BF16)
    M["wom"] = A.alloc([8, 1024], BF16)
    for c in range(8):
        self.dma("pool", M["wzm"][:, c, :], wv_in[:, c, OFF_ZM:OFF_ZM + 1024], "ldw", w=["wzm"])
        self.dma("pool", M["wom"][:, c, :], wv_in[:, c, OFF_OM:OFF_OM + 1024], "ldw", w=["wom"])
    for nm, shp, dt in (("cacc", [8, 128], F32), ("c_act", [8, 128], BF16),
                        ("vaug", [4, 257], BF16), ("kmw", [4, 128], BF16), ("qmT", [4, 128], BF16), ("kmT", [4, 128], BF16),
                        ("gt", [8], F32), ("lf", [4], F32), ("ie", [4], F32), ("b_tok", [4], F32), ("a_tok", [4], F32),
                        ("a_row", [128], F32), ("cm_row", [128], F32), ("M_row", [128], F32), ("negM_row", [128], F32),
                        ("AT", [1], F32), ("Mend_row", [1], F32), ("dg", [4], F32), ("Mend_bc", [4], F32),
                        ("tmp4", [4], F32), ("wk", [4], F32), ("wC", [4], F32), ("M_tok", [4], F32), ("emt", [4], F32),
                        ("DT", [128], F32), ("Wbc", [128], F32), ("scT", [128], BF16), ("qtil", [128], BF16),
                        ("CTb", [4, 257], BF16), ("hh", [256], F32), ("hn", [256], BF16), ("hnm", [8, 128], F32),
                        ("st", [8], F32), ("so", [128], F32), ("szm", [128], F32), ("t1", [128], F32), ("t2", [128], F32),
                        ):
        M[nm] = A.alloc(shp, dt)
        if nm in ("DT", "Wbc", "scT", "qtil", "hh", "hn", "st"):
            M[nm + "_b"] = A.alloc(shp, dt)
    self.v("pool", "memset", M["vaug"][:, :, 256:257], 1.0, w=["vaug1"])
    M["so_all"] = A.alloc([8, 512], BF16)
    M["sz_all"] = A.alloc([8, 512], BF16)


def _mlstm_gates4(self, tok0, tiles, n=512):
    ps, M, hT = self.ps, self.M, self.hT
    for fb in range(8):
        for (wname, bank, func, dst, dk) in (("wom", 5, AF.Sigmoid, "so_all", "so_all"), ("wzm", 6, AF.Silu, "sz_all", "sz_all")):
            for c in range(8):
                self.mm(ps[:, bank, 0:n], M[wname][:, c, fb * 128:(fb + 1) * 128], hT[:, c, tok0:tok0 + n], start=(c == 0), stop=(c == 7),
                        r=[wname] + [(("hT", t), c) for t in tiles], w=[("P", bank)])
            self.act(M[dst][:, fb, 0:n], ps[:, bank, 0:n], func, r=[("P", bank)], w=[(dk, fb)])


def _mlstm_tile(self, hTt, hkeys, ntok, tcol, with_out, mout_dst, moutkey, gate_off=None):
    A, P, ps, C, M = self.A, self.P, self.ps, self.C, self.M
    N = ntok
    B = lambda i: ("P", i)
    tv = C["tvalid"][:, tcol:tcol + 1] if tcol is not None else M["one1"]
    nb = M["negbig"][:, tcol:tcol + 1] if tcol is not None else M["zero1"]
    tvk = ["tvalid", "negbig", "one1", "zero1"]
    cb = M["convbuf"]
    xmps = ps[:, 0:2, :].rearrange("p a (b t) -> p (a b) t", t=128)
    for fb in range(8):
        for c in range(8):
            self.mm(xmps[:, fb, 0:N], M["wxm"][:, c, fb * 128:(fb + 1) * 128], hTt[:, c, :], start=(c == 0), stop=(c == 7),
                    r=["wxm"] + [(k, c) for k in hkeys], w=[B(fb // 4)])
    for half in range(2):
        self.act(cb[:, 4 * half:4 * half + 4, 3:3 + N], xmps[:, 4 * half:4 * half + 4, 0:N], AF.Copy,
                 r=[B(half)] + tvk, w=[("cb", half)], scale=tv)
    for half in range(2):
        for c in range(8):
            self.mm(ps[0:N, 2 + half, :], hTt[:, c, :], M["wxm"][:, c, half * 512:(half + 1) * 512], start=(c == 0), stop=(c == 7),
                    r=["wxm"] + [(k, c) for k in hkeys], w=[B(2 + half)])
        self.act(M["vaug"][0:N, 2 * half:2 * half + 2, 0:256], ps[0:N, 2 + half, :].rearrange("p (h v) -> p h v", h=2), AF.Copy,
                 r=[B(2 + half), "vaug1"], w=[("vaug", half)])
    gps = ps[0:N, 7, 0:8]
    for c in range(8):
        self.mm(gps, hTt[:, c, :], M["wg"][:, c, :], start=(c == 0), stop=(c == 7),
                r=["wg"] + [(k, c) for k in hkeys], w=[B(7)])
    gt = M["gt"]
    self.v("dve", "tensor_tensor", gt[0:N], gps, C["b_if_bc"][0:N], ALU.add, r=[B(7), "b_if_bc"], w=["gt"])
    lf = M["lf"]
    self.act(lf[0:N], gt[0:N, 4:8], AF.Exp, r=["gt"], w=["lf"], scale=-1.0)
    self.act(lf[0:N], lf[0:N], AF.Ln, r=["lf"], w=["lf"], bias=M["one1"][0:N])
    self.v("dve", "tensor_scalar", lf[0:N], lf[0:N], tv[0:N], M["neg1"][0:N], op0=ALU.mult, op1=ALU.mult, r=["lf", "neg1"] + tvk, w=["lf"])
    ie = M["ie"]
    self.v("dve", "tensor_scalar", ie[0:N], gt[0:N, 0:4], tv[0:N], nb[0:N], op0=ALU.mult, op1=ALU.add, r=["gt"] + tvk, w=["ie"])
    tri, ident = C["tri"], C["ident"]
    self.mm(ps[0:N, 7, 8:12], tri[0:N, 0:N], lf[0:N], r=["lf", "tri"], w=[B(7)])
    self.mm(ps[:, 7, 12:16], M["ones"][0:N, :], lf[0:N], r=["lf", "ones"], w=[B(7)])
    self.mm(ps[0:4, 7, 144:144 + N], lf[0:N], tri[0:N, 0:N], r=["lf", "tri"], w=[B(7)])
    b_tok, a_tok = M["b_tok"], M["a_tok"]
    self.v("dve", "tensor_copy", b_tok[0:N], ps[0:N, 7, 8:12], r=[B(7)], w=["b_tok"])
    self.v("dve", "tensor_tensor", a_tok[0:N], ie[0:N], b_tok[0:N], ALU.subtract, r=["ie", "b_tok"], w=["a_tok"])
    self.mm(ps[0:4, 7, 16:16 + N], a_tok[0:N], ident[0:N, 0:N], r=["a_tok", "ident"], w=[B(7)])
    a_row = M["a_row"]
    self.v("dve", "tensor_copy", a_row[0:4, 0:N], ps[0:4, 7, 16:16 + N], r=[B(7)], w=["a_row"])
    cw, cbias = C["conv_wT"], C["conv_bT"]
    for fb in range(8):
        self.act(M["cacc"][:, fb, 0:N], cb[:, fb, 3:3 + N], AF.Identity, r=[("cb", fb // 4), "conv_wT", "conv_bT"], w=[("cacc", fb)],
                 scale=cw[:, fb, 3:4], bias=cbias[:, fb:fb + 1])
    for j in range(3):
        for fb in range(8):
            ca = M["cacc"][:, fb, 0:N]
            self.v("dve", "scalar_tensor_tensor", ca, cb[:, fb, j:j + N], cw[:, fb, j:j + 1], ca, op0=ALU.mult, op1=ALU.add,
                   r=[("cb", fb // 4), ("cacc", fb)], w=[("cacc", fb)])
    for fb in range(8):
        self.act(M["c_act"][:, fb, 0:N], M["cacc"][:, fb, 0:N], AF.Silu, r=[("cacc", fb)], w=[("c_act", fb)])
    for half in range(2):
        self.v("pool", "tensor_copy", cb[:, 4 * half:4 * half + 4, 0:3], cb[:, 4 * half:4 * half + 4, N:N + 3],
               r=[("cb", half)], w=[("cb", half)])
    kmps = ps[0:N, 4, :].rearrange("p (h k) -> p h k", h=4)
    for h in range(4):
        for ec in range(2):
            self.mm(kmps[:, h, :], M["c_act"][:, 2 * h + ec, 0:N], M["wmk"][:, ec, h, :], start=(ec == 0), stop=(ec == 1),
                    r=[("c_act", 2 * h + ec), "wmk"], w=[B(4)])
    if with_out:
        qps = ps[:, 5, :].rearrange("p (h t) -> p h t", h=4)
        kps = ps[:, 6, :].rearrange("p (h t) -> p h t", h=4)
        for h in range(4):
            for ec in range(2):
                self.mm(qps[:, h, 0:N], M["wmq"][:, ec, h, :], M["c_act"][:, 2 * h + ec, 0:N], start=(ec == 0), stop=(ec == 1),
                        r=[("c_act", 2 * h + ec), "wmq"], w=[B(5)])
            for ec in range(2):
                self.mm(kps[:, h, 0:N], M["wmk"][:, ec, h, :], M["c_act"][:, 2 * h + ec, 0:N], start=(ec == 0), stop=(ec == 1),
                        r=[("c_act", 2 * h + ec), "wmk"], w=[B(6)])
        self.act(M["qmT"][:, :, 0:N], qps[:, :, 0:N], AF.Copy, r=[B(5)], w=["qmT"], scale=float(128 ** -0.5))
        self.act(M["kmT"][:, :, 0:N], kps[:, :, 0:N], AF.Copy, r=[B(6)], w=["kmT"])
        self.v("pool", "tensor_copy", M["CTb"], M["CT"], r=["CT"], w=["CTb"])
        self.v("dve", "tensor_tensor_scan", M["cm_row"][0:4, 0:N], M["ones"][0:4, 0:N], a_row[0:4, 0:N], -1.0e30,
               op0=ALU.mult, op1=ALU.max, r=["a_row", "ones"], w=["cm_row"])
        self.v("dve", "tensor_tensor", M["M_row"][0:4, 0:N], M["cm_row"][0:4, 0:N], M["m_row"][0:4, 0:1].to_broadcast([4, N]), ALU.max,
               r=["cm_row", "m_row"], w=["M_row"])
        self.v("dve", "tensor_scalar", M["negM_row"][0:4, 0:N], M["M_row"][0:4, 0:N], -1.0, None, op0=ALU.mult,
               r=["M_row"], w=["negM_row"])
        self.mm(ps[0:N, 7, 276:280], M["M_row"][0:4, 0:N], ident[0:4, 0:4], r=["M_row", "ident"], w=[B(7)])
        self.v("dve", "tensor_tensor", M["emt"][0:N], b_tok[0:N], ps[0:N, 7, 276:280], ALU.add, r=["b_tok", B(7)], w=["emt"])
        self.act(M["emt"][0:N], M["emt"][0:N], AF.Exp, r=["emt"], w=["emt"], scale=-1.0)
    self.v("dve", "tensor_reduce", M["AT"][0:4], a_row[0:4, 0:N], AX.X, ALU.max, r=["a_row"], w=["AT"])
    self.v("dve", "tensor_tensor", M["Mend_row"][0:4], M["AT"][0:4], M["m_row"][0:4], ALU.max, r=["AT", "m_row"], w=["Mend_row"])
    self.v("dve", "tensor_tensor", M["dg"][0:4], ident[0:4, 0:4], M["Mend_row"][0:4, 0:1].to_broadcast([4, 4]), ALU.mult,
           r=["Mend_row", "ident"], w=["dg"])
    self.mm(ps[:, 7, 272:276], M["ones"][0:4, :], M["dg"][0:4], r=["dg", "ones"], w=[B(7)])
    self.v("dve", "tensor_copy", M["Mend_bc"], ps[:, 7, 272:276], r=[B(7)], w=["Mend_bc"])
    self.v("dve", "tensor_tensor", M["wk"][0:N], a_tok[0:N], M["Mend_bc"][0:N], ALU.subtract, r=["a_tok", "Mend_bc"], w=["wk"])
    self.act(M["wk"][0:N], M["wk"][0:N], AF.Exp, r=["wk"], w=["wk"])
    self.v("dve", "tensor_tensor", M["wC"], M["m_bc"], M["Mend_bc"], ALU.subtract, r=["m_bc", "Mend_bc"], w=["wC"])
    self.act(M["wC"], M["wC"], AF.Exp, r=["wC"], w=["wC"])
    self.v("dve", "tensor_tensor", M["kmw"][0:N], kmps, M["wk"][0:N].unsqueeze(2).to_broadcast([N, 4, 128]), ALU.mult,
           r=[B(4), "wk"], w=["kmw"])
    if with_out:
        def head_body(h):
            hsl = slice(h, h + 1)
            par = h % 2
            sfx = "" if par == 0 else "_b"
            hb0, hb1 = (0, 1) if par == 0 else (5, 6)
            pl, pm, pst = ps[:, hb0, 0:N], ps[:, hb0, 128:128 + N], ps[:, hb0, 256:256 + N]
            self.mm(pl, M["ehsel"][0:4, h, :], M["negM_row"][0:4, 0:N], r=["ehsel", "negM_row"], w=[B(hb0)])
            yield
            self.mm(pm, M["ehsel"][0:4, h, :], M["negM_row"][0:4, 0:N], start=True, stop=False, r=["ehsel", "negM_row"], w=[B(hb0)])
            yield
            self.mm(pm[0:N], C["ident_bf"][0:N, 0:N], C["cmaskT_bf"][0:N, 0:N], start=False, stop=True,
                    r=["ident_bf", "cmaskT_bf"], w=[B(hb0)])
            yield
            self.mm(pst[0:N], M["kmT"][:, h, 0:N], M["qmT"][:, h, 0:N], r=["kmT", "qmT"], w=[B(hb0)])
            yield
            self.act(M["DT" + sfx][0:N, 0:N], pm[0:N], AF.Exp, r=[B(hb0), "a_tok"], w=["DT" + sfx], bias=a_tok[0:N, hsl])
            yield
            self.act(M["Wbc" + sfx][:, 0:N], pl, AF.Exp, r=[B(hb0), "m_bc"], w=["Wbc" + sfx], bias=M["m_bc"][:, hsl])
            yield
            self.v("dve", "tensor_tensor", M["scT" + sfx][0:N, 0:N], pst[0:N], M["DT" + sfx][0:N, 0:N], ALU.mult, r=[B(hb0), "DT" + sfx], w=["scT" + sfx])
            yield
            self.v("pool", "tensor_tensor", M["qtil" + sfx][:, 0:N], M["qmT"][:, h, 0:N], M["Wbc" + sfx][:, 0:N], ALU.mult,
                   r=["qmT", "Wbc" + sfx], w=["qtil" + sfx])
            yield
            nd = ps[0:N, hb1, 0:257]
            self.mm(nd, M["scT" + sfx][0:N, 0:N], M["vaug"][0:N, h, :], start=True, stop=False, r=["scT" + sfx, ("vaug", h // 2), "vaug1"], w=[B(hb1)])
            yield
            self.mm(nd, M["qtil" + sfx][:, 0:N], M["CTb"][:, h, :], start=False, stop=True, r=["qtil" + sfx, "CTb"], w=[B(hb1)])
            yield
            st = M["st" + sfx]
            self.act(st[0:N, 0:1], nd[:, 256:257], AF.Abs, r=[B(hb1)], w=["st" + sfx])
            yield
            self.v("dve", "tensor_tensor", st[0:N, 0:1], st[0:N, 0:1], M["emt"][0:N, hsl], ALU.max, r=["st" + sfx, "emt"], w=["st" + sfx])
            yield
            self.v("dve", "reciprocal", st[0:N, 0:1], st[0:N, 0:1], r=["st" + sfx], w=["st" + sfx])
            yield
            self.act(M["hh" + sfx][0:N], nd[:, 0:256], AF.Copy, r=[B(hb1), "st" + sfx], w=["hh" + sfx, "st1" + sfx], scale=st[0:N, 0:1], accum_out=st[0:N, 1:2])
            yield
            self.act(M["hn" + sfx][0:N], M["hh" + sfx][0:N], AF.Square, r=["hh" + sfx], w=["hn" + sfx, "st2" + sfx], accum_out=st[0:N, 2:3])
            yield
            self.v("dve", "tensor_scalar", st[0:N, 3:4], st[0:N, 1:2], 1.0 / 256, None, op0=ALU.mult, r=["st1" + sfx], w=["st3" + sfx])
            yield
            self.v("dve", "tensor_tensor", st[0:N, 4:5], st[0:N, 3:4], st[0:N, 3:4], ALU.mult, r=["st3" + sfx], w=["st4" + sfx])
            yield
            self.v("dve", "scalar_tensor_tensor", st[0:N, 5:6], st[0:N, 2:3], 1.0 / 256, st[0:N, 4:5], op0=ALU.mult, op1=ALU.subtract,
                   r=["st2" + sfx, "st4" + sfx], w=["st5" + sfx])
            yield
            self.v("dve", "tensor_scalar", st[0:N, 5:6], st[0:N, 5:6], EPS, None, op0=ALU.add, r=["st5" + sfx], w=["st5" + sfx])
            yield
            self.act(st[0:N, 5:6], st[0:N, 5:6], AF.Ln, r=["st5" + sfx], w=["st5" + sfx])
            yield
            self.act(st[0:N, 5:6], st[0:N, 5:6], AF.Exp, r=["st5" + sfx], w=["st5" + sfx], scale=-0.5)
            yield
            self.v("dve", "scalar_tensor_tensor", st[0:N, 6:7], st[0:N, 3:4], -1.0, st[0:N, 5:6], op0=ALU.mult, op1=ALU.mult,
                   r=["st3" + sfx, "st5" + sfx], w=["st6" + sfx])
            yield
            self.act(M["hn" + sfx][0:N], M["hh" + sfx][0:N], AF.Identity, r=["hh" + sfx, "st5" + sfx, "st6" + sfx], w=["hn" + sfx], scale=st[0:N, 5:6], bias=st[0:N, 6:7])
            yield
            pt = ps[:, 4, 128 * par:128 * par + 128].bitcast(BF16).rearrange("p (b t) -> p b t", b=2)
            for vb in range(2):
                self.tr(pt[:, vb, 0:N], M["hn" + sfx][0:N, vb * 128:(vb + 1) * 128], C["ident_bf"][0:N, 0:N], r=["hn" + sfx, "ident_bf"], w=[B(4)])
            yield
            for vb in range(2):
                fb = 2 * h + vb
                self.act(M["hnm"][:, fb, 0:N], pt[:, vb, 0:N], AF.Copy, r=[B(4), "m_normT"], w=[("hnm", fb)], scale=C["m_normT"][:, fb:fb + 1])
            yield
        for pair in ((0, 1), (2, 3)):
            gens = [head_body(h) for h in pair]
            live = list(gens)
            while live:
                for gen in list(live):
                    try:
                        next(gen)
                    except StopIteration:
                        live.remove(gen)
    for h in range(4):
        bk = 2 + (h % 2)
        dps = ps[:, bk, 0:257]
        self.mm(dps, M["kmw"][0:N, h, :], M["vaug"][0:N, h, :], r=["kmw", ("vaug", h // 2), "vaug1"], w=[B(bk)])
        self.v("dve", "scalar_tensor_tensor", M["CT"][:, h, :], M["CT"][:, h, :], M["wC"][:, h:h + 1], dps, op0=ALU.mult, op1=ALU.add,
               r=["wC", B(bk), "CT", "CTb"], w=["CT"])
    self.v("dve", "tensor_tensor", M["m_row"][0:4], M["Mend_row"][0:4], ps[0:4, 7, 144 + N - 1:144 + N], ALU.add,
           r=["Mend_row", B(7)], w=["m_row"])
    self.v("dve", "tensor_tensor", M["m_bc"], M["Mend_bc"], ps[:, 7, 12:16], ALU.add, r=["Mend_bc", B(7)], w=["m_bc"])
    if with_out:
        for fb in range(8):
            if gate_off is None:
                for (wname, bank) in (("wom", 5), ("wzm", 6)):
                    for c in range(8):
                        self.mm(ps[:, bank, 0:N], M[wname][:, c, fb * 128:(fb + 1) * 128], hTt[:, c, :], start=(c == 0), stop=(c == 7),
                                r=[wname] + [(k, c) for k in hkeys], w=[B(bank)])
                self.act(M["so"][:, 0:N], ps[:, 5, 0:N], AF.Sigmoid, r=[B(5)], w=["so"])
                self.act(M["szm"][:, 0:N], ps[:, 6, 0:N], AF.Silu, r=[B(6)], w=["szm"])
                so_ap, sz_ap, sok, szk = M["so"][:, 0:N], M["szm"][:, 0:N], "so", "szm"
            else:
                so_ap, sz_ap = M["so_all"][:, fb, gate_off:gate_off + N], M["sz_all"][:, fb, gate_off:gate_off + N]
                sok, szk = ("so_all", fb), ("sz_all", fb)
            self.v("dve", "tensor_tensor", M["t1"][:, 0:N], so_ap, M["hnm"][:, fb, 0:N], ALU.mult, r=[sok, ("hnm", fb)], w=["t1"])
            self.v("dve", "scalar_tensor_tensor", M["t2"][:, 0:N], M["c_act"][:, fb, 0:N], C["m_skipT"][:, fb:fb + 1], M["t1"][:, 0:N],
                   op0=ALU.mult, op1=ALU.add, r=[("c_act", fb), "t1", "m_skipT"], w=["t2"])
            self.v("dve", "tensor_tensor", mout_dst[:, fb, :], M["t2"][:, 0:N], sz_ap, ALU.mult, r=["t2", szk], w=[(moutkey, fb)])


Builder.mlstm_setup = _mlstm_setup
Builder.mlstm_local_setup = _mlstm_local_setup
Builder.mlstm_tile = _mlstm_tile
Builder.mlstm_gates4 = _mlstm_gates4


def _phase_mlstm(self):
    A, P, ps, C, hT = self.A, self.P, self.ps, self.C, self.hT
    self.moutS = A.alloc([8, TS], BF16)
    A.push()
    self.mlstm_setup()
    M = self.M
    self.v("pool", "memset", M["CT"], 0.0, w=["CT"])
    A.push()
    self.W1 = {
        "xt": [A.alloc([1024]) for _ in range(2)],
        "ss": A.alloc([4]),
        "junk": A.alloc([1024], BF16),
        "xn": [A.alloc([1024], BF16) for _ in range(2)],
    }
    self.mlstm_prefix()
    if DEBUG.get("sbuf"):
        print("mlstm prefix sbuf top", A.top)
    A.pop()
    P.barrier()
    self.mlstm_local_setup()
    if DEBUG.get("sbuf"):
        print("mlstm local sbuf top", A.top)
    for j in range(DEBUG.get("nloc", 16)):
        if j % 4 == 0:
            self.mlstm_gates4(HALO + j * 128, [16 + j + q for q in range(4)])
        self.mlstm_tile(hT[:, :, HALO + j * 128:HALO + (j + 1) * 128], [("hT", 16 + j)], 128, 48 + j, True,
                        hT[:, :, j * 128:(j + 1) * 128], ("hT", j), gate_off=(j % 4) * 128)
    o_conv = self.outp("convp", [128, 8, 3])
    o_C = self.outp("Cp", [128, 4, 257])
    o_m = self.outp("mp", [4, 1])
    self.dma("sp", o_conv, M["convbuf"][:, :, 0:3], "out", r=[("cb", 0), ("cb", 1)])
    self.dma("sp", o_C, M["CT"], "out", r=["CT"])
    self.dma("sp", o_m, M["m_row"][0:4], "out", r=["m_row"])
    C0 = self.inp("C0T", [4, 128, 4, 257])
    m0r = self.inp("m0row", [4, 4, 1])
    m0b = self.inp("m0bc", [4, 128, 4])
    cv0 = self.inp("conv0", [4, 128, 8, 3])
    o_convs = self.outp("convs", [4, 128, 8, 3])
    o_Cs = self.outp("Cs", [4, 128, 4, 257])
    o_ms = self.outp("ms", [4, 4, 1])
    if not DEBUG.get("no_smp"):
        self.mlstm_gates4(HALO + SEG, [32], n=TS)
    for j in range(4 if not DEBUG.get("no_smp") else 0):
        self.dma("sp", M["CT"], C0[j], "ldx", w=["CT"])
        self.dma("sp", M["m_row"][0:4], m0r[j], "ldx", w=["m_row"])
        self.dma("sp", M["m_bc"], m0b[j], "ldx", w=["m_bc"])
        self.dma("sp", M["convbuf"][:, :, 0:3], cv0[j], "ldx", w=[("cb", 0), ("cb", 1)])
        self.mlstm_tile(hT[:, :, HALO + SEG + 8 * j:HALO + SEG + 8 * j + 8], [("hT", 32)], 8, None, True,
                        self.moutS[:, :, 8 * j:8 * j + 8], ("moutS", j), gate_off=8 * j)
        self.dma("sp", o_convs[j], M["convbuf"][:, :, 0:3], "out", r=[("cb", 0), ("cb", 1)])
        self.dma("sp", o_Cs[j], M["CT"], "out", r=["CT"])
        self.dma("sp", o_ms[j], M["m_row"][0:4], "out", r=["m_row"])
    if DEBUG.get("mdump"):
        for nm, shp, dt in (("convbuf", [128, 8, 131], F32), ("c_act", [128, 8, 128], BF16), ("vaug", [128, 4, 257], BF16),
                            ("kmw", [128, 4, 128], BF16), ("gt", [128, 8], F32), ("lf", [128, 4], F32), ("ie", [128, 4], F32),
                            ("b_tok", [128, 4], F32), ("a_tok", [128, 4], F32), ("a_row", [128, 128], F32),
                            ("Mend_bc", [128, 4], F32), ("wk", [128, 4], F32), ("wC", [128, 4], F32), ("CT", [128, 4, 257], F32),
                            ("m_bc", [128, 4], F32), ("hh", [128, 256], F32), ("hnm", [128, 8, 128], F32), ("st", [128, 8], F32),
                            ("emt", [128, 4], F32), ("qmT", [128, 4, 128], BF16), ("kmT", [128, 4, 128], BF16), ("DT", [128, 128], F32),
                            ("Wbc", [128, 128], F32), ("M_row", [128, 128], F32)):
            DEBUG["md_" + nm] = True
            self.dbg("md_" + nm, M[nm], shp, dt)
        for nm, ap, shp, dt in (("hTp1", M["hTp"][1], [128, 8, 128], BF16), ("xn1", self.W1["xn"][1], [128, 1024], BF16),
                                ("xt1", self.W1["xt"][1], [128, 1024], F32), ("ss", self.W1["ss"], [128, 4], F32),
                                ("wg", M["wg"], [128, 8, 8], BF16), ("mult", C["mult"], [128, 8, 5], F32)):
            DEBUG["md_" + nm] = True
            self.dbg("md_" + nm, ap, shp, dt)
    A.pop()
    self.P.barrier()
    self.dbg("mout", self.hT[:, :, 0:SEG], [128, 8, SEG], BF16)
    self.dbg("moutS", self.moutS, [128, 8, TS], BF16)


Builder.phase_mlstm = _phase_mlstm


def _phase_out(self):
    A, P, ps, C, hT = self.A, self.P, self.ps, self.C, self.hT
    wv_in = self.w_in.rearrange("(c p) n -> p c n", p=128)
    A.push()
    wpa = A.alloc([4, 1024], BF16)
    wpm = A.alloc([8, 1024], BF16)
    wout = A.alloc([8, 1024], BF16)
    wga = A.alloc([8, 128], BF16)
    wgm = A.alloc([8, 128], BF16)
    merged = A.alloc([8, SEG + TS], BF16)
    fgain = A.alloc([1024])
    sga, sgm, t1, t2 = (A.alloc([512]) for _ in range(4))
    xt = [A.alloc([1024])] * 2
    yt = [A.alloc([1024]) for _ in range(2)]
    ss = A.alloc([4])
    self.dma("sp", fgain, self.inp("fgain_bc", [128, 1024]), "ld0", w=["fgain"])
    wpa_d = self.inp("w_pa", [512, 1024]).rearrange("(c p) n -> p c n", p=128)
    wpm_d = self.inp("w_pm", [1024, 1024]).rearrange("(c p) n -> p c n", p=128)
    wout_d = self.inp("w_out", [1024, 1024]).rearrange("(c p) n -> p c n", p=128)
    for c in range(4):
        self.dma("pool", wpa[:, c, :], wpa_d[:, c, :], "ldw", w=["wpa"])
    for c in range(8):
        self.dma("pool", wpm[:, c, :], wpm_d[:, c, :], "ldw", w=["wpm"])
        self.dma("pool", wout[:, c, :], wout_d[:, c, :], "ldw", w=["wout"])
    chunks = []
    for k in range(4):
        chunks.append(dict(n=512, m0=512 * k,
                           att=lambda hp, k=k: self.attT[:, hp, 512 * k:512 * k + 512],
                           mout=lambda fb, k=k: hT[:, fb, 512 * k:512 * k + 512],
                           hs=lambda c, k=k: hT[:, c, HALO + 512 * k:HALO + 512 * k + 512]))
    chunks.append(dict(n=TS, m0=SEG,
                       att=lambda hp: self.attT[:, hp, SEG:SEG + TS],
                       mout=lambda fb: self.moutS[:, fb, :],
                       hs=lambda c: hT[:, c, HALO + SEG:HALO + SEG + TS]))
    for cb in range(8):
        cs = slice(cb * 128, (cb + 1) * 128)
        self.dma("pool", wga, wv_in[:, :, OFF_GA + cb * 128:OFF_GA + (cb + 1) * 128], "ldw", w=["wga"])
        self.dma("pool", wgm, wv_in[:, :, OFF_GM + cb * 128:OFF_GM + (cb + 1) * 128], "ldw", w=["wgm"])
        for ch in chunks:
            n = ch["n"]
            for hp in range(4):
                self.mm(ps[:, 0, 0:n], wpa[:, hp, cs], ch["att"](hp), start=(hp == 0), stop=(hp == 3), r=["wpa"], w=[("P", 0)])
            for fb in range(8):
                self.mm(ps[:, 1, 0:n], wpm[:, fb, cs], ch["mout"](fb), start=(fb == 0), stop=(fb == 7), r=["wpm"], w=[("P", 1)])
            for c in range(8):
                self.mm(ps[:, 2, 0:n], wga[:, c, :], ch["hs"](c), start=(c == 0), stop=(c == 7), r=["wga"], w=[("P", 2)])
            for c in range(8):
                self.mm(ps[:, 3, 0:n], wgm[:, c, :], ch["hs"](c), start=(c == 0), stop=(c == 7), r=["wgm"], w=[("P", 3)])
            self.act(sga[:, 0:n], ps[:, 2, 0:n], AF.Sigmoid, r=[("P", 2)], w=["sga"])
            self.act(sgm[:, 0:n], ps[:, 3, 0:n], AF.Sigmoid, r=[("P", 3)], w=["sgm"])
            self.v("dve", "tensor_tensor", t1[:, 0:n], ps[:, 0, 0:n], sga[:, 0:n], ALU.mult, r=[("P", 0), "sga"], w=["t1"])
            self.v("dve", "tensor_tensor", t2[:, 0:n], ps[:, 1, 0:n], sgm[:, 0:n], ALU.mult, r=[("P", 1), "sgm"], w=["t2"])
            self.v("pool", "tensor_tensor", merged[:, cb, ch["m0"]:ch["m0"] + n], t1[:, 0:n], t2[:, 0:n], ALU.add,
                   r=["t1", "t2"], w=[("merged", cb)])
    self.dbg("merged", merged, [128, 8, SEG + TS], BF16)
    xh = self.din["xh"]
    xs = self.din["xs"]
    y_p = self.outp("y_p", [SEG, 1024])
    y_s = self.outp("y_s", [TS, 1024])
    tiles = [(xh[HALO + j * 128:HALO + (j + 1) * 128, :], y_p[j * 128:(j + 1) * 128, :], 128, j * 128, C["gate_p"]) for j in range(NLT)]
    tiles.append((xs, y_s, TS, SEG, C["gate_s"]))
    for i, (xsrc, ydst, n, m0, gate) in enumerate(tiles):
        i2 = i % 2
        self.dma("sp", xt[i2][0:n], xsrc, "ldx", w=[("xt", 0)])
        for half in range(2):
            hs_ = slice(half * 512, (half + 1) * 512)
            for cb in range(8):
                self.mm(ps[0:n, 4 + half, :], merged[:, cb, m0:m0 + n], wout[:, cb, hs_], start=(cb == 0), stop=(cb == 7),
                        r=["wout", ("merged", cb)], w=[("P", 4 + half)])
            self.v("dve", "tensor_tensor", yt[i2][0:n, hs_], ps[0:n, 4 + half, :], gate[0:n, hs_], ALU.mult,
                   r=[("P", 4 + half), ("gate_p", half), ("gate_s", half)], w=[("yt", i2, half)])
            self.v("pool", "tensor_tensor", yt[i2][0:n, hs_], yt[i2][0:n, hs_], xt[i2][0:n, hs_], ALU.add,
                   r=[("yt", i2, half), ("xt", 0)], w=[("yt", i2, half)])
        sl = ss[:, i % 4:i % 4 + 1]
        sk = ("ss", i % 4)
        self.act(xt[0][0:n], yt[i2][0:n], AF.Square, r=[("yt", i2, 0), ("yt", i2, 1)], w=[("xt", 0), sk], accum_out=sl[0:n])
        self.v("dve", "tensor_scalar", sl[0:n], sl[0:n], 1.0 / D, EPS, op0=ALU.mult, op1=ALU.add, r=[sk], w=[sk])
        self.act(sl[0:n], sl[0:n], AF.Ln, r=[sk], w=[sk])
        self.act(sl[0:n], sl[0:n], AF.Exp, r=[sk], w=[sk], scale=-0.5)
        self.act(yt[i2][0:n], yt[i2][0:n], AF.Copy, r=[("yt", i2, 0), ("yt", i2, 1), sk], w=[("yt", i2, 0), ("yt", i2, 1)], scale=sl[0:n])
        self.v("pool", "tensor_tensor", yt[i2][0:n], yt[i2][0:n], fgain[0:n], ALU.mult,
               r=[("yt", i2, 0), ("yt", i2, 1), "fgain"], w=[("yt", i2, 0), ("yt", i2, 1)])
        self.dma("sp", ydst, yt[i2][0:n], "out", r=[("yt", i2, 0), ("yt", i2, 1)])
    A.pop()


Builder.phase_out = _phase_out


def _sample_copies(self):
    LB = [w for (w, d) in GROUPS]
    self.s_cache = cache = [self.inp("cache%d" % g, [4, LB[g], 2, 512]) for g in range(3)]
    self.s_kvs = kvs = [self.outp("kvs%d" % g, [4, LB[g], 2, 512]) for g in range(3)]
    self.s_cat = cat = [self.scratch("cat%d" % g, [4, LB[g] + 8, 2, 512]) for g in range(2)]
    for g in range(3):
        for j in range(4):
            nsp = 4 if g == 2 else 1
            rows = LB[g] - 8
            step = (rows + nsp - 1) // nsp
            for a in range(0, rows, step):
                b_ = min(rows, a + step)
                self.dma("sp", kvs[g][j, a:b_], cache[g][j, 8 + a:8 + b_], "cpy", w=[("kvs", g, j, a)])
            if g < 2:
                self.dma("sp", cat[g][j, 0:LB[g]], cache[g][j], "cpy", w=[("catb", g, j)])


def _phase_sample_attn(self):
    A, P, ps, C, hT = self.A, self.P, self.ps, self.C, self.hT
    wv_in = self.w_in.rearrange("(c p) n -> p c n", p=128)
    LB = [w for (w, d) in GROUPS]
    cache, kvs, cat = self.s_cache, self.s_kvs, self.s_cat
    A.push()
    ident, F_sb = C["ident"], C["F"]
    ones = A.alloc([128])
    self.v("pool", "memset", ones, 1.0, w=["s_ones"])
    z8 = A.alloc([8])
    self.v("pool", "memset", z8, 0.0, w=["z8"])
    onesel = A.alloc([8, 8])
    self.dma("sp", onesel, self.inp("onesel", [128, 8, 8]), "ld0", w=["onesel"])
    selt = A.alloc([8, 128])
    self.dma("sp", selt[0:8], self.inp("selt", [8, 8, 128]), "ld0", w=["selt"])
    biasS = A.alloc([3, 8])
    bold = A.alloc([3, 8])
    tmpF = A.alloc([8])
    dgF = A.alloc([8])
    for g in range(3):
        self.mm(ps[:, 0, 0:8], F_sb[0:8, g, 0:128], ident[0:8, 0:8], r=[("F", g), "ident"], w=[("P", 0)])
        self.act(tmpF, ps[:, 0, 0:8], AF.Copy, r=[("P", 0)], w=["tmpF"])
        self.mm(ps[:, 0, 8:16], C["antiid"], tmpF, r=["tmpF", "antiid"], w=[("P", 0)])
        self.act(biasS[:, g, :], ps[:, 0, 8:16], AF.Copy, r=[("P", 0)], w=[("biasS", g)])
        self.v("dve", "tensor_tensor", dgF[0:8], ident[0:8, 0:8], F_sb[0:8, g, 128:129].to_broadcast([8, 8]), ALU.mult,
               r=[("F", g), "ident"], w=["dgF"])
        self.mm(ps[0:8, 0, 16:24], ones[0:8, 0:8], dgF[0:8], r=["dgF", "s_ones"], w=[("P", 0)])
        self.act(bold[0:8, g, :], ps[0:8, 0, 16:24], AF.Copy, r=[("P", 0)], w=[("bold", g)])
    wq = [A.alloc([8, 512], BF16) for _ in range(2)]
    qj = [A.alloc([1536]) for _ in range(4)]
    kj = A.alloc([1536])
    vj = A.alloc([1536])
    oldkv = [A.alloc([3, 2, 512]) for _ in range(1)][0]
    wcnt = 0
    for kind, off in (("q", OFF_Q), ("k", OFF_K), ("v", OFF_V)):
        for g in range(3):
            wb = wq[wcnt % 2]
            wk_ = ("swq", wcnt % 2)
            wcnt += 1
            self.dma("pool", wb, wv_in[:, :, off + g * 512:off + (g + 1) * 512], "ldw", w=[wk_])
            for j in range(4):
                bank = 1 + (j % 2)
                for c in range(8):
                    self.mm(ps[0:8, bank, :], hT[:, c, HALO + SEG + 8 * j:HALO + SEG + 8 * j + 8], wb[:, c, :],
                            start=(c == 0), stop=(c == 7), r=[wk_, (("hT", 32), c)], w=[("P", bank)])
                if kind == "q":
                    self.act(qj[j][0:8, g * 512:(g + 1) * 512], ps[0:8, bank, :], AF.Copy, r=[("P", bank)], w=[("qj", j, g)], scale=0.125)
                else:
                    st = kj if kind == "k" else vj
                    sk = ("kvst", kind, j % 3)
                    sl_ = st[0:8, (j % 3) * 512:(j % 3 + 1) * 512]
                    self.act(sl_, ps[0:8, bank, :], AF.Copy, r=[("P", bank)], w=[sk])
                    kvi = 0 if kind == "k" else 1
                    self.dma("sp", kvs[g][j, LB[g] - 8:LB[g], kvi, :], sl_, "out", r=[sk], w=[("kvsn", g, j, kvi)])
                    if g < 2:
                        self.dma("sp", cat[g][j, LB[g]:LB[g] + 8, kvi, :], sl_, "out", r=[sk], w=[("catn", g, j, kvi)])
    szs = A.alloc([4, TS])
    wz = wq[0]
    self.dma("pool", wz, wv_in[:, :, OFF_ZA:OFF_ZA + 512], "ldw", w=[("swq", 0)])
    for hp in range(4):
        for c in range(8):
            self.mm(ps[:, 3, 0:TS], wz[:, c, hp * 128:(hp + 1) * 128], hT[:, c, HALO + SEG:HALO + SEG + TS], start=(c == 0), stop=(c == 7),
                    r=[("swq", 0), (("hT", 32), c)], w=[("P", 3)])
        self.act(szs[:, hp, :], ps[:, 3, 0:TS], AF.Silu, r=[("P", 3)], w=[("szs", hp)])
    NKG = 6
    Kg = [A.alloc([2, 512]) for _ in range(NKG)]
    prod = A.alloc([512])
    lg = [A.alloc([8]) for _ in range(4)]
    Pz = [A.alloc([8, 8]) for _ in range(NKG)]
    numS = A.alloc([512])
    denS = A.alloc([8])
    pold = A.alloc([8])
    tmpo = A.alloc([512])
    oj = A.alloc([512])
    u = 0
    for j in range(4):
        for g in range(3):
            self.dma("sp", oldkv[0:8, g], cache[g][j, 0:8], "gat", w=[("old", g)])
        self.mm(ps[0:8, 5, :], z8[0:8, 0:8], qj[j][0:8, 0:512], start=True, stop=False, r=["z8", ("qj", j, 0)], w=[("P", 5)])
        self.mm(ps[0:8, 6, 0:8], z8[0:8, 0:8], qj[j][0:8, 0:8], start=True, stop=False, r=["z8", ("qj", j, 0)], w=[("P", 6)])
        first = False
        for g, (win, d) in enumerate(GROUPS):
            for t in range(8):
                kb = Kg[u % NKG]
                kk = ("Kg", u % NKG)
                pz = Pz[u % NKG]
                pk = ("Pz", u % NKG)
                lgt = lg[u % 4]
                lk = ("lg", u % 4)
                u += 1
                a0 = d + t
                if g < 2:
                    src = cat[g][j, a0:a0 + 127 * d + 1:d]
                    deps = [("catb", g, j), ("catn", g, j, 0), ("catn", g, j, 1)]
                else:
                    src = kvs[2][j, a0 - 8:a0 - 8 + 127 * d + 1:d]
                    deps = [("kvs", 2, j, a) for a in range(0, LB[2] - 8, (LB[2] - 8 + 3) // 4)] + [("kvsn", 2, j, 0), ("kvsn", 2, j, 1)]
                self.dma("sp", kb, src, "gat", r=deps, w=[kk])
                self.mm(ps[:, 4, :], selt[0:8, t, :], qj[j][0:8, g * 512:(g + 1) * 512], r=["selt", ("qj", j, g)], w=[("P", 4)])
                self.v("dve", "tensor_tensor", prod, kb[:, 0, :], ps[:, 4, :], ALU.mult, r=[kk, ("P", 4)], w=["prod"])
                self.v("dve", "tensor_reduce", lgt, prod.rearrange("p (h e) -> p h e", h=8), AX.X, ALU.add, r=["prod"], w=[lk])
                self.v("dve", "tensor_tensor", lgt, lgt, biasS[:, g, :], ALU.add, r=[lk, ("biasS", g)], w=[lk])
                pe_ = pz[:, 0, :]
                self.act(pe_, lgt, AF.Exp, r=[lk], w=[pk])
                last = (g == 2 and t == 7)
                kv3 = kb[:, 1, :].rearrange("p (h e) -> p h e", h=8)
                self.v("dve", "tensor_tensor", kv3, kv3, pe_.unsqueeze(2).to_broadcast([128, 8, 64]), ALU.mult, r=[pk, kk], w=[kk])
                self.mm(ps[0:8, 5, :], onesel[:, t, :], kb[:, 1, :], start=False, stop=last, r=["onesel", kk], w=[("P", 5)])
                self.mm(ps[0:8, 6, 0:8], onesel[:, t, :], pe_, start=False, stop=last, r=["onesel", pk], w=[("P", 6)])
        self.v("dve", "tensor_copy", numS[0:8], ps[0:8, 5, :], r=[("P", 5)], w=["numS"])
        self.v("dve", "tensor_copy", denS[0:8], ps[0:8, 6, 0:8], r=[("P", 6)], w=["denS"])
        for g in range(3):
            self.v("dve", "tensor_tensor", tmpo[0:8], qj[j][0:8, g * 512:(g + 1) * 512], oldkv[0:8, g, 0, :], ALU.mult,
                   r=[("qj", j, g), ("old", g)], w=["tmpo"])
            self.v("dve", "tensor_reduce", pold[0:8], tmpo[0:8].rearrange("p (h e) -> p h e", h=8), AX.X, ALU.add, r=["tmpo"], w=["pold"])
            self.v("dve", "tensor_tensor", pold[0:8], pold[0:8], bold[0:8, g, :], ALU.add, r=["pold", ("bold", g)], w=["pold"])
            self.act(pold[0:8], pold[0:8], AF.Exp, r=["pold"], w=["pold"])
            self.v("dve", "tensor_tensor", denS[0:8], denS[0:8], pold[0:8], ALU.add, r=["denS", "pold"], w=["denS"])
            self.v("dve", "tensor_tensor", tmpo[0:8].rearrange("p (h e) -> p h e", h=8),
                   oldkv[0:8, g, 1, :].rearrange("p (h e) -> p h e", h=8), pold[0:8].unsqueeze(2).to_broadcast([8, 8, 64]), ALU.mult,
                   r=[("old", g), "pold"], w=["tmpo"])
            self.v("dve", "tensor_tensor", numS[0:8], numS[0:8], tmpo[0:8], ALU.add, r=["numS", "tmpo"], w=["numS"])
        self.v("dve", "reciprocal", denS[0:8], denS[0:8], r=["denS"], w=["denS"])
        self.v("dve", "tensor_tensor", oj[0:8].rearrange("p (h e) -> p h e", h=8), numS[0:8].rearrange("p (h e) -> p h e", h=8),
               denS[0:8].unsqueeze(2).to_broadcast([8, 8, 64]), ALU.mult, r=["numS", "denS"], w=["oj"])
        for hp in range(4):
            self.tr(ps[:, 7, 0:8], oj[0:8, hp * 128:(hp + 1) * 128], ident[0:8, 0:8], r=["oj", "ident"], w=[("P", 7)])
            self.v("dve", "tensor_tensor", self.attT[:, hp, SEG + 8 * j:SEG + 8 * j + 8], ps[:, 7, 0:8], szs[:, hp, 8 * j:8 * j + 8], ALU.mult,
                   r=[("P", 7), ("szs", hp)], w=[("attTs", hp, j)])
    A.pop()


Builder.phase_sample_attn = _phase_sample_attn
Builder.sample_copies = _sample_copies


_PROG_CACHE = {}


def kernel(**inputs):
    inp = {k: np.asarray(v) for k, v in inputs.items()}
    if "prog" not in _PROG_CACHE:
        _PROG_CACHE["prog"] = build_program(99)
    b = _PROG_CACHE["prog"]
    cores = list(range(NCORES))
    maps = make_in_maps(inp, cores)
    maps = [{k: np.ascontiguousarray(v, dtype=np.float32) for k, v in m.items() if k in b.din} for m in maps]
    res = run_bass_kernel_spmd(b.nc, maps, core_ids=cores).results
    f32 = np.float32
    y_p = np.zeros((2, SEQ, D), f32)
    y_s = np.zeros((32, 8, D), f32)
    kvp = [np.zeros((1, 2, w, 2, 8, 64), f32) for (w, d) in GROUPS]
    kvs = [np.zeros((1, 32, w, 2, 8, 64), f32) for (w, d) in GROUPS]
    conv_p = np.zeros((1, 2, 3, D), f32)
    conv_s = np.zeros((1, 32, 3, D), f32)
    C_p = np.zeros((1, 2, 4, 256, 128), f32)
    C_s = np.zeros((1, 32, 4, 256, 128), f32)
    n_p = np.zeros((1, 2, 4, 128), f32)
    n_s = np.zeros((1, 32, 4, 128), f32)
    m_p = np.zeros((1, 2, 4), f32)
    m_s = np.zeros((1, 32, 4), f32)
    for c in cores:
        r = res[c]
        bb, p = c // 4, c % 4
        sq = slice(4 * c, 4 * c + 4)
        y_p[bb, p * SEG:(p + 1) * SEG] = r["y_p"]
        y_s[sq] = np.asarray(r["y_s"]).reshape(4, 8, D)
        for g in range(3):
            kvs[g][0, sq] = np.asarray(r["kvs%d" % g]).reshape(4, -1, 2, 8, 64)
        Cs = np.asarray(r["Cs"])
        C_s[0, sq] = Cs[..., :256].transpose(0, 2, 3, 1)
        n_s[0, sq] = Cs[..., 256].transpose(0, 2, 1)
        m_s[0, sq] = np.asarray(r["ms"])[:, :, 0]
        conv_s[0, sq] = np.asarray(r["convs"]).transpose(0, 3, 2, 1).reshape(4, 3, D)
        if p == 3:
            for g in range(3):
                kvp[g][0, bb] = np.asarray(r["kvp%d" % g]).reshape(-1, 2, 8, 64)
            Cp = np.asarray(r["Cp"])
            C_p[0, bb] = Cp[..., :256].transpose(1, 2, 0)
            n_p[0, bb] = Cp[..., 256].T
            m_p[0, bb] = np.asarray(r["mp"])[:, 0]
            conv_p[0, bb] = np.asarray(r["convp"]).transpose(2, 1, 0).reshape(3, D)
    return (y_p, y_s, kvp[0], kvs[0], kvp[1], kvs[1], kvp[2], kvs[2],
            conv_p, conv_s, C_p, C_s, n_p, n_s, m_p, m_s)


def _mlstm_prefix(self):
    A, P, ps, C, M, hT = self.A, self.P, self.ps, self.C, self.M, self.hT
    NT = 48
    xf = self.inp("xf", [PREFIX - HALO, 1024])
    ident, tri = C["ident"], C["tri"]
    hTp = [A.alloc([8, 128], BF16) for _ in range(2)]
    gt_all = A.alloc([NT, 8])
    lf = A.alloc([NT, 4])
    ie = A.alloc([NT, 4])
    tot = A.alloc([NT, 4])
    incl = A.alloc([NT, 4])
    a_all = A.alloc([NT, 4])
    wk_all = A.alloc([NT, 4])
    negtv = A.alloc([NT])
    pm = A.alloc([4])
    row1 = A.alloc([1])
    dg = A.alloc([4])
    Mg_bc = A.alloc([4])
    cb4 = A.alloc([4, 8, 131])
    hT4 = A.alloc([8, 512], BF16)
    caccs = [A.alloc([8, 128]) for _ in range(2)]
    cexp = A.alloc([8, 128])
    c_act = [A.alloc([8, 128], BF16) for _ in range(2)]
    vaug = [A.alloc([4, 257], BF16) for _ in range(2)]
    kmw = [A.alloc([4, 128], BF16) for _ in range(2)]
    for i in range(2):
        self.v("pool", "memset", vaug[i][:, :, 256:257], 1.0, w=[("pvaug1", i)])

    hTd = self.scratch("hTd", [32, 128, 8, 128], BF16)

    def tile_src(i, slot):
        if i < 32:
            hp_ = hTp[i % 2]
            self.norm_tile(xf[i * 128:(i + 1) * 128, :], 128, hp_, "p", slot, ("hTp", i % 2), bank0=5)
            self.dma("sp", hTd[i], hp_, "hts", r=[(("hTp", i % 2), c) for c in range(8)], w=[("hTd", i)])
            return hp_, [("hTp", i % 2)]
        ti = i - 32
        return hT[:, :, ti * 128:(ti + 1) * 128], [("hT", ti)]

    for i in range(NT):
        hTt, hk = tile_src(i, i)
        bank = 7 if i % 2 == 0 else 4
        gps = ps[:, bank, 0:8]
        for c in range(8):
            self.mm(gps, hTt[:, c, :], M["wg"][:, c, :], start=(c == 0), stop=(c == 7),
                    r=["wg"] + [(k, c) for k in hk], w=[("P", bank)])
        self.v("dve", "tensor_tensor", gt_all[:, i, :], gps, C["b_if_bc"], ALU.add, r=[("P", bank), "b_if_bc"], w=["gt_all"])
    tv48 = C["tvalid"][:, 0:NT]
    self.v("dve", "tensor_scalar", negtv, tv48, -1.0, None, op0=ALU.mult, r=["tvalid"], w=["negtv"])
    self.act(lf, gt_all[:, :, 4:8], AF.Exp, r=["gt_all"], w=["lf"], scale=-1.0)
    self.act(lf, lf, AF.Ln, r=["lf"], w=["lf"], bias=M["one1"])
    self.v("dve", "tensor_tensor", lf, lf, negtv.unsqueeze(2).to_broadcast([128, NT, 4]), ALU.mult, r=["lf", "negtv"], w=["lf"])
    self.v("dve", "tensor_tensor", ie, gt_all[:, :, 0:4], tv48.unsqueeze(2).to_broadcast([128, NT, 4]), ALU.mult,
           r=["gt_all", "tvalid"], w=["ie"])
    self.v("dve", "tensor_tensor", ie, ie, M["negbig"][:, 0:NT].unsqueeze(2).to_broadcast([128, NT, 4]), ALU.add,
           r=["ie", "negbig"], w=["ie"])
    lf2 = lf.rearrange("p t h -> p (t h)")
    self.mm(ps[:, 0, 0:NT * 4], tri, lf2, r=["lf", "tri"], w=[("P", 0)])
    self.mm(ps[:, 1, 0:NT * 4], M["ones"], lf2, r=["lf", "ones"], w=[("P", 1)])
    self.v("dve", "tensor_copy", tot.rearrange("p t h -> p (t h)"), ps[:, 1, 0:NT * 4], r=[("P", 1)], w=["tot"])
    for h in range(4):
        self.v("dve", "tensor_tensor_scan", incl[:, :, h], M["ones"][:, 0:NT], tot[:, :, h], 0.0, op0=ALU.mult, op1=ALU.add,
               r=["tot", "ones"], w=[("incl", h)])
    inclk = [("incl", h) for h in range(4)]
    self.v("dve", "tensor_tensor", a_all, ie, incl, ALU.subtract, r=["ie"] + inclk, w=["a_all"])
    self.v("dve", "tensor_tensor", a_all, a_all, tot, ALU.add, r=["a_all", "tot"], w=["a_all"])
    self.v("dve", "tensor_tensor", a_all.rearrange("p t h -> p (t h)"), a_all.rearrange("p t h -> p (t h)"), ps[:, 0, 0:NT * 4],
           ALU.subtract, r=["a_all", ("P", 0)], w=["a_all"])
    self.v("dve", "tensor_reduce", pm, a_all.rearrange("p t h -> p h t"), AX.X, ALU.max, r=["a_all"], w=["pm"])
    self.mm(ps[0:4, 2, 0:128], pm, ident, r=["pm", "ident"], w=[("P", 2)])
    self.v("dve", "tensor_reduce", row1[0:4], ps[0:4, 2, 0:128], AX.X, ALU.max, r=[("P", 2)], w=["row1"])
    self.v("dve", "tensor_scalar", row1[0:4], row1[0:4], 0.0, None, op0=ALU.max, r=["row1"], w=["row1"])
    self.v("dve", "tensor_tensor", dg[0:4], ident[0:4, 0:4], row1[0:4, 0:1].to_broadcast([4, 4]), ALU.mult, r=["row1", "ident"], w=["pdg"])
    self.mm(ps[:, 2, 128:132], M["ones"][0:4, :], dg[0:4], r=["pdg", "ones"], w=[("P", 2)])
    self.v("dve", "tensor_copy", Mg_bc, ps[:, 2, 128:132], r=[("P", 2)], w=["Mg_bc"])
    self.v("dve", "tensor_tensor", wk_all, a_all, Mg_bc.unsqueeze(1).to_broadcast([128, NT, 4]), ALU.subtract,
           r=["a_all", "Mg_bc"], w=["wk_all"])
    self.act(wk_all, wk_all, AF.Exp, r=["wk_all"], w=["wk_all"])
    self.v("dve", "tensor_tensor", M["m_bc"], incl[:, NT - 1, :], Mg_bc, ALU.add, r=inclk + ["Mg_bc"], w=["m_bc"])
    self.mm(ps[0:4, 2, 136:137], M["m_bc"][0:1, 0:4], M["ones"][0:1, 0:1], r=["m_bc", "ones"], w=[("P", 2)])
    self.v("dve", "tensor_copy", M["m_row"][0:4], ps[0:4, 2, 136:137], r=[("P", 2)], w=["m_row"])
    cw, cbias = C["conv_wT"], C["conv_bT"]
    hist = A.alloc([8, 3])
    self.v("pool", "memset", hist, 0.0, w=["hist"])
    for g0 in range(0, NT, 4):
        if g0 < 32:
            for q in range(4):
                i = g0 + q
                self.dma("sp", hT4[:, :, q * 128:(q + 1) * 128], hTd[i], "ldx", r=[("hTd", i)], w=[(("hT4", q), c) for c in range(8)])
            src4 = hT4
            hk4 = [("hT4", q) for q in range(4)]
            tsl = lambda q: hT4[:, :, q * 128:(q + 1) * 128]
        else:
            t0 = g0 - 32
            src4 = hT[:, :, t0 * 128:(t0 + 4) * 128]
            hk4 = [("hT", t0 + q) for q in range(4)]
            tsl = lambda q, t0=t0: hT[:, :, (t0 + q) * 128:(t0 + q + 1) * 128]
        for fb in range(8):
            bank = fb % 2
            for c in range(8):
                self.mm(ps[:, bank, :], M["wxm"][:, c, fb * 128:(fb + 1) * 128], src4[:, c, :], start=(c == 0), stop=(c == 7),
                        r=["wxm"] + [(k, c) for k in hk4], w=[("P", bank)])
            self.act(cb4[:, :, fb, 3:131], ps[:, bank, :].rearrange("p (q t) -> p q t", q=4), AF.Copy,
                     r=[("P", bank)], w=[("pcb", q) for q in range(4)])
        def tile_body(q):
            i = g0 + q
            par = i % 2
            cacc = caccs[par]
            vb0 = 2 if par == 0 else 0
            kmb = 4 if par == 0 else 7
            hTt, hk = tsl(q), [hk4[q]]
            tv = C["tvalid"][:, i:i + 1]
            cbp = cb4[:, q]
            for half in range(2):
                for c in range(8):
                    self.mm(ps[:, vb0 + half, :], hTt[:, c, :], M["wxm"][:, c, half * 512:(half + 1) * 512], start=(c == 0), stop=(c == 7),
                            r=["wxm"] + [(k, c) for k in hk], w=[("P", vb0 + half)])
                self.act(vaug[par][:, 2 * half:2 * half + 2, 0:256], ps[:, vb0 + half, :].rearrange("p (h v) -> p h v", h=2), AF.Copy,
                         r=[("P", vb0 + half), ("pvaug1", par)], w=[("pvaug", par)])
                yield
            for fb in range(8):
                self.act(cacc[:, fb, :], cbp[:, fb, 3:131], AF.Identity, r=[("pcb", q), "conv_wT", "conv_bT"], w=[("pcacc", par, fb)],
                         scale=cw[:, fb, 3:4], bias=cbias[:, fb:fb + 1])
            yield
            for j in range(3):
                for fb in range(8):
                    self.v("dve", "scalar_tensor_tensor", cacc[:, fb, :], cbp[:, fb, j:j + 128], cw[:, fb, j:j + 1], cacc[:, fb, :],
                           op0=ALU.mult, op1=ALU.add, r=[("pcb", q), ("pcacc", par, fb)], w=[("pcacc", par, fb)])
                yield
            for fb in range(8):
                self.act(c_act[par][:, fb, :], cacc[:, fb, :], AF.Silu, r=[("pcacc", par, fb)], w=[("pc_act", par, fb)])
            yield
            kmps = ps[:, kmb, :].rearrange("p (h k) -> p h k", h=4)
            for h in range(4):
                for ec in range(2):
                    self.mm(kmps[:, h, :], c_act[par][:, 2 * h + ec, :], M["wmk"][:, ec, h, :], start=(ec == 0), stop=(ec == 1),
                            r=[("pc_act", par, 2 * h + ec), "wmk"], w=[("P", kmb)])
            yield
            self.v("dve", "tensor_tensor", kmw[par], kmps, wk_all[:, i, :].unsqueeze(2).to_broadcast([128, 4, 128]), ALU.mult,
                   r=[("P", kmb), "wk_all"], w=[("pkmw", par)])
            yield
            for h in range(4):
                bk = vb0 + (h % 2)
                dps = ps[:, bk, 0:257]
                self.mm(dps, kmw[par][:, h, :], vaug[par][:, h, :], r=[("pkmw", par), ("pvaug", par)], w=[("P", bk)])
                self.v("dve", "tensor_tensor", M["CT"][:, h, :], M["CT"][:, h, :], dps, ALU.add, r=[("P", bk), "CT"], w=["CT"])
                yield

        for q in range(4):
            i = g0 + q
            tv = C["tvalid"][:, i:i + 1]
            cbp = cb4[:, q]
            self.v("pool", "tensor_copy", cbp[:, :, 0:3], hist, r=["hist"], w=[("pcb", q)])
            self.v("dve", "tensor_scalar", hist, cbp[:, :, 128:131], tv, M["one1"], op0=ALU.mult, op1=ALU.mult,
                   r=[("pcb", q), "tvalid", "one1"], w=["hist"])
        for pair in ((0, 1), (2, 3)):
            live = [tile_body(q) for q in pair]
            while live:
                for gen in list(live):
                    try:
                        next(gen)
                    except StopIteration:
                        live.remove(gen)
    self.v("pool", "tensor_copy", M["convbuf"][:, :, 0:3], hist, r=["hist"], w=[("cb", 0), ("cb", 1)])


Builder.mlstm_prefix = _mlstm_prefix
```

```python
import numpy as np
import concourse.bass as bass
import concourse.mybir as mybir
from concourse.bass_utils import run_bass_kernel_spmd

F32 = mybir.dt.float32
BF16 = mybir.dt.bfloat16
AF = mybir.ActivationFunctionType
ALU = mybir.AluOpType
AX = mybir.AxisListType

NCORES = 8
D = 1024
SEQ = 8192
SEG = 2048
NLT = SEG // 128
HALO = 2048
PREFIX = 6144
TS = 32
PW = 10248
OFF_Q, OFF_K, OFF_V, OFF_ZA, OFF_XM, OFF_ZM, OFF_OM, OFF_I, OFF_F, OFF_GA, OFF_GM = (
    0, 1536, 3072, 4608, 5120, 6144, 7168, 8192, 8196, 8200, 9224)
GROUPS = ((128, 1), (512, 4), (2048, 16))
EPS = 1e-6
NEG = -30000.0
RAW_GAP = 1

DEBUG = {}


class _Op:
    __slots__ = ("eng", "fn", "deps", "stream", "signal", "val", "idx", "pos", "slot")


class Prog:
    ENGS = ("pe", "act", "dve", "pool", "sp")

    def __init__(self, nc):
        self.nc = nc
        self.ops = {e: [] for e in self.ENGS}
        self.lastw = {}
        self.readers = {}
        self.streams = {}
        self.barrier_deps = []
        self.nops = 0

    def add(self, eng, fn, r=(), w=(), dma=None):
        op = _Op()
        op.eng = eng
        op.fn = fn
        op.stream = dma
        op.signal = False
        op.val = None
        op.idx = self.nops
        self.nops += 1
        deps = {}
        for k in r:
            d = self.lastw.get(k)
            if d is not None:
                deps[d.idx] = (d, True)
            if isinstance(k, tuple) and k[0] == "P":
                for d in self.readers.get(k, ()):
                    if d.eng != eng and d.idx not in deps:
                        deps[d.idx] = (d, False)
        for k in w:
            d = self.lastw.get(k)
            if d is not None and d.idx not in deps:
                deps[d.idx] = (d, False)
            for d in self.readers.get(k, ()):
                if d.idx not in deps:
                    deps[d.idx] = (d, False)
        for d in self.barrier_deps:
            if d.idx not in deps:
                deps[d.idx] = (d, True)
        op.deps = list(deps.values())
        for k in w:
            self.lastw[k] = op
            self.readers[k] = []
        for k in r:
            self.readers.setdefault(k, []).append(op)
        op.pos = len(self.ops[eng])
        self.ops[eng].append(op)
        if dma is not None:
            self.streams.setdefault(dma, []).append(op)
        return op

    KSEM = 8
    KSEM_STREAM = {"cpy": 32}
    PERSIST = ("cpy",)

    def kof(self, s):
        return self.KSEM_STREAM.get(s, self.KSEM)

    def barrier(self):
        deps = []
        for e in self.ENGS:
            if self.ops[e]:
                for op in reversed(self.ops[e]):
                    if op.stream is None:
                        deps.append(op)
                        break
        for s, lst in self.streams.items():
            if s in self.PERSIST:
                continue
            deps.extend(lst[-self.kof(s):])
        self.barrier_deps = deps
        self.lastw = {k: v for k, v in self.lastw.items() if v.stream in self.PERSIST}
        self.readers = {}

    def emit(self, final_streams):
        nc = self.nc
        K = self.KSEM
        need = {}
        for e in self.ENGS:
            for op in self.ops[e]:
                lst = []
                for d, raw in op.deps:
                    if d.stream is None and d.eng == e:
                        if e in ("pe", "sp"):
                            continue
                        if not raw:
                            continue
                        if op.pos - d.pos > RAW_GAP:
                            continue
                    d.signal = True
                    lst.append(d)
                need[op.idx] = lst
        for e in self.ENGS:
            cnt = 0
            for op in self.ops[e]:
                if op.stream is None and op.signal:
                    cnt += 1
                    op.val = cnt
        for s, lst in self.streams.items():
            Ks = self.kof(s)
            for i, op in enumerate(lst):
                op.slot = i % Ks
                op.val = 16 * (i // Ks + 1)
        import contextlib
        with contextlib.ExitStack() as st:
            sems = {e: st.enter_context(nc.semaphore("s_" + e)) for e in self.ENGS}
            ssems = {s: [st.enter_context(nc.semaphore("d_%s_%d" % (s, k))) for k in range(min(self.kof(s), len(lst)))]
                     for s, lst in self.streams.items()}
            block = st.enter_context(nc.Block())

            def run(e, eng):
                waited = {}

                def wait(key, v):
                    if v > waited.get(key, 0):
                        waited[key] = v
                        sem = ssems[key[1]][key[2]] if key[0] == "s" else sems[key[1]]
                        eng.wait_ge(sem, v)

                for op in self.ops[e]:
                    w = {}
                    for d in need[op.idx]:
                        key = ("s", d.stream, d.slot) if d.stream is not None else ("e", d.eng)
                        if d.val > w.get(key, 0):
                            w[key] = d.val
                    for key, v in w.items():
                        wait(key, v)
                    if op.stream is not None and op.val > 16:
                        wait(("s", op.stream, op.slot), op.val - 16)
                    ins = op.fn(eng)
                    if op.stream is not None:
                        ins.then_inc(ssems[op.stream][op.slot], 16)
                    elif op.signal:
                        ins.then_inc(sems[e], 1)
                if e == "sp":
                    for s in final_streams:
                        if s in self.streams:
                            for op in self.streams[s][-self.kof(s):]:
                                wait(("s", s, op.slot), op.val)

            block.tensor(lambda t: run("pe", t))
            block.scalar(lambda t: run("act", t))
            block.vector(lambda t: run("dve", t))
            block.gpsimd(lambda t: run("pool", t))
            block.sync(lambda t: run("sp", t))


class Arena:
    def __init__(self, nc, words):
        self.t = nc.alloc_sbuf_tensor("arena", [128, words], F32)
        self.words = words
        self.top = 0
        self.marks = []

    def push(self):
        self.marks.append(self.top)

    def pop(self):
        self.top = self.marks.pop()

    def alloc(self, shape, dt=F32):
        n = int(np.prod(shape))
        words = (n + 1) // 2 if dt == BF16 else n
        words = (words + 7) // 8 * 8
        assert self.top + words <= self.words, ("SBUF arena overflow", self.top, words, self.words)
        ap = self.t[:, self.top:self.top + words]
        self.top += words
        if dt == BF16:
            ap = ap.bitcast(BF16)
        ap = ap[:, 0:n]
        if len(shape) == 2:
            ap = ap.rearrange("p (a b) -> p a b", a=shape[0])
        elif len(shape) == 3:
            ap = ap.rearrange("p (a b c) -> p a b c", a=shape[0], b=shape[1])
        elif len(shape) == 4:
            ap = ap.rearrange("p (a b c d) -> p a b c d", a=shape[0], b=shape[1], c=shape[2])
        return ap


def _t5_bucket(dist):
    n = np.asarray(dist).astype(np.int64)
    max_exact = 16
    nf = np.maximum(n, 1).astype(np.float32)
    large = max_exact + (np.log(nf / max_exact) / np.log(np.float32(2048 / max_exact))
                         * (32 - max_exact)).astype(np.int64)
    large = np.minimum(large, 31)
    return np.where(n < max_exact, n, large).astype(np.int32)


def _consts():
    c = {}
    c["ident"] = np.eye(128, dtype=np.float32)
    c["antiid"] = np.eye(128, dtype=np.float32)[::-1].copy()
    s = np.arange(128)
    c["tri"] = (s[:, None] <= s[None, :]).astype(np.float32)
    c["cmaskT"] = np.where(s[:, None] <= s[None, :], 0.0, NEG).astype(np.float32)
    oh = np.zeros((32, 3, 129), np.float32)
    for g, (_, d) in enumerate(GROUPS):
        b = _t5_bucket(np.arange(129) * d)
        oh[b, g, np.arange(129)] = 1.0
    c["ohd"] = oh
    selp = np.zeros((5, 128), np.float32); selp[0] = 1.0
    sels = np.zeros((5, 128), np.float32)
    for i in range(TS):
        sels[1 + i // 8, i] = 1.0
    c["selp"] = selp
    c["sels"] = sels
    return c


def _fm(v):
    return np.ascontiguousarray(v.reshape(-1, 128).T)


class Builder:
    def __init__(self):
        nc = bass.Bass("TRN2", target_bir_lowering=False)
        self.nc = nc
        self.P = Prog(nc)
        self.A = Arena(nc, 53184)
        self.ps = nc.alloc_psum_tensor("psum", [128, 8, 512], F32)
        self.din = {}
        self.dout = {}
        self.uid = 0

    def inp(self, name, shape, dt=F32):
        t = self.nc.dram_tensor(name, list(shape), dt, kind="ExternalInput").ap()
        self.din[name] = t
        return t

    def outp(self, name, shape, dt=F32):
        t = self.nc.dram_tensor(name, list(shape), dt, kind="ExternalOutput").ap()
        self.dout[name] = t
        return t

    def scratch(self, name, shape, dt=F32):
        return self.nc.dram_tensor(name, list(shape), dt).ap()

    def mm(self, out, lhsT, rhs, start=True, stop=True, r=(), w=()):
        return self.P.add("pe", lambda e: e.matmul(out, lhsT, rhs, start=start, stop=stop), r, w)

    def tr(self, out, in_, ident, r=(), w=()):
        return self.P.add("pe", lambda e: e.transpose(out, in_, ident), r, w)

    def act(self, out, in_, func, r=(), w=(), **kw):
        return self.P.add("act", lambda e: e.activation(out=out, in_=in_, func=func, **kw), r, w)

    def v(self, eng, name, *args, r=(), w=(), **kw):
        return self.P.add(eng, lambda e: getattr(e, name)(*args, **kw), r, w)

    def dma(self, eng, out, in_, stream, r=(), w=(), **kw):
        return self.P.add(eng, lambda e: e.dma_start(out=out, in_=in_, **kw), r, w, dma=stream)

    def dbg(self, name, ap, shape, dt=F32, r=()):
        if not DEBUG.get(name):
            return
        o = self.outp("dbg_" + name, shape, dt)
        self.P.barrier()
        self.dma("sp", o, ap, "dbg", r=r)

    def key(self, base):
        self.uid += 1
        return (base, self.uid)

    def phase0(self):
        A, P, ps = self.A, self.P, self.ps
        C = self.C = {}

        def load(name, shape, parts=128, dt=F32, eng="sp", src=None):
            t = A.alloc(list(shape[1:]), dt)
            src = self.inp(name, shape) if src is None else src
            self.dma(eng, t[0:parts], src, "ld0", w=[name])
            C[name] = t
            return t

        load("ident", [128, 128])
        C["ident_bf"] = A.alloc([128], BF16)
        self.dma("pool", C["ident_bf"], self.din["ident"], "ldc", w=["ident_bf"])
        C["antiid_bf"] = A.alloc([128], BF16)
        self.dma("pool", C["antiid_bf"], self.inp("antiid", [128, 128]), "ldc", w=["antiid_bf"])
        load("tri", [128, 128])
        load("antiid", [128, 128], src=self.din["antiid"])
        C["cmaskT_bf"] = A.alloc([128], BF16)
        self.dma("pool", C["cmaskT_bf"], self.inp("cmaskT", [128, 128]), "ldc", w=["cmaskT_bf"])
        load("ohd", [32, 3, 129], parts=32)
        load("selp", [5, 128], parts=5)
        load("sels", [5, 128], parts=5)
        load("rel_table", [32, 24], parts=32)
        for nm in ("gainT", "conv_bT", "m_normT", "m_skipT"):
            load(nm, [128, 8])
        load("b_adaT", [128, 24])
        load("conv_wT", [128, 8, 4])
        load("b_if_bc", [128, 8])
        load("tvalid", [128, 64])
        load("cT", [128, 8, 5])

        siluT = A.alloc([8, 5], BF16)
        self.act(siluT, C["cT"], AF.Silu, r=["cT"], w=["siluT"])

        ada = A.alloc([24, 5])
        mult = A.alloc([8, 5])
        gate_p = A.alloc([1024])
        gate_s = A.alloc([1024])
        A.push()
        load("b_gate_rows", [5, 1024], parts=5)
        gate_rows = A.alloc([1024])
        w_ada = self.inp("w_ada", [1024, 3072])
        wada = A.alloc([8, 3072], BF16)
        wv = w_ada.rearrange("(c p) n -> p c n", p=128)
        for c in range(8):
            self.dma("pool", wada[:, c, :], wv[:, c, :], "ldw", w=[("wada", c)])
        adaps = ps[:, 0, 0:120].rearrange("p (a b) -> p a b", a=24)
        for cb in range(24):
            for c in range(8):
                self.mm(adaps[:, cb, :], wada[:, c, cb * 128:(cb + 1) * 128], siluT[:, c, :],
                        start=(c == 0), stop=(c == 7), r=[("wada", c), "siluT"], w=[("P", 0)])
        for half in range(2):
            for c in range(8):
                self.mm(ps[0:5, 1 + half, :], siluT[:, c, :], wada[:, c, 2048 + half * 512:2048 + (half + 1) * 512],
                        start=(c == 0), stop=(c == 7), r=[("wada", c), "siluT"], w=[("P", 1 + half)])
        self.v("dve", "tensor_tensor", ada, adaps, C["b_adaT"].unsqueeze(2).to_broadcast([128, 24, 5]), ALU.add,
               r=[("P", 0), "b_adaT"], w=["ada"])
        self.v("dve", "tensor_scalar", mult, ada[:, 8:16, :], 1.0, None, op0=ALU.add, r=["ada"], w=["mult"])
        self.v("dve", "tensor_tensor", mult, mult, C["gainT"].unsqueeze(2).to_broadcast([128, 8, 5]), ALU.mult,
               r=["mult", "gainT"], w=["mult"])
        C["mult"] = mult
        C["shift"] = ada[:, 0:8, :]
        C["ada"] = ada
        for half in range(2):
            self.v("dve", "tensor_tensor", gate_rows[0:5, half * 512:(half + 1) * 512], ps[0:5, 1 + half, :],
                   C["b_gate_rows"][0:5, half * 512:(half + 1) * 512], ALU.add,
                   r=[("P", 1 + half), "b_gate_rows"], w=[("gate_rows", half)])
        for half in range(2):
            sl = slice(half * 512, (half + 1) * 512)
            self.mm(ps[:, 3, :], C["selp"][0:5, :], gate_rows[0:5, sl], r=[("gate_rows", half), "selp"], w=[("P", 3)])
            self.act(gate_p[:, sl], ps[:, 3, :], AF.Copy, r=[("P", 3)], w=[("gate_p", half)])
            self.mm(ps[:, 4, :], C["sels"][0:5, :], gate_rows[0:5, sl], r=[("gate_rows", half), "sels"], w=[("P", 4)])
            self.act(gate_s[:, sl], ps[:, 4, :], AF.Copy, r=[("P", 4)], w=[("gate_s", half)])
        A.pop()
        P.barrier()
        C["gate_p"] = gate_p
        C["gate_s"] = gate_s
        self.dbg("ada", ada, [128, 24, 5], r=["ada"])
        self.dbg("gate_s", gate_s, [128, 1024], r=[("gate_s", 0), ("gate_s", 1)])

    def norm_tile(self, xsrc, ntok, hT_dst, kind, slot, hkey, bank0=6):
        A, P, ps, C = self.A, self.P, self.ps, self.C
        W = self.W1
        i3, i2 = slot % len(W["xt"]), slot % 2
        xt = W["xt"][i3]
        self.dma("sp", xt[0:ntok], xsrc, "ldx", w=[("xt", i3)])
        ss = W["ss"][:, slot % 4:slot % 4 + 1]
        self.act(W["junk"][0:ntok], xt[0:ntok], AF.Square, r=[("xt", i3)], w=["junk", ("ss", slot % 4)],
                 accum_out=ss[0:ntok])
        self.v("dve", "tensor_scalar", ss[0:ntok], ss[0:ntok], 1.0 / D, EPS, op0=ALU.mult, op1=ALU.add,
               r=[("ss", slot % 4)], w=[("ss", slot % 4)])
        self.act(ss[0:ntok], ss[0:ntok], AF.Ln, r=[("ss", slot % 4)], w=[("ss", slot % 4)])
        self.act(ss[0:ntok], ss[0:ntok], AF.Exp, r=[("ss", slot % 4)], w=[("ss", slot % 4)], scale=-0.5)
        xn = W["xn"][i2]
        self.act(xn[0:ntok], xt[0:ntok], AF.Copy, r=[("xt", i3), ("ss", slot % 4)], w=[("xn", i2)], scale=ss[0:ntok])
        if DEBUG.get("stage", 9) < 1:
            return
        bank = bank0 + i2
        pt = ps[:, bank, :].bitcast(BF16).rearrange("p (c t) -> p c t", c=8)
        for c in range(8):
            self.tr(pt[:, c, 0:ntok], xn[0:ntok, c * 128:(c + 1) * 128], C["ident_bf"][0:ntok, 0:ntok],
                    r=[("xn", i2), "ident_bf"], w=[("P", bank0 + i2)])
        if DEBUG.get("stage", 9) < 2:
            return
        for c in range(8):
            if kind == "p":
                if True:
                    self.act(hT_dst[:, c, :], pt[:, c, 0:ntok], AF.Identity, r=[("P", bank0 + i2), "mult", "ada"], w=[(hkey, c)],
                             scale=C["mult"][:, c, 0:1], bias=C["shift"][:, c, 0:1])
                else:
                    self.v("dve", "tensor_scalar", hT_dst[:, c, :], pt[:, c, 0:ntok], C["mult"][:, c, 0:1], C["shift"][:, c, 0:1],
                           op0=ALU.mult, op1=ALU.add, r=[("P", bank0 + i2), "mult", "ada"], w=[(hkey, c)])
            else:
                tmp = W["stmp"]
                tmp0 = W["stmp0"]
                self.act(tmp0, pt[:, c, 0:ntok], AF.Copy, r=[("P", bank0 + i2)], w=["stmp0"])
                self.v("dve", "tensor_tensor", tmp.rearrange("p (s t) -> p s t", s=4),
                       tmp0.rearrange("p (s t) -> p s t", s=4),
                       C["mult"][:, c, 1:5].unsqueeze(2).to_broadcast([128, 4, 8]), ALU.mult,
                       r=["stmp0", "mult"], w=["stmp"])
                self.v("dve", "tensor_tensor", hT_dst[:, c, :].rearrange("p (s t) -> p s t", s=4),
                       tmp.rearrange("p (s t) -> p s t", s=4),
                       C["shift"][:, c, 1:5].unsqueeze(2).to_broadcast([128, 4, 8]), ALU.add,
                       r=["stmp", "ada"], w=[(hkey, c)])

    def phase1(self):
        A = self.A
        self.hT = A.alloc([8, HALO + SEG + TS], BF16)
        A.push()
        self.W1 = {
            "xt": [A.alloc([1024]) for _ in range(3)],
            "ss": A.alloc([4]),
            "junk": A.alloc([1024], BF16),
            "xn": [A.alloc([1024], BF16) for _ in range(2)],
            "stmp": A.alloc([32]),
            "stmp0": A.alloc([32]),
        }
        xh = self.inp("xh", [HALO + SEG, 1024])
        xs = self.inp("xs", [TS, 1024])
        slot = 0
        if not DEBUG.get("skip_s"):
            self.norm_tile(xs, TS, self.hT[:, :, HALO + SEG:HALO + SEG + TS], "s", slot, ("hT", 32))
        slot += 1
        for ti in range(DEBUG.get("ntiles", (HALO + SEG) // 128)):
            self.norm_tile(xh[ti * 128:(ti + 1) * 128, :], 128, self.hT[:, :, ti * 128:(ti + 1) * 128], "p", slot, ("hT", ti))
            slot += 1
        self.dbg("hT", self.hT, [128, 8, HALO + SEG + TS], BF16, r=[])
        self.dbg("xn", self.W1["xn"][1], [128, 1024], BF16, r=[])
        A.pop()


def build_program(upto=99):
    b = Builder()
    b.phase0()
    b.sample_copies()
    b.w_in = b.inp("w_in", [1024, PW])
    if upto >= 1:
        b.P.barrier()
        b.phase1()
    if upto >= 2:
        b.P.barrier()
        b.attT = b.A.alloc([4, SEG + TS], BF16)
        b.C["F"] = b.A.alloc([3, 129])
        b.A.push()
        b.phase_bias()
        b.P.barrier()
        if DEBUG.get("att_stage", 9) >= 1:
            b.phase_attention()
        b.A.pop()
        b.P.barrier()
        if not DEBUG.get("no_sattn"):
            b.phase_sample_attn()
        b.dbg("attT2", b.attT, [128, 4, SEG + TS], BF16)
    if upto >= 3:
        b.P.barrier()
        b.phase_mlstm()
    if upto >= 4:
        b.P.barrier()
        b.phase_out()
    if DEBUG.get("dmult"):
        DEBUG["mult_end"] = True
        b.dbg("mult_end", b.C["mult"], [128, 8, 5])
    b.P.emit(final_streams=list(b.P.streams.keys()))
    return b


def make_in_maps(inp, cores):
    consts = _consts()
    f32 = np.float32
    maps = []
    xp = inp["x_prompt"]
    for c in cores:
        b, p = c // 4, c % 4
        s0 = p * SEG
        m = dict(consts)
        ext = np.zeros((PREFIX + SEG, D), f32)
        lo = s0 - PREFIX
        src_lo = max(lo, 0)
        ext[src_lo - lo:] = xp[b, src_lo:s0 + SEG]
        m["xf"] = np.ascontiguousarray(ext[:PREFIX - HALO])
        m["xh"] = np.ascontiguousarray(ext[PREFIX - HALO:])
        m["xs"] = np.ascontiguousarray(inp["x_sample"][4 * c:4 * c + 4].reshape(TS, D))
        tv = np.zeros(64, f32)
        tv[(src_lo - lo) // 128:] = 1.0
        m["tvalid"] = np.ascontiguousarray(np.broadcast_to(tv, (128, 64)))
        call = np.concatenate([inp["c_prompt"][b:b + 1], inp["c_sample"][4 * c:4 * c + 4]], 0)
        m["cT"] = np.ascontiguousarray(call.T.reshape(8, 128, 5).transpose(1, 0, 2))
        m["w_ada"] = inp["w_ada"][0]
        m["rel_table"] = inp["rel_table"]
        m["gainT"] = _fm(inp["norm_gain"][0])
        m["conv_bT"] = _fm(inp["conv_b"][0])
        m["m_normT"] = _fm(inp["m_norm"][0])
        m["m_skipT"] = _fm(inp["m_skip"][0])
        m["b_adaT"] = _fm(inp["b_ada"][0])
        m["conv_wT"] = np.ascontiguousarray(inp["conv_w"][0].reshape(4, 8, 128).transpose(2, 1, 0))
        m["b_gate_rows"] = np.ascontiguousarray(np.broadcast_to(inp["b_ada"][0][2048:], (5, 1024)))
        m["fgain_bc"] = np.ascontiguousarray(np.broadcast_to(inp["final_gain"], (128, 1024)))
        m["b_if_bc"] = np.ascontiguousarray(np.broadcast_to(inp["b_if"][0], (128, 8)))
        m["w_in"] = inp["w_in"][0]
        st_ = np.zeros((8, 8, 128), f32)
        for t in range(8):
            st_[t, t, :] = 1.0
        m["selt"] = st_
        osl = np.zeros((128, 8, 8), f32)
        for t in range(8):
            osl[:, t, t] = 1.0
        m["onesel"] = osl
        for g, nm in enumerate(("cache_kv_w128", "cache_kv_w512", "cache_kv_w2048")):
            m["cache%d" % g] = np.ascontiguousarray(inp[nm][0, 4 * c:4 * c + 4].reshape(4, -1, 2, 512))
        m["w_pa"] = inp["w_pa"][0]
        m["w_pm"] = inp["w_pm"][0]
        m["w_out"] = inp["w_out"][0]
        m["w_mq"] = inp["w_mq"][0]
        m["w_mk"] = inp["w_mk"][0]
        eh = np.zeros((4, 4, 128), f32)
        for h in range(4):
            eh[h, h, :] = 1.0
        m["ehsel"] = eh
        sq = slice(4 * c, 4 * c + 4)
        Cst = inp["state_C"][0, sq]
        nst = inp["state_n"][0, sq]
        c0 = np.concatenate([Cst.transpose(0, 3, 1, 2), nst.transpose(0, 2, 1)[..., None]], axis=-1)
        m["C0T"] = np.ascontiguousarray(c0)
        mst = inp["state_m"][0, sq]
        m["m0row"] = np.ascontiguousarray(mst[:, :, None])
        m["m0bc"] = np.ascontiguousarray(np.broadcast_to(mst[:, None, :], (4, 128, 4)))
        cvs = inp["state_conv"][0, sq]
        m["conv0"] = np.ascontiguousarray(cvs.reshape(4, 3, 8, 128).transpose(0, 3, 2, 1))
        m["coremask"] = np.full((128, 128), NEG if p == 0 else 0.0, f32)
        sm = np.zeros((128, 4, 128), f32)
        for e in range(64):
            sm[e, 0, e] = 1.0
            sm[e, 1, 64 + e] = 1.0
            sm[64 + e, 2, e] = 1.0
            sm[64 + e, 3, 64 + e] = 1.0
        m["selmats"] = sm
        maps.append(m)
    return maps


def run_cores(inp, cores, upto=99):
    b = build_program(upto)
    maps = make_in_maps(inp, cores)
    maps = [{k: np.ascontiguousarray(v, dtype=np.float32) for k, v in m.items() if k in b.din} for m in maps]
    res = run_bass_kernel_spmd(b.nc, maps, core_ids=list(range(len(cores))))
    return res.results


def _phase_bias(self):
    A, P, ps, C = self.A, self.P, self.ps, self.C
    F_sb = C["F"]
    gv = A.alloc([3, 2, 256])
    self.v("pool", "memset", gv[0:8], NEG, w=["gv"])
    for g in range(3):
        self.mm(ps[0:8, 5, 0:129], C["rel_table"][0:32, g * 8:(g + 1) * 8], C["ohd"][0:32, g, :],
                r=["rel_table", "ohd"], w=[("P", 5)])
        self.act(F_sb[0:8, g, :], ps[0:8, 5, 0:129], AF.Copy, r=[("P", 5)], w=[("F", g)])
        self.v("pool", "tensor_copy", gv[0:8, g, 1, 127:255], F_sb[0:8, g, 0:128], r=[("F", g), "gv"], w=[("gv", g)])
        self.v("pool", "tensor_copy", gv[0:8, g, 0, 0:128], F_sb[0:8, g, 1:129], r=[("F", g), "gv"], w=[("gv", g)])
    gvd = self.scratch("gvd", [8, 3, 2, 256])
    self.dma("sp", gvd, gv[0:8], "gvw", r=[("gv", 0), ("gv", 1), ("gv", 2)], w=["gvd"])
    biasH = A.alloc([24, 256], BF16)
    for g in range(3):
        for h in range(8):
            for kb in range(2):
                off = ((h * 3 + g) * 2 + kb) * 256
                src = bass.AP(tensor=gvd.tensor, offset=off, ap=[[1, 128], [1, 128]])
                self.dma("pool", biasH[:, g * 8 + h, kb * 128:(kb + 1) * 128], src, "ldc", r=["gvd"], w=[("biasH", g)])
    C["biasH"] = biasH
    cm = A.alloc([128], BF16)
    self.dma("pool", cm, self.inp("coremask", [128, 128]), "ldc", w=["coremask"])
    C["coremask"] = cm
    C["selmats"] = A.alloc([4, 128])
    self.dma("sp", C["selmats"], self.inp("selmats", [128, 4, 128]), "ld0", w=["selmats"])


def _phase_attention(self):
    A, P, ps, C, hT = self.A, self.P, self.ps, self.C, self.hT
    w_in = self.w_in
    wv_in = w_in.rearrange("(c p) n -> p c n", p=128)
    kvp = [self.outp("kvp%d" % g, [GROUPS[g][0], 2, 512]) for g in range(3)]
    A.push()
    acc = A.alloc([2, SEG])
    wq = A.alloc([8, 128], BF16)
    wkv = A.alloc([8, 256], BF16)
    wz = A.alloc([8, 128], BF16)
    qT = A.alloc([SEG], BF16)
    kT = A.alloc([4096], BF16)
    vaug = A.alloc([32, 2, 128], BF16)
    pT = [A.alloc([256], BF16) for _ in range(4)]
    stage = [A.alloc([256]) for _ in range(2)]
    rbuf = A.alloc([512])
    att = A.alloc([512])
    sz = A.alloc([512])
    if not DEBUG.get("no_vones"):
        self.v("pool", "memset", vaug[:, :, :, 64:128], 1.0, w=["vones"])
    cnt = 0
    STG = DEBUG.get("att_stage", 9)
    for hp in range(DEBUG.get("att_hp", 4)):
        for g, (win, d) in enumerate(GROUPS):
            if g not in DEBUG.get("att_groups", (0, 1, 2)):
                continue
            U = SEG // d
            U2 = U + 128
            col = g * 512 + hp * 128
            allc = lambda nm: [(nm, c) for c in range(8)]
            self.dma("pool", wq, wv_in[:, :, OFF_Q + col:OFF_Q + col + 128], "ldw", w=allc("wq"))
            self.dma("pool", wkv[:, :, 0:128], wv_in[:, :, OFF_K + col:OFF_K + col + 128], "ldw", w=allc("wkv"))
            self.dma("pool", wkv[:, :, 128:256], wv_in[:, :, OFF_V + col:OFF_V + col + 128], "ldw", w=allc("wkv"))
            hq = [hT[:, c, HALO:HALO + SEG].rearrange("p (u r) -> p r u", r=d) for c in range(8)]
            hk = [hT[:, c, HALO - 128 * d:HALO + SEG].rearrange("p (u r) -> p r u", r=d) for c in range(8)]

            def chunks(Ux, total):
                res = []
                if Ux >= 512:
                    for r in range(d):
                        u0 = 0
                        while u0 < Ux:
                            n = min(512, Ux - u0)
                            res.append((r, 1, u0, n))
                            u0 += n
                else:
                    nr = 512 // Ux
                    for r0 in range(0, d, nr):
                        res.append((r0, nr, 0, Ux))
                return res

            def tile_keys(view_lo, r0, nr, u0, n, c):
                lo = view_lo + r0 + d * u0
                hi = view_lo + r0 + nr - 1 + d * (u0 + n - 1)
                return [(("hT", t), c) for t in range(lo // 128, hi // 128 + 1)]

            for (r0, nr, u0, n) in chunks(U, SEG):
                bank = cnt % 2
                cnt += 1
                pso = ps[:, bank, 0:nr * n]
                pso3 = pso if nr == 1 else pso.rearrange("p (a b) -> p a b", a=nr)
                for c in range(8):
                    rhs = hq[c][:, r0, u0:u0 + n] if nr == 1 else hq[c][:, r0:r0 + nr, :]
                    self.mm(pso3, wq[:, c, :], rhs, start=(c == 0), stop=(c == 7),
                            r=[("wq", c)] + tile_keys(HALO, r0, nr, u0, n, c), w=[("P", bank)])
                f0 = r0 * U + u0
                self.act(qT[:, f0:f0 + nr * n], pso, AF.Copy, r=[("P", bank)], w=["qT"], scale=0.125)
            if STG < 2:
                continue
            for (r0, nr, u0, n) in chunks(U2, U2 * d):
                bank = cnt % 2
                cnt += 1
                pso = ps[:, bank, 0:nr * n]
                pso3 = pso if nr == 1 else pso.rearrange("p (a b) -> p a b", a=nr)
                for c in range(8):
                    rhs = hk[c][:, r0, u0:u0 + n] if nr == 1 else hk[c][:, r0:r0 + nr, :]
                    self.mm(pso3, wkv[:, c, 0:128], rhs, start=(c == 0), stop=(c == 7),
                            r=[("wkv", c)] + tile_keys(HALO - 128 * d, r0, nr, u0, n, c), w=[("P", bank)])
                f0 = r0 * U2 + u0
                self.act(kT[:, f0:f0 + nr * n], pso, AF.Copy, r=[("P", bank)], w=["kT"])
            nm = U2 // 128
            if STG < 3:
                continue
            for r in range(d):
                for m in range(nm):
                    blk = r * nm + m
                    bank = cnt % 2
                    cnt += 1
                    pso = ps[:, bank, 0:256]
                    lastb = (m == nm - 1)
                    for c in range(8):
                        self.mm(pso if lastb else pso[:, 128:256], hk[c][:, r, 128 * m:128 * m + 128],
                                wkv[:, c, :] if lastb else wkv[:, c, 128:256], start=(c == 0), stop=(c == 7),
                                r=[("wkv", c)] + tile_keys(HALO - 128 * d, r, 1, 128 * m, 128, c), w=[("P", bank)])
                    self.act(vaug[:, blk, :, 0:64], pso[:, 128:256].rearrange("p (h e) -> p h e", h=2), AF.Copy,
                             r=[("P", bank), "vones"], w=[("vaug", blk)])
                    if m == nm - 1 and not DEBUG.get("no_kvout"):
                        st = stage[blk % 2]
                        if DEBUG.get("kv_act"):
                            self.act(st, pso, AF.Copy, r=[("P", bank)], w=[("stage", blk % 2)])
                        else:
                            self.v("dve", "tensor_copy", st, pso, r=[("P", bank)], w=[("stage", blk % 2)])
                        dst = kvp[g].rearrange("(i r) k c -> r i k c", r=d)[r, :, :, hp * 128:(hp + 1) * 128]
                        if DEBUG.get("kv_plain"):
                            dst = kvp[g][0:128, :, hp * 128:(hp + 1) * 128]
                        if not DEBUG.get("kv_nodma"):
                            self.dma("sp", dst, st.rearrange("p (k c) -> p k c", k=2), "out", r=[("stage", blk % 2)], w=[])
            if STG < 4:
                continue
            for r in range(d):
                for n in range(U // 128):
                    for h in range(2):
                        gh = g * 8 + hp * 2 + h
                        hs = slice(h * 64, h * 64 + 64)
                        si = cnt % 3
                        cnt += 1
                        sbk, obk = (2, 3, 6)[si], (4, 5, 7)[si]
                        S = ps[:, sbk, 0:256]
                        O = ps[:, obk, 0:128]
                        self.mm(S, C["antiid_bf"], C["biasH"][:, gh, :], start=True, stop=False,
                                r=["antiid_bf", ("biasH", g)], w=[("P", sbk)])
                        if n == 0:
                            self.mm(S[:, 0:128], C["ident_bf"], C["coremask"], start=False, stop=False,
                                    r=["ident_bf", "coremask"], w=[("P", sbk)])
                        q_ap = qT[hs, r * U + 128 * n:r * U + 128 * n + 128]
                        self.mm(S[:, 0:128], kT[hs, r * U2 + 128 * n:r * U2 + 128 * n + 128], q_ap, start=False, stop=False,
                                r=["qT", "kT"], w=[("P", sbk)])
                        self.mm(S[:, 128:256], kT[hs, r * U2 + 128 * (n + 1):r * U2 + 128 * (n + 2)], q_ap, start=False, stop=True,
                                r=["qT", "kT"], w=[("P", sbk)])
                        self.act(pT[si], S, AF.Exp, r=[("P", sbk)], w=[("pT", si)])
                        b0 = r * nm + n
                        self.mm(O, vaug[:, b0, h, :], pT[si][:, 0:128], start=True, stop=False,
                                r=[("vaug", b0), ("pT", si)], w=[("P", obk)])
                        self.mm(O, vaug[:, b0 + 1, h, :], pT[si][:, 128:256], start=False, stop=True,
                                r=[("vaug", b0 + 1), ("pT", si)], w=[("P", obk)])
                        av = acc[:, h, :].rearrange("p (u r) -> p r u", r=d)[:, r, 128 * n:128 * n + 128]
                        if d == 1:
                            ak = [("acc", h, n // 4, rr) for rr in range(16)]
                        elif d == 4:
                            ak = [("acc", h, n, r + 4 * j) for j in range(4)]
                        else:
                            ak = [("acc", h, qq, r) for qq in range(4)]
                        if g == 0:
                            self.v("dve", "tensor_copy", av, O, r=[("P", obk)], w=ak)
                        else:
                            self.v("dve", "tensor_tensor", av, av, O, ALU.add, r=[("P", obk)] + ak, w=ak)
        if STG < 5:
            continue
        P.barrier()
        zc = OFF_ZA + hp * 128
        self.dma("pool", wz, wv_in[:, :, zc:zc + 128], "ldw", w=[("wz", c) for c in range(8)])
        for k in range(4):
            tk = slice(512 * k, 512 * k + 512)
            for j, (bank, sm) in enumerate(((6, (0, 1)), (7, (2, 3)))):
                for h in range(2):
                    self.mm(ps[:, bank, :], C["selmats"][:, sm[h], :], acc[:, h, tk], start=(h == 0), stop=(h == 1),
                            r=["selmats"], w=[("P", 6 + j)])
            self.v("dve", "reciprocal", rbuf, ps[:, 7, :], r=[("P", 7)], w=["rbuf"])
            self.v("dve", "tensor_tensor", att, ps[:, 6, :], rbuf, ALU.mult, r=[("P", 6), "rbuf"], w=["att"])
            for c in range(8):
                self.mm(ps[:, 0, :], wz[:, c, :], hT[:, c, HALO + 512 * k:HALO + 512 * k + 512], start=(c == 0), stop=(c == 7),
                        r=[("wz", c)] + [(("hT", t), c) for t in range(16 + 4 * k, 16 + 4 * k + 4)], w=[("P", 0)])
            self.act(sz, ps[:, 0, :], AF.Silu, r=[("P", 0)], w=["sz"])
            self.v("dve", "tensor_tensor", self.attT[:, hp, tk], att, sz, ALU.mult, r=["att", "sz"], w=[("attT", hp)])
        P.barrier()
    A.pop()
    self.dbg("attT", self.attT, [128, 4, SEG + TS], BF16)


Builder.phase_bias = _phase_bias
Builder.phase_attention = _phase_attention


def _mlstm_setup(self):
    A, P, C = self.A, self.P, self.C
    wv_in = self.w_in.rearrange("(c p) n -> p c n", p=128)
    M = self.M = {}
    M["wxm"] = A.alloc([8, 1024], BF16)
    M["wg"] = A.alloc([8, 8], BF16)
    M["wmq"] = A.alloc([2, 4, 128], BF16)
    M["wmk"] = A.alloc([2, 4, 128], BF16)
    for c in range(8):
        self.dma("pool", M["wxm"][:, c, :], wv_in[:, c, OFF_XM:OFF_XM + 1024], "ldw", w=["wxm"])
        self.dma("pool", M["wg"][:, c, :], wv_in[:, c, OFF_I:OFF_I + 8], "ldw", w=["wg"])
    wq_d = self.inp("w_mq", [4, 256, 128]).rearrange("h (c p) k -> p c h k", p=128)
    wk_d = self.inp("w_mk", [4, 256, 128]).rearrange("h (c p) k -> p c h k", p=128)
    for ec in range(2):
        for h in range(4):
            self.dma("pool", M["wmq"][:, ec, h, :], wq_d[:, ec, h, :], "ldw", w=["wmq"])
            self.dma("pool", M["wmk"][:, ec, h, :], wk_d[:, ec, h, :], "ldw", w=["wmk"])
    M["ones"] = A.alloc([128])
    self.v("pool", "memset", M["ones"], 1.0, w=["ones"])
    M["ehsel"] = A.alloc([4, 128])
    self.dma("sp", M["ehsel"][0:4], self.inp("ehsel", [4, 4, 128]), "ld0", w=["ehsel"])
    M["negbig"] = A.alloc([64])
    self.v("dve", "tensor_scalar", M["negbig"], C["tvalid"], 1.0e4, -1.0e4, op0=ALU.mult, op1=ALU.add,
           r=["tvalid"], w=["negbig"])
    M["one1"] = A.alloc([1])
    M["zero1"] = A.alloc([1])
    self.v("pool", "memset", M["one1"], 1.0, w=["one1"])
    self.v("pool", "memset", M["zero1"], 0.0, w=["zero1"])
    M["neg1"] = A.alloc([1])
    self.v("pool", "memset", M["neg1"], -1.0, w=["neg1"])
    M["CT"] = A.alloc([4, 257], F32)
    M["m_row"] = A.alloc([1], F32)
    M["m_bc"] = A.alloc([4], F32)
    M["convbuf"] = A.alloc([8, 131], F32)


def _mlstm_local_setup(self):
    A, P, C, M = self.A, self.P, self.C, self.M
    wv_in = self.w_in.rearrange("(c p) n -> p c n", p=128)
    M["wzm"] = A.alloc([8, 1024], BF16)
    M["wom"] = A.alloc([8, 1024], BF16)
    for c in range(8):
        self.dma("pool", M["wzm"][:, c, :], wv_in[:, c, OFF_ZM:OFF_ZM + 1024], "ldw", w=["wzm"])
        self.dma("pool", M["wom"][:, c, :], wv_in[:, c, OFF_OM:OFF_OM + 1024], "ldw", w=["wom"])
    for nm, shp, dt in (("cacc", [8, 128], F32), ("c_act", [8, 128], BF16),
                        ("vaug", [4, 257], BF16), ("kmw", [4, 128], BF16), ("qmT", [4, 128], BF16), ("kmT", [4, 128], BF16),
                        ("gt", [8], F32), ("lf", [4], F32), ("ie", [4], F32), ("b_tok", [4], F32), ("a_tok", [4], F32),
                        ("a_row", [128], F32), ("cm_row", [128], F32), ("M_row", [128], F32), ("negM_row", [128], F32),
                        ("AT", [1], F32), ("Mend_row", [1], F32), ("dg", [4], F32), ("Mend_bc", [4], F32),
                        ("tmp4", [4], F32), ("wk", [4], F32), ("wC", [4], F32), ("M_tok", [4], F32), ("emt", [4], F32),
                        ("DT", [128], F32), ("Wbc", [128], F32), ("scT", [128], BF16), ("qtil", [128], BF16),
                        ("CTb", [4, 257], BF16), ("hh", [256], F32), ("hn", [256], BF16), ("hnm", [8, 128], F32),
                        ("st", [8], F32), ("so", [128], F32), ("szm", [128], F32), ("t1", [128], F32), ("t2", [128], F32),
                        ):
        M[nm] = A.alloc(shp, dt)
        if nm in ("DT", "Wbc", "scT", "qtil", "hh", "hn", "st"):
            M[nm + "_b"] = A.alloc(shp, dt)
    self.v("pool", "memset", M["vaug"][:, :, 256:257], 1.0, w=["vaug1"])
    M["so_all"] = A.alloc([8, 512], BF16)
    M["sz_all"] = A.alloc([8, 512], BF16)


def _mlstm_gates4(self, tok0, tiles, n=512):
    ps, M, hT = self.ps, self.M, self.hT
    for fb in range(8):
        for (wname, bank, func, dst, dk) in (("wom", 5, AF.Sigmoid, "so_all", "so_all"), ("wzm", 6, AF.Silu, "sz_all", "sz_all")):
            for c in range(8):
                self.mm(ps[:, bank, 0:n], M[wname][:, c, fb * 128:(fb + 1) * 128], hT[:, c, tok0:tok0 + n], start=(c == 0), stop=(c == 7),
                        r=[wname] + [(("hT", t), c) for t in tiles], w=[("P", bank)])
            self.act(M[dst][:, fb, 0:n], ps[:, bank, 0:n], func, r=[("P", bank)], w=[(dk, fb)])


def _mlstm_tile(self, hTt, hkeys, ntok, tcol, with_out, mout_dst, moutkey, gate_off=None):
    A, P, ps, C, M = self.A, self.P, self.ps, self.C, self.M
    N = ntok
    B = lambda i: ("P", i)
    tv = C["tvalid"][:, tcol:tcol + 1] if tcol is not None else M["one1"]
    nb = M["negbig"][:, tcol:tcol + 1] if tcol is not None else M["zero1"]
    tvk = ["tvalid", "negbig", "one1", "zero1"]
    cb = M["convbuf"]
    xmps = ps[:, 0:2, :].rearrange("p a (b t) -> p (a b) t", t=128)
    for fb in range(8):
        for c in range(8):
            self.mm(xmps[:, fb, 0:N], M["wxm"][:, c, fb * 128:(fb + 1) * 128], hTt[:, c, :], start=(c == 0), stop=(c == 7),
                    r=["wxm"] + [(k, c) for k in hkeys], w=[B(fb // 4)])
    for half in range(2):
        self.act(cb[:, 4 * half:4 * half + 4, 3:3 + N], xmps[:, 4 * half:4 * half + 4, 0:N], AF.Copy,
                 r=[B(half)] + tvk, w=[("cb", half)], scale=tv)
    for half in range(2):
        for c in range(8):
            self.mm(ps[0:N, 2 + half, :], hTt[:, c, :], M["wxm"][:, c, half * 512:(half + 1) * 512], start=(c == 0), stop=(c == 7),
                    r=["wxm"] + [(k, c) for k in hkeys], w=[B(2 + half)])
        self.act(M["vaug"][0:N, 2 * half:2 * half + 2, 0:256], ps[0:N, 2 + half, :].rearrange("p (h v) -> p h v", h=2), AF.Copy,
                 r=[B(2 + half), "vaug1"], w=[("vaug", half)])
    gps = ps[0:N, 7, 0:8]
    for c in range(8):
        self.mm(gps, hTt[:, c, :], M["wg"][:, c, :], start=(c == 0), stop=(c == 7),
                r=["wg"] + [(k, c) for k in hkeys], w=[B(7)])
    gt = M["gt"]
    self.v("dve", "tensor_tensor", gt[0:N], gps, C["b_if_bc"][0:N], ALU.add, r=[B(7), "b_if_bc"], w=["gt"])
    lf = M["lf"]
    self.act(lf[0:N], gt[0:N, 4:8], AF.Exp, r=["gt"], w=["lf"], scale=-1.0)
    self.act(lf[0:N], lf[0:N], AF.Ln, r=["lf"], w=["lf"], bias=M["one1"][0:N])
    self.v("dve", "tensor_scalar", lf[0:N], lf[0:N], tv[0:N], M["neg1"][0:N], op0=ALU.mult, op1=ALU.mult, r=["lf", "neg1"] + tvk, w=["lf"])
    ie = M["ie"]
    self.v("dve", "tensor_scalar", ie[0:N], gt[0:N, 0:4], tv[0:N], nb[0:N], op0=ALU.mult, op1=ALU.add, r=["gt"] + tvk, w=["ie"])
    tri, ident = C["tri"], C["ident"]
    self.mm(ps[0:N, 7, 8:12], tri[0:N, 0:N], lf[0:N], r=["lf", "tri"], w=[B(7)])
    self.mm(ps[:, 7, 12:16], M["ones"][0:N, :], lf[0:N], r=["lf", "ones"], w=[B(7)])
    self.mm(ps[0:4, 7, 144:144 + N], lf[0:N], tri[0:N, 0:N], r=["lf", "tri"], w=[B(7)])
    b_tok, a_tok = M["b_tok"], M["a_tok"]
    self.v("dve", "tensor_copy", b_tok[0:N], ps[0:N, 7, 8:12], r=[B(7)], w=["b_tok"])
    self.v("dve", "tensor_tensor", a_tok[0:N], ie[0:N], b_tok[0:N], ALU.subtract, r=["ie", "b_tok"], w=["a_tok"])
    self.mm(ps[0:4, 7, 16:16 + N], a_tok[0:N], ident[0:N, 0:N], r=["a_tok", "ident"], w=[B(7)])
    a_row = M["a_row"]
    self.v("dve", "tensor_copy", a_row[0:4, 0:N], ps[0:4, 7, 16:16 + N], r=[B(7)], w=["a_row"])
    cw, cbias = C["conv_wT"], C["conv_bT"]
    for fb in range(8):
        self.act(M["cacc"][:, fb, 0:N], cb[:, fb, 3:3 + N], AF.Identity, r=[("cb", fb // 4), "conv_wT", "conv_bT"], w=[("cacc", fb)],
                 scale=cw[:, fb, 3:4], bias=cbias[:, fb:fb + 1])
    for j in range(3):
        for fb in range(8):
            ca = M["cacc"][:, fb, 0:N]
            self.v("dve", "scalar_tensor_tensor", ca, cb[:, fb, j:j + N], cw[:, fb, j:j + 1], ca, op0=ALU.mult, op1=ALU.add,
                   r=[("cb", fb // 4), ("cacc", fb)], w=[("cacc", fb)])
    for fb in range(8):
        self.act(M["c_act"][:, fb, 0:N], M["cacc"][:, fb, 0:N], AF.Silu, r=[("cacc", fb)], w=[("c_act", fb)])
    for half in range(2):
        self.v("pool", "tensor_copy", cb[:, 4 * half:4 * half + 4, 0:3], cb[:, 4 * half:4 * half + 4, N:N + 3],
               r=[("cb", half)], w=[("cb", half)])
    kmps = ps[0:N, 4, :].rearrange("p (h k) -> p h k", h=4)
    for h in range(4):
        for ec in range(2):
            self.mm(kmps[:, h, :], M["c_act"][:, 2 * h + ec, 0:N], M["wmk"][:, ec, h, :], start=(ec == 0), stop=(ec == 1),
                    r=[("c_act", 2 * h + ec), "wmk"], w=[B(4)])
    if with_out:
        qps = ps[:, 5, :].rearrange("p (h t) -> p h t", h=4)
        kps = ps[:, 6, :].rearrange("p (h t) -> p h t", h=4)
        for h in range(4):
            for ec in range(2):
                self.mm(qps[:, h, 0:N], M["wmq"][:, ec, h, :], M["c_act"][:, 2 * h + ec, 0:N], start=(ec == 0), stop=(ec == 1),
                        r=[("c_act", 2 * h + ec), "wmq"], w=[B(5)])
            for ec in range(2):
                self.mm(kps[:, h, 0:N], M["wmk"][:, ec, h, :], M["c_act"][:, 2 * h + ec, 0:N], start=(ec == 0), stop=(ec == 1),
                        r=[("c_act", 2 * h + ec), "wmk"], w=[B(6)])
        self.act(M["qmT"][:, :, 0:N], qps[:, :, 0:N], AF.Copy, r=[B(5)], w=["qmT"], scale=float(128 ** -0.5))
        self.act(M["kmT"][:, :, 0:N], kps[:, :, 0:N], AF.Copy, r=[B(6)], w=["kmT"])
        self.v("pool", "tensor_copy", M["CTb"], M["CT"], r=["CT"], w=["CTb"])
        self.v("dve", "tensor_tensor_scan", M["cm_row"][0:4, 0:N], M["ones"][0:4, 0:N], a_row[0:4, 0:N], -1.0e30,
               op0=ALU.mult, op1=ALU.max, r=["a_row", "ones"], w=["cm_row"])
        self.v("dve", "tensor_tensor", M["M_row"][0:4, 0:N], M["cm_row"][0:4, 0:N], M["m_row"][0:4, 0:1].to_broadcast([4, N]), ALU.max,
               r=["cm_row", "m_row"], w=["M_row"])
        self.v("dve", "tensor_scalar", M["negM_row"][0:4, 0:N], M["M_row"][0:4, 0:N], -1.0, None, op0=ALU.mult,
               r=["M_row"], w=["negM_row"])
        self.mm(ps[0:N, 7, 276:280], M["M_row"][0:4, 0:N], ident[0:4, 0:4], r=["M_row", "ident"], w=[B(7)])
        self.v("dve", "tensor_tensor", M["emt"][0:N], b_tok[0:N], ps[0:N, 7, 276:280], ALU.add, r=["b_tok", B(7)], w=["emt"])
        self.act(M["emt"][0:N], M["emt"][0:N], AF.Exp, r=["emt"], w=["emt"], scale=-1.0)
    self.v("dve", "tensor_reduce", M["AT"][0:4], a_row[0:4, 0:N], AX.X, ALU.max, r=["a_row"], w=["AT"])
    self.v("dve", "tensor_tensor", M["Mend_row"][0:4], M["AT"][0:4], M["m_row"][0:4], ALU.max, r=["AT", "m_row"], w=["Mend_row"])
    self.v("dve", "tensor_tensor", M["dg"][0:4], ident[0:4, 0:4], M["Mend_row"][0:4, 0:1].to_broadcast([4, 4]), ALU.mult,
           r=["Mend_row", "ident"], w=["dg"])
    self.mm(ps[:, 7, 272:276], M["ones"][0:4, :], M["dg"][0:4], r=["dg", "ones"], w=[B(7)])
    self.v("dve", "tensor_copy", M["Mend_bc"], ps[:, 7, 272:276], r=[B(7)], w=["Mend_bc"])
    self.v("dve", "tensor_tensor", M["wk"][0:N], a_tok[0:N], M["Mend_bc"][0:N], ALU.subtract, r=["a_tok", "Mend_bc"], w=["wk"])
    self.act(M["wk"][0:N], M["wk"][0:N], AF.Exp, r=["wk"], w=["wk"])
    self.v("dve", "tensor_tensor", M["wC"], M["m_bc"], M["Mend_bc"], ALU.subtract, r=["m_bc", "Mend_bc"], w=["wC"])
    self.act(M["wC"], M["wC"], AF.Exp, r=["wC"], w=["wC"])
    self.v("dve", "tensor_tensor", M["kmw"][0:N], kmps, M["wk"][0:N].unsqueeze(2).to_broadcast([N, 4, 128]), ALU.mult,
           r=[B(4), "wk"], w=["kmw"])
    if with_out:
        def head_body(h):
            hsl = slice(h, h + 1)
            par = h % 2
            sfx = "" if par == 0 else "_b"
            hb0, hb1 = (0, 1) if par == 0 else (5, 6)
            pl, pm, pst = ps[:, hb0, 0:N], ps[:, hb0, 128:128 + N], ps[:, hb0, 256:256 + N]
            self.mm(pl, M["ehsel"][0:4, h, :], M["negM_row"][0:4, 0:N], r=["ehsel", "negM_row"], w=[B(hb0)])
            yield
            self.mm(pm, M["ehsel"][0:4, h, :], M["negM_row"][0:4, 0:N], start=True, stop=False, r=["ehsel", "negM_row"], w=[B(hb0)])
            yield
            self.mm(pm[0:N], C["ident_bf"][0:N, 0:N], C["cmaskT_bf"][0:N, 0:N], start=False, stop=True,
                    r=["ident_bf", "cmaskT_bf"], w=[B(hb0)])
            yield
            self.mm(pst[0:N], M["kmT"][:, h, 0:N], M["qmT"][:, h, 0:N], r=["kmT", "qmT"], w=[B(hb0)])
            yield
            self.act(M["DT" + sfx][0:N, 0:N], pm[0:N], AF.Exp, r=[B(hb0), "a_tok"], w=["DT" + sfx], bias=a_tok[0:N, hsl])
            yield
            self.act(M["Wbc" + sfx][:, 0:N], pl, AF.Exp, r=[B(hb0), "m_bc"], w=["Wbc" + sfx], bias=M["m_bc"][:, hsl])
            yield
            self.v("dve", "tensor_tensor", M["scT" + sfx][0:N, 0:N], pst[0:N], M["DT" + sfx][0:N, 0:N], ALU.mult, r=[B(hb0), "DT" + sfx], w=["scT" + sfx])
            yield
            self.v("pool", "tensor_tensor", M["qtil" + sfx][:, 0:N], M["qmT"][:, h, 0:N], M["Wbc" + sfx][:, 0:N], ALU.mult,
                   r=["qmT", "Wbc" + sfx], w=["qtil" + sfx])
            yield
            nd = ps[0:N, hb1, 0:257]
            self.mm(nd, M["scT" + sfx][0:N, 0:N], M["vaug"][0:N, h, :], start=True, stop=False, r=["scT" + sfx, ("vaug", h // 2), "vaug1"], w=[B(hb1)])
            yield
            self.mm(nd, M["qtil" + sfx][:, 0:N], M["CTb"][:, h, :], start=False, stop=True, r=["qtil" + sfx, "CTb"], w=[B(hb1)])
            yield
            st = M["st" + sfx]
            self.act(st[0:N, 0:1], nd[:, 256:257], AF.Abs, r=[B(hb1)], w=["st" + sfx])
            yield
            self.v("dve", "tensor_tensor", st[0:N, 0:1], st[0:N, 0:1], M["emt"][0:N, hsl], ALU.max, r=["st" + sfx, "emt"], w=["st" + sfx])
            yield
            self.v("dve", "reciprocal", st[0:N, 0:1], st[0:N, 0:1], r=["st" + sfx], w=["st" + sfx])
            yield
            self.act(M["hh" + sfx][0:N], nd[:, 0:256], AF.Copy, r=[B(hb1), "st" + sfx], w=["hh" + sfx, "st1" + sfx], scale=st[0:N, 0:1], accum_out=st[0:N, 1:2])
            yield
            self.act(M["hn" + sfx][0:N], M["hh" + sfx][0:N], AF.Square, r=["hh" + sfx], w=["hn" + sfx, "st2" + sfx], accum_out=st[0:N, 2:3])
            yield
            self.v("dve", "tensor_scalar", st[0:N, 3:4], st[0:N, 1:2], 1.0 / 256, None, op0=ALU.mult, r=["st1" + sfx], w=["st3" + sfx])
            yield
            self.v("dve", "tensor_tensor", st[0:N, 4:5], st[0:N, 3:4], st[0:N, 3:4], ALU.mult, r=["st3" + sfx], w=["st4" + sfx])
            yield
            self.v("dve", "scalar_tensor_tensor", st[0:N, 5:6], st[0:N, 2:3], 1.0 / 256, st[0:N, 4:5], op0=ALU.mult, op1=ALU.subtract,
                   r=["st2" + sfx, "st4" + sfx], w=["st5" + sfx])
            yield
            self.v("dve", "tensor_scalar", st[0:N, 5:6], st[0:N, 5:6], 0.0, EPS, op0=ALU.max, op1=ALU.add, r=["st5" + sfx], w=["st5" + sfx])
            yield
            self.act(st[0:N, 5:6], st[0:N, 5:6], AF.Ln, r=["st5" + sfx], w=["st5" + sfx])
            yield
            self.act(st[0:N, 5:6], st[0:N, 5:6], AF.Exp, r=["st5" + sfx], w=["st5" + sfx], scale=-0.5)
            yield
            self.v("dve", "scalar_tensor_tensor", st[0:N, 6:7], st[0:N, 3:4], -1.0, st[0:N, 5:6], op0=ALU.mult, op1=ALU.mult,
                   r=["st3" + sfx, "st5" + sfx], w=["st6" + sfx])
            yield
            self.act(M["hn" + sfx][0:N], M["hh" + sfx][0:N], AF.Identity, r=["hh" + sfx, "st5" + sfx, "st6" + sfx], w=["hn" + sfx], scale=st[0:N, 5:6], bias=st[0:N, 6:7])
            yield
            pt = ps[:, 4, 128 * par:128 * par + 128].bitcast(BF16).rearrange("p (b t) -> p b t", b=2)
            for vb in range(2):
                self.tr(pt[:, vb, 0:N], M["hn" + sfx][0:N, vb * 128:(vb + 1) * 128], C["ident_bf"][0:N, 0:N], r=["hn" + sfx, "ident_bf"], w=[B(4)])
            yield
            for vb in range(2):
                fb = 2 * h + vb
                self.act(M["hnm"][:, fb, 0:N], pt[:, vb, 0:N], AF.Copy, r=[B(4), "m_normT"], w=[("hnm", fb)], scale=C["m_normT"][:, fb:fb + 1])
            yield
        for pair in ((0, 1), (2, 3)):
            gens = [head_body(h) for h in pair]
            live = list(gens)
            while live:
                for gen in list(live):
                    try:
                        next(gen)
                    except StopIteration:
                        live.remove(gen)
    for h in range(4):
        bk = 2 + (h % 2)
        dps = ps[:, bk, 0:257]
        self.mm(dps, M["kmw"][0:N, h, :], M["vaug"][0:N, h, :], r=["kmw", ("vaug", h // 2), "vaug1"], w=[B(bk)])
        self.v("dve", "scalar_tensor_tensor", M["CT"][:, h, :], M["CT"][:, h, :], M["wC"][:, h:h + 1], dps, op0=ALU.mult, op1=ALU.add,
               r=["wC", B(bk), "CT", "CTb"], w=["CT"])
    self.v("dve", "tensor_tensor", M["m_row"][0:4], M["Mend_row"][0:4], ps[0:4, 7, 144 + N - 1:144 + N], ALU.add,
           r=["Mend_row", B(7)], w=["m_row"])
    self.v("dve", "tensor_tensor", M["m_bc"], M["Mend_bc"], ps[:, 7, 12:16], ALU.add, r=["Mend_bc", B(7)], w=["m_bc"])
    if with_out:
        for fb in range(8):
            if gate_off is None:
                for (wname, bank) in (("wom", 5), ("wzm", 6)):
                    for c in range(8):
                        self.mm(ps[:, bank, 0:N], M[wname][:, c, fb * 128:(fb + 1) * 128], hTt[:, c, :], start=(c == 0), stop=(c == 7),
                                r=[wname] + [(k, c) for k in hkeys], w=[B(bank)])
                self.act(M["so"][:, 0:N], ps[:, 5, 0:N], AF.Sigmoid, r=[B(5)], w=["so"])
                self.act(M["szm"][:, 0:N], ps[:, 6, 0:N], AF.Silu, r=[B(6)], w=["szm"])
                so_ap, sz_ap, sok, szk = M["so"][:, 0:N], M["szm"][:, 0:N], "so", "szm"
            else:
                so_ap, sz_ap = M["so_all"][:, fb, gate_off:gate_off + N], M["sz_all"][:, fb, gate_off:gate_off + N]
                sok, szk = ("so_all", fb), ("sz_all", fb)
            self.v("dve", "tensor_tensor", M["t1"][:, 0:N], so_ap, M["hnm"][:, fb, 0:N], ALU.mult, r=[sok, ("hnm", fb)], w=["t1"])
            self.v("dve", "scalar_tensor_tensor", M["t2"][:, 0:N], M["c_act"][:, fb, 0:N], C["m_skipT"][:, fb:fb + 1], M["t1"][:, 0:N],
                   op0=ALU.mult, op1=ALU.add, r=[("c_act", fb), "t1", "m_skipT"], w=["t2"])
            self.v("dve", "tensor_tensor", mout_dst[:, fb, :], M["t2"][:, 0:N], sz_ap, ALU.mult, r=["t2", szk], w=[(moutkey, fb)])


Builder.mlstm_setup = _mlstm_setup
Builder.mlstm_local_setup = _mlstm_local_setup
Builder.mlstm_tile = _mlstm_tile
Builder.mlstm_gates4 = _mlstm_gates4


def _phase_mlstm(self):
    A, P, ps, C, hT = self.A, self.P, self.ps, self.C, self.hT
    self.moutS = A.alloc([8, TS], BF16)
    A.push()
    self.mlstm_setup()
    M = self.M
    self.v("pool", "memset", M["CT"], 0.0, w=["CT"])
    A.push()
    self.W1 = {
        "xt": [A.alloc([1024]) for _ in range(2)],
        "ss": A.alloc([4]),
        "junk": A.alloc([1024], BF16),
        "xn": [A.alloc([1024], BF16) for _ in range(2)],
    }
    self.mlstm_prefix()
    if DEBUG.get("sbuf"):
        print("mlstm prefix sbuf top", A.top)
    A.pop()
    P.barrier()
    self.mlstm_local_setup()
    if DEBUG.get("sbuf"):
        print("mlstm local sbuf top", A.top)
    for j in range(DEBUG.get("nloc", 16)):
        if j % 4 == 0:
            self.mlstm_gates4(HALO + j * 128, [16 + j + q for q in range(4)])
        self.mlstm_tile(hT[:, :, HALO + j * 128:HALO + (j + 1) * 128], [("hT", 16 + j)], 128, 48 + j, True,
                        hT[:, :, j * 128:(j + 1) * 128], ("hT", j), gate_off=(j % 4) * 128)
    o_conv = self.outp("convp", [128, 8, 3])
    o_C = self.outp("Cp", [128, 4, 257])
    o_m = self.outp("mp", [4, 1])
    self.dma("sp", o_conv, M["convbuf"][:, :, 0:3], "out", r=[("cb", 0), ("cb", 1)])
    self.dma("sp", o_C, M["CT"], "out", r=["CT"])
    self.dma("sp", o_m, M["m_row"][0:4], "out", r=["m_row"])
    C0 = self.inp("C0T", [4, 128, 4, 257])
    m0r = self.inp("m0row", [4, 4, 1])
    m0b = self.inp("m0bc", [4, 128, 4])
    cv0 = self.inp("conv0", [4, 128, 8, 3])
    o_convs = self.outp("convs", [4, 128, 8, 3])
    o_Cs = self.outp("Cs", [4, 128, 4, 257])
    o_ms = self.outp("ms", [4, 4, 1])
    if not DEBUG.get("no_smp"):
        self.mlstm_gates4(HALO + SEG, [32], n=TS)
    for j in range(4 if not DEBUG.get("no_smp") else 0):
        self.dma("sp", M["CT"], C0[j], "ldx", w=["CT"])
        self.dma("sp", M["m_row"][0:4], m0r[j], "ldx", w=["m_row"])
        self.dma("sp", M["m_bc"], m0b[j], "ldx", w=["m_bc"])
        self.dma("sp", M["convbuf"][:, :, 0:3], cv0[j], "ldx", w=[("cb", 0), ("cb", 1)])
        self.mlstm_tile(hT[:, :, HALO + SEG + 8 * j:HALO + SEG + 8 * j + 8], [("hT", 32)], 8, None, True,
                        self.moutS[:, :, 8 * j:8 * j + 8], ("moutS", j), gate_off=8 * j)
        self.dma("sp", o_convs[j], M["convbuf"][:, :, 0:3], "out", r=[("cb", 0), ("cb", 1)])
        self.dma("sp", o_Cs[j], M["CT"], "out", r=["CT"])
        self.dma("sp", o_ms[j], M["m_row"][0:4], "out", r=["m_row"])
    if DEBUG.get("mdump"):
        for nm, shp, dt in (("convbuf", [128, 8, 131], F32), ("c_act", [128, 8, 128], BF16), ("vaug", [128, 4, 257], BF16),
                            ("kmw", [128, 4, 128], BF16), ("gt", [128, 8], F32), ("lf", [128, 4], F32), ("ie", [128, 4], F32),
                            ("b_tok", [128, 4], F32), ("a_tok", [128, 4], F32), ("a_row", [128, 128], F32),
                            ("Mend_bc", [128, 4], F32), ("wk", [128, 4], F32), ("wC", [128, 4], F32), ("CT", [128, 4, 257], F32),
                            ("m_bc", [128, 4], F32), ("hh", [128, 256], F32), ("hnm", [128, 8, 128], F32), ("st", [128, 8], F32),
                            ("emt", [128, 4], F32), ("qmT", [128, 4, 128], BF16), ("kmT", [128, 4, 128], BF16), ("DT", [128, 128], F32),
                            ("Wbc", [128, 128], F32), ("M_row", [128, 128], F32)):
            DEBUG["md_" + nm] = True
            self.dbg("md_" + nm, M[nm], shp, dt)
        for nm, ap, shp, dt in (("hTp1", M["hTp"][1], [128, 8, 128], BF16), ("xn1", self.W1["xn"][1], [128, 1024], BF16),
                                ("xt1", self.W1["xt"][1], [128, 1024], F32), ("ss", self.W1["ss"], [128, 4], F32),
                                ("wg", M["wg"], [128, 8, 8], BF16), ("mult", C["mult"], [128, 8, 5], F32)):
            DEBUG["md_" + nm] = True
            self.dbg("md_" + nm, ap, shp, dt)
    A.pop()
    self.P.barrier()
    self.dbg("mout", self.hT[:, :, 0:SEG], [128, 8, SEG], BF16)
    self.dbg("moutS", self.moutS, [128, 8, TS], BF16)


Builder.phase_mlstm = _phase_mlstm


def _phase_out(self):
    A, P, ps, C, hT = self.A, self.P, self.ps, self.C, self.hT
    wv_in = self.w_in.rearrange("(c p) n -> p c n", p=128)
    A.push()
    wpa = A.alloc([4, 1024], BF16)
    wpm = A.alloc([8, 1024], BF16)
    wout = A.alloc([8, 1024], BF16)
    wga = A.alloc([8, 128], BF16)
    wgm = A.alloc([8, 128], BF16)
    merged = A.alloc([8, SEG + TS], BF16)
    fgain = A.alloc([1024])
    sga, sgm, t1, t2 = (A.alloc([512]) for _ in range(4))
    xt = [A.alloc([1024])] * 2
    yt = [A.alloc([1024]) for _ in range(2)]
    ss = A.alloc([4])
    self.dma("sp", fgain, self.inp("fgain_bc", [128, 1024]), "ld0", w=["fgain"])
    wpa_d = self.inp("w_pa", [512, 1024]).rearrange("(c p) n -> p c n", p=128)
    wpm_d = self.inp("w_pm", [1024, 1024]).rearrange("(c p) n -> p c n", p=128)
    wout_d = self.inp("w_out", [1024, 1024]).rearrange("(c p) n -> p c n", p=128)
    for c in range(4):
        self.dma("pool", wpa[:, c, :], wpa_d[:, c, :], "ldw", w=["wpa"])
    for c in range(8):
        self.dma("pool", wpm[:, c, :], wpm_d[:, c, :], "ldw", w=["wpm"])
        self.dma("pool", wout[:, c, :], wout_d[:, c, :], "ldw", w=["wout"])
    chunks = []
    for k in range(4):
        chunks.append(dict(n=512, m0=512 * k,
                           att=lambda hp, k=k: self.attT[:, hp, 512 * k:512 * k + 512],
                           mout=lambda fb, k=k: hT[:, fb, 512 * k:512 * k + 512],
                           hs=lambda c, k=k: hT[:, c, HALO + 512 * k:HALO + 512 * k + 512]))
    chunks.append(dict(n=TS, m0=SEG,
                       att=lambda hp: self.attT[:, hp, SEG:SEG + TS],
                       mout=lambda fb: self.moutS[:, fb, :],
                       hs=lambda c: hT[:, c, HALO + SEG:HALO + SEG + TS]))
    for cb in range(8):
        cs = slice(cb * 128, (cb + 1) * 128)
        self.dma("pool", wga, wv_in[:, :, OFF_GA + cb * 128:OFF_GA + (cb + 1) * 128], "ldw", w=["wga"])
        self.dma("pool", wgm, wv_in[:, :, OFF_GM + cb * 128:OFF_GM + (cb + 1) * 128], "ldw", w=["wgm"])
        for ch in chunks:
            n = ch["n"]
            for hp in range(4):
                self.mm(ps[:, 0, 0:n], wpa[:, hp, cs], ch["att"](hp), start=(hp == 0), stop=(hp == 3), r=["wpa"], w=[("P", 0)])
            for fb in range(8):
                self.mm(ps[:, 1, 0:n], wpm[:, fb, cs], ch["mout"](fb), start=(fb == 0), stop=(fb == 7), r=["wpm"], w=[("P", 1)])
            for c in range(8):
                self.mm(ps[:, 2, 0:n], wga[:, c, :], ch["hs"](c), start=(c == 0), stop=(c == 7), r=["wga"], w=[("P", 2)])
            for c in range(8):
                self.mm(ps[:, 3, 0:n], wgm[:, c, :], ch["hs"](c), start=(c == 0), stop=(c == 7), r=["wgm"], w=[("P", 3)])
            self.act(sga[:, 0:n], ps[:, 2, 0:n], AF.Sigmoid, r=[("P", 2)], w=["sga"])
            self.act(sgm[:, 0:n], ps[:, 3, 0:n], AF.Sigmoid, r=[("P", 3)], w=["sgm"])
            self.v("dve", "tensor_tensor", t1[:, 0:n], ps[:, 0, 0:n], sga[:, 0:n], ALU.mult, r=[("P", 0), "sga"], w=["t1"])
            self.v("dve", "tensor_tensor", t2[:, 0:n], ps[:, 1, 0:n], sgm[:, 0:n], ALU.mult, r=[("P", 1), "sgm"], w=["t2"])
            self.v("pool", "tensor_tensor", merged[:, cb, ch["m0"]:ch["m0"] + n], t1[:, 0:n], t2[:, 0:n], ALU.add,
                   r=["t1", "t2"], w=[("merged", cb)])
    self.dbg("merged", merged, [128, 8, SEG + TS], BF16)
    xh = self.din["xh"]
    xs = self.din["xs"]
    y_p = self.outp("y_p", [SEG, 1024])
    y_s = self.outp("y_s", [TS, 1024])
    tiles = [(xh[HALO + j * 128:HALO + (j + 1) * 128, :], y_p[j * 128:(j + 1) * 128, :], 128, j * 128, C["gate_p"]) for j in range(NLT)]
    tiles.append((xs, y_s, TS, SEG, C["gate_s"]))
    for i, (xsrc, ydst, n, m0, gate) in enumerate(tiles):
        i2 = i % 2
        self.dma("sp", xt[i2][0:n], xsrc, "ldx", w=[("xt", 0)])
        for half in range(2):
            hs_ = slice(half * 512, (half + 1) * 512)
            for cb in range(8):
                self.mm(ps[0:n, 4 + half, :], merged[:, cb, m0:m0 + n], wout[:, cb, hs_], start=(cb == 0), stop=(cb == 7),
                        r=["wout", ("merged", cb)], w=[("P", 4 + half)])
            self.v("dve", "tensor_tensor", yt[i2][0:n, hs_], ps[0:n, 4 + half, :], gate[0:n, hs_], ALU.mult,
                   r=[("P", 4 + half), ("gate_p", half), ("gate_s", half)], w=[("yt", i2, half)])
            self.v("pool", "tensor_tensor", yt[i2][0:n, hs_], yt[i2][0:n, hs_], xt[i2][0:n, hs_], ALU.add,
                   r=[("yt", i2, half), ("xt", 0)], w=[("yt", i2, half)])
        sl = ss[:, i % 4:i % 4 + 1]
        sk = ("ss", i % 4)
        self.act(xt[0][0:n], yt[i2][0:n], AF.Square, r=[("yt", i2, 0), ("yt", i2, 1)], w=[("xt", 0), sk], accum_out=sl[0:n])
        self.v("dve", "tensor_scalar", sl[0:n], sl[0:n], 1.0 / D, EPS, op0=ALU.mult, op1=ALU.add, r=[sk], w=[sk])
        self.act(sl[0:n], sl[0:n], AF.Ln, r=[sk], w=[sk])
        self.act(sl[0:n], sl[0:n], AF.Exp, r=[sk], w=[sk], scale=-0.5)
        self.act(yt[i2][0:n], yt[i2][0:n], AF.Copy, r=[("yt", i2, 0), ("yt", i2, 1), sk], w=[("yt", i2, 0), ("yt", i2, 1)], scale=sl[0:n])
        self.v("pool", "tensor_tensor", yt[i2][0:n], yt[i2][0:n], fgain[0:n], ALU.mult,
               r=[("yt", i2, 0), ("yt", i2, 1), "fgain"], w=[("yt", i2, 0), ("yt", i2, 1)])
        self.dma("sp", ydst, yt[i2][0:n], "out", r=[("yt", i2, 0), ("yt", i2, 1)])
    A.pop()


Builder.phase_out = _phase_out


def _sample_copies(self):
    LB = [w for (w, d) in GROUPS]
    self.s_cache = cache = [self.inp("cache%d" % g, [4, LB[g], 2, 512]) for g in range(3)]
    self.s_kvs = kvs = [self.outp("kvs%d" % g, [4, LB[g], 2, 512]) for g in range(3)]
    self.s_cat = cat = [self.scratch("cat%d" % g, [4, LB[g] + 8, 2, 512]) for g in range(2)]
    for g in range(3):
        for j in range(4):
            nsp = 4 if g == 2 else 1
            rows = LB[g] - 8
            step = (rows + nsp - 1) // nsp
            for a in range(0, rows, step):
                b_ = min(rows, a + step)
                self.dma("sp", kvs[g][j, a:b_], cache[g][j, 8 + a:8 + b_], "cpy", w=[("kvs", g, j, a)])
            if g < 2:
                self.dma("sp", cat[g][j, 0:LB[g]], cache[g][j], "cpy", w=[("catb", g, j)])


def _phase_sample_attn(self):
    A, P, ps, C, hT = self.A, self.P, self.ps, self.C, self.hT
    wv_in = self.w_in.rearrange("(c p) n -> p c n", p=128)
    LB = [w for (w, d) in GROUPS]
    cache, kvs, cat = self.s_cache, self.s_kvs, self.s_cat
    A.push()
    ident, F_sb = C["ident"], C["F"]
    ones = A.alloc([128])
    self.v("pool", "memset", ones, 1.0, w=["s_ones"])
    z8 = A.alloc([8])
    self.v("pool", "memset", z8, 0.0, w=["z8"])
    onesel = A.alloc([8, 8])
    self.dma("sp", onesel, self.inp("onesel", [128, 8, 8]), "ld0", w=["onesel"])
    selt = A.alloc([8, 128])
    self.dma("sp", selt[0:8], self.inp("selt", [8, 8, 128]), "ld0", w=["selt"])
    biasS = A.alloc([3, 8])
    bold = A.alloc([3, 8])
    tmpF = A.alloc([8])
    dgF = A.alloc([8])
    for g in range(3):
        self.mm(ps[:, 0, 0:8], F_sb[0:8, g, 0:128], ident[0:8, 0:8], r=[("F", g), "ident"], w=[("P", 0)])
        self.act(tmpF, ps[:, 0, 0:8], AF.Copy, r=[("P", 0)], w=["tmpF"])
        self.mm(ps[:, 0, 8:16], C["antiid"], tmpF, r=["tmpF", "antiid"], w=[("P", 0)])
        self.act(biasS[:, g, :], ps[:, 0, 8:16], AF.Copy, r=[("P", 0)], w=[("biasS", g)])
        self.v("dve", "tensor_tensor", dgF[0:8], ident[0:8, 0:8], F_sb[0:8, g, 128:129].to_broadcast([8, 8]), ALU.mult,
               r=[("F", g), "ident"], w=["dgF"])
        self.mm(ps[0:8, 0, 16:24], ones[0:8, 0:8], dgF[0:8], r=["dgF", "s_ones"], w=[("P", 0)])
        self.act(bold[0:8, g, :], ps[0:8, 0, 16:24], AF.Copy, r=[("P", 0)], w=[("bold", g)])
    wq = [A.alloc([8, 512], BF16) for _ in range(2)]
    qj = [A.alloc([1536]) for _ in range(4)]
    kj = A.alloc([1536])
    vj = A.alloc([1536])
    oldkv = [A.alloc([3, 2, 512]) for _ in range(1)][0]
    wcnt = 0
    for kind, off in (("q", OFF_Q), ("k", OFF_K), ("v", OFF_V)):
        for g in range(3):
            wb = wq[wcnt % 2]
            wk_ = ("swq", wcnt % 2)
            wcnt += 1
            self.dma("pool", wb, wv_in[:, :, off + g * 512:off + (g + 1) * 512], "ldw", w=[wk_])
            for j in range(4):
                bank = 1 + (j % 2)
                for c in range(8):
                    self.mm(ps[0:8, bank, :], hT[:, c, HALO + SEG + 8 * j:HALO + SEG + 8 * j + 8], wb[:, c, :],
                            start=(c == 0), stop=(c == 7), r=[wk_, (("hT", 32), c)], w=[("P", bank)])
                if kind == "q":
                    self.act(qj[j][0:8, g * 512:(g + 1) * 512], ps[0:8, bank, :], AF.Copy, r=[("P", bank)], w=[("qj", j, g)], scale=0.125)
                else:
                    st = kj if kind == "k" else vj
                    sk = ("kvst", kind, j % 3)
                    sl_ = st[0:8, (j % 3) * 512:(j % 3 + 1) * 512]
                    self.act(sl_, ps[0:8, bank, :], AF.Copy, r=[("P", bank)], w=[sk])
                    kvi = 0 if kind == "k" else 1
                    self.dma("sp", kvs[g][j, LB[g] - 8:LB[g], kvi, :], sl_, "out", r=[sk], w=[("kvsn", g, j, kvi)])
                    if g < 2:
                        self.dma("sp", cat[g][j, LB[g]:LB[g] + 8, kvi, :], sl_, "out", r=[sk], w=[("catn", g, j, kvi)])
    szs = A.alloc([4, TS])
    wz = wq[0]
    self.dma("pool", wz, wv_in[:, :, OFF_ZA:OFF_ZA + 512], "ldw", w=[("swq", 0)])
    for hp in range(4):
        for c in range(8):
            self.mm(ps[:, 3, 0:TS], wz[:, c, hp * 128:(hp + 1) * 128], hT[:, c, HALO + SEG:HALO + SEG + TS], start=(c == 0), stop=(c == 7),
                    r=[("swq", 0), (("hT", 32), c)], w=[("P", 3)])
        self.act(szs[:, hp, :], ps[:, 3, 0:TS], AF.Silu, r=[("P", 3)], w=[("szs", hp)])
    NKG = 6
    Kg = [A.alloc([2, 512]) for _ in range(NKG)]
    prod = A.alloc([512])
    lg = [A.alloc([8]) for _ in range(4)]
    Pz = [A.alloc([8, 8]) for _ in range(NKG)]
    numS = A.alloc([512])
    denS = A.alloc([8])
    pold = A.alloc([8])
    tmpo = A.alloc([512])
    oj = A.alloc([512])
    u = 0
    for j in range(4):
        for g in range(3):
            self.dma("sp", oldkv[0:8, g], cache[g][j, 0:8], "gat", w=[("old", g)])
        self.mm(ps[0:8, 5, :], z8[0:8, 0:8], qj[j][0:8, 0:512], start=True, stop=False, r=["z8", ("qj", j, 0)], w=[("P", 5)])
        self.mm(ps[0:8, 6, 0:8], z8[0:8, 0:8], qj[j][0:8, 0:8], start=True, stop=False, r=["z8", ("qj", j, 0)], w=[("P", 6)])
        first = False
        for g, (win, d) in enumerate(GROUPS):
            for t in range(8):
                kb = Kg[u % NKG]
                kk = ("Kg", u % NKG)
                pz = Pz[u % NKG]
                pk = ("Pz", u % NKG)
                lgt = lg[u % 4]
                lk = ("lg", u % 4)
                u += 1
                a0 = d + t
                if g < 2:
                    src = cat[g][j, a0:a0 + 127 * d + 1:d]
                    deps = [("catb", g, j), ("catn", g, j, 0), ("catn", g, j, 1)]
                else:
                    src = kvs[2][j, a0 - 8:a0 - 8 + 127 * d + 1:d]
                    deps = [("kvs", 2, j, a) for a in range(0, LB[2] - 8, (LB[2] - 8 + 3) // 4)] + [("kvsn", 2, j, 0), ("kvsn", 2, j, 1)]
                self.dma("sp", kb, src, "gat", r=deps, w=[kk])
                self.mm(ps[:, 4, :], selt[0:8, t, :], qj[j][0:8, g * 512:(g + 1) * 512], r=["selt", ("qj", j, g)], w=[("P", 4)])
                self.v("dve", "tensor_tensor", prod, kb[:, 0, :], ps[:, 4, :], ALU.mult, r=[kk, ("P", 4)], w=["prod"])
                self.v("dve", "tensor_reduce", lgt, prod.rearrange("p (h e) -> p h e", h=8), AX.X, ALU.add, r=["prod"], w=[lk])
                self.v("dve", "tensor_tensor", lgt, lgt, biasS[:, g, :], ALU.add, r=[lk, ("biasS", g)], w=[lk])
                pe_ = pz[:, 0, :]
                self.act(pe_, lgt, AF.Exp, r=[lk], w=[pk])
                last = (g == 2 and t == 7)
                kv3 = kb[:, 1, :].rearrange("p (h e) -> p h e", h=8)
                self.v("dve", "tensor_tensor", kv3, kv3, pe_.unsqueeze(2).to_broadcast([128, 8, 64]), ALU.mult, r=[pk, kk], w=[kk])
                self.mm(ps[0:8, 5, :], onesel[:, t, :], kb[:, 1, :], start=False, stop=last, r=["onesel", kk], w=[("P", 5)])
                self.mm(ps[0:8, 6, 0:8], onesel[:, t, :], pe_, start=False, stop=last, r=["onesel", pk], w=[("P", 6)])
        self.v("dve", "tensor_copy", numS[0:8], ps[0:8, 5, :], r=[("P", 5)], w=["numS"])
        self.v("dve", "tensor_copy", denS[0:8], ps[0:8, 6, 0:8], r=[("P", 6)], w=["denS"])
        for g in range(3):
            self.v("dve", "tensor_tensor", tmpo[0:8], qj[j][0:8, g * 512:(g + 1) * 512], oldkv[0:8, g, 0, :], ALU.mult,
                   r=[("qj", j, g), ("old", g)], w=["tmpo"])
            self.v("dve", "tensor_reduce", pold[0:8], tmpo[0:8].rearrange("p (h e) -> p h e", h=8), AX.X, ALU.add, r=["tmpo"], w=["pold"])
            self.v("dve", "tensor_tensor", pold[0:8], pold[0:8], bold[0:8, g, :], ALU.add, r=["pold", ("bold", g)], w=["pold"])
            self.act(pold[0:8], pold[0:8], AF.Exp, r=["pold"], w=["pold"])
            self.v("dve", "tensor_tensor", denS[0:8], denS[0:8], pold[0:8], ALU.add, r=["denS", "pold"], w=["denS"])
            self.v("dve", "tensor_tensor", tmpo[0:8].rearrange("p (h e) -> p h e", h=8),
                   oldkv[0:8, g, 1, :].rearrange("p (h e) -> p h e", h=8), pold[0:8].unsqueeze(2).to_broadcast([8, 8, 64]), ALU.mult,
                   r=[("old", g), "pold"], w=["tmpo"])
            self.v("dve", "tensor_tensor", numS[0:8], numS[0:8], tmpo[0:8], ALU.add, r=["numS", "tmpo"], w=["numS"])
        self.v("dve", "reciprocal", denS[0:8], denS[0:8], r=["denS"], w=["denS"])
        self.v("dve", "tensor_tensor", oj[0:8].rearrange("p (h e) -> p h e", h=8), numS[0:8].rearrange("p (h e) -> p h e", h=8),
               denS[0:8].unsqueeze(2).to_broadcast([8, 8, 64]), ALU.mult, r=["numS", "denS"], w=["oj"])
        for hp in range(4):
            self.tr(ps[:, 7, 0:8], oj[0:8, hp * 128:(hp + 1) * 128], ident[0:8, 0:8], r=["oj", "ident"], w=[("P", 7)])
            self.v("dve", "tensor_tensor", self.attT[:, hp, SEG + 8 * j:SEG + 8 * j + 8], ps[:, 7, 0:8], szs[:, hp, 8 * j:8 * j + 8], ALU.mult,
                   r=[("P", 7), ("szs", hp)], w=[("attTs", hp, j)])
    A.pop()


Builder.phase_sample_attn = _phase_sample_attn
Builder.sample_copies = _sample_copies


_PROG_CACHE = {}


def kernel(**inputs):
    inp = {k: np.asarray(v) for k, v in inputs.items()}
    if "prog" not in _PROG_CACHE:
        _PROG_CACHE["prog"] = build_program(99)
    b = _PROG_CACHE["prog"]
    cores = list(range(NCORES))
    maps = make_in_maps(inp, cores)
    maps = [{k: np.ascontiguousarray(v, dtype=np.float32) for k, v in m.items() if k in b.din} for m in maps]
    res = run_bass_kernel_spmd(b.nc, maps, core_ids=cores).results
    f32 = np.float32
    y_p = np.zeros((2, SEQ, D), f32)
    y_s = np.zeros((32, 8, D), f32)
    kvp = [np.zeros((1, 2, w, 2, 8, 64), f32) for (w, d) in GROUPS]
    kvs = [np.zeros((1, 32, w, 2, 8, 64), f32) for (w, d) in GROUPS]
    conv_p = np.zeros((1, 2, 3, D), f32)
    conv_s = np.zeros((1, 32, 3, D), f32)
    C_p = np.zeros((1, 2, 4, 256, 128), f32)
    C_s = np.zeros((1, 32, 4, 256, 128), f32)
    n_p = np.zeros((1, 2, 4, 128), f32)
    n_s = np.zeros((1, 32, 4, 128), f32)
    m_p = np.zeros((1, 2, 4), f32)
    m_s = np.zeros((1, 32, 4), f32)
    for c in cores:
        r = res[c]
        bb, p = c // 4, c % 4
        sq = slice(4 * c, 4 * c + 4)
        y_p[bb, p * SEG:(p + 1) * SEG] = r["y_p"]
        y_s[sq] = np.asarray(r["y_s"]).reshape(4, 8, D)
        for g in range(3):
            kvs[g][0, sq] = np.asarray(r["kvs%d" % g]).reshape(4, -1, 2, 8, 64)
        Cs = np.asarray(r["Cs"])
        C_s[0, sq] = Cs[..., :256].transpose(0, 2, 3, 1)
        n_s[0, sq] = Cs[..., 256].transpose(0, 2, 1)
        m_s[0, sq] = np.asarray(r["ms"])[:, :, 0]
        conv_s[0, sq] = np.asarray(r["convs"]).transpose(0, 3, 2, 1).reshape(4, 3, D)
        if p == 3:
            for g in range(3):
                kvp[g][0, bb] = np.asarray(r["kvp%d" % g]).reshape(-1, 2, 8, 64)
            Cp = np.asarray(r["Cp"])
            C_p[0, bb] = Cp[..., :256].transpose(1, 2, 0)
            n_p[0, bb] = Cp[..., 256].T
            m_p[0, bb] = np.asarray(r["mp"])[:, 0]
            conv_p[0, bb] = np.asarray(r["convp"]).transpose(2, 1, 0).reshape(3, D)
    return (y_p, y_s, kvp[0], kvs[0], kvp[1], kvs[1], kvp[2], kvs[2],
            conv_p, conv_s, C_p, C_s, n_p, n_s, m_p, m_s)


def _mlstm_prefix(self):
    A, P, ps, C, M, hT = self.A, self.P, self.ps, self.C, self.M, self.hT
    NT = 48
    xf = self.inp("xf", [PREFIX - HALO, 1024])
    ident, tri = C["ident"], C["tri"]
    hTp = [A.alloc([8, 128], BF16) for _ in range(2)]
    gt_all = A.alloc([NT, 8])
    lf = A.alloc([NT, 4])
    ie = A.alloc([NT, 4])
    tot = A.alloc([NT, 4])
    incl = A.alloc([NT, 4])
    a_all = A.alloc([NT, 4])
    wk_all = A.alloc([NT, 4])
    negtv = A.alloc([NT])
    pm = A.alloc([4])
    row1 = A.alloc([1])
    dg = A.alloc([4])
    Mg_bc = A.alloc([4])
    cb4 = A.alloc([4, 8, 131])
    hT4 = A.alloc([8, 512], BF16)
    caccs = [A.alloc([8, 128]) for _ in range(2)]
    cexp = A.alloc([8, 128])
    c_act = [A.alloc([8, 128], BF16) for _ in range(2)]
    vaug = [A.alloc([4, 257], BF16) for _ in range(2)]
    kmw = [A.alloc([4, 128], BF16) for _ in range(2)]
    for i in range(2):
        self.v("pool", "memset", vaug[i][:, :, 256:257], 1.0, w=[("pvaug1", i)])

    hTd = self.scratch("hTd", [32, 128, 8, 128], BF16)

    def tile_src(i, slot):
        if i < 32:
            hp_ = hTp[i % 2]
            self.norm_tile(xf[i * 128:(i + 1) * 128, :], 128, hp_, "p", slot, ("hTp", i % 2), bank0=5)
            self.dma("sp", hTd[i], hp_, "hts", r=[(("hTp", i % 2), c) for c in range(8)], w=[("hTd", i)])
            return hp_, [("hTp", i % 2)]
        ti = i - 32
        return hT[:, :, ti * 128:(ti + 1) * 128], [("hT", ti)]

    for i in range(NT):
        hTt, hk = tile_src(i, i)
        bank = 7 if i % 2 == 0 else 4
        gps = ps[:, bank, 0:8]
        for c in range(8):
            self.mm(gps, hTt[:, c, :], M["wg"][:, c, :], start=(c == 0), stop=(c == 7),
                    r=["wg"] + [(k, c) for k in hk], w=[("P", bank)])
        self.v("dve", "tensor_tensor", gt_all[:, i, :], gps, C["b_if_bc"], ALU.add, r=[("P", bank), "b_if_bc"], w=["gt_all"])
    tv48 = C["tvalid"][:, 0:NT]
    self.v("dve", "tensor_scalar", negtv, tv48, -1.0, None, op0=ALU.mult, r=["tvalid"], w=["negtv"])
    self.act(lf, gt_all[:, :, 4:8], AF.Exp, r=["gt_all"], w=["lf"], scale=-1.0)
    self.act(lf, lf, AF.Ln, r=["lf"], w=["lf"], bias=M["one1"])
    self.v("dve", "tensor_tensor", lf, lf, negtv.unsqueeze(2).to_broadcast([128, NT, 4]), ALU.mult, r=["lf", "negtv"], w=["lf"])
    self.v("dve", "tensor_tensor", ie, gt_all[:, :, 0:4], tv48.unsqueeze(2).to_broadcast([128, NT, 4]), ALU.mult,
           r=["gt_all", "tvalid"], w=["ie"])
    self.v("dve", "tensor_tensor", ie, ie, M["negbig"][:, 0:NT].unsqueeze(2).to_broadcast([128, NT, 4]), ALU.add,
           r=["ie", "negbig"], w=["ie"])
    lf2 = lf.rearrange("p t h -> p (t h)")
    self.mm(ps[:, 0, 0:NT * 4], tri, lf2, r=["lf", "tri"], w=[("P", 0)])
    self.mm(ps[:, 1, 0:NT * 4], M["ones"], lf2, r=["lf", "ones"], w=[("P", 1)])
    self.v("dve", "tensor_copy", tot.rearrange("p t h -> p (t h)"), ps[:, 1, 0:NT * 4], r=[("P", 1)], w=["tot"])
    for h in range(4):
        self.v("dve", "tensor_tensor_scan", incl[:, :, h], M["ones"][:, 0:NT], tot[:, :, h], 0.0, op0=ALU.mult, op1=ALU.add,
               r=["tot", "ones"], w=[("incl", h)])
    inclk = [("incl", h) for h in range(4)]
    self.v("dve", "tensor_tensor", a_all, ie, incl, ALU.subtract, r=["ie"] + inclk, w=["a_all"])
    self.v("dve", "tensor_tensor", a_all, a_all, tot, ALU.add, r=["a_all", "tot"], w=["a_all"])
    self.v("dve", "tensor_tensor", a_all.rearrange("p t h -> p (t h)"), a_all.rearrange("p t h -> p (t h)"), ps[:, 0, 0:NT * 4],
           ALU.subtract, r=["a_all", ("P", 0)], w=["a_all"])
    self.v("dve", "tensor_reduce", pm, a_all.rearrange("p t h -> p h t"), AX.X, ALU.max, r=["a_all"], w=["pm"])
    self.mm(ps[0:4, 2, 0:128], pm, ident, r=["pm", "ident"], w=[("P", 2)])
    self.v("dve", "tensor_reduce", row1[0:4], ps[0:4, 2, 0:128], AX.X, ALU.max, r=[("P", 2)], w=["row1"])
    self.v("dve", "tensor_scalar", row1[0:4], row1[0:4], 0.0, None, op0=ALU.max, r=["row1"], w=["row1"])
    self.v("dve", "tensor_tensor", dg[0:4], ident[0:4, 0:4], row1[0:4, 0:1].to_broadcast([4, 4]), ALU.mult, r=["row1", "ident"], w=["pdg"])
    self.mm(ps[:, 2, 128:132], M["ones"][0:4, :], dg[0:4], r=["pdg", "ones"], w=[("P", 2)])
    self.v("dve", "tensor_copy", Mg_bc, ps[:, 2, 128:132], r=[("P", 2)], w=["Mg_bc"])
    self.v("dve", "tensor_tensor", wk_all, a_all, Mg_bc.unsqueeze(1).to_broadcast([128, NT, 4]), ALU.subtract,
           r=["a_all", "Mg_bc"], w=["wk_all"])
    self.act(wk_all, wk_all, AF.Exp, r=["wk_all"], w=["wk_all"])
    self.v("dve", "tensor_tensor", M["m_bc"], incl[:, NT - 1, :], Mg_bc, ALU.add, r=inclk + ["Mg_bc"], w=["m_bc"])
    self.mm(ps[0:4, 2, 136:137], M["m_bc"][0:1, 0:4], M["ones"][0:1, 0:1], r=["m_bc", "ones"], w=[("P", 2)])
    self.v("dve", "tensor_copy", M["m_row"][0:4], ps[0:4, 2, 136:137], r=[("P", 2)], w=["m_row"])
    cw, cbias = C["conv_wT"], C["conv_bT"]
    hist = A.alloc([8, 3])
    self.v("pool", "memset", hist, 0.0, w=["hist"])
    for g0 in range(0, NT, 4):
        if g0 < 32:
            for q in range(4):
                i = g0 + q
                self.dma("sp", hT4[:, :, q * 128:(q + 1) * 128], hTd[i], "ldx", r=[("hTd", i)], w=[(("hT4", q), c) for c in range(8)])
            src4 = hT4
            hk4 = [("hT4", q) for q in range(4)]
            tsl = lambda q: hT4[:, :, q * 128:(q + 1) * 128]
        else:
            t0 = g0 - 32
            src4 = hT[:, :, t0 * 128:(t0 + 4) * 128]
            hk4 = [("hT", t0 + q) for q in range(4)]
            tsl = lambda q, t0=t0: hT[:, :, (t0 + q) * 128:(t0 + q + 1) * 128]
        for fb in range(8):
            bank = fb % 2
            for c in range(8):
                self.mm(ps[:, bank, :], M["wxm"][:, c, fb * 128:(fb + 1) * 128], src4[:, c, :], start=(c == 0), stop=(c == 7),
                        r=["wxm"] + [(k, c) for k in hk4], w=[("P", bank)])
            self.act(cb4[:, :, fb, 3:131], ps[:, bank, :].rearrange("p (q t) -> p q t", q=4), AF.Copy,
                     r=[("P", bank)], w=[("pcb", q) for q in range(4)])
        def tile_body(q):
            i = g0 + q
            par = i % 2
            cacc = caccs[par]
            vb0 = 2 if par == 0 else 0
            kmb = 4 if par == 0 else 7
            hTt, hk = tsl(q), [hk4[q]]
            tv = C["tvalid"][:, i:i + 1]
            cbp = cb4[:, q]
            for half in range(2):
                for c in range(8):
                    self.mm(ps[:, vb0 + half, :], hTt[:, c, :], M["wxm"][:, c, half * 512:(half + 1) * 512], start=(c == 0), stop=(c == 7),
                            r=["wxm"] + [(k, c) for k in hk], w=[("P", vb0 + half)])
                self.act(vaug[par][:, 2 * half:2 * half + 2, 0:256], ps[:, vb0 + half, :].rearrange("p (h v) -> p h v", h=2), AF.Copy,
                         r=[("P", vb0 + half), ("pvaug1", par)], w=[("pvaug", par)])
                yield
            for fb in range(8):
                self.act(cacc[:, fb, :], cbp[:, fb, 3:131], AF.Identity, r=[("pcb", q), "conv_wT", "conv_bT"], w=[("pcacc", par, fb)],
                         scale=cw[:, fb, 3:4], bias=cbias[:, fb:fb + 1])
            yield
            for j in range(3):
                for fb in range(8):
                    self.v("dve", "scalar_tensor_tensor", cacc[:, fb, :], cbp[:, fb, j:j + 128], cw[:, fb, j:j + 1], cacc[:, fb, :],
                           op0=ALU.mult, op1=ALU.add, r=[("pcb", q), ("pcacc", par, fb)], w=[("pcacc", par, fb)])
                yield
            for fb in range(8):
                self.act(c_act[par][:, fb, :], cacc[:, fb, :], AF.Silu, r=[("pcacc", par, fb)], w=[("pc_act", par, fb)])
            yield
            kmps = ps[:, kmb, :].rearrange("p (h k) -> p h k", h=4)
            for h in range(4):
                for ec in range(2):
                    self.mm(kmps[:, h, :], c_act[par][:, 2 * h + ec, :], M["wmk"][:, ec, h, :], start=(ec == 0), stop=(ec == 1),
                            r=[("pc_act", par, 2 * h + ec), "wmk"], w=[("P", kmb)])
            yield
            self.v("dve", "tensor_tensor", kmw[par], kmps, wk_all[:, i, :].unsqueeze(2).to_broadcast([128, 4, 128]), ALU.mult,
                   r=[("P", kmb), "wk_all"], w=[("pkmw", par)])
            yield
            for h in range(4):
                bk = vb0 + (h % 2)
                dps = ps[:, bk, 0:257]
                self.mm(dps, kmw[par][:, h, :], vaug[par][:, h, :], r=[("pkmw", par), ("pvaug", par)], w=[("P", bk)])
                self.v("dve", "tensor_tensor", M["CT"][:, h, :], M["CT"][:, h, :], dps, ALU.add, r=[("P", bk), "CT"], w=["CT"])
                yield

        for q in range(4):
            i = g0 + q
            tv = C["tvalid"][:, i:i + 1]
            cbp = cb4[:, q]
            self.v("pool", "tensor_copy", cbp[:, :, 0:3], hist, r=["hist"], w=[("pcb", q)])
            self.v("dve", "tensor_scalar", hist, cbp[:, :, 128:131], tv, M["one1"], op0=ALU.mult, op1=ALU.mult,
                   r=[("pcb", q), "tvalid", "one1"], w=["hist"])
        for pair in ((0, 1), (2, 3)):
            live = [tile_body(q) for q in pair]
            while live:
                for gen in list(live):
                    try:
                        next(gen)
                    except StopIteration:
                        live.remove(gen)
    self.v("pool", "tensor_copy", M["convbuf"][:, :, 0:3], hist, r=["hist"], w=[("cb", 0), ("cb", 1)])


Builder.mlstm_prefix = _mlstm_prefix
```

```python
import numpy as np
import concourse.bass as bass
import concourse.mybir as mybir
from concourse.bass_utils import run_bass_kernel_spmd

F32 = mybir.dt.float32
BF16 = mybir.dt.bfloat16
AF = mybir.ActivationFunctionType
ALU = mybir.AluOpType
AX = mybir.AxisListType

NCORES = 8
D = 1024
SEQ = 8192
SEG = 2048
NLT = SEG // 128
HALO = 2048
PREFIX = 6144
TS = 32
PW = 10248
OFF_Q, OFF_K, OFF_V, OFF_ZA, OFF_XM, OFF_ZM, OFF_OM, OFF_I, OFF_F, OFF_GA, OFF_GM = (
    0, 1536, 3072, 4608, 5120, 6144, 7168, 8192, 8196, 8200, 9224)
GROUPS = ((128, 1), (512, 4), (2048, 16))
EPS = 1e-6
NEG = -30000.0
RAW_GAP = 1

DEBUG = {}


class _Op:
    __slots__ = ("eng", "fn", "deps", "stream", "signal", "val", "idx", "pos", "slot")


class Prog:
    ENGS = ("pe", "act", "dve", "pool", "sp")

    def __init__(self, nc):
        self.nc = nc
        self.ops = {e: [] for e in self.ENGS}
        self.lastw = {}
        self.readers = {}
        self.streams = {}
        self.barrier_deps = []
        self.nops = 0

    def add(self, eng, fn, r=(), w=(), dma=None):
        op = _Op()
        op.eng = eng
        op.fn = fn
        op.stream = dma
        op.signal = False
        op.val = None
        op.idx = self.nops
        self.nops += 1
        deps = {}
        for k in r:
            d = self.lastw.get(k)
            if d is not None:
                deps[d.idx] = (d, True)
            if isinstance(k, tuple) and k[0] == "P":
                for d in self.readers.get(k, ()):
                    if d.eng != eng and d.idx not in deps:
                        deps[d.idx] = (d, False)
        for k in w:
            d = self.lastw.get(k)
            if d is not None and d.idx not in deps:
                deps[d.idx] = (d, False)
            for d in self.readers.get(k, ()):
                if d.idx not in deps:
                    deps[d.idx] = (d, False)
        for d in self.barrier_deps:
            if d.idx not in deps:
                deps[d.idx] = (d, True)
        op.deps = list(deps.values())
        for k in w:
            self.lastw[k] = op
            self.readers[k] = []
        for k in r:
            self.readers.setdefault(k, []).append(op)
        op.pos = len(self.ops[eng])
        self.ops[eng].append(op)
        if dma is not None:
            self.streams.setdefault(dma, []).append(op)
        return op

    KSEM = 8
    KSEM_STREAM = {"cpy": 32}
    PERSIST = ("cpy",)

    def kof(self, s):
        return self.KSEM_STREAM.get(s, self.KSEM)

    def barrier(self):
        deps = []
        for e in self.ENGS:
            if self.ops[e]:
                for op in reversed(self.ops[e]):
                    if op.stream is None:
                        deps.append(op)
                        break
        for s, lst in self.streams.items():
            if s in self.PERSIST:
                continue
            deps.extend(lst[-self.kof(s):])
        self.barrier_deps = deps
        self.lastw = {k: v for k, v in self.lastw.items() if v.stream in self.PERSIST}
        self.readers = {}

    def emit(self, final_streams):
        nc = self.nc
        K = self.KSEM
        need = {}
        for e in self.ENGS:
            for op in self.ops[e]:
                lst = []
                for d, raw in op.deps:
                    if d.stream is None and d.eng == e:
                        if e in ("pe", "sp"):
                            continue
                        if not raw:
                            continue
                        if op.pos - d.pos > RAW_GAP:
                            continue
                    d.signal = True
                    lst.append(d)
                need[op.idx] = lst
        for e in self.ENGS:
            cnt = 0
            for op in self.ops[e]:
                if op.stream is None and op.signal:
                    cnt += 1
                    op.val = cnt
        for s, lst in self.streams.items():
            Ks = self.kof(s)
            for i, op in enumerate(lst):
                op.slot = i % Ks
                op.val = 16 * (i // Ks + 1)
        import contextlib
        with contextlib.ExitStack() as st:
            sems = {e: st.enter_context(nc.semaphore("s_" + e)) for e in self.ENGS}
            ssems = {s: [st.enter_context(nc.semaphore("d_%s_%d" % (s, k))) for k in range(min(self.kof(s), len(lst)))]
                     for s, lst in self.streams.items()}
            block = st.enter_context(nc.Block())

            def run(e, eng):
                waited = {}

                def wait(key, v):
                    if v > waited.get(key, 0):
                        waited[key] = v
                        sem = ssems[key[1]][key[2]] if key[0] == "s" else sems[key[1]]
                        eng.wait_ge(sem, v)

                for op in self.ops[e]:
                    w = {}
                    for d in need[op.idx]:
                        key = ("s", d.stream, d.slot) if d.stream is not None else ("e", d.eng)
                        if d.val > w.get(key, 0):
                            w[key] = d.val
                    for key, v in w.items():
                        wait(key, v)
                    if op.stream is not None and op.val > 16:
                        wait(("s", op.stream, op.slot), op.val - 16)
                    ins = op.fn(eng)
                    if op.stream is not None:
                        ins.then_inc(ssems[op.stream][op.slot], 16)
                    elif op.signal:
                        ins.then_inc(sems[e], 1)
                if e == "sp":
                    for s in final_streams:
                        if s in self.streams:
                            for op in self.streams[s][-self.kof(s):]:
                                wait(("s", s, op.slot), op.val)

            block.tensor(lambda t: run("pe", t))
            block.scalar(lambda t: run("act", t))
            block.vector(lambda t: run("dve", t))
            block.gpsimd(lambda t: run("pool", t))
            block.sync(lambda t: run("sp", t))


class Arena:
    def __init__(self, nc, words):
        self.t = nc.alloc_sbuf_tensor("arena", [128, words], F32)
        self.words = words
        self.top = 0
        self.marks = []

    def push(self):
        self.marks.append(self.top)

    def pop(self):
        self.top = self.marks.pop()

    def alloc(self, shape, dt=F32):
        n = int(np.prod(shape))
        words = (n + 1) // 2 if dt == BF16 else n
        words = (words + 7) // 8 * 8
        assert self.top + words <= self.words, ("SBUF arena overflow", self.top, words, self.words)
        ap = self.t[:, self.top:self.top + words]
        self.top += words
        if dt == BF16:
            ap = ap.bitcast(BF16)
        ap = ap[:, 0:n]
        if len(shape) == 2:
            ap = ap.rearrange("p (a b) -> p a b", a=shape[0])
        elif len(shape) == 3:
            ap = ap.rearrange("p (a b c) -> p a b c", a=shape[0], b=shape[1])
        elif len(shape) == 4:
            ap = ap.rearrange("p (a b c d) -> p a b c d", a=shape[0], b=shape[1], c=shape[2])
        return ap


def _t5_bucket(dist):
    n = np.asarray(dist).astype(np.int64)
    max_exact = 16
    nf = np.maximum(n, 1).astype(np.float32)
    large = max_exact + (np.log(nf / max_exact) / np.log(np.float32(2048 / max_exact))
                         * (32 - max_exact)).astype(np.int64)
    large = np.minimum(large, 31)
    return np.where(n < max_exact, n, large).astype(np.int32)


def _consts():
    c = {}
    c["ident"] = np.eye(128, dtype=np.float32)
    c["antiid"] = np.eye(128, dtype=np.float32)[::-1].copy()
    s = np.arange(128)
    c["tri"] = (s[:, None] <= s[None, :]).astype(np.float32)
    c["cmaskT"] = np.where(s[:, None] <= s[None, :], 0.0, NEG).astype(np.float32)
    oh = np.zeros((32, 3, 129), np.float32)
    for g, (_, d) in enumerate(GROUPS):
        b = _t5_bucket(np.arange(129) * d)
        oh[b, g, np.arange(129)] = 1.0
    c["ohd"] = oh
    selp = np.zeros((5, 128), np.float32); selp[0] = 1.0
    sels = np.zeros((5, 128), np.float32)
    for i in range(TS):
        sels[1 + i // 8, i] = 1.0
    c["selp"] = selp
    c["sels"] = sels
    return c


def _fm(v):
    return np.ascontiguousarray(v.reshape(-1, 128).T)


class Builder:
    def __init__(self):
        nc = bass.Bass("TRN2", target_bir_lowering=False)
        self.nc = nc
        self.P = Prog(nc)
        self.A = Arena(nc, 53184)
        self.ps = nc.alloc_psum_tensor("psum", [128, 8, 512], F32)
        self.din = {}
        self.dout = {}
        self.uid = 0

    def inp(self, name, shape, dt=F32):
        t = self.nc.dram_tensor(name, list(shape), dt, kind="ExternalInput").ap()
        self.din[name] = t
        return t

    def outp(self, name, shape, dt=F32):
        t = self.nc.dram_tensor(name, list(shape), dt, kind="ExternalOutput").ap()
        self.dout[name] = t
        return t

    def scratch(self, name, shape, dt=F32):
        return self.nc.dram_tensor(name, list(shape), dt).ap()

    def mm(self, out, lhsT, rhs, start=True, stop=True, r=(), w=()):
        return self.P.add("pe", lambda e: e.matmul(out, lhsT, rhs, start=start, stop=stop), r, w)

    def tr(self, out, in_, ident, r=(), w=()):
        return self.P.add("pe", lambda e: e.transpose(out, in_, ident), r, w)

    def act(self, out, in_, func, r=(), w=(), **kw):
        return self.P.add("act", lambda e: e.activation(out=out, in_=in_, func=func, **kw), r, w)

    def v(self, eng, name, *args, r=(), w=(), **kw):
        return self.P.add(eng, lambda e: getattr(e, name)(*args, **kw), r, w)

    def dma(self, eng, out, in_, stream, r=(), w=(), **kw):
        return self.P.add(eng, lambda e: e.dma_start(out=out, in_=in_, **kw), r, w, dma=stream)

    def dbg(self, name, ap, shape, dt=F32, r=()):
        if not DEBUG.get(name):
            return
        o = self.outp("dbg_" + name, shape, dt)
        self.P.barrier()
        self.dma("sp", o, ap, "dbg", r=r)

    def key(self, base):
        self.uid += 1
        return (base, self.uid)

    def phase0(self):
        A, P, ps = self.A, self.P, self.ps
        C = self.C = {}

        def load(name, shape, parts=128, dt=F32, eng="sp", src=None):
            t = A.alloc(list(shape[1:]), dt)
            src = self.inp(name, shape) if src is None else src
            self.dma(eng, t[0:parts], src, "ld0", w=[name])
            C[name] = t
            return t

        load("ident", [128, 128])
        C["ident_bf"] = A.alloc([128], BF16)
        self.dma("pool", C["ident_bf"], self.din["ident"], "ldc", w=["ident_bf"])
        C["antiid_bf"] = A.alloc([128], BF16)
        self.dma("pool", C["antiid_bf"], self.inp("antiid", [128, 128]), "ldc", w=["antiid_bf"])
        load("tri", [128, 128])
        load("antiid", [128, 128], src=self.din["antiid"])
        C["cmaskT_bf"] = A.alloc([128], BF16)
        self.dma("pool", C["cmaskT_bf"], self.inp("cmaskT", [128, 128]), "ldc", w=["cmaskT_bf"])
        load("ohd", [32, 3, 129], parts=32)
        load("selp", [5, 128], parts=5)
        load("sels", [5, 128], parts=5)
        load("rel_table", [32, 24], parts=32)
        for nm in ("gainT", "conv_bT", "m_normT", "m_skipT"):
            load(nm, [128, 8])
        load("b_adaT", [128, 24])
        load("conv_wT", [128, 8, 4])
        load("b_if_bc", [128, 8])
        load("tvalid", [128, 64])
        load("cT", [128, 8, 5])

        siluT = A.alloc([8, 5], BF16)
        self.act(siluT, C["cT"], AF.Silu, r=["cT"], w=["siluT"])

        ada = A.alloc([24, 5])
        mult = A.alloc([8, 5])
        gate_p = A.alloc([1024])
        gate_s = A.alloc([1024])
        A.push()
        load("b_gate_rows", [5, 1024], parts=5)
        gate_rows = A.alloc([1024])
        w_ada = self.inp("w_ada", [1024, 3072])
        wada = A.alloc([8, 3072], BF16)
        wv = w_ada.rearrange("(c p) n -> p c n", p=128)
        for c in range(8):
            self.dma("pool", wada[:, c, :], wv[:, c, :], "ldw", w=[("wada", c)])
        adaps = ps[:, 0, 0:120].rearrange("p (a b) -> p a b", a=24)
        for cb in range(24):
            for c in range(8):
                self.mm(adaps[:, cb, :], wada[:, c, cb * 128:(cb + 1) * 128], siluT[:, c, :],
                        start=(c == 0), stop=(c == 7), r=[("wada", c), "siluT"], w=[("P", 0)])
        for half in range(2):
            for c in range(8):
                self.mm(ps[0:5, 1 + half, :], siluT[:, c, :], wada[:, c, 2048 + half * 512:2048 + (half + 1) * 512],
                        start=(c == 0), stop=(c == 7), r=[("wada", c), "siluT"], w=[("P", 1 + half)])
        self.v("dve", "tensor_tensor", ada, adaps, C["b_adaT"].unsqueeze(2).to_broadcast([128, 24, 5]), ALU.add,
               r=[("P", 0), "b_adaT"], w=["ada"])
        self.v("dve", "tensor_scalar", mult, ada[:, 8:16, :], 1.0, None, op0=ALU.add, r=["ada"], w=["mult"])
        self.v("dve", "tensor_tensor", mult, mult, C["gainT"].unsqueeze(2).to_broadcast([128, 8, 5]), ALU.mult,
               r=["mult", "gainT"], w=["mult"])
        C["mult"] = mult
        C["shift"] = ada[:, 0:8, :]
        C["ada"] = ada
        for half in range(2):
            self.v("dve", "tensor_tensor", gate_rows[0:5, half * 512:(half + 1) * 512], ps[0:5, 1 + half, :],
                   C["b_gate_rows"][0:5, half * 512:(half + 1) * 512], ALU.add,
                   r=[("P", 1 + half), "b_gate_rows"], w=[("gate_rows", half)])
        for half in range(2):
            sl = slice(half * 512, (half + 1) * 512)
            self.mm(ps[:, 3, :], C["selp"][0:5, :], gate_rows[0:5, sl], r=[("gate_rows", half), "selp"], w=[("P", 3)])
            self.act(gate_p[:, sl], ps[:, 3, :], AF.Copy, r=[("P", 3)], w=[("gate_p", half)])
            self.mm(ps[:, 4, :], C["sels"][0:5, :], gate_rows[0:5, sl], r=[("gate_rows", half), "sels"], w=[("P", 4)])
            self.act(gate_s[:, sl], ps[:, 4, :], AF.Copy, r=[("P", 4)], w=[("gate_s", half)])
        A.pop()
        P.barrier()
        C["gate_p"] = gate_p
        C["gate_s"] = gate_s
        self.dbg("ada", ada, [128, 24, 5], r=["ada"])
        self.dbg("gate_s", gate_s, [128, 1024], r=[("gate_s", 0), ("gate_s", 1)])

    def norm_tile(self, xsrc, ntok, hT_dst, kind, slot, hkey, bank0=6):
        A, P, ps, C = self.A, self.P, self.ps, self.C
        W = self.W1
        i3, i2 = slot % len(W["xt"]), slot % 2
        xt = W["xt"][i3]
        self.dma("sp", xt[0:ntok], xsrc, "ldx", w=[("xt", i3)])
        ss = W["ss"][:, slot % 4:slot % 4 + 1]
        self.act(W["junk"][0:ntok], xt[0:ntok], AF.Square, r=[("xt", i3)], w=["junk", ("ss", slot % 4)],
                 accum_out=ss[0:ntok])
        self.v("dve", "tensor_scalar", ss[0:ntok], ss[0:ntok], 1.0 / D, EPS, op0=ALU.mult, op1=ALU.add,
               r=[("ss", slot % 4)], w=[("ss", slot % 4)])
        self.act(ss[0:ntok], ss[0:ntok], AF.Ln, r=[("ss", slot % 4)], w=[("ss", slot % 4)])
        self.act(ss[0:ntok], ss[0:ntok], AF.Exp, r=[("ss", slot % 4)], w=[("ss", slot % 4)], scale=-0.5)
        xn = W["xn"][i2]
        self.act(xn[0:ntok], xt[0:ntok], AF.Copy, r=[("xt", i3), ("ss", slot % 4)], w=[("xn", i2)], scale=ss[0:ntok])
        if DEBUG.get("stage", 9) < 1:
            return
        bank = bank0 + i2
        pt = ps[:, bank, :].bitcast(BF16).rearrange("p (c t) -> p c t", c=8)
        for c in range(8):
            self.tr(pt[:, c, 0:ntok], xn[0:ntok, c * 128:(c + 1) * 128], C["ident_bf"][0:ntok, 0:ntok],
                    r=[("xn", i2), "ident_bf"], w=[("P", bank0 + i2)])
        if DEBUG.get("stage", 9) < 2:
            return
        for c in range(8):
            if kind == "p":
                if True:
                    self.act(hT_dst[:, c, :], pt[:, c, 0:ntok], AF.Identity, r=[("P", bank0 + i2), "mult", "ada"], w=[(hkey, c)],
                             scale=C["mult"][:, c, 0:1], bias=C["shift"][:, c, 0:1])
                else:
                    self.v("dve", "tensor_scalar", hT_dst[:, c, :], pt[:, c, 0:ntok], C["mult"][:, c, 0:1], C["shift"][:, c, 0:1],
                           op0=ALU.mult, op1=ALU.add, r=[("P", bank0 + i2), "mult", "ada"], w=[(hkey, c)])
            else:
                tmp = W["stmp"]
                tmp0 = W["stmp0"]
                self.act(tmp0, pt[:, c, 0:ntok], AF.Copy, r=[("P", bank0 + i2)], w=["stmp0"])
                self.v("dve", "tensor_tensor", tmp.rearrange("p (s t) -> p s t", s=4),
                       tmp0.rearrange("p (s t) -> p s t", s=4),
                       C["mult"][:, c, 1:5].unsqueeze(2).to_broadcast([128, 4, 8]), ALU.mult,
                       r=["stmp0", "mult"], w=["stmp"])
                self.v("dve", "tensor_tensor", hT_dst[:, c, :].rearrange("p (s t) -> p s t", s=4),
                       tmp.rearrange("p (s t) -> p s t", s=4),
                       C["shift"][:, c, 1:5].unsqueeze(2).to_broadcast([128, 4, 8]), ALU.add,
                       r=["stmp", "ada"], w=[(hkey, c)])

    def phase1(self):
        A = self.A
        self.hT = A.alloc([8, HALO + SEG + TS], BF16)
        A.push()
        self.W1 = {
            "xt": [A.alloc([1024]) for _ in range(3)],
            "ss": A.alloc([4]),
            "junk": A.alloc([1024], BF16),
            "xn": [A.alloc([1024], BF16) for _ in range(2)],
            "stmp": A.alloc([32]),
            "stmp0": A.alloc([32]),
        }
        xh = self.inp("xh", [HALO + SEG, 1024])
        xs = self.inp("xs", [TS, 1024])
        slot = 0
        if not DEBUG.get("skip_s"):
            self.norm_tile(xs, TS, self.hT[:, :, HALO + SEG:HALO + SEG + TS], "s", slot, ("hT", 32))
        slot += 1
        for ti in range(DEBUG.get("ntiles", (HALO + SEG) // 128)):
            self.norm_tile(xh[ti * 128:(ti + 1) * 128, :], 128, self.hT[:, :, ti * 128:(ti + 1) * 128], "p", slot, ("hT", ti))
            slot += 1
        self.dbg("hT", self.hT, [128, 8, HALO + SEG + TS], BF16, r=[])
        self.dbg("xn", self.W1["xn"][1], [128, 1024], BF16, r=[])
        A.pop()


def build_program(upto=99):
    b = Builder()
    b.phase0()
    b.sample_copies()
    b.w_in = b.inp("w_in", [1024, PW])
    if upto >= 1:
        b.P.barrier()
        b.phase1()
    if upto >= 2:
        b.P.barrier()
        b.attT = b.A.alloc([4, SEG + TS], BF16)
        b.C["F"] = b.A.alloc([3, 129])
        b.A.push()
        b.phase_bias()
        b.P.barrier()
        if DEBUG.get("att_stage", 9) >= 1:
            b.phase_attention()
        b.A.pop()
        b.P.barrier()
        if not DEBUG.get("no_sattn"):
            b.phase_sample_attn()
        b.dbg("attT2", b.attT, [128, 4, SEG + TS], BF16)
    if upto >= 3:
        b.P.barrier()
        b.phase_mlstm()
    if upto >= 4:
        b.P.barrier()
        b.phase_out()
    if DEBUG.get("dmult"):
        DEBUG["mult_end"] = True
        b.dbg("mult_end", b.C["mult"], [128, 8, 5])
    b.P.emit(final_streams=list(b.P.streams.keys()))
    return b


def make_in_maps(inp, cores):
    consts = _consts()
    f32 = np.float32
    maps = []
    xp = inp["x_prompt"]
    for c in cores:
        b, p = c // 4, c % 4
        s0 = p * SEG
        m = dict(consts)
        ext = np.zeros((PREFIX + SEG, D), f32)
        lo = s0 - PREFIX
        src_lo = max(lo, 0)
        ext[src_lo - lo:] = xp[b, src_lo:s0 + SEG]
        m["xf"] = np.ascontiguousarray(ext[:PREFIX - HALO])
        m["xh"] = np.ascontiguousarray(ext[PREFIX - HALO:])
        m["xs"] = np.ascontiguousarray(inp["x_sample"][4 * c:4 * c + 4].reshape(TS, D))
        tv = np.zeros(64, f32)
        tv[(src_lo - lo) // 128:] = 1.0
        m["tvalid"] = np.ascontiguousarray(np.broadcast_to(tv, (128, 64)))
        call = np.concatenate([inp["c_prompt"][b:b + 1], inp["c_sample"][4 * c:4 * c + 4]], 0)
        m["cT"] = np.ascontiguousarray(call.T.reshape(8, 128, 5).transpose(1, 0, 2))
        m["w_ada"] = inp["w_ada"][0]
        m["rel_table"] = inp["rel_table"]
        m["gainT"] = _fm(inp["norm_gain"][0])
        m["conv_bT"] = _fm(inp["conv_b"][0])
        m["m_normT"] = _fm(inp["m_norm"][0])
        m["m_skipT"] = _fm(inp["m_skip"][0])
        m["b_adaT"] = _fm(inp["b_ada"][0])
        m["conv_wT"] = np.ascontiguousarray(inp["conv_w"][0].reshape(4, 8, 128).transpose(2, 1, 0))
        m["b_gate_rows"] = np.ascontiguousarray(np.broadcast_to(inp["b_ada"][0][2048:], (5, 1024)))
        m["fgain_bc"] = np.ascontiguousarray(np.broadcast_to(inp["final_gain"], (128, 1024)))
        m["b_if_bc"] = np.ascontiguousarray(np.broadcast_to(inp["b_if"][0], (128, 8)))
        m["w_in"] = inp["w_in"][0]
        st_ = np.zeros((8, 8, 128), f32)
        for t in range(8):
            st_[t, t, :] = 1.0
        m["selt"] = st_
        osl = np.zeros((128, 8, 8), f32)
        for t in range(8):
            osl[:, t, t] = 1.0
        m["onesel"] = osl
        for g, nm in enumerate(("cache_kv_w128", "cache_kv_w512", "cache_kv_w2048")):
            m["cache%d" % g] = np.ascontiguousarray(inp[nm][0, 4 * c:4 * c + 4].reshape(4, -1, 2, 512))
        m["w_pa"] = inp["w_pa"][0]
        m["w_pm"] = inp["w_pm"][0]
        m["w_out"] = inp["w_out"][0]
        m["w_mq"] = inp["w_mq"][0]
        m["w_mk"] = inp["w_mk"][0]
        eh = np.zeros((4, 4, 128), f32)
        for h in range(4):
            eh[h, h, :] = 1.0
        m["ehsel"] = eh
        sq = slice(4 * c, 4 * c + 4)
        Cst = inp["state_C"][0, sq]
        nst = inp["state_n"][0, sq]
        c0 = np.concatenate([Cst.transpose(0, 3, 1, 2), nst.transpose(0, 2, 1)[..., None]], axis=-1)
        m["C0T"] = np.ascontiguousarray(c0)
        mst = inp["state_m"][0, sq]
        m["m0row"] = np.ascontiguousarray(mst[:, :, None])
        m["m0bc"] = np.ascontiguousarray(np.broadcast_to(mst[:, None, :], (4, 128, 4)))
        cvs = inp["state_conv"][0, sq]
        m["conv0"] = np.ascontiguousarray(cvs.reshape(4, 3, 8, 128).transpose(0, 3, 2, 1))
        m["coremask"] = np.full((128, 128), NEG if p == 0 else 0.0, f32)
        sm = np.zeros((128, 4, 128), f32)
        for e in range(64):
            sm[e, 0, e] = 1.0
            sm[e, 1, 64 + e] = 1.0
            sm[64 + e, 2, e] = 1.0
            sm[64 + e, 3, 64 + e] = 1.0
        m["selmats"] = sm
        maps.append(m)
    return maps


def run_cores(inp, cores, upto=99):
    b = build_program(upto)
    maps = make_in_maps(inp, cores)
    maps = [{k: np.ascontiguousarray(v, dtype=np.float32) for k, v in m.items() if k in b.din} for m in maps]
    res = run_bass_kernel_spmd(b.nc, maps, core_ids=list(range(len(cores))))
    return res.results


def _phase_bias(self):
    A, P, ps, C = self.A, self.P, self.ps, self.C
    F_sb = C["F"]
    gv = A.alloc([3, 2, 256])
    self.v("pool", "memset", gv[0:8], NEG, w=["gv"])
    for g in range(3):
        self.mm(ps[0:8, 5, 0:129], C["rel_table"][0:32, g * 8:(g + 1) * 8], C["ohd"][0:32, g, :],
                r=["rel_table", "ohd"], w=[("P", 5)])
        self.act(F_sb[0:8, g, :], ps[0:8, 5, 0:129], AF.Copy, r=[("P", 5)], w=[("F", g)])
        self.v("pool", "tensor_copy", gv[0:8, g, 1, 127:255], F_sb[0:8, g, 0:128], r=[("F", g), "gv"], w=[("gv", g)])
        self.v("pool", "tensor_copy", gv[0:8, g, 0, 0:128], F_sb[0:8, g, 1:129], r=[("F", g), "gv"], w=[("gv", g)])
    gvd = self.scratch("gvd", [8, 3, 2, 256])
    self.dma("sp", gvd, gv[0:8], "gvw", r=[("gv", 0), ("gv", 1), ("gv", 2)], w=["gvd"])
    biasH = A.alloc([24, 256], BF16)
    for g in range(3):
        for h in range(8):
            for kb in range(2):
                off = ((h * 3 + g) * 2 + kb) * 256
                src = bass.AP(tensor=gvd.tensor, offset=off, ap=[[1, 128], [1, 128]])
                self.dma("pool", biasH[:, g * 8 + h, kb * 128:(kb + 1) * 128], src, "ldc", r=["gvd"], w=[("biasH", g)])
    C["biasH"] = biasH
    cm = A.alloc([128], BF16)
    self.dma("pool", cm, self.inp("coremask", [128, 128]), "ldc", w=["coremask"])
    C["coremask"] = cm
    C["selmats"] = A.alloc([4, 128])
    self.dma("sp", C["selmats"], self.inp("selmats", [128, 4, 128]), "ld0", w=["selmats"])


def _phase_attention(self):
    A, P, ps, C, hT = self.A, self.P, self.ps, self.C, self.hT
    w_in = self.w_in
    wv_in = w_in.rearrange("(c p) n -> p c n", p=128)
    kvp = [self.outp("kvp%d" % g, [GROUPS[g][0], 2, 512]) for g in range(3)]
    A.push()
    acc = A.alloc([2, SEG])
    wq = A.alloc([8, 128], BF16)
    wkv = A.alloc([8, 256], BF16)
    wz = A.alloc([8, 128], BF16)
    qT = A.alloc([SEG], BF16)
    kT = A.alloc([4096], BF16)
    vaug = A.alloc([32, 2, 128], BF16)
    pT = [A.alloc([256], BF16) for _ in range(4)]
    stage = [A.alloc([256]) for _ in range(2)]
    rbuf = A.alloc([512])
    att = A.alloc([512])
    sz = A.alloc([512])
    if not DEBUG.get("no_vones"):
        self.v("pool", "memset", vaug[:, :, :, 64:128], 1.0, w=["vones"])
    cnt = 0
    STG = DEBUG.get("att_stage", 9)
    for hp in range(DEBUG.get("att_hp", 4)):
        for g, (win, d) in enumerate(GROUPS):
            if g not in DEBUG.get("att_groups", (0, 1, 2)):
                continue
            U = SEG // d
            U2 = U + 128
            col = g * 512 + hp * 128
            allc = lambda nm: [(nm, c) for c in range(8)]
            self.dma("pool", wq, wv_in[:, :, OFF_Q + col:OFF_Q + col + 128], "ldw", w=allc("wq"))
            self.dma("pool", wkv[:, :, 0:128], wv_in[:, :, OFF_K + col:OFF_K + col + 128], "ldw", w=allc("wkv"))
            self.dma("pool", wkv[:, :, 128:256], wv_in[:, :, OFF_V + col:OFF_V + col + 128], "ldw", w=allc("wkv"))
            hq = [hT[:, c, HALO:HALO + SEG].rearrange("p (u r) -> p r u", r=d) for c in range(8)]
            hk = [hT[:, c, HALO - 128 * d:HALO + SEG].rearrange("p (u r) -> p r u", r=d) for c in range(8)]

            def chunks(Ux, total):
                res = []
                if Ux >= 512:
                    for r in range(d):
                        u0 = 0
                        while u0 < Ux:
                            n = min(512, Ux - u0)
                            res.append((r, 1, u0, n))
                            u0 += n
                else:
                    nr = 512 // Ux
                    for r0 in range(0, d, nr):
                        res.append((r0, nr, 0, Ux))
                return res

            def tile_keys(view_lo, r0, nr, u0, n, c):
                lo = view_lo + r0 + d * u0
                hi = view_lo + r0 + nr - 1 + d * (u0 + n - 1)
                return [(("hT", t), c) for t in range(lo // 128, hi // 128 + 1)]

            for (r0, nr, u0, n) in chunks(U, SEG):
                bank = cnt % 2
                cnt += 1
                pso = ps[:, bank, 0:nr * n]
                pso3 = pso if nr == 1 else pso.rearrange("p (a b) -> p a b", a=nr)
                for c in range(8):
                    rhs = hq[c][:, r0, u0:u0 + n] if nr == 1 else hq[c][:, r0:r0 + nr, :]
                    self.mm(pso3, wq[:, c, :], rhs, start=(c == 0), stop=(c == 7),
                            r=[("wq", c)] + tile_keys(HALO, r0, nr, u0, n, c), w=[("P", bank)])
                f0 = r0 * U + u0
                self.act(qT[:, f0:f0 + nr * n], pso, AF.Copy, r=[("P", bank)], w=["qT"], scale=0.125)
            if STG < 2:
                continue
            for (r0, nr, u0, n) in chunks(U2, U2 * d):
                bank = cnt % 2
                cnt += 1
                pso = ps[:, bank, 0:nr * n]
                pso3 = pso if nr == 1 else pso.rearrange("p (a b) -> p a b", a=nr)
                for c in range(8):
                    rhs = hk[c][:, r0, u0:u0 + n] if nr == 1 else hk[c][:, r0:r0 + nr, :]
                    self.mm(pso3, wkv[:, c, 0:128], rhs, start=(c == 0), stop=(c == 7),
                            r=[("wkv", c)] + tile_keys(HALO - 128 * d, r0, nr, u0, n, c), w=[("P", bank)])
                f0 = r0 * U2 + u0
                self.act(kT[:, f0:f0 + nr * n], pso, AF.Copy, r=[("P", bank)], w=["kT"])
            nm = U2 // 128
            if STG < 3:
                continue
            for r in range(d):
                for m in range(nm):
                    blk = r * nm + m
                    bank = cnt % 2
                    cnt += 1
                    pso = ps[:, bank, 0:256]
                    lastb = (m == nm - 1)
                    for c in range(8):
                        self.mm(pso if lastb else pso[:, 128:256], hk[c][:, r, 128 * m:128 * m + 128],
                                wkv[:, c, :] if lastb else wkv[:, c, 128:256], start=(c == 0), stop=(c == 7),
                                r=[("wkv", c)] + tile_keys(HALO - 128 * d, r, 1, 128 * m, 128, c), w=[("P", bank)])
                    self.act(vaug[:, blk, :, 0:64], pso[:, 128:256].rearrange("p (h e) -> p h e", h=2), AF.Copy,
                             r=[("P", bank), "vones"], w=[("vaug", blk)])
                    if m == nm - 1 and not DEBUG.get("no_kvout"):
                        st = stage[blk % 2]
                        if DEBUG.get("kv_act"):
                            self.act(st, pso, AF.Copy, r=[("P", bank)], w=[("stage", blk % 2)])
                        else:
                            self.v("dve", "tensor_copy", st, pso, r=[("P", bank)], w=[("stage", blk % 2)])
                        dst = kvp[g].rearrange("(i r) k c -> r i k c", r=d)[r, :, :, hp * 128:(hp + 1) * 128]
                        if DEBUG.get("kv_plain"):
                            dst = kvp[g][0:128, :, hp * 128:(hp + 1) * 128]
                        if not DEBUG.get("kv_nodma"):
                            self.dma("sp", dst, st.rearrange("p (k c) -> p k c", k=2), "out", r=[("stage", blk % 2)], w=[])
            if STG < 4:
                continue
            for r in range(d):
                for n in range(U // 128):
                    for h in range(2):
                        gh = g * 8 + hp * 2 + h
                        hs = slice(h * 64, h * 64 + 64)
                        si = cnt % 3
                        cnt += 1
                        sbk, obk = (2, 3, 6)[si], (4, 5, 7)[si]
                        S = ps[:, sbk, 0:256]
                        O = ps[:, obk, 0:128]
                        self.mm(S, C["antiid_bf"], C["biasH"][:, gh, :], start=True, stop=False,
                                r=["antiid_bf", ("biasH", g)], w=[("P", sbk)])
                        if n == 0:
                            self.mm(S[:, 0:128], C["ident_bf"], C["coremask"], start=False, stop=False,
                                    r=["ident_bf", "coremask"], w=[("P", sbk)])
                        q_ap = qT[hs, r * U + 128 * n:r * U + 128 * n + 128]
                        self.mm(S[:, 0:128], kT[hs, r * U2 + 128 * n:r * U2 + 128 * n + 128], q_ap, start=False, stop=False,
                                r=["qT", "kT"], w=[("P", sbk)])
                        self.mm(S[:, 128:256], kT[hs, r * U2 + 128 * (n + 1):r * U2 + 128 * (n + 2)], q_ap, start=False, stop=True,
                                r=["qT", "kT"], w=[("P", sbk)])
                        self.act(pT[si], S, AF.Exp, r=[("P", sbk)], w=[("pT", si)])
                        b0 = r * nm + n
                        self.mm(O, vaug[:, b0, h, :], pT[si][:, 0:128], start=True, stop=False,
                                r=[("vaug", b0), ("pT", si)], w=[("P", obk)])
                        self.mm(O, vaug[:, b0 + 1, h, :], pT[si][:, 128:256], start=False, stop=True,
                                r=[("vaug", b0 + 1), ("pT", si)], w=[("P", obk)])
                        av = acc[:, h, :].rearrange("p (u r) -> p r u", r=d)[:, r, 128 * n:128 * n + 128]
                        if d == 1:
                            ak = [("acc", h, n // 4, rr) for rr in range(16)]
                        elif d == 4:
                            ak = [("acc", h, n, r + 4 * j) for j in range(4)]
                        else:
                            ak = [("acc", h, qq, r) for qq in range(4)]
                        if g == 0:
                            self.v("dve", "tensor_copy", av, O, r=[("P", obk)], w=ak)
                        else:
                            self.v("dve", "tensor_tensor", av, av, O, ALU.add, r=[("P", obk)] + ak, w=ak)
        if STG < 5:
            continue
        P.barrier()
        zc = OFF_ZA + hp * 128
        self.dma("pool", wz, wv_in[:, :, zc:zc + 128], "ldw", w=[("wz", c) for c in range(8)])
        for k in range(4):
            tk = slice(512 * k, 512 * k + 512)
            for j, (bank, sm) in enumerate(((6, (0, 1)), (7, (2, 3)))):
                for h in range(2):
                    self.mm(ps[:, bank, :], C["selmats"][:, sm[h], :], acc[:, h, tk], start=(h == 0), stop=(h == 1),
                            r=["selmats"], w=[("P", 6 + j)])
            self.v("dve", "reciprocal", rbuf, ps[:, 7, :], r=[("P", 7)], w=["rbuf"])
            self.v("dve", "tensor_tensor", att, ps[:, 6, :], rbuf, ALU.mult, r=[("P", 6), "rbuf"], w=["att"])
            for c in range(8):
                self.mm(ps[:, 0, :], wz[:, c, :], hT[:, c, HALO + 512 * k:HALO + 512 * k + 512], start=(c == 0), stop=(c == 7),
                        r=[("wz", c)] + [(("hT", t), c) for t in range(16 + 4 * k, 16 + 4 * k + 4)], w=[("P", 0)])
            self.act(sz, ps[:, 0, :], AF.Silu, r=[("P", 0)], w=["sz"])
            self.v("dve", "tensor_tensor", self.attT[:, hp, tk], att, sz, ALU.mult, r=["att", "sz"], w=[("attT", hp)])
        P.barrier()
    A.pop()
    self.dbg("attT", self.attT, [128, 4, SEG + TS], BF16)


Builder.phase_bias = _phase_bias
Builder.phase_attention = _phase_attention


def _mlstm_setup(self):
    A, P, C = self.A, self.P, self.C
    wv_in = self.w_in.rearrange("(c p) n -> p c n", p=128)
    M = self.M = {}
    M["wxm"] = A.alloc([8, 1024], BF16)
    M["wg"] = A.alloc([8, 8], BF16)
    M["wmq"] = A.alloc([2, 4, 128], BF16)
    M["wmk"] = A.alloc([2, 4, 128], BF16)
    for c in range(8):
        self.dma("pool", M["wxm"][:, c, :], wv_in[:, c, OFF_XM:OFF_XM + 1024], "ldw", w=["wxm"])
        self.dma("pool", M["wg"][:, c, :], wv_in[:, c, OFF_I:OFF_I + 8], "ldw", w=["wg"])
    wq_d = self.inp("w_mq", [4, 256, 128]).rearrange("h (c p) k -> p c h k", p=128)
    wk_d = self.inp("w_mk", [4, 256, 128]).rearrange("h (c p) k -> p c h k", p=128)
    for ec in range(2):
        for h in range(4):
            self.dma("pool", M["wmq"][:, ec, h, :], wq_d[:, ec, h, :], "ldw", w=["wmq"])
            self.dma("pool", M["wmk"][:, ec, h, :], wk_d[:, ec, h, :], "ldw", w=["wmk"])
    M["ones"] = A.alloc([128])
    self.v("pool", "memset", M["ones"], 1.0, w=["ones"])
    M["ehsel"] = A.alloc([4, 128])
    self.dma("sp", M["ehsel"][0:4], self.inp("ehsel", [4, 4, 128]), "ld0", w=["ehsel"])
    M["negbig"] = A.alloc([64])
    self.v("dve", "tensor_scalar", M["negbig"], C["tvalid"], 1.0e4, -1.0e4, op0=ALU.mult, op1=ALU.add,
           r=["tvalid"], w=["negbig"])
    M["one1"] = A.alloc([1])
    M["zero1"] = A.alloc([1])
    self.v("pool", "memset", M["one1"], 1.0, w=["one1"])
    self.v("pool", "memset", M["zero1"], 0.0, w=["zero1"])
    M["neg1"] = A.alloc([1])
    self.v("pool", "memset", M["neg1"], -1.0, w=["neg1"])
    M["CT"] = A.alloc([4, 257], F32)
    M["m_row"] = A.alloc([1], F32)
    M["m_bc"] = A.alloc([4], F32)
    M["convbuf"] = A.alloc([8, 131], F32)


def _mlstm_local_setup(self):
    A, P, C, M = self.A, self.P, self.C, self.M
    wv_in = self.w_in.rearrange("(c p) n -> p c n", p=128)
    M["wzm"] = A.alloc([8, 1024], BF16)
    M["wom"] = A.alloc([8, 1024], BF16)
    for c in range(8):
        self.dma("pool", M["wzm"][:, c, :], wv_in[:, c, OFF_ZM:OFF_ZM + 1024], "ldw", w=["wzm"])
        self.dma("pool", M["wom"][:, c, :], wv_in[:, c, OFF_OM:OFF_OM + 1024], "ldw", w=["wom"])
    for nm, shp, dt in (("cacc", [8, 128], F32), ("c_act", [8, 128], BF16),
                        ("vaug", [4, 257], BF16), ("kmw", [4, 128], BF16), ("qmT", [4, 128], BF16), ("kmT", [4, 128], BF16),
                        ("gt", [8], F32), ("lf", [4], F32), ("ie", [4], F32), ("b_tok", [4], F32), ("a_tok", [4], F32),
                        ("a_row", [128], F32), ("cm_row", [128], F32), ("M_row", [128], F32), ("negM_row", [128], F32),
                        ("AT", [1], F32), ("Mend_row", [1], F32), ("dg", [4], F32), ("Mend_bc", [4], F32),
                        ("tmp4", [4], F32), ("wk", [4], F32), ("wC", [4], F32), ("M_tok", [4], F32), ("emt", [4], F32),
                        ("DT", [128], F32), ("Wbc", [128], F32), ("scT", [128], BF16), ("qtil", [128], BF16),
                        ("CTb", [4, 257], BF16), ("hh", [256], F32), ("hn", [256], BF16), ("hnm", [8, 128], F32),
                        ("st", [8], F32), ("so", [128], F32), ("szm", [128], F32), ("t1", [128], F32), ("t2", [128], F32),
                        ):
        M[nm] = A.alloc(shp, dt)
        if nm in ("DT", "Wbc", "scT", "qtil", "hh", "hn", "st"):
            M[nm + "_b"] = A.alloc(shp, dt)
    self.v("pool", "memset", M["vaug"][:, :, 256:257], 1.0, w=["vaug1"])
    M["so_all"] = A.alloc([8, 512], BF16)
    M["sz_all"] = A.alloc([8, 512], BF16)


def _mlstm_gates4(self, tok0, tiles, n=512):
    ps, M, hT = self.ps, self.M, self.hT
    for fb in range(8):
        for (wname, bank, func, dst, dk) in (("wom", 5, AF.Sigmoid, "so_all", "so_all"), ("wzm", 6, AF.Silu, "sz_all", "sz_all")):
            for c in range(8):
                self.mm(ps[:, bank, 0:n], M[wname][:, c, fb * 128:(fb + 1) * 128], hT[:, c, tok0:tok0 + n], start=(c == 0), stop=(c == 7),
                        r=[wname] + [(("hT", t), c) for t in tiles], w=[("P", bank)])
            self.act(M[dst][:, fb, 0:n], ps[:, bank, 0:n], func, r=[("P", bank)], w=[(dk, fb)])


def _mlstm_tile(self, hTt, hkeys, ntok, tcol, with_out, mout_dst, moutkey, gate_off=None):
    A, P, ps, C, M = self.A, self.P, self.ps, self.C, self.M
    N = ntok
    B = lambda i: ("P", i)
    tv = C["tvalid"][:, tcol:tcol + 1] if tcol is not None else M["one1"]
    nb = M["negbig"][:, tcol:tcol + 1] if tcol is not None else M["zero1"]
    tvk = ["tvalid", "negbig", "one1", "zero1"]
    cb = M["convbuf"]
    xmps = ps[:, 0:2, :].rearrange("p a (b t) -> p (a b) t", t=128)
    for fb in range(8):
        for c in range(8):
            self.mm(xmps[:, fb, 0:N], M["wxm"][:, c, fb * 128:(fb + 1) * 128], hTt[:, c, :], start=(c == 0), stop=(c == 7),
                    r=["wxm"] + [(k, c) for k in hkeys], w=[B(fb // 4)])
    for half in range(2):
        self.act(cb[:, 4 * half:4 * half + 4, 3:3 + N], xmps[:, 4 * half:4 * half + 4, 0:N], AF.Copy,
                 r=[B(half)] + tvk, w=[("cb", half)], scale=tv)
    for half in range(2):
        for c in range(8):
            self.mm(ps[0:N, 2 + half, :], hTt[:, c, :], M["wxm"][:, c, half * 512:(half + 1) * 512], start=(c == 0), stop=(c == 7),
                    r=["wxm"] + [(k, c) for k in hkeys], w=[B(2 + half)])
        self.act(M["vaug"][0:N, 2 * half:2 * half + 2, 0:256], ps[0:N, 2 + half, :].rearrange("p (h v) -> p h v", h=2), AF.Copy,
                 r=[B(2 + half), "vaug1"], w=[("vaug", half)])
    gps = ps[0:N, 7, 0:8]
    for c in range(8):
        self.mm(gps, hTt[:, c, :], M["wg"][:, c, :], start=(c == 0), stop=(c == 7),
                r=["wg"] + [(k, c) for k in hkeys], w=[B(7)])
    gt = M["gt"]
    self.v("dve", "tensor_tensor", gt[0:N], gps, C["b_if_bc"][0:N], ALU.add, r=[B(7), "b_if_bc"], w=["gt"])
    lf = M["lf"]
    self.act(lf[0:N], gt[0:N, 4:8], AF.Exp, r=["gt"], w=["lf"], scale=-1.0)
    self.act(lf[0:N], lf[0:N], AF.Ln, r=["lf"], w=["lf"], bias=M["one1"][0:N])
    self.v("dve", "tensor_scalar", lf[0:N], lf[0:N], tv[0:N], M["neg1"][0:N], op0=ALU.mult, op1=ALU.mult, r=["lf", "neg1"] + tvk, w=["lf"])
    ie = M["ie"]
    self.v("dve", "tensor_scalar", ie[0:N], gt[0:N, 0:4], tv[0:N], nb[0:N], op0=ALU.mult, op1=ALU.add, r=["gt"] + tvk, w=["ie"])
    tri, ident = C["tri"], C["ident"]
    self.mm(ps[0:N, 7, 8:12], tri[0:N, 0:N], lf[0:N], r=["lf", "tri"], w=[B(7)])
    self.mm(ps[:, 7, 12:16], M["ones"][0:N, :], lf[0:N], r=["lf", "ones"], w=[B(7)])
    self.mm(ps[0:4, 7, 144:144 + N], lf[0:N], tri[0:N, 0:N], r=["lf", "tri"], w=[B(7)])
    b_tok, a_tok = M["b_tok"], M["a_tok"]
    self.v("dve", "tensor_copy", b_tok[0:N], ps[0:N, 7, 8:12], r=[B(7)], w=["b_tok"])
    self.v("dve", "tensor_tensor", a_tok[0:N], ie[0:N], b_tok[0:N], ALU.subtract, r=["ie", "b_tok"], w=["a_tok"])
    self.mm(ps[0:4, 7, 16:16 + N], a_tok[0:N], ident[0:N, 0:N], r=["a_tok", "ident"], w=[B(7)])
    a_row = M["a_row"]
    self.v("dve", "tensor_copy", a_row[0:4, 0:N], ps[0:4, 7, 16:16 + N], r=[B(7)], w=["a_row"])
    cw, cbias = C["conv_wT"], C["conv_bT"]
    for fb in range(8):
        self.act(M["cacc"][:, fb, 0:N], cb[:, fb, 3:3 + N], AF.Identity, r=[("cb", fb // 4), "conv_wT", "conv_bT"], w=[("cacc", fb)],
                 scale=cw[:, fb, 3:4], bias=cbias[:, fb:fb + 1])
    for j in range(3):
        for fb in range(8):
            ca = M["cacc"][:, fb, 0:N]
            self.v("dve", "scalar_tensor_tensor", ca, cb[:, fb, j:j + N], cw[:, fb, j:j + 1], ca, op0=ALU.mult, op1=ALU.add,
                   r=[("cb", fb // 4), ("cacc", fb)], w=[("cacc", fb)])
    for fb in range(8):
        self.act(M["c_act"][:, fb, 0:N], M["cacc"][:, fb, 0:N], AF.Silu, r=[("cacc", fb)], w=[("c_act", fb)])
    for half in range(2):
        self.v("pool", "tensor_copy", cb[:, 4 * half:4 * half + 4, 0:3], cb[:, 4 * half:4 * half + 4, N:N + 3],
               r=[("cb", half)], w=[("cb", half)])
    kmps = ps[0:N, 4, :].rearrange("p (h k) -> p h k", h=4)
    for h in range(4):
        for ec in range(2):
            self.mm(kmps[:, h, :], M["c_act"][:, 2 * h + ec, 0:N], M["wmk"][:, ec, h, :], start=(ec == 0), stop=(ec == 1),
                    r=[("c_act", 2 * h + ec), "wmk"], w=[B(4)])
    if with_out:
        qps = ps[:, 5, :].rearrange("p (h t) -> p h t", h=4)
        kps = ps[:, 6, :].rearrange("p (h t) -> p h t", h=4)
        for h in range(4):
            for ec in range(2):
                self.mm(qps[:, h, 0:N], M["wmq"][:, ec, h, :], M["c_act"][:, 2 * h + ec, 0:N], start=(ec == 0), stop=(ec == 1),
                        r=[("c_act", 2 * h + ec), "wmq"], w=[B(5)])
            for ec in range(2):
                self.mm(kps[:, h, 0:N], M["wmk"][:, ec, h, :], M["c_act"][:, 2 * h + ec, 0:N], start=(ec == 0), stop=(ec == 1),
                        r=[("c_act", 2 * h + ec), "wmk"], w=[B(6)])
        self.act(M["qmT"][:, :, 0:N], qps[:, :, 0:N], AF.Copy, r=[B(5)], w=["qmT"], scale=float(128 ** -0.5))
        self.act(M["kmT"][:, :, 0:N], kps[:, :, 0:N], AF.Copy, r=[B(6)], w=["kmT"])
        self.v("pool", "tensor_copy", M["CTb"], M["CT"], r=["CT"], w=["CTb"])
        self.v("dve", "tensor_tensor_scan", M["cm_row"][0:4, 0:N], M["ones"][0:4, 0:N], a_row[0:4, 0:N], -1.0e30,
               op0=ALU.mult, op1=ALU.max, r=["a_row", "ones"], w=["cm_row"])
        self.v("dve", "tensor_tensor", M["M_row"][0:4, 0:N], M["cm_row"][0:4, 0:N], M["m_row"][0:4, 0:1].to_broadcast([4, N]), ALU.max,
               r=["cm_row", "m_row"], w=["M_row"])
        self.v("dve", "tensor_scalar", M["negM_row"][0:4, 0:N], M["M_row"][0:4, 0:N], -1.0, None, op0=ALU.mult,
               r=["M_row"], w=["negM_row"])
        self.mm(ps[0:N, 7, 276:280], M["M_row"][0:4, 0:N], ident[0:4, 0:4], r=["M_row", "ident"], w=[B(7)])
        self.v("dve", "tensor_tensor", M["emt"][0:N], b_tok[0:N], ps[0:N, 7, 276:280], ALU.add, r=["b_tok", B(7)], w=["emt"])
        self.act(M["emt"][0:N], M["emt"][0:N], AF.Exp, r=["emt"], w=["emt"], scale=-1.0)
    self.v("dve", "tensor_reduce", M["AT"][0:4], a_row[0:4, 0:N], AX.X, ALU.max, r=["a_row"], w=["AT"])
    self.v("dve", "tensor_tensor", M["Mend_row"][0:4], M["AT"][0:4], M["m_row"][0:4], ALU.max, r=["AT", "m_row"], w=["Mend_row"])
    self.v("dve", "tensor_tensor", M["dg"][0:4], ident[0:4, 0:4], M["Mend_row"][0:4, 0:1].to_broadcast([4, 4]), ALU.mult,
           r=["Mend_row", "ident"], w=["dg"])
    self.mm(ps[:, 7, 272:276], M["ones"][0:4, :], M["dg"][0:4], r=["dg", "ones"], w=[B(7)])
    self.v("dve", "tensor_copy", M["Mend_bc"], ps[:, 7, 272:276], r=[B(7)], w=["Mend_bc"])
    self.v("dve", "tensor_tensor", M["wk"][0:N], a_tok[0:N], M["Mend_bc"][0:N], ALU.subtract, r=["a_tok", "Mend_bc"], w=["wk"])
    self.act(M["wk"][0:N], M["wk"][0:N], AF.Exp, r=["wk"], w=["wk"])
    self.v("dve", "tensor_tensor", M["wC"], M["m_bc"], M["Mend_bc"], ALU.subtract, r=["m_bc", "Mend_bc"], w=["wC"])
    self.act(M["wC"], M["wC"], AF.Exp, r=["wC"], w=["wC"])
    self.v("dve", "tensor_tensor", M["kmw"][0:N], kmps, M["wk"][0:N].unsqueeze(2).to_broadcast([N, 4, 128]), ALU.mult,
           r=[B(4), "wk"], w=["kmw"])
    if with_out:
        def head_body(h):
            hsl = slice(h, h + 1)
            par = h % 2
            sfx = "" if par == 0 else "_b"
            hb0, hb1 = (0, 1) if par == 0 else (5, 6)
            pl, pm, pst = ps[:, hb0, 0:N], ps[:, hb0, 128:128 + N], ps[:, hb0, 256:256 + N]
            self.mm(pl, M["ehsel"][0:4, h, :], M["negM_row"][0:4, 0:N], r=["ehsel", "negM_row"], w=[B(hb0)])
            yield
            self.mm(pm, M["ehsel"][0:4, h, :], M["negM_row"][0:4, 0:N], start=True, stop=False, r=["ehsel", "negM_row"], w=[B(hb0)])
            yield
            self.mm(pm[0:N], C["ident_bf"][0:N, 0:N], C["cmaskT_bf"][0:N, 0:N], start=False, stop=True,
                    r=["ident_bf", "cmaskT_bf"], w=[B(hb0)])
            yield
            self.mm(pst[0:N], M["kmT"][:, h, 0:N], M["qmT"][:, h, 0:N], r=["kmT", "qmT"], w=[B(hb0)])
            yield
            self.act(M["DT" + sfx][0:N, 0:N], pm[0:N], AF.Exp, r=[B(hb0), "a_tok"], w=["DT" + sfx], bias=a_tok[0:N, hsl])
            yield
            self.act(M["Wbc" + sfx][:, 0:N], pl, AF.Exp, r=[B(hb0), "m_bc"], w=["Wbc" + sfx], bias=M["m_bc"][:, hsl])
            yield
            self.v("dve", "tensor_tensor", M["scT" + sfx][0:N, 0:N], pst[0:N], M["DT" + sfx][0:N, 0:N], ALU.mult, r=[B(hb0), "DT" + sfx], w=["scT" + sfx])
            yield
            self.v("pool", "tensor_tensor", M["qtil" + sfx][:, 0:N], M["qmT"][:, h, 0:N], M["Wbc" + sfx][:, 0:N], ALU.mult,
                   r=["qmT", "Wbc" + sfx], w=["qtil" + sfx])
            yield
            nd = ps[0:N, hb1, 0:257]
            self.mm(nd, M["scT" + sfx][0:N, 0:N], M["vaug"][0:N, h, :], start=True, stop=False, r=["scT" + sfx, ("vaug", h // 2), "vaug1"], w=[B(hb1)])
            yield
            self.mm(nd, M["qtil" + sfx][:, 0:N], M["CTb"][:, h, :], start=False, stop=True, r=["qtil" + sfx, "CTb"], w=[B(hb1)])
            yield
            st = M["st" + sfx]
            self.act(st[0:N, 0:1], nd[:, 256:257], AF.Abs, r=[B(hb1)], w=["st" + sfx])
            yield
            self.v("dve", "tensor_tensor", st[0:N, 0:1], st[0:N, 0:1], M["emt"][0:N, hsl], ALU.max, r=["st" + sfx, "emt"], w=["st" + sfx])
            yield
            self.v("dve", "reciprocal", st[0:N, 0:1], st[0:N, 0:1], r=["st" + sfx], w=["st" + sfx])
            yield
            self.act(M["hh" + sfx][0:N], nd[:, 0:256], AF.Copy, r=[B(hb1), "st" + sfx], w=["hh" + sfx, "st1" + sfx], scale=st[0:N, 0:1], accum_out=st[0:N, 1:2])
            yield
            self.act(M["hn" + sfx][0:N], M["hh" + sfx][0:N], AF.Square, r=["hh" + sfx], w=["hn" + sfx, "st2" + sfx], accum_out=st[0:N, 2:3])
            yield
            self.v("dve", "tensor_scalar", st[0:N, 3:4], st[0:N, 1:2], 1.0 / 256, None, op0=ALU.mult, r=["st1" + sfx], w=["st3" + sfx])
            yield
            self.v("dve", "tensor_tensor", st[0:N, 4:5], st[0:N, 3:4], st[0:N, 3:4], ALU.mult, r=["st3" + sfx], w=["st4" + sfx])
            yield
            self.v("dve", "scalar_tensor_tensor", st[0:N, 5:6], st[0:N, 2:3], 1.0 / 256, st[0:N, 4:5], op0=ALU.mult, op1=ALU.subtract,
                   r=["st2" + sfx, "st4" + sfx], w=["st5" + sfx])
            yield
            self.v("dve", "tensor_scalar", st[0:N, 5:6], st[0:N, 5:6], 0.0, EPS, op0=ALU.max, op1=ALU.add, r=["st5" + sfx], w=["st5" + sfx])
            yield
            self.act(st[0:N, 5:6], st[0:N, 5:6], AF.Ln, r=["st5" + sfx], w=["st5" + sfx])
            yield
            self.act(st[0:N, 5:6], st[0:N, 5:6], AF.Exp, r=["st5" + sfx], w=["st5" + sfx], scale=-0.5)
            yield
            self.v("dve", "scalar_tensor_tensor", st[0:N, 6:7], st[0:N, 3:4], -1.0, st[0:N, 5:6], op0=ALU.mult, op1=ALU.mult,
                   r=["st3" + sfx, "st5" + sfx], w=["st6" + sfx])
            yield
            self.act(M["hn" + sfx][0:N], M["hh" + sfx][0:N], AF.Identity, r=["hh" + sfx, "st5" + sfx, "st6" + sfx], w=["hn" + sfx], scale=st[0:N, 5:6], bias=st[0:N, 6:7])
            yield
            pt = ps[:, 4, 128 * par:128 * par + 128].bitcast(BF16).rearrange("p (b t) -> p b t", b=2)
            for vb in range(2):
                self.tr(pt[:, vb, 0:N], M["hn" + sfx][0:N, vb * 128:(vb + 1) * 128], C["ident_bf"][0:N, 0:N], r=["hn" + sfx, "ident_bf"], w=[B(4)])
            yield
            for vb in range(2):
                fb = 2 * h + vb
                self.act(M["hnm"][:, fb, 0:N], pt[:, vb, 0:N], AF.Copy, r=[B(4), "m_normT"], w=[("hnm", fb)], scale=C["m_normT"][:, fb:fb + 1])
            yield
        for pair in ((0, 1), (2, 3)):
            gens = [head_body(h) for h in pair]
            live = list(gens)
            while live:
                for gen in list(live):
                    try:
                        next(gen)
                    except StopIteration:
                        live.remove(gen)
    for h in range(4):
        bk = 2 + (h % 2)
        dps = ps[:, bk, 0:257]
        self.mm(dps, M["kmw"][0:N, h, :], M["vaug"][0:N, h, :], r=["kmw", ("vaug", h // 2), "vaug1"], w=[B(bk)])
        self.v("dve", "scalar_tensor_tensor", M["CT"][:, h, :], M["CT"][:, h, :], M["wC"][:, h:h + 1], dps, op0=ALU.mult, op1=ALU.add,
               r=["wC", B(bk), "CT", "CTb"], w=["CT"])
    self.v("dve", "tensor_tensor", M["m_row"][0:4], M["Mend_row"][0:4], ps[0:4, 7, 144 + N - 1:144 + N], ALU.add,
           r=["Mend_row", B(7)], w=["m_row"])
    self.v("dve", "tensor_tensor", M["m_bc"], M["Mend_bc"], ps[:, 7, 12:16], ALU.add, r=["Mend_bc", B(7)], w=["m_bc"])
    if with_out:
        for fb in range(8):
            if gate_off is None:
                for (wname, bank) in (("wom", 5), ("wzm", 6)):
                    for c in range(8):
                        self.mm(ps[:, bank, 0:N], M[wname][:, c, fb * 128:(fb + 1) * 128], hTt[:, c, :], start=(c == 0), stop=(c == 7),
                                r=[wname] + [(k, c) for k in hkeys], w=[B(bank)])
                self.act(M["so"][:, 0:N], ps[:, 5, 0:N], AF.Sigmoid, r=[B(5)], w=["so"])
                self.act(M["szm"][:, 0:N], ps[:, 6, 0:N], AF.Silu, r=[B(6)], w=["szm"])
                so_ap, sz_ap, sok, szk = M["so"][:, 0:N], M["szm"][:, 0:N], "so", "szm"
            else:
                so_ap, sz_ap = M["so_all"][:, fb, gate_off:gate_off + N], M["sz_all"][:, fb, gate_off:gate_off + N]
                sok, szk = ("so_all", fb), ("sz_all", fb)
            self.v("dve", "tensor_tensor", M["t1"][:, 0:N], so_ap, M["hnm"][:, fb, 0:N], ALU.mult, r=[sok, ("hnm", fb)], w=["t1"])
            self.v("dve", "scalar_tensor_tensor", M["t2"][:, 0:N], M["c_act"][:, fb, 0:N], C["m_skipT"][:, fb:fb + 1], M["t1"][:, 0:N],
                   op0=ALU.mult, op1=ALU.add, r=[("c_act", fb), "t1", "m_skipT"], w=["t2"])
            self.v("dve", "tensor_tensor", mout_dst[:, fb, :], M["t2"][:, 0:N], sz_ap, ALU.mult, r=["t2", szk], w=[(moutkey, fb)])


Builder.mlstm_setup = _mlstm_setup
Builder.mlstm_local_setup = _mlstm_local_setup
Builder.mlstm_tile = _mlstm_tile
Builder.mlstm_gates4 = _mlstm_gates4


def _phase_mlstm(self):
    A, P, ps, C, hT = self.A, self.P, self.ps, self.C, self.hT
    self.moutS = A.alloc([8, TS], BF16)
    A.push()
    self.mlstm_setup()
    M = self.M
    self.v("pool", "memset", M["CT"], 0.0, w=["CT"])
    A.push()
    self.W1 = {
        "xt": [A.alloc([1024]) for _ in range(2)],
        "ss": A.alloc([4]),
        "junk": A.alloc([1024], BF16),
        "xn": [A.alloc([1024], BF16) for _ in range(2)],
    }
    self.mlstm_prefix()
    if DEBUG.get("sbuf"):
        print("mlstm prefix sbuf top", A.top)
    A.pop()
    P.barrier()
    self.mlstm_local_setup()
    if DEBUG.get("sbuf"):
        print("mlstm local sbuf top", A.top)
    for j in range(DEBUG.get("nloc", 16)):
        if j % 4 == 0:
            self.mlstm_gates4(HALO + j * 128, [16 + j + q for q in range(4)])
        self.mlstm_tile(hT[:, :, HALO + j * 128:HALO + (j + 1) * 128], [("hT", 16 + j)], 128, 48 + j, True,
                        hT[:, :, j * 128:(j + 1) * 128], ("hT", j), gate_off=(j % 4) * 128)
    o_conv = self.outp("convp", [128, 8, 3])
    o_C = self.outp("Cp", [128, 4, 257])
    o_m = self.outp("mp", [4, 1])
    self.dma("sp", o_conv, M["convbuf"][:, :, 0:3], "out", r=[("cb", 0), ("cb", 1)])
    self.dma("sp", o_C, M["CT"], "out", r=["CT"])
    self.dma("sp", o_m, M["m_row"][0:4], "out", r=["m_row"])
    C0 = self.inp("C0T", [4, 128, 4, 257])
    m0r = self.inp("m0row", [4, 4, 1])
    m0b = self.inp("m0bc", [4, 128, 4])
    cv0 = self.inp("conv0", [4, 128, 8, 3])
    o_convs = self.outp("convs", [4, 128, 8, 3])
    o_Cs = self.outp("Cs", [4, 128, 4, 257])
    o_ms = self.outp("ms", [4, 4, 1])
    if not DEBUG.get("no_smp"):
        self.mlstm_gates4(HALO + SEG, [32], n=TS)
    for j in range(4 if not DEBUG.get("no_smp") else 0):
        self.dma("sp", M["CT"], C0[j], "ldx", w=["CT"])
        self.dma("sp", M["m_row"][0:4], m0r[j], "ldx", w=["m_row"])
        self.dma("sp", M["m_bc"], m0b[j], "ldx", w=["m_bc"])
        self.dma("sp", M["convbuf"][:, :, 0:3], cv0[j], "ldx", w=[("cb", 0), ("cb", 1)])
        self.mlstm_tile(hT[:, :, HALO + SEG + 8 * j:HALO + SEG + 8 * j + 8], [("hT", 32)], 8, None, True,
                        self.moutS[:, :, 8 * j:8 * j + 8], ("moutS", j), gate_off=8 * j)
        self.dma("sp", o_convs[j], M["convbuf"][:, :, 0:3], "out", r=[("cb", 0), ("cb", 1)])
        self.dma("sp", o_Cs[j], M["CT"], "out", r=["CT"])
        self.dma("sp", o_ms[j], M["m_row"][0:4], "out", r=["m_row"])
    if DEBUG.get("mdump"):
        for nm, shp, dt in (("convbuf", [128, 8, 131], F32), ("c_act", [128, 8, 128], BF16), ("vaug", [128, 4, 257], BF16),
                            ("kmw", [128, 4, 128], BF16), ("gt", [128, 8], F32), ("lf", [128, 4], F32), ("ie", [128, 4], F32),
                            ("b_tok", [128, 4], F32), ("a_tok", [128, 4], F32), ("a_row", [128, 128], F32),
                            ("Mend_bc", [128, 4], F32), ("wk", [128, 4], F32), ("wC", [128, 4], F32), ("CT", [128, 4, 257], F32),
                            ("m_bc", [128, 4], F32), ("hh", [128, 256], F32), ("hnm", [128, 8, 128], F32), ("st", [128, 8], F32),
                            ("emt", [128, 4], F32), ("qmT", [128, 4, 128], BF16), ("kmT", [128, 4, 128], BF16), ("DT", [128, 128], F32),
                            ("Wbc", [128, 128], F32), ("M_row", [128, 128], F32)):
            DEBUG["md_" + nm] = True
            self.dbg("md_" + nm, M[nm], shp, dt)
        for nm, ap, shp, dt in (("hTp1", M["hTp"][1], [128, 8, 128], BF16), ("xn1", self.W1["xn"][1], [128, 1024], BF16),
                                ("xt1", self.W1["xt"][1], [128, 1024], F32), ("ss", self.W1["ss"], [128, 4], F32),
                                ("wg", M["wg"], [128, 8, 8], BF16), ("mult", C["mult"], [128, 8, 5], F32)):
            DEBUG["md_" + nm] = True
            self.dbg("md_" + nm, ap, shp, dt)
    A.pop()
    self.P.barrier()
    self.dbg("mout", self.hT[:, :, 0:SEG], [128, 8, SEG], BF16)
    self.dbg("moutS", self.moutS, [128, 8, TS], BF16)


Builder.phase_mlstm = _phase_mlstm


def _phase_out(self):
    A, P, ps, C, hT = self.A, self.P, self.ps, self.C, self.hT
    wv_in = self.w_in.rearrange("(c p) n -> p c n", p=128)
    A.push()
    wpa = A.alloc([4, 1024], BF16)
    wpm = A.alloc([8, 1024], BF16)
    wout = A.alloc([8, 1024], BF16)
    wga2 = [A.alloc([8, 128], BF16) for _ in range(2)]
    wgm2 = [A.alloc([8, 128], BF16) for _ in range(2)]
    merged = A.alloc([8, SEG + TS], BF16)
    fgain = A.alloc([1024])
    sga, sgm, t1, t2 = (A.alloc([512]) for _ in range(4))
    xt = [A.alloc([1024])] * 2
    yt = [A.alloc([1024]) for _ in range(2)]
    ss = A.alloc([4])
    self.dma("sp", fgain, self.inp("fgain_bc", [128, 1024]), "ld0", w=["fgain"])
    wpa_d = self.inp("w_pa", [512, 1024]).rearrange("(c p) n -> p c n", p=128)
    wpm_d = self.inp("w_pm", [1024, 1024]).rearrange("(c p) n -> p c n", p=128)
    wout_d = self.inp("w_out", [1024, 1024]).rearrange("(c p) n -> p c n", p=128)
    for c in range(4):
        self.dma("pool", wpa[:, c, :], wpa_d[:, c, :], "ldw", w=["wpa"])
    for c in range(8):
        self.dma("pool", wpm[:, c, :], wpm_d[:, c, :], "ldw", w=["wpm"])
        self.dma("pool", wout[:, c, :], wout_d[:, c, :], "ldw", w=["wout"])
    chunks = []
    for k in range(4):
        chunks.append(dict(n=512, m0=512 * k,
                           att=lambda hp, k=k: self.attT[:, hp, 512 * k:512 * k + 512],
                           mout=lambda fb, k=k: hT[:, fb, 512 * k:512 * k + 512],
                           hs=lambda c, k=k: hT[:, c, HALO + 512 * k:HALO + 512 * k + 512]))
    chunks.append(dict(n=TS, m0=SEG,
                       att=lambda hp: self.attT[:, hp, SEG:SEG + TS],
                       mout=lambda fb: self.moutS[:, fb, :],
                       hs=lambda c: hT[:, c, HALO + SEG:HALO + SEG + TS]))
    for cb in range(8):
        cs = slice(cb * 128, (cb + 1) * 128)
        wga, wgm = wga2[cb % 2], wgm2[cb % 2]
        kga, kgm = ("wga", cb % 2), ("wgm", cb % 2)
        self.dma("pool", wga, wv_in[:, :, OFF_GA + cb * 128:OFF_GA + (cb + 1) * 128], "ldw", w=[kga])
        self.dma("pool", wgm, wv_in[:, :, OFF_GM + cb * 128:OFF_GM + (cb + 1) * 128], "ldw", w=[kgm])
        for ch in chunks:
            n = ch["n"]
            for hp in range(4):
                self.mm(ps[:, 0, 0:n], wpa[:, hp, cs], ch["att"](hp), start=(hp == 0), stop=(hp == 3), r=["wpa"], w=[("P", 0)])
            for fb in range(8):
                self.mm(ps[:, 1, 0:n], wpm[:, fb, cs], ch["mout"](fb), start=(fb == 0), stop=(fb == 7), r=["wpm"], w=[("P", 1)])
            for c in range(8):
                self.mm(ps[:, 2, 0:n], wga[:, c, :], ch["hs"](c), start=(c == 0), stop=(c == 7), r=[kga], w=[("P", 2)])
            for c in range(8):
                self.mm(ps[:, 3, 0:n], wgm[:, c, :], ch["hs"](c), start=(c == 0), stop=(c == 7), r=[kgm], w=[("P", 3)])
            self.act(sga[:, 0:n], ps[:, 2, 0:n], AF.Sigmoid, r=[("P", 2)], w=["sga"])
            self.act(sgm[:, 0:n], ps[:, 3, 0:n], AF.Sigmoid, r=[("P", 3)], w=["sgm"])
            self.v("dve", "tensor_tensor", t1[:, 0:n], ps[:, 0, 0:n], sga[:, 0:n], ALU.mult, r=[("P", 0), "sga"], w=["t1"])
            self.v("dve", "tensor_tensor", t2[:, 0:n], ps[:, 1, 0:n], sgm[:, 0:n], ALU.mult, r=[("P", 1), "sgm"], w=["t2"])
            self.v("pool", "tensor_tensor", merged[:, cb, ch["m0"]:ch["m0"] + n], t1[:, 0:n], t2[:, 0:n], ALU.add,
                   r=["t1", "t2"], w=[("merged", cb)])
    self.dbg("merged", merged, [128, 8, SEG + TS], BF16)
    xh = self.din["xh"]
    xs = self.din["xs"]
    y_p = self.outp("y_p", [SEG, 1024])
    y_s = self.outp("y_s", [TS, 1024])
    tiles = [(xh[HALO + j * 128:HALO + (j + 1) * 128, :], y_p[j * 128:(j + 1) * 128, :], 128, j * 128, C["gate_p"]) for j in range(NLT)]
    tiles.append((xs, y_s, TS, SEG, C["gate_s"]))
    for i, (xsrc, ydst, n, m0, gate) in enumerate(tiles):
        i2 = i % 2
        self.dma("sp", xt[i2][0:n], xsrc, "ldx", w=[("xt", 0)])
        for half in range(2):
            hs_ = slice(half * 512, (half + 1) * 512)
            for cb in range(8):
                self.mm(ps[0:n, 4 + half, :], merged[:, cb, m0:m0 + n], wout[:, cb, hs_], start=(cb == 0), stop=(cb == 7),
                        r=["wout", ("merged", cb)], w=[("P", 4 + half)])
            self.v("dve", "tensor_tensor", yt[i2][0:n, hs_], ps[0:n, 4 + half, :], gate[0:n, hs_], ALU.mult,
                   r=[("P", 4 + half), ("gate_p", half), ("gate_s", half)], w=[("yt", i2, half)])
            self.v("pool", "tensor_tensor", yt[i2][0:n, hs_], yt[i2][0:n, hs_], xt[i2][0:n, hs_], ALU.add,
                   r=[("yt", i2, half), ("xt", 0)], w=[("yt", i2, half)])
        sl = ss[:, i % 4:i % 4 + 1]
        sk = ("ss", i % 4)
        self.act(xt[0][0:n], yt[i2][0:n], AF.Square, r=[("yt", i2, 0), ("yt", i2, 1)], w=[("xt", 0), sk], accum_out=sl[0:n])
        self.v("dve", "tensor_scalar", sl[0:n], sl[0:n], 1.0 / D, EPS, op0=ALU.mult, op1=ALU.add, r=[sk], w=[sk])
        self.act(sl[0:n], sl[0:n], AF.Ln, r=[sk], w=[sk])
        self.act(sl[0:n], sl[0:n], AF.Exp, r=[sk], w=[sk], scale=-0.5)
        self.act(yt[i2][0:n], yt[i2][0:n], AF.Copy, r=[("yt", i2, 0), ("yt", i2, 1), sk], w=[("yt", i2, 0), ("yt", i2, 1)], scale=sl[0:n])
        self.v("pool", "tensor_tensor", yt[i2][0:n], yt[i2][0:n], fgain[0:n], ALU.mult,
               r=[("yt", i2, 0), ("yt", i2, 1), "fgain"], w=[("yt", i2, 0), ("yt", i2, 1)])
        self.dma("sp", ydst, yt[i2][0:n], "out", r=[("yt", i2, 0), ("yt", i2, 1)])
    A.pop()


Builder.phase_out = _phase_out


def _sample_copies(self):
    LB = [w for (w, d) in GROUPS]
    self.s_cache = cache = [self.inp("cache%d" % g, [4, LB[g], 2, 512]) for g in range(3)]
    self.s_kvs = kvs = [self.outp("kvs%d" % g, [4, LB[g], 2, 512]) for g in range(3)]
    self.s_cat = cat = [self.scratch("cat%d" % g, [4, LB[g] + 8, 2, 512]) for g in range(2)]
    for g in range(3):
        for j in range(4):
            nsp = 4 if g == 2 else 1
            rows = LB[g] - 8
            step = (rows + nsp - 1) // nsp
            for a in range(0, rows, step):
                b_ = min(rows, a + step)
                self.dma("sp", kvs[g][j, a:b_], cache[g][j, 8 + a:8 + b_], "cpy", w=[("kvs", g, j, a)])
            if g < 2:
                self.dma("sp", cat[g][j, 0:LB[g]], cache[g][j], "cpy", w=[("catb", g, j)])


def _phase_sample_attn(self):
    A, P, ps, C, hT = self.A, self.P, self.ps, self.C, self.hT
    wv_in = self.w_in.rearrange("(c p) n -> p c n", p=128)
    LB = [w for (w, d) in GROUPS]
    cache, kvs, cat = self.s_cache, self.s_kvs, self.s_cat
    A.push()
    ident, F_sb = C["ident"], C["F"]
    ones = A.alloc([128])
    self.v("pool", "memset", ones, 1.0, w=["s_ones"])
    z8 = A.alloc([8])
    self.v("pool", "memset", z8, 0.0, w=["z8"])
    onesel = A.alloc([8, 8])
    self.dma("sp", onesel, self.inp("onesel", [128, 8, 8]), "ld0", w=["onesel"])
    selt = A.alloc([8, 128])
    self.dma("sp", selt[0:8], self.inp("selt", [8, 8, 128]), "ld0", w=["selt"])
    biasS = A.alloc([3, 8])
    bold = A.alloc([3, 8])
    tmpF = A.alloc([8])
    dgF = A.alloc([8])
    for g in range(3):
        self.mm(ps[:, 0, 0:8], F_sb[0:8, g, 0:128], ident[0:8, 0:8], r=[("F", g), "ident"], w=[("P", 0)])
        self.act(tmpF, ps[:, 0, 0:8], AF.Copy, r=[("P", 0)], w=["tmpF"])
        self.mm(ps[:, 0, 8:16], C["antiid"], tmpF, r=["tmpF", "antiid"], w=[("P", 0)])
        self.act(biasS[:, g, :], ps[:, 0, 8:16], AF.Copy, r=[("P", 0)], w=[("biasS", g)])
        self.v("dve", "tensor_tensor", dgF[0:8], ident[0:8, 0:8], F_sb[0:8, g, 128:129].to_broadcast([8, 8]), ALU.mult,
               r=[("F", g), "ident"], w=["dgF"])
        self.mm(ps[0:8, 0, 16:24], ones[0:8, 0:8], dgF[0:8], r=["dgF", "s_ones"], w=[("P", 0)])
        self.act(bold[0:8, g, :], ps[0:8, 0, 16:24], AF.Copy, r=[("P", 0)], w=[("bold", g)])
    wq = [A.alloc([8, 512], BF16) for _ in range(2)]
    qj = [A.alloc([1536]) for _ in range(4)]
    kj = A.alloc([1536])
    vj = A.alloc([1536])
    oldkv = [A.alloc([3, 2, 512]) for _ in range(1)][0]
    wcnt = 0
    for kind, off in (("q", OFF_Q), ("k", OFF_K), ("v", OFF_V)):
        for g in range(3):
            wb = wq[wcnt % 2]
            wk_ = ("swq", wcnt % 2)
            wcnt += 1
            self.dma("pool", wb, wv_in[:, :, off + g * 512:off + (g + 1) * 512], "ldw", w=[wk_])
            for j in range(4):
                bank = 1 + (j % 2)
                for c in range(8):
                    self.mm(ps[0:8, bank, :], hT[:, c, HALO + SEG + 8 * j:HALO + SEG + 8 * j + 8], wb[:, c, :],
                            start=(c == 0), stop=(c == 7), r=[wk_, (("hT", 32), c)], w=[("P", bank)])
                if kind == "q":
                    self.act(qj[j][0:8, g * 512:(g + 1) * 512], ps[0:8, bank, :], AF.Copy, r=[("P", bank)], w=[("qj", j, g)], scale=0.125)
                else:
                    st = kj if kind == "k" else vj
                    sk = ("kvst", kind, j % 3)
                    sl_ = st[0:8, (j % 3) * 512:(j % 3 + 1) * 512]
                    self.act(sl_, ps[0:8, bank, :], AF.Copy, r=[("P", bank)], w=[sk])
                    kvi = 0 if kind == "k" else 1
                    self.dma("sp", kvs[g][j, LB[g] - 8:LB[g], kvi, :], sl_, "out", r=[sk], w=[("kvsn", g, j, kvi)])
                    if g < 2:
                        self.dma("sp", cat[g][j, LB[g]:LB[g] + 8, kvi, :], sl_, "out", r=[sk], w=[("catn", g, j, kvi)])
    szs = A.alloc([4, TS])
    wz = wq[0]
    self.dma("pool", wz, wv_in[:, :, OFF_ZA:OFF_ZA + 512], "ldw", w=[("swq", 0)])
    for hp in range(4):
        for c in range(8):
            self.mm(ps[:, 3, 0:TS], wz[:, c, hp * 128:(hp + 1) * 128], hT[:, c, HALO + SEG:HALO + SEG + TS], start=(c == 0), stop=(c == 7),
                    r=[("swq", 0), (("hT", 32), c)], w=[("P", 3)])
        self.act(szs[:, hp, :], ps[:, 3, 0:TS], AF.Silu, r=[("P", 3)], w=[("szs", hp)])
    NKG = 6
    Kg = [A.alloc([2, 512]) for _ in range(NKG)]
    prod = A.alloc([512])
    lg = [A.alloc([8]) for _ in range(4)]
    Pz = [A.alloc([8, 8]) for _ in range(NKG)]
    numS = A.alloc([512])
    denS = A.alloc([8])
    pold = A.alloc([8])
    tmpo = A.alloc([512])
    oj = A.alloc([512])
    u = 0
    for j in range(4):
        for g in range(3):
            self.dma("sp", oldkv[0:8, g], cache[g][j, 0:8], "gat", w=[("old", g)])
        self.mm(ps[0:8, 5, :], z8[0:8, 0:8], qj[j][0:8, 0:512], start=True, stop=False, r=["z8", ("qj", j, 0)], w=[("P", 5)])
        self.mm(ps[0:8, 6, 0:8], z8[0:8, 0:8], qj[j][0:8, 0:8], start=True, stop=False, r=["z8", ("qj", j, 0)], w=[("P", 6)])
        first = False
        for g, (win, d) in enumerate(GROUPS):
            for t in range(8):
                kb = Kg[u % NKG]
                kk = ("Kg", u % NKG)
                pz = Pz[u % NKG]
                pk = ("Pz", u % NKG)
                lgt = lg[u % 4]
                lk = ("lg", u % 4)
                u += 1
                a0 = d + t
                if g < 2:
                    src = cat[g][j, a0:a0 + 127 * d + 1:d]
                    deps = [("catb", g, j), ("catn", g, j, 0), ("catn", g, j, 1)]
                else:
                    src = kvs[2][j, a0 - 8:a0 - 8 + 127 * d + 1:d]
                    deps = [("kvs", 2, j, a) for a in range(0, LB[2] - 8, (LB[2] - 8 + 3) // 4)] + [("kvsn", 2, j, 0), ("kvsn", 2, j, 1)]
                self.dma("sp", kb, src, "gat", r=deps, w=[kk])
                self.mm(ps[:, 4, :], selt[0:8, t, :], qj[j][0:8, g * 512:(g + 1) * 512], r=["selt", ("qj", j, g)], w=[("P", 4)])
                self.v("dve", "tensor_tensor", prod, kb[:, 0, :], ps[:, 4, :], ALU.mult, r=[kk, ("P", 4)], w=["prod"])
                self.v("dve", "tensor_reduce", lgt, prod.rearrange("p (h e) -> p h e", h=8), AX.X, ALU.add, r=["prod"], w=[lk])
                self.v("dve", "tensor_tensor", lgt, lgt, biasS[:, g, :], ALU.add, r=[lk, ("biasS", g)], w=[lk])
                pe_ = pz[:, 0, :]
                self.act(pe_, lgt, AF.Exp, r=[lk], w=[pk])
                last = (g == 2 and t == 7)
                kv3 = kb[:, 1, :].rearrange("p (h e) -> p h e", h=8)
                self.v("dve", "tensor_tensor", kv3, kv3, pe_.unsqueeze(2).to_broadcast([128, 8, 64]), ALU.mult, r=[pk, kk], w=[kk])
                self.mm(ps[0:8, 5, :], onesel[:, t, :], kb[:, 1, :], start=False, stop=last, r=["onesel", kk], w=[("P", 5)])
                self.mm(ps[0:8, 6, 0:8], onesel[:, t, :], pe_, start=False, stop=last, r=["onesel", pk], w=[("P", 6)])
        self.v("dve", "tensor_copy", numS[0:8], ps[0:8, 5, :], r=[("P", 5)], w=["numS"])
        self.v("dve", "tensor_copy", denS[0:8], ps[0:8, 6, 0:8], r=[("P", 6)], w=["denS"])
        for g in range(3):
            self.v("dve", "tensor_tensor", tmpo[0:8], qj[j][0:8, g * 512:(g + 1) * 512], oldkv[0:8, g, 0, :], ALU.mult,
                   r=[("qj", j, g), ("old", g)], w=["tmpo"])
            self.v("dve", "tensor_reduce", pold[0:8], tmpo[0:8].rearrange("p (h e) -> p h e", h=8), AX.X, ALU.add, r=["tmpo"], w=["pold"])
            self.v("dve", "tensor_tensor", pold[0:8], pold[0:8], bold[0:8, g, :], ALU.add, r=["pold", ("bold", g)], w=["pold"])
            self.act(pold[0:8], pold[0:8], AF.Exp, r=["pold"], w=["pold"])
            self.v("dve", "tensor_tensor", denS[0:8], denS[0:8], pold[0:8], ALU.add, r=["denS", "pold"], w=["denS"])
            self.v("dve", "tensor_tensor", tmpo[0:8].rearrange("p (h e) -> p h e", h=8),
                   oldkv[0:8, g, 1, :].rearrange("p (h e) -> p h e", h=8), pold[0:8].unsqueeze(2).to_broadcast([8, 8, 64]), ALU.mult,
                   r=[("old", g), "pold"], w=["tmpo"])
            self.v("dve", "tensor_tensor", numS[0:8], numS[0:8], tmpo[0:8], ALU.add, r=["numS", "tmpo"], w=["numS"])
        self.v("dve", "reciprocal", denS[0:8], denS[0:8], r=["denS"], w=["denS"])
        self.v("dve", "tensor_tensor", oj[0:8].rearrange("p (h e) -> p h e", h=8), numS[0:8].rearrange("p (h e) -> p h e", h=8),
               denS[0:8].unsqueeze(2).to_broadcast([8, 8, 64]), ALU.mult, r=["numS", "denS"], w=["oj"])
        for hp in range(4):
            self.tr(ps[:, 7, 0:8], oj[0:8, hp * 128:(hp + 1) * 128], ident[0:8, 0:8], r=["oj", "ident"], w=[("P", 7)])
            self.v("dve", "tensor_tensor", self.attT[:, hp, SEG + 8 * j:SEG + 8 * j + 8], ps[:, 7, 0:8], szs[:, hp, 8 * j:8 * j + 8], ALU.mult,
                   r=[("P", 7), ("szs", hp)], w=[("attTs", hp, j)])
    A.pop()


Builder.phase_sample_attn = _phase_sample_attn
Builder.sample_copies = _sample_copies


_PROG_CACHE = {}


def kernel(**inputs):
    inp = {k: np.asarray(v) for k, v in inputs.items()}
    if "prog" not in _PROG_CACHE:
        _PROG_CACHE["prog"] = build_program(99)
    b = _PROG_CACHE["prog"]
    cores = list(range(NCORES))
    maps = make_in_maps(inp, cores)
    maps = [{k: np.ascontiguousarray(v, dtype=np.float32) for k, v in m.items() if k in b.din} for m in maps]
    res = run_bass_kernel_spmd(b.nc, maps, core_ids=cores).results
    f32 = np.float32
    y_p = np.zeros((2, SEQ, D), f32)
    y_s = np.zeros((32, 8, D), f32)
    kvp = [np.zeros((1, 2, w, 2, 8, 64), f32) for (w, d) in GROUPS]
    kvs = [np.zeros((1, 32, w, 2, 8, 64), f32) for (w, d) in GROUPS]
    conv_p = np.zeros((1, 2, 3, D), f32)
    conv_s = np.zeros((1, 32, 3, D), f32)
    C_p = np.zeros((1, 2, 4, 256, 128), f32)
    C_s = np.zeros((1, 32, 4, 256, 128), f32)
    n_p = np.zeros((1, 2, 4, 128), f32)
    n_s = np.zeros((1, 32, 4, 128), f32)
    m_p = np.zeros((1, 2, 4), f32)
    m_s = np.zeros((1, 32, 4), f32)
    for c in cores:
        r = res[c]
        bb, p = c // 4, c % 4
        sq = slice(4 * c, 4 * c + 4)
        y_p[bb, p * SEG:(p + 1) * SEG] = r["y_p"]
        y_s[sq] = np.asarray(r["y_s"]).reshape(4, 8, D)
        for g in range(3):
            kvs[g][0, sq] = np.asarray(r["kvs%d" % g]).reshape(4, -1, 2, 8, 64)
        Cs = np.asarray(r["Cs"])
        C_s[0, sq] = Cs[..., :256].transpose(0, 2, 3, 1)
        n_s[0, sq] = Cs[..., 256].transpose(0, 2, 1)
        m_s[0, sq] = np.asarray(r["ms"])[:, :, 0]
        conv_s[0, sq] = np.asarray(r["convs"]).transpose(0, 3, 2, 1).reshape(4, 3, D)
        if p == 3:
            for g in range(3):
                kvp[g][0, bb] = np.asarray(r["kvp%d" % g]).reshape(-1, 2, 8, 64)
            Cp = np.asarray(r["Cp"])
            C_p[0, bb] = Cp[..., :256].transpose(1, 2, 0)
            n_p[0, bb] = Cp[..., 256].T
            m_p[0, bb] = np.asarray(r["mp"])[:, 0]
            conv_p[0, bb] = np.asarray(r["convp"]).transpose(2, 1, 0).reshape(3, D)
    return (y_p, y_s, kvp[0], kvs[0], kvp[1], kvs[1], kvp[2], kvs[2],
            conv_p, conv_s, C_p, C_s, n_p, n_s, m_p, m_s)


def _mlstm_prefix(self):
    A, P, ps, C, M, hT = self.A, self.P, self.ps, self.C, self.M, self.hT
    NT = 48
    xf = self.inp("xf", [PREFIX - HALO, 1024])
    ident, tri = C["ident"], C["tri"]
    hTp = [A.alloc([8, 128], BF16) for _ in range(2)]
    gt_all = A.alloc([NT, 8])
    lf = A.alloc([NT, 4])
    ie = A.alloc([NT, 4])
    tot = A.alloc([NT, 4])
    incl = A.alloc([NT, 4])
    a_all = A.alloc([NT, 4])
    wk_all = A.alloc([NT, 4])
    negtv = A.alloc([NT])
    pm = A.alloc([4])
    row1 = A.alloc([1])
    dg = A.alloc([4])
    Mg_bc = A.alloc([4])
    cb4 = A.alloc([4, 8, 131])
    hT4 = A.alloc([8, 512], BF16)
    caccs = [A.alloc([8, 128]) for _ in range(2)]
    cexp = A.alloc([8, 128])
    c_act = [A.alloc([8, 128], BF16) for _ in range(2)]
    vaug = [A.alloc([4, 257], BF16) for _ in range(2)]
    kmw = [A.alloc([4, 128], BF16) for _ in range(2)]
    for i in range(2):
        self.v("pool", "memset", vaug[i][:, :, 256:257], 1.0, w=[("pvaug1", i)])

    hTd = self.scratch("hTd", [32, 128, 8, 128], BF16)

    def tile_src(i, slot):
        if i < 32:
            hp_ = hTp[i % 2]
            self.norm_tile(xf[i * 128:(i + 1) * 128, :], 128, hp_, "p", slot, ("hTp", i % 2), bank0=5)
            self.dma("sp", hTd[i], hp_, "hts", r=[(("hTp", i % 2), c) for c in range(8)], w=[("hTd", i)])
            return hp_, [("hTp", i % 2)]
        ti = i - 32
        return hT[:, :, ti * 128:(ti + 1) * 128], [("hT", ti)]

    for i in range(NT):
        hTt, hk = tile_src(i, i)
        bank = 7 if i % 2 == 0 else 4
        gps = ps[:, bank, 0:8]
        for c in range(8):
            self.mm(gps, hTt[:, c, :], M["wg"][:, c, :], start=(c == 0), stop=(c == 7),
                    r=["wg"] + [(k, c) for k in hk], w=[("P", bank)])
        self.v("dve", "tensor_tensor", gt_all[:, i, :], gps, C["b_if_bc"], ALU.add, r=[("P", bank), "b_if_bc"], w=["gt_all"])
    tv48 = C["tvalid"][:, 0:NT]
    self.v("dve", "tensor_scalar", negtv, tv48, -1.0, None, op0=ALU.mult, r=["tvalid"], w=["negtv"])
    self.act(lf, gt_all[:, :, 4:8], AF.Exp, r=["gt_all"], w=["lf"], scale=-1.0)
    self.act(lf, lf, AF.Ln, r=["lf"], w=["lf"], bias=M["one1"])
    self.v("dve", "tensor_tensor", lf, lf, negtv.unsqueeze(2).to_broadcast([128, NT, 4]), ALU.mult, r=["lf", "negtv"], w=["lf"])
    self.v("dve", "tensor_tensor", ie, gt_all[:, :, 0:4], tv48.unsqueeze(2).to_broadcast([128, NT, 4]), ALU.mult,
           r=["gt_all", "tvalid"], w=["ie"])
    self.v("dve", "tensor_tensor", ie, ie, M["negbig"][:, 0:NT].unsqueeze(2).to_broadcast([128, NT, 4]), ALU.add,
           r=["ie", "negbig"], w=["ie"])
    lf2 = lf.rearrange("p t h -> p (t h)")
    self.mm(ps[:, 0, 0:NT * 4], tri, lf2, r=["lf", "tri"], w=[("P", 0)])
    self.mm(ps[:, 1, 0:NT * 4], M["ones"], lf2, r=["lf", "ones"], w=[("P", 1)])
    self.v("dve", "tensor_copy", tot.rearrange("p t h -> p (t h)"), ps[:, 1, 0:NT * 4], r=[("P", 1)], w=["tot"])
    for h in range(4):
        self.v("dve", "tensor_tensor_scan", incl[:, :, h], M["ones"][:, 0:NT], tot[:, :, h], 0.0, op0=ALU.mult, op1=ALU.add,
               r=["tot", "ones"], w=[("incl", h)])
    inclk = [("incl", h) for h in range(4)]
    self.v("dve", "tensor_tensor", a_all, ie, incl, ALU.subtract, r=["ie"] + inclk, w=["a_all"])
    self.v("dve", "tensor_tensor", a_all, a_all, tot, ALU.add, r=["a_all", "tot"], w=["a_all"])
    self.v("dve", "tensor_tensor", a_all.rearrange("p t h -> p (t h)"), a_all.rearrange("p t h -> p (t h)"), ps[:, 0, 0:NT * 4],
           ALU.subtract, r=["a_all", ("P", 0)], w=["a_all"])
    self.v("dve", "tensor_reduce", pm, a_all.rearrange("p t h -> p h t"), AX.X, ALU.max, r=["a_all"], w=["pm"])
    self.mm(ps[0:4, 2, 0:128], pm, ident, r=["pm", "ident"], w=[("P", 2)])
    self.v("dve", "tensor_reduce", row1[0:4], ps[0:4, 2, 0:128], AX.X, ALU.max, r=[("P", 2)], w=["row1"])
    self.v("dve", "tensor_scalar", row1[0:4], row1[0:4], 0.0, None, op0=ALU.max, r=["row1"], w=["row1"])
    self.v("dve", "tensor_tensor", dg[0:4], ident[0:4, 0:4], row1[0:4, 0:1].to_broadcast([4, 4]), ALU.mult, r=["row1", "ident"], w=["pdg"])
    self.mm(ps[:, 2, 128:132], M["ones"][0:4, :], dg[0:4], r=["pdg", "ones"], w=[("P", 2)])
    self.v("dve", "tensor_copy", Mg_bc, ps[:, 2, 128:132], r=[("P", 2)], w=["Mg_bc"])
    self.v("dve", "tensor_tensor", wk_all, a_all, Mg_bc.unsqueeze(1).to_broadcast([128, NT, 4]), ALU.subtract,
           r=["a_all", "Mg_bc"], w=["wk_all"])
    self.act(wk_all, wk_all, AF.Exp, r=["wk_all"], w=["wk_all"])
    self.v("dve", "tensor_tensor", M["m_bc"], incl[:, NT - 1, :], Mg_bc, ALU.add, r=inclk + ["Mg_bc"], w=["m_bc"])
    self.mm(ps[0:4, 2, 136:137], M["m_bc"][0:1, 0:4], M["ones"][0:1, 0:1], r=["m_bc", "ones"], w=[("P", 2)])
    self.v("dve", "tensor_copy", M["m_row"][0:4], ps[0:4, 2, 136:137], r=[("P", 2)], w=["m_row"])
    cw, cbias = C["conv_wT"], C["conv_bT"]
    hist = A.alloc([8, 3])
    self.v("pool", "memset", hist, 0.0, w=["hist"])
    for g0 in range(0, NT, 4):
        if g0 < 32:
            for q in range(4):
                i = g0 + q
                self.dma("sp", hT4[:, :, q * 128:(q + 1) * 128], hTd[i], "ldx", r=[("hTd", i)], w=[(("hT4", q), c) for c in range(8)])
            src4 = hT4
            hk4 = [("hT4", q) for q in range(4)]
            tsl = lambda q: hT4[:, :, q * 128:(q + 1) * 128]
        else:
            t0 = g0 - 32
            src4 = hT[:, :, t0 * 128:(t0 + 4) * 128]
            hk4 = [("hT", t0 + q) for q in range(4)]
            tsl = lambda q, t0=t0: hT[:, :, (t0 + q) * 128:(t0 + q + 1) * 128]
        for fb in range(8):
            bank = fb % 2
            for c in range(8):
                self.mm(ps[:, bank, :], M["wxm"][:, c, fb * 128:(fb + 1) * 128], src4[:, c, :], start=(c == 0), stop=(c == 7),
                        r=["wxm"] + [(k, c) for k in hk4], w=[("P", bank)])
            self.act(cb4[:, :, fb, 3:131], ps[:, bank, :].rearrange("p (q t) -> p q t", q=4), AF.Copy,
                     r=[("P", bank)], w=[("pcb", q) for q in range(4)])
        def tile_body(q):
            i = g0 + q
            par = i % 2
            cacc = caccs[par]
            vb0 = 2 if par == 0 else 0
            kmb = 4 if par == 0 else 7
            hTt, hk = tsl(q), [hk4[q]]
            tv = C["tvalid"][:, i:i + 1]
            cbp = cb4[:, q]
            for half in range(2):
                for c in range(8):
                    self.mm(ps[:, vb0 + half, :], hTt[:, c, :], M["wxm"][:, c, half * 512:(half + 1) * 512], start=(c == 0), stop=(c == 7),
                            r=["wxm"] + [(k, c) for k in hk], w=[("P", vb0 + half)])
                self.act(vaug[par][:, 2 * half:2 * half + 2, 0:256], ps[:, vb0 + half, :].rearrange("p (h v) -> p h v", h=2), AF.Copy,
                         r=[("P", vb0 + half), ("pvaug1", par)], w=[("pvaug", par)])
                yield
            for fb in range(8):
                self.act(cacc[:, fb, :], cbp[:, fb, 3:131], AF.Identity, r=[("pcb", q), "conv_wT", "conv_bT"], w=[("pcacc", par, fb)],
                         scale=cw[:, fb, 3:4], bias=cbias[:, fb:fb + 1])
            yield
            for j in range(3):
                for fb in range(8):
                    self.v("dve", "scalar_tensor_tensor", cacc[:, fb, :], cbp[:, fb, j:j + 128], cw[:, fb, j:j + 1], cacc[:, fb, :],
                           op0=ALU.mult, op1=ALU.add, r=[("pcb", q), ("pcacc", par, fb)], w=[("pcacc", par, fb)])
                yield
            for fb in range(8):
                self.act(c_act[par][:, fb, :], cacc[:, fb, :], AF.Silu, r=[("pcacc", par, fb)], w=[("pc_act", par, fb)])
            yield
            kmps = ps[:, kmb, :].rearrange("p (h k) -> p h k", h=4)
            for h in range(4):
                for ec in range(2):
                    self.mm(kmps[:, h, :], c_act[par][:, 2 * h + ec, :], M["wmk"][:, ec, h, :], start=(ec == 0), stop=(ec == 1),
                            r=[("pc_act", par, 2 * h + ec), "wmk"], w=[("P", kmb)])
            yield
            self.v("dve", "tensor_tensor", kmw[par], kmps, wk_all[:, i, :].unsqueeze(2).to_broadcast([128, 4, 128]), ALU.mult,
                   r=[("P", kmb), "wk_all"], w=[("pkmw", par)])
            yield
            for h in range(4):
                bk = vb0 + (h % 2)
                dps = ps[:, bk, 0:257]
                self.mm(dps, kmw[par][:, h, :], vaug[par][:, h, :], r=[("pkmw", par), ("pvaug", par)], w=[("P", bk)])
                self.v("dve", "tensor_tensor", M["CT"][:, h, :], M["CT"][:, h, :], dps, ALU.add, r=[("P", bk), "CT"], w=["CT"])
                yield

        for q in range(4):
            i = g0 + q
            tv = C["tvalid"][:, i:i + 1]
            cbp = cb4[:, q]
            self.v("pool", "tensor_copy", cbp[:, :, 0:3], hist, r=["hist"], w=[("pcb", q)])
            self.v("dve", "tensor_scalar", hist, cbp[:, :, 128:131], tv, M["one1"], op0=ALU.mult, op1=ALU.mult,
                   r=[("pcb", q), "tvalid", "one1"], w=["hist"])
        for pair in ((0, 1), (2, 3)):
            live = [tile_body(q) for q in pair]
            while live:
                for gen in list(live):
                    try:
                        next(gen)
                    except StopIteration:
                        live.remove(gen)
    self.v("pool", "tensor_copy", M["convbuf"][:, :, 0:3], hist, r=["hist"], w=[("cb", 0), ("cb", 1)])


Builder.mlstm_prefix = _mlstm_prefix
```
